# Optimizing a Trainium2 kernel written in Bass

```python
import math
import jax, jax.numpy as jnp
from jax import lax
import numpy as np

D_MODEL = 1024
BATCH = 4
SEQ = 8192
DEPTH = 2
DEC_BATCH = 8
DEC_SEQ = 16
PAST_LEN = 4096

CHUNK = 64
QBLOCK = 128
HA = 8
DHA = 64
DVA = 2 * DHA
HM = 4
DKM = 256
DVM = 256
CONV_M = 4
D_FF = 2816
CONV_F = 3
ROPE_THETA = 10000.0
LN_EPS = 1e-5
ALPHA = (2 * DEPTH) ** 0.25
BETA = (8 * DEPTH) ** -0.25

MIX_A = HA * DVA
MIX_B = HM * DVM
W_QA = HA * 2 * DHA
W_KA = HA * 2 * DHA
W_VA = HA * DVA
W_QKM = 2 * HM * DKM
W_VM = HM * DVM
W_OM = HM * DVM
W_GIF = 2 * HM
W_GA = MIX_A
W_GB = MIX_B
IN_SIZES = (W_QA, W_KA, W_VA, W_QKM, W_VM, W_OM, W_GIF, W_GA, W_GB)
D_IN = W_QA + W_KA + W_VA + W_QKM + W_VM + W_OM + W_GIF + W_GA + W_GB

kernel_name = 'hybrid_diffattn_mlstm_convffn_stream_step'


def _split_in(z):
    out = []
    o = 0
    for s in IN_SIZES:
        out.append(z[..., o:o + s])
        o += s
    return out


def _layernorm(x, g, b):
    xf = x.astype(jnp.float32)
    mu = jnp.mean(xf, axis=-1, keepdims=True)
    var = jnp.mean(jnp.square(xf - mu), axis=-1, keepdims=True)
    return ((xf - mu) * lax.rsqrt(var + LN_EPS) * g + b).astype(x.dtype)


def _rms(x):
    xf = x.astype(jnp.float32)
    return xf * lax.rsqrt(jnp.mean(jnp.square(xf), axis=-1, keepdims=True) + LN_EPS)


def _head_ln(h):
    mu = jnp.mean(h, axis=-1, keepdims=True)
    var = jnp.mean(jnp.square(h - mu), axis=-1, keepdims=True)
    return (h - mu) * lax.rsqrt(var + LN_EPS)


def _rope(x, pos):
    half = DHA // 2
    inv = ROPE_THETA ** (-jnp.arange(half, dtype=jnp.float32) * 2.0 / DHA)
    ang = pos.astype(jnp.float32)[:, None] * inv[None, :]
    cos = jnp.cos(ang)[None, :, None, None, :]
    sin = jnp.sin(ang)[None, :, None, None, :]
    xf = x.astype(jnp.float32)
    x1, x2 = xf[..., :half], xf[..., half:]
    return jnp.concatenate([x1 * cos - x2 * sin, x2 * cos + x1 * sin], axis=-1).astype(x.dtype)


def _causal_dwconv(x, buf, w, b):
    width = w.shape[0]
    L = x.shape[1]
    xp = jnp.concatenate([buf.astype(x.dtype), x], axis=1)
    y = b
    for j in range(width):
        y = y + w[j] * xp[:, j:j + L]
    return y.astype(x.dtype), xp[:, xp.shape[1] - (width - 1):]


def _diff_attn_block(q, k, v, lam, mask):
    s = jnp.einsum('bqhcd,bkhcd->bhcqk', q, k).astype(jnp.float32) * (DHA ** -0.5)
    if mask is not None:
        s = jnp.where(mask, s, -jnp.inf)
    pr = jax.nn.softmax(s, axis=-1)
    a = pr[:, :, 0] - lam * pr[:, :, 1]
    return jnp.einsum('bhqk,bkhv->bqhv', a.astype(v.dtype), v)


def _diff_attn_prompt(q, k, v, lam):
    B, S = q.shape[0], q.shape[1]
    nb = S // QBLOCK
    qb = q.reshape(B, nb, QBLOCK, HA, 2, DHA).swapaxes(0, 1)
    key_pos = jnp.arange(S)

    def block(args):
        qi, bi = args
        qpos = bi * QBLOCK + jnp.arange(QBLOCK)
        vis_end = (qpos // CHUNK + 1) * CHUNK
        mask = key_pos[None, :] < vis_end[:, None]
        return _diff_attn_block(qi, k, v, lam, mask)

    o = lax.map(block, (qb, jnp.arange(nb)))
    return o.swapaxes(0, 1).reshape(B, S, HA, DVA)


def _mlstm_chunk(carry, inp):
    C, n, m = carry
    q, k, v, ig, lf = inp
    L = q.shape[1]
    bcum = jnp.cumsum(lf, axis=1)
    causal = jnp.tril(jnp.ones((L, L), dtype=bool))[None, :, :, None]
    d = bcum[:, :, None, :] - bcum[:, None, :, :] + ig[:, None, :, :]
    d = jnp.where(causal, d, -jnp.inf)
    inter = bcum + m[:, None, :]
    m_t = jnp.maximum(inter, jnp.max(d, axis=2))
    w = jnp.exp(d - m_t[:, :, None, :])
    g = jnp.exp(inter - m_t)
    s = jnp.einsum('bthd,bshd->btsh', q, k) * w
    num = jnp.einsum('btsh,bshv->bthv', s, v) + g[..., None] * jnp.einsum('bthd,bhdv->bthv', q, C)
    den = jnp.sum(s, axis=2) + g * jnp.einsum('bthd,bhd->bth', q, n)
    den = jnp.maximum(jnp.abs(den), jnp.exp(-m_t))
    h = num / den[..., None]
    m_end = m_t[:, -1]
    w_end = jnp.exp(bcum[:, -1:] - bcum + ig - m_end[:, None, :])
    g_end = jnp.exp(bcum[:, -1] + m - m_end)
    kw = k * w_end[..., None]
    C_new = g_end[..., None, None] * C + jnp.einsum('bshd,bshv->bhdv', kw, v)
    n_new = g_end[..., None] * n + jnp.sum(kw, axis=1)
    return (C_new, n_new, m_end), h


def _mlstm_prompt(q, k, v, ig, lf):
    B, S = q.shape[0], q.shape[1]
    nc = S // CHUNK

    def to_chunks(a):
        return a.reshape((B, nc, CHUNK) + a.shape[2:]).swapaxes(0, 1)

    init = (jnp.zeros((B, HM, DKM, DVM), jnp.float32),
            jnp.zeros((B, HM, DKM), jnp.float32),
            jnp.zeros((B, HM), jnp.float32))
    state, h = lax.scan(_mlstm_chunk, init, tuple(map(to_chunks, (q, k, v, ig, lf))))
    return h.swapaxes(0, 1).reshape(B, S, HM, DVM), state


def _layer(x, pos, kv_cache, conv_m_buf, m_state, ffn_buf, p, lam_init):
    f32 = jnp.float32
    B, L = x.shape[0], x.shape[1]
    qa, ka, va, qkm, vm, om, gif, ga, gb = _split_in(x @ p['w_in'])
    qa = _rope(qa.reshape(B, L, HA, 2, DHA), pos)
    ka = _rope(ka.reshape(B, L, HA, 2, DHA), pos)
    va = va.reshape(B, L, HA, DVA)
    lp = p['lam'].astype(f32)
    lam = jnp.exp(jnp.sum(lp[0] * lp[1])) - jnp.exp(jnp.sum(lp[2] * lp[3])) + lam_init
    if kv_cache is None:
        oa = _diff_attn_prompt(qa, ka, va, lam)
    else:
        ck, cv = kv_cache
        k_all = jnp.concatenate([ck.reshape(B, ck.shape[1], HA, 2, DHA).astype(ka.dtype), ka], axis=1)
        v_all = jnp.concatenate([cv.astype(va.dtype), va], axis=1)
        oa = _diff_attn_block(qa, k_all, v_all, lam, None)
    oa = (_rms(oa) * p['subln_g'] * (1.0 - lam_init)).astype(x.dtype).reshape(B, L, MIX_A)
    qk, new_conv_m = _causal_dwconv(qkm, conv_m_buf, p['conv_m_w'], p['conv_m_b'])
    qk = jax.nn.silu(qk).astype(f32)
    qm = qk[..., :HM * DKM].reshape(B, L, HM, DKM)
    km = qk[..., HM * DKM:].reshape(B, L, HM, DKM) * (DKM ** -0.5)
    vmh = vm.astype(f32).reshape(B, L, HM, DVM)
    gif = gif.astype(f32) + p['b_if'].astype(f32)
    ig = gif[..., :HM]
    lf = jax.nn.log_sigmoid(gif[..., HM:])
    if m_state is None:
        hm, (C, n, m) = _mlstm_prompt(qm, km, vmh, ig, lf)
    else:
        init = (m_state[0].astype(f32), m_state[1].astype(f32), m_state[2].astype(f32))
        (C, n, m), hm = _mlstm_chunk(init, (qm, km, vmh, ig, lf))
    hm = (_head_ln(hm).reshape(B, L, MIX_B) * p['mh_g']).astype(x.dtype) * jax.nn.sigmoid(om)
    y = jax.nn.sigmoid(ga) * oa + jax.nn.sigmoid(gb) * hm
    x = _layernorm(ALPHA * x + y @ p['w_out'], p['ln1_g'], p['ln1_b'])
    u, new_ffn = _causal_dwconv(x @ p['w_up'], ffn_buf, p['ffn_conv_w'], p['ffn_conv_b'])
    h = jax.nn.gelu(u[..., :D_FF], approximate=False) * u[..., D_FF:]
    x = _layernorm(ALPHA * x + h @ p['w_down'], p['ln2_g'], p['ln2_b'])
    new_k = ka.reshape(B, L, HA, 2 * DHA)
    return x, (new_k, va, new_conv_m, C, n, m, new_ffn)


def setup_inputs(seed: int = 0) -> dict:
    key = jax.random.key(seed)
    ks = jax.random.split(key, 26)
    f32 = jnp.float32

    def nrm(k, shape, scale=1.0):
        return scale * jax.random.normal(k, shape, f32)

    b_if = jnp.concatenate([nrm(ks[10], (DEPTH, HM), 0.1),
                            jnp.linspace(3.0, 6.0, HM, dtype=f32)[None, :] + nrm(ks[11], (DEPTH, HM), 0.01)], axis=-1)
    return {
        'x_prompt': nrm(ks[0], (BATCH, SEQ, D_MODEL)),
        'x_sample': nrm(ks[1], (DEC_BATCH, DEC_SEQ, D_MODEL)),
        'cache_k': nrm(ks[2], (DEPTH, DEC_BATCH, PAST_LEN, HA, 2 * DHA)),
        'cache_v': nrm(ks[3], (DEPTH, DEC_BATCH, PAST_LEN, HA, DVA)),
        'state_mlstm_conv': nrm(ks[4], (DEPTH, DEC_BATCH, CONV_M - 1, W_QKM)),
        'state_mlstm_C': nrm(ks[5], (DEPTH, DEC_BATCH, HM, DKM, DVM), 0.05),
        'state_mlstm_n': nrm(ks[6], (DEPTH, DEC_BATCH, HM, DKM), 0.05),
        'state_mlstm_m': nrm(ks[7], (DEPTH, DEC_BATCH, HM)),
        'state_ffn_conv': nrm(ks[8], (DEPTH, DEC_BATCH, CONV_F - 1, 2 * D_FF)),
        'w_in': nrm(ks[9], (DEPTH, D_MODEL, D_IN), D_MODEL ** -0.5),
        'b_if': b_if,
        'mlstm_conv_w': nrm(ks[12], (DEPTH, CONV_M, W_QKM), CONV_M ** -0.5),
        'mlstm_conv_b': nrm(ks[13], (DEPTH, W_QKM), 0.01),
        'diff_lambda': nrm(ks[14], (DEPTH, 4, DHA), 0.1),
        'diff_subln_g': 1.0 + nrm(ks[15], (DEPTH, DVA), 0.01),
        'mlstm_norm_g': 1.0 + nrm(ks[16], (DEPTH, MIX_B), 0.01),
        'w_out': nrm(ks[17], (DEPTH, MIX_A, D_MODEL), BETA * MIX_A ** -0.5),
        'ln1_g': 1.0 + nrm(ks[18], (DEPTH, D_MODEL), 0.01),
        'ln1_b': nrm(ks[19], (DEPTH, D_MODEL), 0.01),
        'w_up': nrm(ks[20], (DEPTH, D_MODEL, 2 * D_FF), D_MODEL ** -0.5),
        'ffn_conv_w': nrm(ks[21], (DEPTH, CONV_F, 2 * D_FF), CONV_F ** -0.5),
        'ffn_conv_b': nrm(ks[22], (DEPTH, 2 * D_FF), 0.01),
        'w_down': nrm(ks[23], (DEPTH, D_FF, D_MODEL), BETA * D_FF ** -0.5),
        'ln2_g': 1.0 + nrm(ks[24], (DEPTH, D_MODEL), 0.01),
        'ln2_b': nrm(ks[25], (DEPTH, D_MODEL), 0.01),
    }


def _stk(lst, i):
    return jnp.stack([s[i] for s in lst])


def reference(x_prompt, x_sample, cache_k, cache_v, state_mlstm_conv, state_mlstm_C, state_mlstm_n,
              state_mlstm_m, state_ffn_conv, w_in, b_if, mlstm_conv_w, mlstm_conv_b, diff_lambda,
              diff_subln_g, mlstm_norm_g, w_out, ln1_g, ln1_b, w_up, ffn_conv_w, ffn_conv_b, w_down,
              ln2_g, ln2_b):
    xp, xs = x_prompt, x_sample
    bp = xp.shape[0]
    pos_p = jnp.arange(xp.shape[1])
    pos_s = PAST_LEN + jnp.arange(xs.shape[1])
    sp, ss = [], []
    for l in range(DEPTH):
        p = {'w_in': w_in[l], 'b_if': b_if[l], 'conv_m_w': mlstm_conv_w[l], 'conv_m_b': mlstm_conv_b[l],
             'lam': diff_lambda[l], 'subln_g': diff_subln_g[l], 'mh_g': mlstm_norm_g[l], 'w_out': w_out[l],
             'ln1_g': ln1_g[l], 'ln1_b': ln1_b[l], 'w_up': w_up[l], 'ffn_conv_w': ffn_conv_w[l],
             'ffn_conv_b': ffn_conv_b[l], 'w_down': w_down[l], 'ln2_g': ln2_g[l], 'ln2_b': ln2_b[l]}
        lam_init = 0.8 - 0.6 * math.exp(-0.3 * l)
        conv_m0 = jnp.zeros((bp, CONV_M - 1, W_QKM), xp.dtype)
        ffn0 = jnp.zeros((bp, CONV_F - 1, 2 * D_FF), xp.dtype)
        xp, st_p = _layer(xp, pos_p, None, conv_m0, None, ffn0, p, lam_init)
        xs, st_s = _layer(xs, pos_s, (cache_k[l], cache_v[l]), state_mlstm_conv[l],
                          (state_mlstm_C[l], state_mlstm_n[l], state_mlstm_m[l]), state_ffn_conv[l], p, lam_init)
        sp.append(st_p)
        ss.append(st_s)
    return (xp, xs,
            _stk(sp, 0), _stk(sp, 1), _stk(sp, 2), _stk(sp, 3), _stk(sp, 4), _stk(sp, 5), _stk(sp, 6),
            _stk(ss, 0), _stk(ss, 1), _stk(ss, 2), _stk(ss, 3), _stk(ss, 4), _stk(ss, 5), _stk(ss, 6))
```

```python
import math
import numpy as np
import concourse.bass as bass
import concourse.mybir as mybir
from concourse.bass_utils import run_bass_kernel_spmd

F32 = mybir.dt.float32
BF16 = mybir.dt.bfloat16
AF = mybir.ActivationFunctionType
ALU = mybir.AluOpType
AX = mybir.AxisListType

D = 1024
HA = 8
HM = 4
DFF = 2816
DIN = 9224
NS = 16
ALPHA = (2 * 2) ** 0.25
LN_EPS = 1e-5
C_QA, C_KA, C_VA, C_QKM, C_VM, C_OM, C_GIF, C_GA, C_GB = 0, 1024, 2048, 3072, 5120, 6144, 7168, 7176, 8200


class Buf:
    __slots__ = ("name", "lastw", "readers")

    def __init__(self, name=""):
        self.name = name
        self.lastw = None
        self.readers = []


class Sched:
    def __init__(self, nc, same_engine_sync=True):
        self.nc = nc
        self.engs = {"pe": nc.tensor, "act": nc.scalar, "dve": nc.vector, "pool": nc.gpsimd, "sp": nc.sync}
        self.ins = []
        self.dma_cnt = {}
        self.same = same_engine_sync
        self.phase = ""
        self.names = None

    def add(self, eng, meth, args, kwargs, reads=(), writes=(), dma=None):
        idx = len(self.ins)
        deps = set()
        for r in reads:
            if r.lastw is not None:
                deps.add(r.lastw)
        for w in writes:
            if w.lastw is not None:
                deps.add(w.lastw)
            deps.update(w.readers)
        for r in reads:
            r.readers.append(idx)
        for w in writes:
            w.lastw = idx
            w.readers = []
        dval = None
        if dma is not None:
            self.dma_cnt[dma] = self.dma_cnt.get(dma, 0) + 16
            dval = self.dma_cnt[dma]
        keep = set()
        for d in deps:
            de = self.ins[d]
            if de[5] is None and de[0] == eng:
                if eng == "pe" or not self.same:
                    continue
            keep.add(d)
        self.ins.append([eng, meth, args, kwargs, keep, dma, dval, False, 0, self.phase])
        return idx

    def emit(self):
        nc = self.nc
        for rec in self.ins:
            for d in rec[4]:
                de = self.ins[d]
                if de[5] is None:
                    de[7] = True
        cnt = {e: 0 for e in self.engs}
        for rec in self.ins:
            if rec[7]:
                cnt[rec[0]] += 1
                rec[8] = cnt[rec[0]]
        esem = {e: nc.alloc_semaphore(name="es_" + e) for e in self.engs}
        dsem = {}
        for k in self.dma_cnt:
            dsem[k] = nc.alloc_semaphore(name="ds_%d" % len(dsem))
        waited = {e: {} for e in self.engs}
        nwait = 0
        for rec in self.ins:
            eng, meth, args, kwargs, deps, dma, dval, sig, sigval, phase = rec
            E = self.engs[eng]
            need = {}
            for d in deps:
                de = self.ins[d]
                if de[5] is None:
                    s, v = esem[de[0]], de[8]
                else:
                    s, v = dsem[de[5]], de[6]
                if need.get(s, 0) < v:
                    need[s] = v
            for s, v in need.items():
                if waited[eng].get(s, 0) >= v:
                    continue
                E.wait_ge(s, v)
                waited[eng][s] = v
                nwait += 1
            ins = getattr(E, meth)(*args, **kwargs)
            if self.names is not None:
                self.names[ins.ins.name] = phase
            if dma is not None:
                ins.then_inc(dsem[dma], 16)
            elif sig:
                ins.then_inc(esem[eng], 1)
        for k, v in self.dma_cnt.items():
            nc.sync.wait_ge(dsem[k], v)
        return dict(n=len(self.ins), nwait=nwait, nsem=len(dsem) + 5)


def build(S=8192, P=4096, T=512, NL=2, dbg=None, same=True):
    nc = bass.Bass("TRN2", target_bir_lowering=False)
    sch = Sched(nc, same_engine_sync=same)
    NT = S // T
    NTAB = S + NS

    def din(name, shape, dt=F32):
        return nc.dram_tensor(name, list(shape), dt, kind="ExternalInput").ap()

    def dout(name, shape, dt=F32):
        return nc.dram_tensor(name, list(shape), dt, kind="ExternalOutput").ap()

    def dscr(name, shape, dt):
        return nc.dram_tensor(name, list(shape), dt, kind="Internal").ap()

    def sb(name, shape, dt=F32):
        return nc.alloc_sbuf_tensor(name, list(shape), dt).ap()

    xp = din("xp", [S, D]); xs = din("xs", [NS, D])
    ck = din("ck", [NL, P, D]); cv = din("cv", [NL, P, D])
    smc = din("smc", [NL, 3, 2048]); sC = din("sC", [NL, HM, 256, 256]); sn = din("sn", [NL, HM, 256])
    sm = din("sm", [NL, HM]); sfc = din("sfc", [NL, 2, 2 * DFF])
    w_in = din("w_in", [NL, D, DIN]); b_if = din("b_if", [NL, 8])
    mcw = din("mcw", [NL, 4, 2048]); mcb = din("mcb", [NL, 2048])
    dlam = din("dlam", [NL, 4, 64]); subg = din("subg", [NL, 128]); mhg = din("mhg", [NL, D])
    w_out = din("w_out", [NL, D, D]); ln1g = din("ln1g", [NL, D]); ln1b = din("ln1b", [NL, D])
    w_up = din("w_up", [NL, D, 2 * DFF]); fcw = din("fcw", [NL, 3, 2 * DFF]); fcb = din("fcb", [NL, 2 * DFF])
    w_down = din("w_down", [NL, DFF, D]); ln2g = din("ln2g", [NL, D]); ln2b = din("ln2b", [NL, D])
    cosT = din("cosT", [NTAB, 32]); sinT = din("sinT", [NTAB, 32])

    yp = dout("yp", [S, D]); ys = dout("ys", [NS, D])
    kp = dout("kp", [NL, S, D]); vp = dout("vp", [NL, S, D])
    mcp = dout("mcp", [NL, 3, 2048]); Cp = dout("Cp", [NL, HM, 256, 256]); np_ = dout("np", [NL, HM, 256])
    mp = dout("mp", [NL, HM]); fcp = dout("fcp", [NL, 2, 2 * DFF])
    ks = dout("ks", [NL, NS, D]); vs = dout("vs", [NL, NS, D])
    mcs = dout("mcs", [NL, 3, 2048]); Cs = dout("Cs", [NL, HM, 256, 256]); ns_ = dout("ns", [NL, HM, 256])
    ms = dout("ms", [NL, HM]); fcs = dout("fcs", [NL, 2, 2 * DFF])

    KTp = dscr("KTp", [NL, HA, 128, S], BF16); Vbp = dscr("Vbp", [NL, S, D], BF16)
    KTs = dscr("KTs", [NL, HA, 128, P], BF16); Vbs = dscr("Vbs", [NL, P, D], BF16)
    KTpB = [[Buf() for _ in range(NT)] for _ in range(NL)]
    VbpB = [[Buf() for _ in range(NT)] for _ in range(NL)]
    KTsB = [Buf() for _ in range(NL)]
    VbsB = [Buf() for _ in range(NL)]

    NB = T // 128
    xtok = sb("xtok", [128, NB, D]); xtokB = [Buf() for _ in range(NB)]
    xbf = sb("xbf", [128, D], BF16); xbfB = Buf()
    xT = sb("xT", [128, 8, T], BF16); xTB = [Buf() for _ in range(NB)]
    NSLOT = 4
    ring = [sb("ring%d" % i, [128, 8, 512], BF16) for i in range(NSLOT)]
    ringB = [Buf() for _ in range(NSLOT)]
    zf = sb("zf", [128, D]); zfB = Buf()
    ro = sb("ro", [128, D]); roB = Buf()
    cs_t = sb("cs_t", [128, NB, 2, 32]); csB = Buf()
    QT = sb("QT", [128, HA, T], BF16); QTB = [Buf() for _ in range(NB)]
    KT = sb("KT", [128, HA, T], BF16); KTB = [Buf() for _ in range(NB)]
    Vaug = sb("Vaug", [128, NB, HA, 129], BF16); VaugB = [Buf() for _ in range(NB)]
    VMaug = sb("VMaug", [128, NB, HM, 257], BF16); VMB = [Buf() for _ in range(NB)]
    oab = sb("oab", [128, NB, D], BF16); oabB = [Buf() for _ in range(NB)]
    hmb = sb("hmb", [128, NB, D], BF16); hmbB = [Buf() for _ in range(NB)]
    oaF = oab; oaFB = oabB
    hT = sb("hT", [128, 22, T], BF16); hTB = [Buf() for _ in range(22)]
    qkT = hT; qkTB = hTB
    pcm = [sb("pcm%d" % i, [128, 3 + T]) for i in range(2)]; pcmB = [Buf(), Buf()]
    acc = [sb("acc%d" % i, [128, T]) for i in range(2)]; accB = [Buf(), Buf()]
    rt = acc; rtB = accB
    stg = pcm[0][:, 0:512]; stgB = pcmB[0]
    kch = [sb("kch%d" % i, [128, T], BF16) for i in range(2)]; kchB = [Buf(), Buf()]
    vch = [sb("vch%d" % i, [128, NB, 129], BF16) for i in range(2)]; vchB = [Buf(), Buf()]
    PT = [sb("PT%d" % i, [128, 2, T], BF16) for i in range(2)]; PTB = [Buf(), Buf()]
    Caug = sb("Caug", [128, 2, HM, 257]); CaugB = [Buf() for _ in range(HM)]
    GbRaw = sb("GbRaw", [128, 2 * HM * 257], BF16); GbB = [Buf() for _ in range(HM)]
    Gb = GbRaw.rearrange("p (a h d) -> p a h d", a=2, h=HM)
    ro2 = GbRaw[:, 0:2 * D].bitcast(F32); ro2B = GbB
    kw = sb("kw", [128, D], BF16); kwB = Buf()
    Sm = sb("Sm", [128, HM, 128], BF16); SmB = Buf()
    hmf = sb("hmf", [128, D]); hmfB = Buf()
    sml = sb("sml", [128, 64]); smlB = Buf()
    mhalo = sb("mhalo", [128, NL, 16, 3]); mhaloB = [Buf() for _ in range(NL)]
    fhalo = sb("fhalo", [128, NL, 44, 2]); fhaloB = [Buf() for _ in range(NL)]
    mstate = sb("mstate", [4, NL]); mstateB = [Buf() for _ in range(NL)]
    CaugL = [Caug, sb("Caug1", [128, 2, HM, 257])]
    CaugLB = [CaugB, [Buf() for _ in range(HM)]]
    _g3 = [sb("gw%d" % i, [4, T]) for i in range(3)]; _g3B = [Buf() for _ in range(3)]
    gw = {"t1": _g3[0], "sp": _g3[0], "A": _g3[1], "cl": _g3[1], "ig": _g3[2], "r": _g3[2], "sc": _g3[2]}
    gwB = {"t1": _g3B[0], "sp": _g3B[0], "A": _g3B[1], "cl": _g3B[1], "ig": _g3B[2], "r": _g3B[2], "sc": _g3B[2]}
    gs = {n: sb("gs_" + n, [4, 16]) for n in ("cm", "aL", "mn", "mu", "mpv", "gam")}
    gsB = {n: Buf() for n in gs}
    gD = sb("gD", [4, 16, 4]); gDB = Buf()
    toksc = sb("toksc", [128, NB, 8]); tokscB = Buf()
    gam = sb("gam", [128, 16, 4]); gamB = Buf()
    identb = sb("identb", [128, 128], BF16); identf = sb("identf", [128, 128]); maskBD = sb("maskBD", [128, 128])
    ones4 = sb("ones4", [4, 128]); resetm = sb("resetm", [4, T]); resetm16 = sb("resetm16", [4, NS])
    constB = Buf()
    mcw_s = sb("mcw_s", [128, NL, 16, 4]); mcb_s = sb("mcb_s", [128, NL, 16])
    fcw_s = sb("fcw_s", [128, NL, 44, 3]); fcb_s = sb("fcb_s", [128, NL, 44])
    bif_s = sb("bif_s", [4, NL, 2]); nbif_s = sb("nbif_s", [4, NL, 2])
    neglam = sb("neglam", [128, NL]); lamw = sb("lamw", [128, 4, 64]); lamw2 = sb("lamw2", [128, 4])
    subg_s = sb("subg_s", [128, NL, 128])
    prmB = Buf()
    lnp = sb("lnp", [128, 2, D]); lnpB = [Buf(), Buf()]
    sg = [sb("sg%d" % i, [128, 512], BF16) for i in range(2)]; sgB = [Buf(), Buf()]
    ybf = xbf; ybfB = xbfB
    stt = sb("stt", [128, 2, 6]); mv = sb("mv", [128, 4]); lnB = Buf()
    vtmp = xbf; vtmpB = xbfB
    ktmp = kw.rearrange("p (h t) -> p h t", h=HA); ktmpB = kwB
    ps = nc.alloc_psum_tensor("ps", [128, 8, 512], F32).ap()
    psB = [Buf() for _ in range(8)]

    def psb16(k):
        return ps[:, k, :].bitcast(BF16)

    def MM(out, lhsT, rhs, start, stop, r, w, skip=False):
        kw_ = dict(lhsT=lhsT, rhs=rhs, start=start, stop=stop)
        if skip:
            kw_["skip_group_check"] = True
        sch.add("pe", "matmul", (out,), kw_, r, w)

    def TR(out, in_, r, w):
        sch.add("pe", "transpose", (), dict(out=out, in_=in_, identity=identb[0:in_.shape[0], 0:in_.shape[0]]), r, w)

    def ACT(out, in_, func, r, w, bias=None, scale=None, accum_out=None):
        kw_ = dict(out=out, in_=in_, func=func)
        if bias is not None:
            kw_["bias"] = bias
        if scale is not None:
            kw_["scale"] = scale
        if accum_out is not None:
            kw_["accum_out"] = accum_out
        sch.add("act", "activation", (), kw_, r, w)

    def CP(eng, out, in_, r, w):
        if eng == "act":
            ACT(out, in_, AF.Identity, r, w)
        else:
            sch.add(eng, "tensor_copy", (), dict(out=out, in_=in_), r, w)

    def TT(out, in0, in1, op, r, w, eng="dve"):
        sch.add(eng, "tensor_tensor", (), dict(out=out, in0=in0, in1=in1, op=op), r, w)

    def TS(out, in0, s1, s2, op0, op1, r, w, eng="dve"):
        kw_ = dict(out=out, in0=in0, scalar1=s1, scalar2=s2, op0=op0)
        if op1 is not None:
            kw_["op1"] = op1
        sch.add(eng, "tensor_scalar", (), kw_, r, w)

    def STT(out, in0, scalar, in1, op0, op1, r, w, eng="dve"):
        sch.add(eng, "scalar_tensor_tensor", (), dict(out=out, in0=in0, scalar=scalar, in1=in1, op0=op0, op1=op1), r, w)

    def RSQRT(ap, addc, r, w):
        TS(ap, ap, addc, None, ALU.add, None, r, w)
        ACT(ap, ap, AF.Sqrt, r, w)
        sch.add("dve", "reciprocal", (), dict(out=ap, in_=ap), r, w)

    def RED(out, in_, op, r, w):
        sch.add("dve", "tensor_reduce", (), dict(out=out, in_=in_, axis=AX.X, op=op), r, w)

    def MEMSET(eng, ap, val, r, w):
        sch.add(eng, "memset", (ap, val), {}, r, w)

    def DMA(q, out, in_, r, w, key, slow=False):
        kw_ = dict(out=out, in_=in_)
        if slow:
            kw_["allow_slow_non_contiguous"] = True
        sch.add(q, "dma_start", (), kw_, r, w, dma=key)

    MEMSET("pool", identf, 1.0, [], [constB])
    sch.add("pool", "affine_select", (), dict(out=identf, in_=identf, compare_op=ALU.is_equal, fill=0.0, base=0,
                                              pattern=[[-1, 128]], channel_multiplier=1), [constB], [constB])
    CP("dve", identb, identf, [constB], [constB])
    MEMSET("pool", maskBD, 1.0, [constB], [constB])
    sch.add("pool", "affine_select", (), dict(out=maskBD, in_=maskBD, compare_op=ALU.is_ge, fill=0.0, base=0,
                                              pattern=[[1, 128]], channel_multiplier=-1), [constB], [constB])
    MEMSET("pool", maskBD[0:64, 64:128], 0.0, [constB], [constB])
    MEMSET("pool", ones4, 1.0, [constB], [constB])
    MEMSET("pool", resetm, 1.0, [constB], [constB])
    MEMSET("pool", resetm.rearrange("p (c l) -> p c l", l=64)[:, :, 0:1], 0.0, [constB], [constB])
    MEMSET("pool", resetm16, 1.0, [constB], [constB])
    MEMSET("pool", resetm16[:, 0:1], 0.0, [constB], [constB])
    MEMSET("pool", Vaug[:, :, :, 128:129], 1.0, [], VaugB)
    MEMSET("pool", VMaug[:, :, :, 256:257], 1.0, [], VMB)
    for i in range(2):
        MEMSET("pool", vch[i][:, :, 128:129], 1.0, [], [vchB[i]])
    pk = Buf()
    for l in range(NL):
        for j in range(4):
            DMA("sp", mcw_s[:, l, :, j], mcw[l, j].rearrange("(c p) -> p c", p=128), [], [prmB], pk, slow=True)
        DMA("sp", mcb_s[:, l, :], mcb[l].rearrange("(c p) -> p c", p=128), [], [prmB], pk, slow=True)
        for j in range(3):
            DMA("sp", fcw_s[:, l, :, j], fcw[l, j].rearrange("(c p) -> p c", p=128), [], [prmB], pk, slow=True)
        DMA("sp", fcb_s[:, l, :], fcb[l].rearrange("(c p) -> p c", p=128), [], [prmB], pk, slow=True)
        DMA("sp", bif_s[:, l, :], b_if[l].rearrange("(j p) -> p j", p=4), [], [prmB], pk, slow=True)
        DMA("sp", subg_s[:, l, :], subg[l].partition_broadcast(128), [], [prmB], pk)
        DMA("sp", lamw, dlam[l].partition_broadcast(128), [prmB], [prmB], pk)
        lam_init = 0.8 - 0.6 * math.exp(-0.3 * l)
        lv = lamw.rearrange("p (a b) d -> p a b d", b=2)
        TT(lamw[:, 0:2, :].rearrange("p a d -> p a d"), lv[:, :, 0, :], lv[:, :, 1, :], ALU.mult, [prmB], [prmB])
        RED(lamw2[:, 0:2], lamw[:, 0:2, :], ALU.add, [prmB], [prmB])
        ACT(lamw2[:, 2:4], lamw2[:, 0:2], AF.Exp, [prmB], [prmB])
        TT(neglam[:, l:l + 1], lamw2[:, 3:4], lamw2[:, 2:3], ALU.subtract, [prmB], [prmB])
        TS(neglam[:, l:l + 1], neglam[:, l:l + 1], -lam_init, None, ALU.add, None, [prmB], [prmB])
        TS(subg_s[:, l, :], subg_s[:, l, :], (1.0 - lam_init) * math.sqrt(128.0), None, ALU.mult, None, [prmB], [prmB])
    TS(nbif_s, bif_s, -1.0, None, ALU.mult, None, [prmB], [prmB])

    wq = []
    wstate = dict(loaded=0, used=0, released=0)

    def wspec_step(l, last):
        sp_ = []
        for c0 in (C_QA, C_QA + 512, C_KA, C_KA + 512, C_VA, C_VA + 512):
            sp_.append(("in", l, c0, 512))
        for c0 in range(C_QKM, C_QKM + 2048, 512):
            sp_.append(("in", l, c0, 512))
        if last:
            for c0 in range(C_QKM, C_QKM + 2048, 512):
                sp_.append(("in", l, c0, 512))
        sp_.append(("in", l, C_VM, 512)); sp_.append(("in", l, C_VM + 512, 512))
        sp_.append(("in", l, C_GIF, 8))
        for c0 in (C_OM, C_OM + 512, C_GA, C_GA + 512, C_GB, C_GB + 512):
            sp_.append(("in", l, c0, 512))
        sp_.append(("out", l, 0, 512)); sp_.append(("out", l, 512, 512))
        for g in range(11):
            sp_.append(("up", l, g * 512, 512))
        if last:
            for g in range(11):
                sp_.append(("up", l, g * 512, 512))
        for nh in range(2):
            for pc in range(3):
                sp_.append(("down", l, nh, pc))
        return sp_

    def wload(i):
        kind, l, a, b = wq[i]
        slot = i % NSLOT
        if kind == "in":
            src = w_in[l].rearrange("(kc p) n -> p kc n", p=128)[:, :, a:a + b]
            dst = ring[slot][:, :, 0:b]
        elif kind == "out":
            src = w_out[l].rearrange("(kc p) n -> p kc n", p=128)[:, :, a:a + b]
            dst = ring[slot][:, :, 0:b]
        elif kind == "up":
            src = w_up[l].rearrange("(kc p) n -> p kc n", p=128)[:, :, a:a + b]
            dst = ring[slot][:, :, 0:b]
        else:
            k0 = b * 8
            k1 = min(22, k0 + 8)
            src = w_down[l].rearrange("(kc p) n -> p kc n", p=128)[:, k0:k1, a * 512:(a + 1) * 512]
            dst = ring[slot][:, 0:k1 - k0, :]
        DMA("pool", dst, src, [], [ringB[slot]], ringB[slot])

    def wfill():
        while wstate["loaded"] < min(len(wq), wstate["released"] + NSLOT):
            wload(wstate["loaded"])
            wstate["loaded"] += 1

    def wnext(expect):
        i = wstate["used"]
        assert wq[i] == expect, (wq[i], expect)
        wfill()
        assert wstate["loaded"] > i
        wstate["used"] += 1
        return ring[i % NSLOT], ringB[i % NSLOT]

    def wdone(n=1):
        wstate["released"] += n
        assert wstate["released"] <= wstate["used"]
        wfill()

    rot = dict(pp=0, tp=0)

    def pbank():
        k = rot["pp"] % 4
        rot["pp"] += 1
        return k

    def tbank():
        k = 6 + rot["tp"] % 2
        rot["tp"] += 1
        return k

    evr = dict(i=0)

    def eveng():
        evr["i"] += 1
        return "act" if evr["i"] % 2 else "dve"

    def step(l, tp):
        ntok, bs, nblk, L = tp["ntok"], tp["bs"], tp["nblk"], tp["L"]
        cpb = bs // L
        nch = ntok // L
        last = tp["last"]
        Cg = CaugL[l]; CgB = CaugLB[l]

        def load_lnp(slot, src):
            DMA("sp", lnp[:, slot, :], src[l].partition_broadcast(128), [], [lnpB[slot]], lnpB[slot])
        load_lnp(0, mhg)
        if l == 0:
            for b in range(nblk):
                DMA("sp", xtok[0:bs, b, :], tp["x_src"][b * bs:(b + 1) * bs, :], [], [xtokB[b]], xtokB[b])
        DMA("sp", cs_t[0:bs, 0:nblk, 0, :], cosT[tp["pos0"]:tp["pos0"] + ntok, :].rearrange("(b p) d -> p b d", p=bs), [], [csB], csB)
        DMA("sp", cs_t[0:bs, 0:nblk, 1, :], sinT[tp["pos0"]:tp["pos0"] + ntok, :].rearrange("(b p) d -> p b d", p=bs), [], [csB], csB)
        if tp["load_state"]:
            for h in range(HM):
                DMA("sp", Cg[:, :, h, 0:256], tp["sC"][l, h].rearrange("(c p) v -> p c v", p=128), [], [CgB[h]], CgB[h])
                DMA("sp", Cg[:, :, h, 256:257], tp["sn"][l, h].rearrange("(c p o) -> p c o", p=128, o=1), [], [CgB[h]], CgB[h], slow=True)
            DMA("sp", mstate[:, l:l + 1], tp["sm"][l].rearrange("(p o) -> p o", o=1), [], [mstateB[l]], mstateB[l], slow=True)
            for j in range(3):
                DMA("sp", mhalo[:, l, :, j], tp["smc"][l, j].rearrange("(c p) -> p c", p=128), [], [mhaloB[l]], mhaloB[l], slow=True)
            for j in range(2):
                DMA("sp", fhalo[:, l, :, j], tp["sfc"][l, j].rearrange("(c p) -> p c", p=128), [], [fhaloB[l]], fhaloB[l], slow=True)
        elif tp["first"]:
            for h in range(HM):
                MEMSET("pool", Cg[:, :, h, :], 0.0, [], [CgB[h]])
            MEMSET("pool", mstate[:, l:l + 1], 0.0, [], [mstateB[l]])
            MEMSET("pool", mhalo[:, l, :, :], 0.0, [], [mhaloB[l]])
            MEMSET("pool", fhalo[:, l, :, :], 0.0, [], [fhaloB[l]])

        def make_xT():
            for b in range(nblk):
                CP(eveng(), xbf[0:bs, :], xtok[0:bs, b, :], [xtokB[b]], [xbfB])
                k = tbank()
                pt = psb16(k)
                for c in range(8):
                    TR(pt[:, c * bs:(c + 1) * bs], xbf[0:bs, c * 128:(c + 1) * 128], [xbfB, constB], [psB[k]])
                CP(eveng(), xT[:, :, b * bs:(b + 1) * bs], pt[:, 0:8 * bs].rearrange("p (c t) -> p c t", c=8), [psB[k]], [xTB[b]])

        sch.phase = "xT"
        make_xT()

        def proj_tok(wslot, wB, b, ncols, kcn=8, lhs=None, lhsB=None):
            k = pbank()
            out = ps[0:bs, k, 0:ncols]
            for kc in range(kcn):
                lt = xT[:, kc, b * bs:(b + 1) * bs] if lhs is None else lhs(kc)
                MM(out, lt, wslot[:, kc, 0:ncols], kc == 0, kc == kcn - 1, [wB, xTB[b] if lhsB is None else lhsB], [psB[k]])
            return k, out

        def rope_block(src_zf, zfB, dst, roB, b):
            sv = src_zf.rearrange("p (g two d) -> p g two d", two=2, d=32)
            dv = dst.rearrange("p (g two d) -> p g two d", two=2, d=32)
            cosb = cs_t[0:bs, b, 0:1, :].to_broadcast([bs, 16, 32])
            sinb = cs_t[0:bs, b, 1:2, :].to_broadcast([bs, 16, 32])
            t0 = rt[0][0:bs, :].rearrange("p (g d) -> p g d", d=32)
            t1 = rt[1][0:bs, :].rearrange("p (g d) -> p g d", d=32)
            TT(t0, sv[:, :, 0, :], cosb, ALU.mult, [zfB, csB], [rtB[0]])
            TT(t1, sv[:, :, 1, :], sinb, ALU.mult, [zfB, csB], [rtB[1]])
            TT(dv[:, :, 0, :], t0, t1, ALU.subtract, [rtB[0], rtB[1]], roB)
            TT(t0, sv[:, :, 1, :], cosb, ALU.mult, [zfB, csB], [rtB[0]])
            TT(t1, sv[:, :, 0, :], sinb, ALU.mult, [zfB, csB], [rtB[1]])
            TT(dv[:, :, 1, :], t0, t1, ALU.add, [rtB[0], rtB[1]], roB)

        sch.phase = "qkv"
        zfs = [(zf, zfB), (hmf, hmfB)]
        ros = [(ro, [roB]), (ro2, ro2B)]
        pst = dict(n=0)

        def post(which, b, zb, zbB):
            if which == "v":
                DMA("sp", tp["v_out"][l, b * bs:(b + 1) * bs, :], zb[0:bs, :], [zbB], [], zbB)
                CP("dve", Vaug[0:bs, b, :, 0:128], zb[0:bs, :].rearrange("p (h d) -> p h d", h=HA), [zbB], [VaugB[b]])
                if tp["Vb_dst"] is not None:
                    DMA("sp", tp["Vb_dst"][b * bs:(b + 1) * bs, :].rearrange("p (h d) -> p h d", h=HA),
                        Vaug[0:bs, b, :, 0:128], [VaugB[b]], [tp["VbB"]], VaugB[b])
                return
            rb, rbB = ros[pst["n"] % 2]
            pst["n"] += 1
            rope_block(zb[0:bs, :], zbB, rb[0:bs, :], rbB, b)
            if which == "k":
                DMA("sp", tp["k_out"][l, b * bs:(b + 1) * bs, :], rb[0:bs, :], rbB, [], rbB[0])
            CP("act", xbf[0:bs, :], rb[0:bs, :], rbB, [xbfB])
            kb_ = tbank()
            pt = psb16(kb_)
            for h in range(HA):
                TR(pt[:, h * bs:(h + 1) * bs], xbf[0:bs, h * 128:(h + 1) * 128], [xbfB, constB], [psB[kb_]])
            dstT, dstB = (QT, QTB) if which == "q" else (KT, KTB)
            CP("dve", dstT[:, :, b * bs:(b + 1) * bs], pt[:, 0:HA * bs].rearrange("p (h t) -> p h t", h=HA), [psB[kb_]], [dstB[b]])
            if which == "k" and b == nblk - 1 and tp["KT_dst"] is not None:
                DMA("sp", tp["KT_dst"].rearrange("h p t -> p h t"), KT[:, :, 0:ntok], KTB[0:nblk], [tp["KTB"]], KTB[0])

        pend = None
        nz = 0
        for which in ("q", "k", "v"):
            c0 = {"q": C_QA, "k": C_KA, "v": C_VA}[which]
            w0, w0B = wnext(("in", l, c0, 512))
            w1, w1B = wnext(("in", l, c0 + 512, 512))
            for b in range(nblk):
                zb, zbB = zfs[nz % 2]
                nz += 1
                for hf, (ws, wB) in enumerate(((w0, w0B), (w1, w1B))):
                    k, o = proj_tok(ws, wB, b, 512)
                    CP(eveng(), zb[0:bs, hf * 512:(hf + 1) * 512], o, [psB[k]], [zbB])
                if pend is not None:
                    post(*pend)
                pend = (which, b, zb, zbB)
            wdone(2)
        post(*pend)

        sch.phase = "qkm"
        m_pend = [None]
        for g in range(4):
            ws, wB = wnext(("in", l, C_QKM + g * 512, 512))
            for cc4 in range(4):
                cc = g * 4 + cc4
                k = pbank()
                o = ps[:, k, 0:ntok]
                for kc in range(8):
                    MM(o, ws[:, kc, cc4 * 128:(cc4 + 1) * 128], xT[:, kc, 0:ntok], kc == 0, kc == 7, [wB] + xTB[0:nblk], [psB[k]])
                pi = cc % 2
                pcv = pcm[pi]
                CP("pool", pcv[:, 0:3], mhalo[:, l, cc, :], [mhaloB[l]], [pcmB[pi]])
                CP("act", pcv[:, 3:3 + ntok], o, [psB[k]], [pcmB[pi]])
                a = acc[pi][:, 0:ntok]
                ACT(a, o, AF.Identity, [psB[k], prmB], [accB[pi]], bias=mcb_s[:, l, cc:cc + 1], scale=mcw_s[:, l, cc, 3:4])
                CP("pool", mhalo[:, l, cc, :], pcv[:, ntok:ntok + 3], [pcmB[pi]], [mhaloB[l]])
                for j in range(3):
                    STT(a, pcv[:, j:j + ntok], mcw_s[:, l, cc, j:j + 1], a, ALU.mult, ALU.add, [pcmB[pi], prmB, accB[pi]], [accB[pi]])
                if m_pend[0] is not None:
                    ACT(qkT[:, m_pend[0][0], 0:ntok], m_pend[0][1], AF.Silu, [accB[m_pend[0][2]]], [qkTB[m_pend[0][0]]])
                m_pend[0] = (cc, a, pi)
            wdone(1)
        ACT(qkT[:, m_pend[0][0], 0:ntok], m_pend[0][1], AF.Silu, [accB[m_pend[0][2]]], [qkTB[m_pend[0][0]]])
        if last:
            for g in range(4):
                ws, wB = wnext(("in", l, C_QKM + g * 512, 512))
                k, o = proj_tok(ws, wB, nblk - 1, 512)
                CP("dve", stg[0:bs, :], o, [psB[k]], [stgB])
                DMA("sp", tp["mc_out"][l, :, g * 512:(g + 1) * 512], stg[bs - 3:bs, :], [stgB], [], stgB)
                wdone(1)
        sch.phase = "vm"
        w0, w0B = wnext(("in", l, C_VM, 512))
        w1, w1B = wnext(("in", l, C_VM + 512, 512))
        for b in range(nblk):
            for hf, (ws, wB) in enumerate(((w0, w0B), (w1, w1B))):
                k, o = proj_tok(ws, wB, b, 512)
                CP(eveng(), VMaug[0:bs, b, 2 * hf:2 * hf + 2, 0:256], o.rearrange("p (h d) -> p h d", h=2), [psB[k]], [VMB[b]])
        wdone(2)
        sch.phase = "gates"
        ws, wB = wnext(("in", l, C_GIF, 8))
        kI = pbank(); kF = pbank()
        for kc in range(8):
            MM(ps[0:4, kI, 0:ntok], ws[:, kc, 0:4], xT[:, kc, 0:ntok], kc == 0, kc == 7, [wB] + xTB[0:nblk], [psB[kI]])
        for kc in range(8):
            MM(ps[0:4, kF, 0:ntok], ws[:, kc, 4:8], xT[:, kc, 0:ntok], kc == 0, kc == 7, [wB] + xTB[0:nblk], [psB[kF]])
        wdone(1)
        g_ = {n: gw[n][:, 0:ntok] for n in gw}
        s_ = {n: gs[n][:, 0:nch] for n in gs}
        ACT(g_["ig"], ps[0:4, kI, 0:ntok], AF.Identity, [psB[kI], prmB], [gwB["ig"]], bias=bif_s[:, l, 0:1])
        ACT(g_["t1"], ps[0:4, kF, 0:ntok], AF.Exp, [psB[kF], prmB], [gwB["t1"]], bias=nbif_s[:, l, 1:2], scale=-1.0)
        TS(g_["t1"], g_["t1"], 1.0, None, ALU.add, None, [gwB["t1"]], [gwB["t1"]])
        ACT(g_["sp"], g_["t1"], AF.Ln, [gwB["t1"]], [gwB["sp"]])
        rm = resetm[:, 0:ntok] if L == 64 else resetm16[:, 0:ntok]
        sch.add("dve", "tensor_tensor_scan", (), dict(out=g_["A"], data0=rm, data1=g_["sp"], initial=0.0, op0=ALU.mult, op1=ALU.add),
                [gwB["sp"], constB], [gwB["A"]])
        TT(g_["r"], g_["ig"], g_["A"], ALU.add, [gwB["ig"], gwB["A"]], [gwB["r"]])
        RED(s_["cm"], g_["r"].rearrange("p (c l) -> p c l", l=L), ALU.max, [gwB["r"]], [gsB["cm"]])
        TS(s_["aL"], g_["A"].rearrange("p (c l) -> p c l", l=L)[:, :, L - 1], -1.0, None, ALU.mult, None, [gwB["A"]], [gsB["aL"]])
        sch.add("dve", "tensor_tensor_scan", (), dict(out=s_["mn"], data0=s_["cm"], data1=s_["aL"], initial=mstate[:, l:l + 1],
                                                       op0=ALU.max, op1=ALU.add), [gsB["cm"], gsB["aL"], mstateB[l]], [gsB["mn"]])
        TT(s_["mu"], s_["mn"], s_["aL"], ALU.subtract, [gsB["mn"], gsB["aL"]], [gsB["mu"]])
        CP("dve", gs["mpv"][:, 0:1], mstate[:, l:l + 1], [mstateB[l]], [gsB["mpv"]])
        if nch > 1:
            CP("dve", gs["mpv"][:, 1:nch], gs["mn"][:, 0:nch - 1], [gsB["mn"]], [gsB["mpv"]])
        CP("dve", mstate[:, l:l + 1], gs["mn"][:, nch - 1:nch], [gsB["mn"], gsB["mpv"]], [mstateB[l]])
        TT(s_["gam"], s_["mpv"], s_["mu"], ALU.subtract, [gsB["mpv"], gsB["mu"]], [gsB["gam"]])
        ACT(s_["gam"], s_["gam"], AF.Exp, [gsB["gam"]], [gsB["gam"]])
        mub = s_["mu"].rearrange("p (c o) -> p c o", o=1).to_broadcast([4, nch, L])
        TT(g_["sc"].rearrange("p (c l) -> p c l", l=L), g_["r"].rearrange("p (c l) -> p c l", l=L), mub, ALU.subtract, [gwB["r"], gsB["mu"]], [gwB["sc"]])
        ACT(g_["sc"], g_["sc"], AF.Exp, [gwB["sc"]], [gwB["sc"]])
        TS(g_["sc"], g_["sc"], 1.0 / 16.0, None, ALU.mult, None, [gwB["sc"]], [gwB["sc"]])
        TT(g_["cl"].rearrange("p (c l) -> p c l", l=L), g_["A"].rearrange("p (c l) -> p c l", l=L), mub, ALU.subtract, [gwB["A"], gsB["mu"]], [gwB["cl"]])
        ACT(g_["cl"], g_["cl"], AF.Exp, [gwB["cl"]], [gwB["cl"]])
        kG = pbank()
        pg = ps[0:bs, kG, 0:nblk * 8].rearrange("p (b e) -> p b e", e=8)
        for b in range(nblk):
            MM(pg[:, b, 0:4], g_["sc"][:, b * bs:(b + 1) * bs], identf[0:4, 0:4], True, True, [gwB["sc"], constB], [psB[kG]])
            MM(pg[:, b, 4:8], g_["cl"][:, b * bs:(b + 1) * bs], identf[0:4, 0:4], True, True, [gwB["cl"], constB], [psB[kG]])
        CP("dve", toksc[0:bs, 0:nblk, :], pg, [psB[kG]], [tokscB])
        TT(gD[:, 0:nch, :], s_["gam"].rearrange("p (c o) -> p c o", o=1).to_broadcast([4, nch, 4]),
           identf[0:4, 0:4].rearrange("p (o h) -> p o h", o=1).to_broadcast([4, nch, 4]), ALU.mult, [gsB["gam"], constB], [gDB])
        kG2 = pbank()
        MM(ps[:, kG2, 0:nch * 4], ones4, gD[:, 0:nch, :].rearrange("p c h -> p (c h)"), True, True, [gDB, constB], [psB[kG2]])
        CP("dve", gam[:, 0:nch, :], ps[:, kG2, 0:nch * 4].rearrange("p (c h) -> p c h", h=4), [psB[kG2]], [gamB])

        sch.phase = "attn"
        prior = tp["prior"]
        ngr = len(prior) + 1
        att = dict(si=0)
        for h in range(HA):
            started = set()
            items = []
            for g in range(ngr):
                own = g == ngr - 1
                for kb in range(nblk if own else T // 128):
                    items.append((g, own, kb))

            def emit_scores(it):
                g, own, kb = it
                if not own:
                    sl = (h * ngr + g) % 2
                    if kb == 0:
                        kd, vd, dB = prior[g]
                        DMA("sp", kch[sl][:, 0:T], kd[h], dB, [kchB[sl]], kchB[sl])
                        DMA("sp", vch[sl][:, :, 0:128], vd[:, h * 128:(h + 1) * 128].rearrange("(b p) d -> p b d", p=128), dB, [vchB[sl]], vchB[sl])
                    kT_ = kch[sl][:, kb * 128:(kb + 1) * 128]; kTB_ = kchB[sl]
                    v_ = vch[sl][:, kb, :]; vB_ = vchB[sl]
                    q0 = 0
                    nk = 128
                else:
                    kT_ = KT[:, h, kb * bs:(kb + 1) * bs]; kTB_ = KTB[kb]
                    v_ = Vaug[0:bs, kb, h, :]; vB_ = VaugB[kb]
                    q0 = kb * bs if tp["mask"] else 0
                    nk = bs
                nq = ntok - q0
                sb_ = att["si"] % 2
                att["si"] += 1
                b0, b1 = 2 * sb_, 2 * sb_ + 1
                MM(ps[0:nk, b0, 0:nq], kT_[0:64, :], QT[0:64, h, q0:ntok], True, True, [kTB_] + QTB[0:nblk], [psB[b0]])
                MM(ps[0:nk, b1, 0:nq], kT_[64:128, :], QT[64:128, h, q0:ntok], True, True, [kTB_] + QTB[0:nblk], [psB[b1]])
                ACT(PT[sb_][0:nk, :, 0:nq], ps[0:nk, b0:b1 + 1, 0:nq], AF.Exp, [psB[b0], psB[b1]], [PTB[sb_]], scale=0.125)
                if own and tp["mask"]:
                    MEMSET("pool", PT[sb_][64:128, :, 0:64], 0.0, [PTB[sb_]], [PTB[sb_]])
                return (g, own, kb, sb_, nk, q0, v_, vB_)

            def emit_av(st):
                g, own, kb, sb_, nk, q0, v_, vB_ = st
                nkb = nblk if own else T // 128
                qb0 = kb if (own and tp["mask"]) else 0
                for qb in range(qb0, nblk):
                    for mp_ in range(2):
                        a_ = qb * 2 + mp_
                        bank = 4 + a_ // 3
                        off = (a_ % 3) * 129
                        col = qb * bs - q0
                        first = (g == 0 and kb == 0) and bank not in started
                        started.add(bank)
                        lastk = own and (kb == (qb if tp["mask"] else nkb - 1))
                        MM(ps[0:bs, bank, off:off + 129], PT[sb_][0:nk, mp_, col:col + bs], v_, first, lastk, [PTB[sb_], vB_], [psB[bank]], skip=True)

            prev = None
            for it in items:
                st = emit_scores(it)
                if prev is not None:
                    emit_av(prev)
                prev = st
            emit_av(prev)
            CP("dve", zf[0:bs, 0:387], ps[0:bs, 4, 0:387], [psB[4]], [zfB])
            CP("dve", zf[0:bs, 387:774], ps[0:bs, 5, 0:387], [psB[5]], [zfB])
            CP("dve", hmf[0:bs, 0:258], ps[0:bs, 6, 0:258], [psB[6]], [hmfB])

            def accv(a_):
                if a_ < 6:
                    return zf[0:bs, a_ * 129:(a_ + 1) * 129], zfB
                return hmf[0:bs, (a_ - 6) * 129:(a_ - 5) * 129], hmfB

            for qb in range(nblk):
                o0, o0B = accv(qb * 2)
                o1, o1B = accv(qb * 2 + 1)
                r0 = sml[0:bs, 0:1]; r1 = sml[0:bs, 1:2]
                sch.add("dve", "reciprocal", (), dict(out=r0, in_=o0[:, 128:129]), [o0B], [smlB])
                sch.add("dve", "reciprocal", (), dict(out=r1, in_=o1[:, 128:129]), [o1B], [smlB])
                TT(r1, r1, neglam[0:bs, l:l + 1], ALU.mult, [smlB, prmB], [smlB])
                TS(o0[:, 0:128], o0[:, 0:128], r0, None, ALU.mult, None, [o0B, smlB], [o0B])
                STT(oaF[0:bs, qb, h * 128:(h + 1) * 128], o1[:, 0:128], r1, o0[:, 0:128], ALU.mult, ALU.add,
                    [o1B, o0B, smlB], [oaFB[qb]])
        sch.phase = "subln"
        for qb in range(nblk):
            ov = oaF[0:bs, qb, :].rearrange("p (h d) -> p h d", h=HA)
            TT(zf[0:bs, :], oaF[0:bs, qb, :], oaF[0:bs, qb, :], ALU.mult, [oaFB[qb]], [zfB])
            RED(sml[0:bs, 8:16], zf[0:bs, :].rearrange("p (h d) -> p h d", h=HA), ALU.add, [zfB], [smlB])
            RSQRT(sml[0:bs, 8:16], 128.0 * LN_EPS, [smlB], [smlB])
            TT(ov, ov, sml[0:bs, 8:16].rearrange("p (h o) -> p h o", o=1).to_broadcast([bs, HA, 128]), ALU.mult, [oaFB[qb], smlB], [oaFB[qb]])
            TT(ov, ov, subg_s[0:bs, l, :].rearrange("p (o d) -> p o d", o=1).to_broadcast([bs, HA, 128]), ALU.mult, [oaFB[qb], prmB], [oabB[qb]])

        sch.phase = "mlstm"
        for b in range(nblk):
            kb_ = tbank()
            pt = psb16(kb_)
            for j in range(8):
                TR(pt[0:bs, j * 128:(j + 1) * 128], qkT[:, 8 + j, b * bs:(b + 1) * bs], [qkTB[8 + j], constB], [psB[kb_]])
            for h in range(HM):
                TS(kw[0:bs, h * 256:(h + 1) * 256], pt[0:bs, h * 256:(h + 1) * 256], toksc[0:bs, b, h:h + 1], None, ALU.mult, None,
                   [psB[kb_], tokscB], [kwB])
            kS = pbank()
            for h in range(HM):
                for dk in range(2):
                    MM(ps[0:bs, kS, h * 128:h * 128 + bs], qkT[:, 8 + 2 * h + dk, b * bs:(b + 1) * bs], qkT[:, 2 * h + dk, b * bs:(b + 1) * bs],
                       dk == 0, dk == 1, [qkTB[8 + 2 * h + dk], qkTB[2 * h + dk]], [psB[kS]])
            for h in range(HM):
                STT(Sm[0:bs, h, 0:bs], ps[0:bs, kS, h * 128:h * 128 + bs], toksc[0:bs, b, h:h + 1], maskBD[0:bs, 0:bs], ALU.mult, ALU.mult,
                    [psB[kS], tokscB, constB], [SmB])
            for ci in range(cpb):
                c = b * cpb + ci
                p0, p1 = ci * L, ci * L + L
                t0_, t1_ = b * bs + ci * L, b * bs + ci * L + L
                for h in range(HM):
                    ACT(Gb[:, :, h, :], Cg[:, :, h, :], AF.Identity, [CgB[h], gamB], [GbB[h]], scale=gam[:, c, h:h + 1])
                for h in range(HM):
                    kN = 2 + (h % 2)
                    nd = ps[p0:p1, kN, 0:257]
                    MM(nd, qkT[:, 2 * h, t0_:t1_], Gb[:, 0, h, :], True, False, [qkTB[2 * h], GbB[h]], [psB[kN]])
                    MM(nd, qkT[:, 2 * h + 1, t0_:t1_], Gb[:, 1, h, :], False, False, [qkTB[2 * h + 1], GbB[h]], [psB[kN]])
                    MM(nd, Sm[p0:p1, h, p0:p1], VMaug[p0:p1, b, h, :], False, True, [SmB, VMB[b]], [psB[kN]])
                    dd = sml[p0:p1, 16 + h:17 + h]
                    TS(dd, nd[:, 256:257], -1.0, None, ALU.mult, None, [psB[kN]], [smlB])
                    TT(dd, dd, nd[:, 256:257], ALU.max, [smlB, psB[kN]], [smlB])
                    TT(dd, dd, toksc[p0:p1, b, 4 + h:5 + h], ALU.max, [smlB, tokscB], [smlB])
                    sch.add("dve", "reciprocal", (), dict(out=dd, in_=dd), [smlB], [smlB])
                    TS(hmf[p0:p1, h * 256:(h + 1) * 256], nd[:, 0:256], dd, None, ALU.mult, None, [psB[kN], smlB], [hmfB])
                for h in range(HM):
                    for dk in range(2):
                        kC = 4 + dk
                        dC = ps[:, kC, 0:257]
                        MM(dC, kw[p0:p1, h * 256 + dk * 128:h * 256 + dk * 128 + 128], VMaug[p0:p1, b, h, :], True, True, [kwB, VMB[b]], [psB[kC]])
                        STT(Cg[:, dk, h, :], Cg[:, dk, h, :], gam[:, c, h:h + 1], dC, ALU.mult, ALU.add, [CgB[h], gamB, psB[kC]], [CgB[h]])
            hv = hmf[0:bs, :].rearrange("p (h d) -> p h d", h=HM)
            RED(sml[0:bs, 24:28], hv, ALU.add, [hmfB], [smlB])
            TS(sml[0:bs, 24:28], sml[0:bs, 24:28], 1.0 / 256.0, None, ALU.mult, None, [smlB], [smlB])
            TT(hv, hv, sml[0:bs, 24:28].rearrange("p (h o) -> p h o", o=1).to_broadcast([bs, HM, 256]), ALU.subtract, [hmfB, smlB], [hmfB])
            TT(zf[0:bs, :], hmf[0:bs, :], hmf[0:bs, :], ALU.mult, [hmfB], [zfB])
            RED(sml[0:bs, 28:32], zf[0:bs, :].rearrange("p (h d) -> p h d", h=HM), ALU.add, [zfB], [smlB])
            RSQRT(sml[0:bs, 28:32], 256.0 * LN_EPS, [smlB], [smlB])
            TS(sml[0:bs, 28:32], sml[0:bs, 28:32], 16.0, None, ALU.mult, None, [smlB], [smlB])
            TT(hv, hv, sml[0:bs, 28:32].rearrange("p (h o) -> p h o", o=1).to_broadcast([bs, HM, 256]), ALU.mult, [hmfB, smlB], [hmfB])
            TT(hmb[0:bs, b, :], hmf[0:bs, :], lnp[0:bs, 0, :], ALU.mult, [hmfB, lnpB[0]], [hmbB[b]])

        if dbg is not None and l == dbg[0] and tp["tok0"] == dbg[1] * T and not tp["load_state"]:
            d_oa = dout("d_oa", [128, NB, D], BF16); d_hm = dout("d_hm", [128, NB, D], BF16)
            DMA("sp", d_oa, oab, oabB, [], oabB[0])
            DMA("sp", d_hm, hmb, hmbB, [], hmbB[0])
        sch.phase = "merge"
        for gi, c0 in enumerate((C_OM, C_OM + 512, C_GA, C_GA + 512, C_GB, C_GB + 512)):
            ws, wB = wnext(("in", l, c0, 512))
            hf = gi % 2
            for b in range(nblk):
                k, o = proj_tok(ws, wB, b, 512)
                si_ = (gi * nblk + b) % 2
                ACT(sg[si_][0:bs, :], o, AF.Sigmoid, [psB[k]], [sgB[si_]])
                if gi in (2, 3):
                    tgt, tB = oab, oabB
                else:
                    tgt, tB = hmb, hmbB
                TT(tgt[0:bs, b, hf * 512:(hf + 1) * 512], tgt[0:bs, b, hf * 512:(hf + 1) * 512], sg[si_][0:bs, :], ALU.mult, [sgB[si_], tB[b]], [tB[b]])
            wdone(1)
        for b in range(nblk):
            TT(ybf[0:bs, :], oab[0:bs, b, :], hmb[0:bs, b, :], ALU.add, [oabB[b], hmbB[b]], [ybfB])
            k = tbank()
            pt = psb16(k)
            for c in range(8):
                TR(pt[:, c * bs:(c + 1) * bs], ybf[0:bs, c * 128:(c + 1) * 128], [ybfB, constB], [psB[k]])
            CP(eveng(), xT[:, :, b * bs:(b + 1) * bs], pt[:, 0:8 * bs].rearrange("p (c t) -> p c t", c=8), [psB[k]], [xTB[b]])

        def layernorm_inplace(b, gi, bi):
            xv = xtok[0:bs, b, :]
            for hh in range(2):
                sch.add("dve", "bn_stats", (), dict(out=stt[0:bs, hh, :], in_=xtok[0:bs, b, hh * 512:(hh + 1) * 512]), [xtokB[b]], [lnB])
            sch.add("dve", "bn_aggr", (), dict(out=mv[0:bs, 0:2], in_=stt[0:bs, :, :]), [lnB], [lnB])
            CP("dve", mv[0:bs, 2:3], mv[0:bs, 1:2], [lnB], [lnB])
            RSQRT(mv[0:bs, 2:3], LN_EPS, [lnB], [lnB])
            TS(xv, xv, mv[0:bs, 0:1], mv[0:bs, 2:3], ALU.subtract, ALU.mult, [xtokB[b], lnB], [xtokB[b]])
            TT(xv, xv, lnp[0:bs, 0, :], ALU.mult, [xtokB[b], lnpB[0]], [xtokB[b]])
            TT(xv, xv, lnp[0:bs, 1, :], ALU.add, [xtokB[b], lnpB[1]], [xtokB[b]])

        sch.phase = "wout"
        load_lnp(0, ln1g); load_lnp(1, ln1b)
        w0, w0B = wnext(("out", l, 0, 512))
        w1, w1B = wnext(("out", l, 512, 512))
        for b in range(nblk):
            for hf, (ws, wB) in enumerate(((w0, w0B), (w1, w1B))):
                k, o = proj_tok(ws, wB, b, 512)
                xs_ = xtok[0:bs, b, hf * 512:(hf + 1) * 512]
                STT(xs_, xs_, ALPHA, o, ALU.mult, ALU.add, [xtokB[b], psB[k]], [xtokB[b]])
            layernorm_inplace(b, 1, 2)
        wdone(2)
        make_xT()

        sch.phase = "up"
        up_pend = [None]

        def up_final(cc, a, pi):
            if cc < 22:
                ACT(hT[:, cc, 0:ntok], a, AF.Gelu, [accB[pi]], [hTB[cc]])
            else:
                TT(hT[:, cc - 22, 0:ntok], hT[:, cc - 22, 0:ntok], a, ALU.mult, [hTB[cc - 22], accB[pi]], [hTB[cc - 22]], eng="pool")

        for g in range(11):
            ws, wB = wnext(("up", l, g * 512, 512))
            for cc4 in range(4):
                cc = g * 4 + cc4
                k = pbank()
                o = ps[:, k, 0:ntok]
                for kc in range(8):
                    MM(o, ws[:, kc, cc4 * 128:(cc4 + 1) * 128], xT[:, kc, 0:ntok], kc == 0, kc == 7, [wB] + xTB[0:nblk], [psB[k]])
                pi = cc % 2
                pcv = pcm[pi]
                CP("pool", pcv[:, 0:2], fhalo[:, l, cc, :], [fhaloB[l]], [pcmB[pi]])
                CP("act", pcv[:, 2:2 + ntok], o, [psB[k]], [pcmB[pi]])
                a = acc[pi][:, 0:ntok]
                ACT(a, o, AF.Identity, [psB[k], prmB], [accB[pi]], bias=fcb_s[:, l, cc:cc + 1], scale=fcw_s[:, l, cc, 2:3])
                CP("pool", fhalo[:, l, cc, :], pcv[:, ntok:ntok + 2], [pcmB[pi]], [fhaloB[l]])
                for j in range(2):
                    STT(a, pcv[:, j:j + ntok], fcw_s[:, l, cc, j:j + 1], a, ALU.mult, ALU.add, [pcmB[pi], prmB, accB[pi]], [accB[pi]])
                if up_pend[0] is not None:
                    up_final(*up_pend[0])
                up_pend[0] = (cc, a, pi)
            wdone(1)
        up_final(*up_pend[0])
        if last:
            for g in range(11):
                ws, wB = wnext(("up", l, g * 512, 512))
                k, o = proj_tok(ws, wB, nblk - 1, 512)
                CP("dve", stg[0:bs, :], o, [psB[k]], [stgB])
                DMA("sp", tp["fc_out"][l, :, g * 512:(g + 1) * 512], stg[bs - 2:bs, :], [stgB], [], stgB)
                wdone(1)
        sch.phase = "down"
        load_lnp(0, ln2g); load_lnp(1, ln2b)
        for nh in range(2):
            pcs = [wnext(("down", l, nh, pc)) for pc in range(3)]
            for b in range(nblk):
                k = pbank()
                o = ps[0:bs, k, 0:512]
                for kc in range(22):
                    ws, wB = pcs[kc // 8]
                    MM(o, hT[:, kc, b * bs:(b + 1) * bs], ws[:, kc % 8, :], kc == 0, kc == 21, [wB, hTB[kc]], [psB[k]])
                xs_ = xtok[0:bs, b, nh * 512:(nh + 1) * 512]
                STT(xs_, xs_, ALPHA, o, ALU.mult, ALU.add, [xtokB[b], psB[k]], [xtokB[b]])
            wdone(3)
        for b in range(nblk):
            layernorm_inplace(b, 3, 4)
            if l == NL - 1:
                DMA("sp", tp["y_out"][b * bs:(b + 1) * bs, :], xtok[0:bs, b, :], [xtokB[b]], [], xtokB[b])
        if last:
            for h in range(HM):
                DMA("sp", tp["C_out"][l, h].rearrange("(c p) v -> p c v", p=128), Cg[:, :, h, 0:256], [CgB[h]], [], CgB[h])
                DMA("sp", tp["n_out"][l, h].rearrange("(c p o) -> p c o", p=128, o=1), Cg[:, :, h, 256:257], [CgB[h]], [], CgB[h], slow=True)
            DMA("sp", tp["m_out"][l].rearrange("(p o) -> p o", o=1), mstate[:, l:l + 1], [mstateB[l]], [], mstateB[l], slow=True)

    def sample_prep():
        sch.phase = "prep"
        for l in range(NL):
            for blk in range(P // 128):
                DMA("pool", vtmp, ck[l, blk * 128:(blk + 1) * 128, :], [], [vtmpB], vtmpB)
                k = tbank()
                pt = psb16(k)
                for h in range(HA):
                    TR(pt[:, h * 128:(h + 1) * 128], vtmp[:, h * 128:(h + 1) * 128], [vtmpB, constB], [psB[k]])
                CP(eveng(), ktmp, pt.rearrange("p (h t) -> p h t", h=HA), [psB[k]], [ktmpB])
                DMA("sp", KTs[l][:, :, blk * 128:(blk + 1) * 128].rearrange("h p t -> p h t"), ktmp, [ktmpB], [KTsB[l]], ktmpB)
                DMA("pool", ybf, cv[l, blk * 128:(blk + 1) * 128, :], [], [ybfB], ybfB)
                DMA("sp", Vbs[l, blk * 128:(blk + 1) * 128, :], ybf, [ybfB], [VbsB[l]], ybfB)

    steps = []
    for i in range(NT):
        for l in range(NL):
            steps.append((l, i, False))
    for l in range(NL):
        steps.append((l, 0, True))
    for (l, i, samp) in steps:
        wq.extend(wspec_step(l, samp or i == NT - 1))

    prep_done = False
    for (l, i, samp) in steps:
        if samp and not prep_done:
            sample_prep()
            prep_done = True
        if not samp:
            tp = dict(ntok=T, bs=128, nblk=NB, L=64, last=(i == NT - 1), first=(i == 0), load_state=False,
                      x_src=xp[i * T:(i + 1) * T, :], pos0=i * T, tok0=i * T, mask=True,
                      prior=[(KTp[l][:, :, g * T:(g + 1) * T], Vbp[l, g * T:(g + 1) * T, :], [KTpB[l][g], VbpB[l][g]]) for g in range(i)],
                      KT_dst=KTp[l][:, :, i * T:(i + 1) * T], KTB=KTpB[l][i], Vb_dst=Vbp[l, i * T:(i + 1) * T, :], VbB=VbpB[l][i],
                      k_out=kp[:, i * T:(i + 1) * T, :], v_out=vp[:, i * T:(i + 1) * T, :], y_out=yp[i * T:(i + 1) * T, :],
                      mc_out=mcp, fc_out=fcp, C_out=Cp, n_out=np_, m_out=mp)
        else:
            tp = dict(ntok=NS, bs=NS, nblk=1, L=NS, last=True, first=False, load_state=True,
                      x_src=xs, pos0=S, tok0=0, mask=False,
                      prior=[(KTs[l][:, :, g * T:(g + 1) * T], Vbs[l, g * T:(g + 1) * T, :], [KTsB[l], VbsB[l]]) for g in range(P // T)],
                      KT_dst=None, KTB=None, Vb_dst=None, VbB=None,
                      k_out=ks, v_out=vs, y_out=ys, mc_out=mcs, fc_out=fcs, C_out=Cs, n_out=ns_, m_out=ms,
                      sC=sC, sn=sn, sm=sm, smc=smc, sfc=sfc)
        step(l, tp)
    assert wstate["used"] == len(wq) and wstate["released"] == len(wq), (wstate, len(wq))
    print("SBUF/PSUM allocation done")
    info = sch.emit()
    return nc, info


def rope_tables(S, P):
    half = 32
    inv = (np.float32(10000.0) ** (-np.arange(half, dtype=np.float32) * np.float32(2.0) / np.float32(64))).astype(np.float32)
    pos = np.concatenate([np.arange(S), P + np.arange(NS)]).astype(np.float32)
    ang = (pos[:, None] * inv[None, :]).astype(np.float32)
    return np.cos(ang).astype(np.float32), np.sin(ang).astype(np.float32)


_CACHE = {}


def run(inputs, S, P, T, n_prompt, n_sample, n_cores):
    key = (S, P, T)
    if key not in _CACHE:
        _CACHE[key] = build(S=S, P=P, T=T)
    nc, info = _CACHE[key]
    f = lambda a: np.ascontiguousarray(np.asarray(a, dtype=np.float32))
    cosT, sinT = rope_tables(S, P)
    NL = 2
    in_maps = []
    for c in range(n_cores):
        b = c % n_prompt
        s = c % n_sample
        m = {
            "xp": f(inputs["x_prompt"][b]), "xs": f(inputs["x_sample"][s]),
            "ck": f(inputs["cache_k"][:, s]).reshape(NL, P, D), "cv": f(inputs["cache_v"][:, s]).reshape(NL, P, D),
            "smc": f(inputs["state_mlstm_conv"][:, s]), "sC": f(inputs["state_mlstm_C"][:, s]),
            "sn": f(inputs["state_mlstm_n"][:, s]), "sm": f(inputs["state_mlstm_m"][:, s]),
            "sfc": f(inputs["state_ffn_conv"][:, s]),
            "w_in": f(inputs["w_in"]), "b_if": f(inputs["b_if"]), "mcw": f(inputs["mlstm_conv_w"]), "mcb": f(inputs["mlstm_conv_b"]),
            "dlam": f(inputs["diff_lambda"]), "subg": f(inputs["diff_subln_g"]), "mhg": f(inputs["mlstm_norm_g"]),
            "w_out": f(inputs["w_out"]), "ln1g": f(inputs["ln1_g"]), "ln1b": f(inputs["ln1_b"]),
            "w_up": f(inputs["w_up"]), "fcw": f(inputs["ffn_conv_w"]), "fcb": f(inputs["ffn_conv_b"]),
            "w_down": f(inputs["w_down"]), "ln2g": f(inputs["ln2_g"]), "ln2b": f(inputs["ln2_b"]),
            "cosT": cosT, "sinT": sinT,
        }
        in_maps.append(m)
    res = run_bass_kernel_spmd(nc, in_maps, core_ids=list(range(n_cores)))
    R = res.results
    pc = list(range(n_prompt))
    sc = list(range(n_sample))
    st = lambda name, cores, ax=0: np.stack([np.asarray(R[c][name], dtype=np.float32) for c in cores], axis=ax)
    y_prompt = st("yp", pc)
    y_sample = st("ys", sc)
    k_prompt = st("kp", pc, 1).reshape(NL, n_prompt, S, HA, 128)
    v_prompt = st("vp", pc, 1).reshape(NL, n_prompt, S, HA, 128)
    outs = (y_prompt, y_sample, k_prompt, v_prompt,
            st("mcp", pc, 1), st("Cp", pc, 1), st("np", pc, 1), st("mp", pc, 1), st("fcp", pc, 1),
            st("ks", sc, 1).reshape(NL, n_sample, NS, HA, 128), st("vs", sc, 1).reshape(NL, n_sample, NS, HA, 128),
            st("mcs", sc, 1), st("Cs", sc, 1), st("ns", sc, 1), st("ms", sc, 1), st("fcs", sc, 1))
    return outs


def kernel(**inputs):
    return run(inputs, S=8192, P=4096, T=512, n_prompt=4, n_sample=8, n_cores=8)
```

```python
import math
import numpy as np
import concourse.bass as bass
import concourse.mybir as mybir
from concourse.bass_utils import run_bass_kernel_spmd

F32 = mybir.dt.float32
BF16 = mybir.dt.bfloat16
AF = mybir.ActivationFunctionType
ALU = mybir.AluOpType
AX = mybir.AxisListType

D = 1024
HA = 8
HM = 4
DFF = 2816
DIN = 9224
NS = 16
ALPHA = (2 * 2) ** 0.25
LN_EPS = 1e-5
C_QA, C_KA, C_VA, C_QKM, C_VM, C_OM, C_GIF, C_GA, C_GB = 0, 1024, 2048, 3072, 5120, 6144, 7168, 7176, 8200


class Buf:
    __slots__ = ("name", "lastw", "readers")

    def __init__(self, name=""):
        self.name = name
        self.lastw = None
        self.readers = []


class Sched:
    def __init__(self, nc, same_engine_sync=True):
        self.nc = nc
        self.engs = {"pe": nc.tensor, "act": nc.scalar, "dve": nc.vector, "pool": nc.gpsimd, "sp": nc.sync}
        self.ins = []
        self.dma_cnt = {}
        self.same = same_engine_sync
        self.phase = ""
        self.names = None

    def add(self, eng, meth, args, kwargs, reads=(), writes=(), dma=None):
        idx = len(self.ins)
        deps = set()
        for r in reads:
            if r.lastw is not None:
                deps.add(r.lastw)
        for w in writes:
            if w.lastw is not None:
                deps.add(w.lastw)
            deps.update(w.readers)
        for r in reads:
            r.readers.append(idx)
        for w in writes:
            w.lastw = idx
            w.readers = []
        dval = None
        if dma is not None:
            self.dma_cnt[dma] = self.dma_cnt.get(dma, 0) + 16
            dval = self.dma_cnt[dma]
        keep = set()
        for d in deps:
            de = self.ins[d]
            if de[5] is None and de[0] == eng:
                if eng == "pe" or not self.same:
                    continue
            keep.add(d)
        self.ins.append([eng, meth, args, kwargs, keep, dma, dval, False, 0, self.phase])
        return idx

    def emit(self):
        nc = self.nc
        for rec in self.ins:
            for d in rec[4]:
                de = self.ins[d]
                if de[5] is None:
                    de[7] = True
        cnt = {e: 0 for e in self.engs}
        for rec in self.ins:
            if rec[7]:
                cnt[rec[0]] += 1
                rec[8] = cnt[rec[0]]
        esem = {e: nc.alloc_semaphore(name="es_" + e) for e in self.engs}
        dsem = {}
        for k in self.dma_cnt:
            dsem[k] = nc.alloc_semaphore(name="ds_%d" % len(dsem))
        waited = {e: {} for e in self.engs}
        nwait = 0
        for rec in self.ins:
            eng, meth, args, kwargs, deps, dma, dval, sig, sigval, phase = rec
            E = self.engs[eng]
            need = {}
            for d in deps:
                de = self.ins[d]
                if de[5] is None:
                    s, v = esem[de[0]], de[8]
                else:
                    s, v = dsem[de[5]], de[6]
                if need.get(s, 0) < v:
                    need[s] = v
            for s, v in need.items():
                if waited[eng].get(s, 0) >= v:
                    continue
                E.wait_ge(s, v)
                waited[eng][s] = v
                nwait += 1
            ins = getattr(E, meth)(*args, **kwargs)
            if self.names is not None:
                self.names[ins.ins.name] = phase
            if dma is not None:
                ins.then_inc(dsem[dma], 16)
            elif sig:
                ins.then_inc(esem[eng], 1)
        for k, v in self.dma_cnt.items():
            nc.sync.wait_ge(dsem[k], v)
        return dict(n=len(self.ins), nwait=nwait, nsem=len(dsem) + 5)


def build(S=8192, P=4096, T=512, NL=2, dbg=None, same=True):
    nc = bass.Bass("TRN2", target_bir_lowering=False)
    sch = Sched(nc, same_engine_sync=same)
    NT = S // T
    NTAB = S + NS

    def din(name, shape, dt=F32):
        return nc.dram_tensor(name, list(shape), dt, kind="ExternalInput").ap()

    def dout(name, shape, dt=F32):
        return nc.dram_tensor(name, list(shape), dt, kind="ExternalOutput").ap()

    def dscr(name, shape, dt):
        return nc.dram_tensor(name, list(shape), dt, kind="Internal").ap()

    def sb(name, shape, dt=F32):
        return nc.alloc_sbuf_tensor(name, list(shape), dt).ap()

    xp = din("xp", [S, D]); xs = din("xs", [NS, D])
    ck = din("ck", [NL, P, D]); cv = din("cv", [NL, P, D])
    smc = din("smc", [NL, 3, 2048]); sC = din("sC", [NL, HM, 256, 256]); sn = din("sn", [NL, HM, 256])
    sm = din("sm", [NL, HM]); sfc = din("sfc", [NL, 2, 2 * DFF])
    w_in = din("w_in", [NL, D, DIN]); b_if = din("b_if", [NL, 8])
    mcw = din("mcw", [NL, 4, 2048]); mcb = din("mcb", [NL, 2048])
    dlam = din("dlam", [NL, 4, 64]); subg = din("subg", [NL, 128]); mhg = din("mhg", [NL, D])
    w_out = din("w_out", [NL, D, D]); ln1g = din("ln1g", [NL, D]); ln1b = din("ln1b", [NL, D])
    w_up = din("w_up", [NL, D, 2 * DFF]); fcw = din("fcw", [NL, 3, 2 * DFF]); fcb = din("fcb", [NL, 2 * DFF])
    w_down = din("w_down", [NL, DFF, D]); ln2g = din("ln2g", [NL, D]); ln2b = din("ln2b", [NL, D])
    cosT = din("cosT", [NTAB, 32]); sinT = din("sinT", [NTAB, 32])

    yp = dout("yp", [S, D]); ys = dout("ys", [NS, D])
    kp = dout("kp", [NL, S, D]); vp = dout("vp", [NL, S, D])
    mcp = dout("mcp", [NL, 3, 2048]); Cp = dout("Cp", [NL, HM, 256, 256]); np_ = dout("np", [NL, HM, 256])
    mp = dout("mp", [NL, HM]); fcp = dout("fcp", [NL, 2, 2 * DFF])
    ks = dout("ks", [NL, NS, D]); vs = dout("vs", [NL, NS, D])
    mcs = dout("mcs", [NL, 3, 2048]); Cs = dout("Cs", [NL, HM, 256, 256]); ns_ = dout("ns", [NL, HM, 256])
    ms = dout("ms", [NL, HM]); fcs = dout("fcs", [NL, 2, 2 * DFF])

    KTp = dscr("KTp", [NL, HA, 128, S], BF16); Vbp = dscr("Vbp", [NL, S, D], BF16)
    KTs = dscr("KTs", [NL, HA, 128, P], BF16); Vbs = dscr("Vbs", [NL, P, D], BF16)
    KTpB = [[Buf() for _ in range(NT)] for _ in range(NL)]
    VbpB = [[Buf() for _ in range(NT)] for _ in range(NL)]
    KTsB = [Buf() for _ in range(NL)]
    VbsB = [Buf() for _ in range(NL)]

    NB = T // 128
    xtok = sb("xtok", [128, NB, D]); xtokB = [Buf() for _ in range(NB)]
    xbf = sb("xbf", [128, D], BF16); xbfB = Buf()
    xT = sb("xT", [128, 8, T], BF16); xTB = [Buf() for _ in range(NB)]
    NSLOT = 4
    ring = [sb("ring%d" % i, [128, 8, 512], BF16) for i in range(NSLOT)]
    ringB = [Buf() for _ in range(NSLOT)]
    zf = sb("zf", [128, D]); zfB = Buf()
    ro = sb("ro", [128, D]); roB = Buf()
    cs_t = sb("cs_t", [128, NB, 2, 32]); csB = Buf()
    QT = sb("QT", [128, HA, T], BF16); QTB = [Buf() for _ in range(NB)]
    KT = sb("KT", [128, HA, T], BF16); KTB = [Buf() for _ in range(NB)]
    Vaug = sb("Vaug", [128, NB, HA, 129], BF16); VaugB = [Buf() for _ in range(NB)]
    VMaug = sb("VMaug", [128, NB, HM, 257], BF16); VMB = [Buf() for _ in range(NB)]
    oab = sb("oab", [128, NB, D], BF16); oabB = [Buf() for _ in range(NB)]
    hmb = sb("hmb", [128, NB, D], BF16); hmbB = [Buf() for _ in range(NB)]
    oaF = oab; oaFB = oabB
    hT = sb("hT", [128, 22, T], BF16); hTB = [Buf() for _ in range(22)]
    qkT = hT; qkTB = hTB
    pcm = [sb("pcm%d" % i, [128, 3 + T]) for i in range(2)]; pcmB = [Buf(), Buf()]
    acc = [sb("acc%d" % i, [128, T]) for i in range(2)]; accB = [Buf(), Buf()]
    rt = acc; rtB = accB
    stg = pcm[0][:, 0:512]; stgB = pcmB[0]
    kch = [sb("kch%d" % i, [128, T], BF16) for i in range(2)]; kchB = [Buf(), Buf()]
    vch = [sb("vch%d" % i, [128, NB, 129], BF16) for i in range(2)]; vchB = [Buf(), Buf()]
    PT = [sb("PT%d" % i, [128, 2, T], BF16) for i in range(2)]; PTB = [Buf(), Buf()]
    Caug = sb("Caug", [128, 2, HM, 257]); CaugB = [Buf() for _ in range(HM)]
    GbRaw = sb("GbRaw", [128, 2 * HM * 257], BF16); GbB = [Buf() for _ in range(HM)]
    Gb = GbRaw.rearrange("p (a h d) -> p a h d", a=2, h=HM)
    Gb2 = sb("Gb2", [128, 2, HM, 257], BF16); Gb2B = [Buf() for _ in range(HM)]
    GbL = [(Gb, GbB), (Gb2, Gb2B)]
    ro2 = GbRaw[:, 0:2 * D].bitcast(F32); ro2B = GbB
    kw = sb("kw", [128, D], BF16); kwB = Buf()
    Sm = sb("Sm", [128, HM, 128], BF16); SmB = Buf()
    hmf = sb("hmf", [128, D]); hmfB = Buf()
    sml = sb("sml", [128, 64]); smlB = Buf()
    mhalo = sb("mhalo", [128, NL, 16, 3]); mhaloB = [Buf() for _ in range(NL)]
    fhalo = sb("fhalo", [128, NL, 44, 2]); fhaloB = [Buf() for _ in range(NL)]
    mstate = sb("mstate", [4, NL]); mstateB = [Buf() for _ in range(NL)]
    CaugL = [Caug, sb("Caug1", [128, 2, HM, 257])]
    CaugLB = [CaugB, [Buf() for _ in range(HM)]]
    _g3 = [PT[0].rearrange("p a t -> p (a t)").bitcast(F32)[0:4, :], PT[1].rearrange("p a t -> p (a t)").bitcast(F32)[0:4, :],
           kw.bitcast(F32)[0:4, :]]
    _g3B = [PTB[0], PTB[1], kwB]
    gw = {"t1": _g3[0], "sp": _g3[0], "A": _g3[1], "cl": _g3[1], "ig": _g3[2], "r": _g3[2], "sc": _g3[2]}
    gwB = {"t1": _g3B[0], "sp": _g3B[0], "A": _g3B[1], "cl": _g3B[1], "ig": _g3B[2], "r": _g3B[2], "sc": _g3B[2]}
    gs = {n: sb("gs_" + n, [4, 16]) for n in ("cm", "aL", "mn", "mu", "mpv", "gam")}
    gsB = {n: Buf() for n in gs}
    gD = sb("gD", [4, 16, 4]); gDB = Buf()
    toksc = sb("toksc", [128, NB, 8]); tokscB = Buf()
    gam = sb("gam", [128, 16, 4]); gamB = Buf()
    identb = sb("identb", [128, 128], BF16); identf = sb("identf", [128, 128]); maskBD = sb("maskBD", [128, 128])
    ones4 = sb("ones4", [4, 128]); resetm = sb("resetm", [4, T]); resetm16 = sb("resetm16", [4, NS])
    constB = Buf()
    mcw_s = sb("mcw_s", [128, NL, 16, 4]); mcb_s = sb("mcb_s", [128, NL, 16])
    fcw_s = sb("fcw_s", [128, NL, 44, 3]); fcb_s = sb("fcb_s", [128, NL, 44])
    bif_s = sb("bif_s", [4, NL, 2]); nbif_s = sb("nbif_s", [4, NL, 2])
    neglam = sb("neglam", [128, NL]); lamw = sb("lamw", [128, 4, 64]); lamw2 = sb("lamw2", [128, 4])
    subg_s = sb("subg_s", [128, NL, 128])
    prmB = Buf()
    lnp = sb("lnp", [128, 2, D]); lnpB = [Buf(), Buf()]
    sg = [sb("sg%d" % i, [128, 512], BF16) for i in range(2)]; sgB = [Buf(), Buf()]
    ybf = xbf; ybfB = xbfB
    stt = sb("stt", [128, 2, 6]); mv = sb("mv", [128, 4]); lnB = Buf()
    vtmp = xbf; vtmpB = xbfB
    ktmp = kw.rearrange("p (h t) -> p h t", h=HA); ktmpB = kwB
    ps = nc.alloc_psum_tensor("ps", [128, 8, 512], F32).ap()
    psB = [Buf() for _ in range(8)]

    def psb16(k):
        return ps[:, k, :].bitcast(BF16)

    def MM(out, lhsT, rhs, start, stop, r, w, skip=False):
        kw_ = dict(lhsT=lhsT, rhs=rhs, start=start, stop=stop)
        if skip:
            kw_["skip_group_check"] = True
        sch.add("pe", "matmul", (out,), kw_, r, w)

    def TR(out, in_, r, w):
        sch.add("pe", "transpose", (), dict(out=out, in_=in_, identity=identb[0:in_.shape[0], 0:in_.shape[0]]), r, w)

    def ACT(out, in_, func, r, w, bias=None, scale=None, accum_out=None):
        kw_ = dict(out=out, in_=in_, func=func)
        if bias is not None:
            kw_["bias"] = bias
        if scale is not None:
            kw_["scale"] = scale
        if accum_out is not None:
            kw_["accum_out"] = accum_out
        sch.add("act", "activation", (), kw_, r, w)

    def CP(eng, out, in_, r, w):
        if eng == "act":
            ACT(out, in_, AF.Identity, r, w)
        else:
            sch.add(eng, "tensor_copy", (), dict(out=out, in_=in_), r, w)

    def TT(out, in0, in1, op, r, w, eng="dve"):
        sch.add(eng, "tensor_tensor", (), dict(out=out, in0=in0, in1=in1, op=op), r, w)

    def TS(out, in0, s1, s2, op0, op1, r, w, eng="dve"):
        kw_ = dict(out=out, in0=in0, scalar1=s1, scalar2=s2, op0=op0)
        if op1 is not None:
            kw_["op1"] = op1
        sch.add(eng, "tensor_scalar", (), kw_, r, w)

    def STT(out, in0, scalar, in1, op0, op1, r, w, eng="dve"):
        sch.add(eng, "scalar_tensor_tensor", (), dict(out=out, in0=in0, scalar=scalar, in1=in1, op0=op0, op1=op1), r, w)

    def RSQRT(ap, addc, r, w):
        TS(ap, ap, addc, None, ALU.add, None, r, w)
        ACT(ap, ap, AF.Sqrt, r, w)
        sch.add("dve", "reciprocal", (), dict(out=ap, in_=ap), r, w)

    def RED(out, in_, op, r, w):
        sch.add("dve", "tensor_reduce", (), dict(out=out, in_=in_, axis=AX.X, op=op), r, w)

    def MEMSET(eng, ap, val, r, w):
        sch.add(eng, "memset", (ap, val), {}, r, w)

    def DMA(q, out, in_, r, w, key, slow=False):
        kw_ = dict(out=out, in_=in_)
        if slow:
            kw_["allow_slow_non_contiguous"] = True
        sch.add(q, "dma_start", (), kw_, r, w, dma=key)

    MEMSET("pool", identf, 1.0, [], [constB])
    sch.add("pool", "affine_select", (), dict(out=identf, in_=identf, compare_op=ALU.is_equal, fill=0.0, base=0,
                                              pattern=[[-1, 128]], channel_multiplier=1), [constB], [constB])
    CP("dve", identb, identf, [constB], [constB])
    MEMSET("pool", maskBD, 1.0, [constB], [constB])
    sch.add("pool", "affine_select", (), dict(out=maskBD, in_=maskBD, compare_op=ALU.is_ge, fill=0.0, base=0,
                                              pattern=[[1, 128]], channel_multiplier=-1), [constB], [constB])
    MEMSET("pool", maskBD[0:64, 64:128], 0.0, [constB], [constB])
    MEMSET("pool", ones4, 1.0, [constB], [constB])
    MEMSET("pool", resetm, 1.0, [constB], [constB])
    MEMSET("pool", resetm.rearrange("p (c l) -> p c l", l=64)[:, :, 0:1], 0.0, [constB], [constB])
    MEMSET("pool", resetm16, 1.0, [constB], [constB])
    MEMSET("pool", resetm16[:, 0:1], 0.0, [constB], [constB])
    MEMSET("pool", Vaug[:, :, :, 128:129], 1.0, [], VaugB)
    MEMSET("pool", VMaug[:, :, :, 256:257], 1.0, [], VMB)
    for i in range(2):
        MEMSET("pool", vch[i][:, :, 128:129], 1.0, [], [vchB[i]])
    pk = Buf()
    for l in range(NL):
        for j in range(4):
            DMA("sp", mcw_s[:, l, :, j], mcw[l, j].rearrange("(c p) -> p c", p=128), [], [prmB], pk, slow=True)
        DMA("sp", mcb_s[:, l, :], mcb[l].rearrange("(c p) -> p c", p=128), [], [prmB], pk, slow=True)
        for j in range(3):
            DMA("sp", fcw_s[:, l, :, j], fcw[l, j].rearrange("(c p) -> p c", p=128), [], [prmB], pk, slow=True)
        DMA("sp", fcb_s[:, l, :], fcb[l].rearrange("(c p) -> p c", p=128), [], [prmB], pk, slow=True)
        DMA("sp", bif_s[:, l, :], b_if[l].rearrange("(j p) -> p j", p=4), [], [prmB], pk, slow=True)
        DMA("sp", subg_s[:, l, :], subg[l].partition_broadcast(128), [], [prmB], pk)
        DMA("sp", lamw, dlam[l].partition_broadcast(128), [prmB], [prmB], pk)
        lam_init = 0.8 - 0.6 * math.exp(-0.3 * l)
        lv = lamw.rearrange("p (a b) d -> p a b d", b=2)
        TT(lamw[:, 0:2, :].rearrange("p a d -> p a d"), lv[:, :, 0, :], lv[:, :, 1, :], ALU.mult, [prmB], [prmB])
        RED(lamw2[:, 0:2], lamw[:, 0:2, :], ALU.add, [prmB], [prmB])
        ACT(lamw2[:, 2:4], lamw2[:, 0:2], AF.Exp, [prmB], [prmB])
        TT(neglam[:, l:l + 1], lamw2[:, 3:4], lamw2[:, 2:3], ALU.subtract, [prmB], [prmB])
        TS(neglam[:, l:l + 1], neglam[:, l:l + 1], -lam_init, None, ALU.add, None, [prmB], [prmB])
        TS(subg_s[:, l, :], subg_s[:, l, :], (1.0 - lam_init) * math.sqrt(128.0), None, ALU.mult, None, [prmB], [prmB])
    TS(nbif_s, bif_s, -1.0, None, ALU.mult, None, [prmB], [prmB])

    wq = []
    wstate = dict(loaded=0, used=0, released=0)

    def wspec_step(l, last):
        sp_ = []
        for c0 in (C_QA, C_QA + 512, C_KA, C_KA + 512, C_VA, C_VA + 512):
            sp_.append(("in", l, c0, 512))
        for c0 in range(C_QKM, C_QKM + 2048, 512):
            sp_.append(("in", l, c0, 512))
        if last:
            for c0 in range(C_QKM, C_QKM + 2048, 512):
                sp_.append(("in", l, c0, 512))
        sp_.append(("in", l, C_VM, 512)); sp_.append(("in", l, C_VM + 512, 512))
        sp_.append(("in", l, C_GIF, 8))
        for c0 in (C_OM, C_OM + 512, C_GA, C_GA + 512, C_GB, C_GB + 512):
            sp_.append(("in", l, c0, 512))
        sp_.append(("out", l, 0, 512)); sp_.append(("out", l, 512, 512))
        for g in range(11):
            sp_.append(("up", l, g * 512, 512))
        if last:
            for g in range(11):
                sp_.append(("up", l, g * 512, 512))
        for nh in range(2):
            for pc in range(3):
                sp_.append(("down", l, nh, pc))
        return sp_

    def wload(i):
        kind, l, a, b = wq[i]
        slot = i % NSLOT
        if kind == "in":
            src = w_in[l].rearrange("(kc p) n -> p kc n", p=128)[:, :, a:a + b]
            dst = ring[slot][:, :, 0:b]
        elif kind == "out":
            src = w_out[l].rearrange("(kc p) n -> p kc n", p=128)[:, :, a:a + b]
            dst = ring[slot][:, :, 0:b]
        elif kind == "up":
            src = w_up[l].rearrange("(kc p) n -> p kc n", p=128)[:, :, a:a + b]
            dst = ring[slot][:, :, 0:b]
        else:
            k0 = b * 8
            k1 = min(22, k0 + 8)
            src = w_down[l].rearrange("(kc p) n -> p kc n", p=128)[:, k0:k1, a * 512:(a + 1) * 512]
            dst = ring[slot][:, 0:k1 - k0, :]
        DMA("pool", dst, src, [], [ringB[slot]], ringB[slot])

    def wfill():
        while wstate["loaded"] < min(len(wq), wstate["released"] + NSLOT):
            wload(wstate["loaded"])
            wstate["loaded"] += 1

    def wnext(expect):
        i = wstate["used"]
        assert wq[i] == expect, (wq[i], expect)
        wfill()
        assert wstate["loaded"] > i
        wstate["used"] += 1
        return ring[i % NSLOT], ringB[i % NSLOT]

    def wdone(n=1):
        wstate["released"] += n
        assert wstate["released"] <= wstate["used"]
        wfill()

    rot = dict(pp=0, tp=0)

    def pbank():
        k = rot["pp"] % 4
        rot["pp"] += 1
        return k

    def tbank():
        k = 6 + rot["tp"] % 2
        rot["tp"] += 1
        return k

    evr = dict(i=0)

    def eveng():
        evr["i"] += 1
        return "act" if evr["i"] % 2 else "dve"

    def step(l, tp):
        ntok, bs, nblk, L = tp["ntok"], tp["bs"], tp["nblk"], tp["L"]
        cpb = bs // L
        nch = ntok // L
        last = tp["last"]
        Cg = CaugL[l]; CgB = CaugLB[l]

        def load_lnp(slot, src):
            DMA("sp", lnp[:, slot, :], src[l].partition_broadcast(128), [], [lnpB[slot]], lnpB[slot])
        load_lnp(0, mhg)
        if l == 0:
            for b in range(nblk):
                DMA("sp", xtok[0:bs, b, :], tp["x_src"][b * bs:(b + 1) * bs, :], [], [xtokB[b]], xtokB[b])
        DMA("sp", cs_t[0:bs, 0:nblk, 0, :], cosT[tp["pos0"]:tp["pos0"] + ntok, :].rearrange("(b p) d -> p b d", p=bs), [], [csB], csB)
        DMA("sp", cs_t[0:bs, 0:nblk, 1, :], sinT[tp["pos0"]:tp["pos0"] + ntok, :].rearrange("(b p) d -> p b d", p=bs), [], [csB], csB)
        if tp["load_state"]:
            for h in range(HM):
                DMA("sp", Cg[:, :, h, 0:256], tp["sC"][l, h].rearrange("(c p) v -> p c v", p=128), [], [CgB[h]], CgB[h])
                DMA("sp", Cg[:, :, h, 256:257], tp["sn"][l, h].rearrange("(c p o) -> p c o", p=128, o=1), [], [CgB[h]], CgB[h], slow=True)
            DMA("sp", mstate[:, l:l + 1], tp["sm"][l].rearrange("(p o) -> p o", o=1), [], [mstateB[l]], mstateB[l], slow=True)
            for j in range(3):
                DMA("sp", mhalo[:, l, :, j], tp["smc"][l, j].rearrange("(c p) -> p c", p=128), [], [mhaloB[l]], mhaloB[l], slow=True)
            for j in range(2):
                DMA("sp", fhalo[:, l, :, j], tp["sfc"][l, j].rearrange("(c p) -> p c", p=128), [], [fhaloB[l]], fhaloB[l], slow=True)
        elif tp["first"]:
            for h in range(HM):
                MEMSET("pool", Cg[:, :, h, :], 0.0, [], [CgB[h]])
            MEMSET("pool", mstate[:, l:l + 1], 0.0, [], [mstateB[l]])
            MEMSET("pool", mhalo[:, l, :, :], 0.0, [], [mhaloB[l]])
            MEMSET("pool", fhalo[:, l, :, :], 0.0, [], [fhaloB[l]])

        def make_xT():
            for b in range(nblk):
                CP(eveng(), xbf[0:bs, :], xtok[0:bs, b, :], [xtokB[b]], [xbfB])
                k = tbank()
                pt = psb16(k)
                for c in range(8):
                    TR(pt[:, c * bs:(c + 1) * bs], xbf[0:bs, c * 128:(c + 1) * 128], [xbfB, constB], [psB[k]])
                CP(eveng(), xT[:, :, b * bs:(b + 1) * bs], pt[:, 0:8 * bs].rearrange("p (c t) -> p c t", c=8), [psB[k]], [xTB[b]])

        sch.phase = "xT"
        make_xT()

        def proj_tok(wslot, wB, b, ncols, kcn=8, lhs=None, lhsB=None):
            k = pbank()
            out = ps[0:bs, k, 0:ncols]
            for kc in range(kcn):
                lt = xT[:, kc, b * bs:(b + 1) * bs] if lhs is None else lhs(kc)
                MM(out, lt, wslot[:, kc, 0:ncols], kc == 0, kc == kcn - 1, [wB, xTB[b] if lhsB is None else lhsB], [psB[k]])
            return k, out

        def rope_block(src_zf, zfB, dst, roB, b):
            sv = src_zf.rearrange("p (g two d) -> p g two d", two=2, d=32)
            dv = dst.rearrange("p (g two d) -> p g two d", two=2, d=32)
            cosb = cs_t[0:bs, b, 0:1, :].to_broadcast([bs, 16, 32])
            sinb = cs_t[0:bs, b, 1:2, :].to_broadcast([bs, 16, 32])
            t0 = rt[0][0:bs, :].rearrange("p (g d) -> p g d", d=32)
            t1 = rt[1][0:bs, :].rearrange("p (g d) -> p g d", d=32)
            TT(t0, sv[:, :, 0, :], cosb, ALU.mult, [zfB, csB], [rtB[0]])
            TT(t1, sv[:, :, 1, :], sinb, ALU.mult, [zfB, csB], [rtB[1]])
            TT(dv[:, :, 0, :], t0, t1, ALU.subtract, [rtB[0], rtB[1]], roB)
            TT(t0, sv[:, :, 1, :], cosb, ALU.mult, [zfB, csB], [rtB[0]])
            TT(t1, sv[:, :, 0, :], sinb, ALU.mult, [zfB, csB], [rtB[1]])
            TT(dv[:, :, 1, :], t0, t1, ALU.add, [rtB[0], rtB[1]], roB)

        sch.phase = "qkv"
        zfs = [(zf, zfB), (hmf, hmfB)]
        ros = [(ro, [roB]), (ro2, ro2B)]
        pst = dict(n=0)

        def post(which, b, zb, zbB):
            if which == "v":
                DMA("sp", tp["v_out"][l, b * bs:(b + 1) * bs, :], zb[0:bs, :], [zbB], [], zbB)
                CP("dve", Vaug[0:bs, b, :, 0:128], zb[0:bs, :].rearrange("p (h d) -> p h d", h=HA), [zbB], [VaugB[b]])
                if tp["Vb_dst"] is not None:
                    DMA("sp", tp["Vb_dst"][b * bs:(b + 1) * bs, :].rearrange("p (h d) -> p h d", h=HA),
                        Vaug[0:bs, b, :, 0:128], [VaugB[b]], [tp["VbB"]], VaugB[b])
                return
            rb, rbB = ros[pst["n"] % 2]
            pst["n"] += 1
            rope_block(zb[0:bs, :], zbB, rb[0:bs, :], rbB, b)
            if which == "k":
                DMA("sp", tp["k_out"][l, b * bs:(b + 1) * bs, :], rb[0:bs, :], rbB, [], rbB[0])
            CP("act", xbf[0:bs, :], rb[0:bs, :], rbB, [xbfB])
            kb_ = tbank()
            pt = psb16(kb_)
            for h in range(HA):
                TR(pt[:, h * bs:(h + 1) * bs], xbf[0:bs, h * 128:(h + 1) * 128], [xbfB, constB], [psB[kb_]])
            dstT, dstB = (QT, QTB) if which == "q" else (KT, KTB)
            CP("act", dstT[:, :, b * bs:(b + 1) * bs], pt[:, 0:HA * bs].rearrange("p (h t) -> p h t", h=HA), [psB[kb_]], [dstB[b]])
            if which == "k" and b == nblk - 1 and tp["KT_dst"] is not None:
                DMA("sp", tp["KT_dst"].rearrange("h p t -> p h t"), KT[:, :, 0:ntok], KTB[0:nblk], [tp["KTB"]], KTB[0])

        pend = None
        nz = 0
        for which in ("q", "k", "v"):
            c0 = {"q": C_QA, "k": C_KA, "v": C_VA}[which]
            w0, w0B = wnext(("in", l, c0, 512))
            w1, w1B = wnext(("in", l, c0 + 512, 512))
            for b in range(nblk):
                zb, zbB = zfs[nz % 2]
                nz += 1
                for hf, (ws, wB) in enumerate(((w0, w0B), (w1, w1B))):
                    k, o = proj_tok(ws, wB, b, 512)
                    CP("act" if which != "v" else eveng(), zb[0:bs, hf * 512:(hf + 1) * 512], o, [psB[k]], [zbB])
                if pend is not None:
                    post(*pend)
                pend = (which, b, zb, zbB)
            wdone(2)
        post(*pend)

        sch.phase = "qkm"
        m_pend = [None]
        for g in range(4):
            ws, wB = wnext(("in", l, C_QKM + g * 512, 512))
            for cc4 in range(4):
                cc = g * 4 + cc4
                k = pbank()
                o = ps[:, k, 0:ntok]
                for kc in range(8):
                    MM(o, ws[:, kc, cc4 * 128:(cc4 + 1) * 128], xT[:, kc, 0:ntok], kc == 0, kc == 7, [wB] + xTB[0:nblk], [psB[k]])
                pi = cc % 2
                pcv = pcm[pi]
                CP("pool", pcv[:, 0:3], mhalo[:, l, cc, :], [mhaloB[l]], [pcmB[pi]])
                CP("act", pcv[:, 3:3 + ntok], o, [psB[k]], [pcmB[pi]])
                a = acc[pi][:, 0:ntok]
                ACT(a, o, AF.Identity, [psB[k], prmB], [accB[pi]], bias=mcb_s[:, l, cc:cc + 1], scale=mcw_s[:, l, cc, 3:4])
                CP("pool", mhalo[:, l, cc, :], pcv[:, ntok:ntok + 3], [pcmB[pi]], [mhaloB[l]])
                for j in range(3):
                    STT(a, pcv[:, j:j + ntok], mcw_s[:, l, cc, j:j + 1], a, ALU.mult, ALU.add, [pcmB[pi], prmB, accB[pi]], [accB[pi]])
                if m_pend[0] is not None:
                    ACT(qkT[:, m_pend[0][0], 0:ntok], m_pend[0][1], AF.Silu, [accB[m_pend[0][2]]], [qkTB[m_pend[0][0]]])
                m_pend[0] = (cc, a, pi)
            wdone(1)
        ACT(qkT[:, m_pend[0][0], 0:ntok], m_pend[0][1], AF.Silu, [accB[m_pend[0][2]]], [qkTB[m_pend[0][0]]])
        if last:
            for g in range(4):
                ws, wB = wnext(("in", l, C_QKM + g * 512, 512))
                k, o = proj_tok(ws, wB, nblk - 1, 512)
                CP("dve", stg[0:bs, :], o, [psB[k]], [stgB])
                DMA("sp", tp["mc_out"][l, :, g * 512:(g + 1) * 512], stg[bs - 3:bs, :], [stgB], [], stgB)
                wdone(1)
        sch.phase = "vm"
        w0, w0B = wnext(("in", l, C_VM, 512))
        w1, w1B = wnext(("in", l, C_VM + 512, 512))
        for b in range(nblk):
            for hf, (ws, wB) in enumerate(((w0, w0B), (w1, w1B))):
                k, o = proj_tok(ws, wB, b, 512)
                CP(eveng(), VMaug[0:bs, b, 2 * hf:2 * hf + 2, 0:256], o.rearrange("p (h d) -> p h d", h=2), [psB[k]], [VMB[b]])
        wdone(2)
        sch.phase = "gates"
        ws, wB = wnext(("in", l, C_GIF, 8))
        kI = pbank(); kF = pbank()
        for kc in range(8):
            MM(ps[0:4, kI, 0:ntok], ws[:, kc, 0:4], xT[:, kc, 0:ntok], kc == 0, kc == 7, [wB] + xTB[0:nblk], [psB[kI]])
        for kc in range(8):
            MM(ps[0:4, kF, 0:ntok], ws[:, kc, 4:8], xT[:, kc, 0:ntok], kc == 0, kc == 7, [wB] + xTB[0:nblk], [psB[kF]])
        wdone(1)
        g_ = {n: gw[n][:, 0:ntok] for n in gw}
        s_ = {n: gs[n][:, 0:nch] for n in gs}
        ACT(g_["ig"], ps[0:4, kI, 0:ntok], AF.Identity, [psB[kI], prmB], [gwB["ig"]], bias=bif_s[:, l, 0:1])
        ACT(g_["t1"], ps[0:4, kF, 0:ntok], AF.Exp, [psB[kF], prmB], [gwB["t1"]], bias=nbif_s[:, l, 1:2], scale=-1.0)
        TS(g_["t1"], g_["t1"], 1.0, None, ALU.add, None, [gwB["t1"]], [gwB["t1"]])
        ACT(g_["sp"], g_["t1"], AF.Ln, [gwB["t1"]], [gwB["sp"]])
        rm = resetm[:, 0:ntok] if L == 64 else resetm16[:, 0:ntok]
        sch.add("dve", "tensor_tensor_scan", (), dict(out=g_["A"], data0=rm, data1=g_["sp"], initial=0.0, op0=ALU.mult, op1=ALU.add),
                [gwB["sp"], constB], [gwB["A"]])
        TT(g_["r"], g_["ig"], g_["A"], ALU.add, [gwB["ig"], gwB["A"]], [gwB["r"]])
        RED(s_["cm"], g_["r"].rearrange("p (c l) -> p c l", l=L), ALU.max, [gwB["r"]], [gsB["cm"]])
        TS(s_["aL"], g_["A"].rearrange("p (c l) -> p c l", l=L)[:, :, L - 1], -1.0, None, ALU.mult, None, [gwB["A"]], [gsB["aL"]])
        sch.add("dve", "tensor_tensor_scan", (), dict(out=s_["mn"], data0=s_["cm"], data1=s_["aL"], initial=mstate[:, l:l + 1],
                                                       op0=ALU.max, op1=ALU.add), [gsB["cm"], gsB["aL"], mstateB[l]], [gsB["mn"]])
        TT(s_["mu"], s_["mn"], s_["aL"], ALU.subtract, [gsB["mn"], gsB["aL"]], [gsB["mu"]])
        CP("dve", gs["mpv"][:, 0:1], mstate[:, l:l + 1], [mstateB[l]], [gsB["mpv"]])
        if nch > 1:
            CP("dve", gs["mpv"][:, 1:nch], gs["mn"][:, 0:nch - 1], [gsB["mn"]], [gsB["mpv"]])
        CP("dve", mstate[:, l:l + 1], gs["mn"][:, nch - 1:nch], [gsB["mn"], gsB["mpv"]], [mstateB[l]])
        TT(s_["gam"], s_["mpv"], s_["mu"], ALU.subtract, [gsB["mpv"], gsB["mu"]], [gsB["gam"]])
        ACT(s_["gam"], s_["gam"], AF.Exp, [gsB["gam"]], [gsB["gam"]])
        mub = s_["mu"].rearrange("p (c o) -> p c o", o=1).to_broadcast([4, nch, L])
        TT(g_["sc"].rearrange("p (c l) -> p c l", l=L), g_["r"].rearrange("p (c l) -> p c l", l=L), mub, ALU.subtract, [gwB["r"], gsB["mu"]], [gwB["sc"]])
        ACT(g_["sc"], g_["sc"], AF.Exp, [gwB["sc"]], [gwB["sc"]])
        TS(g_["sc"], g_["sc"], 1.0 / 16.0, None, ALU.mult, None, [gwB["sc"]], [gwB["sc"]])
        TT(g_["cl"].rearrange("p (c l) -> p c l", l=L), g_["A"].rearrange("p (c l) -> p c l", l=L), mub, ALU.subtract, [gwB["A"], gsB["mu"]], [gwB["cl"]])
        ACT(g_["cl"], g_["cl"], AF.Exp, [gwB["cl"]], [gwB["cl"]])
        kG = pbank()
        pg = ps[0:bs, kG, 0:nblk * 8].rearrange("p (b e) -> p b e", e=8)
        for b in range(nblk):
            MM(pg[:, b, 0:4], g_["sc"][:, b * bs:(b + 1) * bs], identf[0:4, 0:4], True, True, [gwB["sc"], constB], [psB[kG]])
            MM(pg[:, b, 4:8], g_["cl"][:, b * bs:(b + 1) * bs], identf[0:4, 0:4], True, True, [gwB["cl"], constB], [psB[kG]])
        CP("dve", toksc[0:bs, 0:nblk, :], pg, [psB[kG]], [tokscB])
        TT(gD[:, 0:nch, :], s_["gam"].rearrange("p (c o) -> p c o", o=1).to_broadcast([4, nch, 4]),
           identf[0:4, 0:4].rearrange("p (o h) -> p o h", o=1).to_broadcast([4, nch, 4]), ALU.mult, [gsB["gam"], constB], [gDB])
        kG2 = pbank()
        MM(ps[:, kG2, 0:nch * 4], ones4, gD[:, 0:nch, :].rearrange("p c h -> p (c h)"), True, True, [gDB, constB], [psB[kG2]])
        CP("dve", gam[:, 0:nch, :], ps[:, kG2, 0:nch * 4].rearrange("p (c h) -> p c h", h=4), [psB[kG2]], [gamB])

        sch.phase = "attn"
        prior = tp["prior"]
        ngr = len(prior) + 1
        att = dict(si=0)
        sub_pend = [None]

        def subln_head(h):
            ovh = oaF[0:bs, 0:nblk, h * 128:(h + 1) * 128]
            sqv = hmf[0:bs, 512:512 + nblk * 128].rearrange("p (q d) -> p q d", d=128)
            TT(sqv, ovh, ovh, ALU.mult, oaFB[0:nblk], [hmfB])
            RED(sml[0:bs, 8:8 + nblk], sqv, ALU.add, [hmfB], [smlB])
            RSQRT(sml[0:bs, 8:8 + nblk], 128.0 * LN_EPS, [smlB], [smlB])
            TT(ovh, ovh, sml[0:bs, 8:8 + nblk].rearrange("p (q o) -> p q o", o=1).to_broadcast([bs, nblk, 128]), ALU.mult, oaFB[0:nblk] + [smlB], oaFB[0:nblk])
            TT(ovh, ovh, subg_s[0:bs, l, :].rearrange("p (o d) -> p o d", o=1).to_broadcast([bs, nblk, 128]), ALU.mult, oaFB[0:nblk] + [prmB], oabB[0:nblk])

        for h in range(HA):
            started = set()
            items = []
            for g in range(ngr):
                own = g == ngr - 1
                for kb in range(nblk if own else T // 128):
                    items.append((g, own, kb))

            def emit_scores(it):
                g, own, kb = it
                if not own:
                    sl = (h * ngr + g) % 2
                    if kb == 0:
                        kd, vd, dB = prior[g]
                        DMA("sp", kch[sl][:, 0:T], kd[h], dB, [kchB[sl]], kchB[sl])
                        DMA("sp", vch[sl][:, :, 0:128], vd[:, h * 128:(h + 1) * 128].rearrange("(b p) d -> p b d", p=128), dB, [vchB[sl]], vchB[sl])
                    kT_ = kch[sl][:, kb * 128:(kb + 1) * 128]; kTB_ = kchB[sl]
                    v_ = vch[sl][:, kb, :]; vB_ = vchB[sl]
                    q0 = 0
                    nk = 128
                else:
                    kT_ = KT[:, h, kb * bs:(kb + 1) * bs]; kTB_ = KTB[kb]
                    v_ = Vaug[0:bs, kb, h, :]; vB_ = VaugB[kb]
                    q0 = kb * bs if tp["mask"] else 0
                    nk = bs
                nq = ntok - q0
                sb_ = att["si"] % 2
                att["si"] += 1
                b0, b1 = 2 * sb_, 2 * sb_ + 1
                MM(ps[0:nk, b0, 0:nq], kT_[0:64, :], QT[0:64, h, q0:ntok], True, True, [kTB_] + QTB[0:nblk], [psB[b0]])
                MM(ps[0:nk, b1, 0:nq], kT_[64:128, :], QT[64:128, h, q0:ntok], True, True, [kTB_] + QTB[0:nblk], [psB[b1]])
                ACT(PT[sb_][0:nk, :, 0:nq], ps[0:nk, b0:b1 + 1, 0:nq], AF.Exp, [psB[b0], psB[b1]], [PTB[sb_]], scale=0.125)
                if own and tp["mask"]:
                    MEMSET("pool", PT[sb_][64:128, :, 0:64], 0.0, [PTB[sb_]], [PTB[sb_]])
                return (g, own, kb, sb_, nk, q0, v_, vB_)

            def emit_av(st):
                g, own, kb, sb_, nk, q0, v_, vB_ = st
                nkb = nblk if own else T // 128
                qb0 = kb if (own and tp["mask"]) else 0
                for qb in range(qb0, nblk):
                    for mp_ in range(2):
                        a_ = qb * 2 + mp_
                        bank = 4 + a_ // 3
                        off = (a_ % 3) * 129
                        col = qb * bs - q0
                        first = (g == 0 and kb == 0) and bank not in started
                        started.add(bank)
                        lastk = own and (kb == (qb if tp["mask"] else nkb - 1))
                        MM(ps[0:bs, bank, off:off + 129], PT[sb_][0:nk, mp_, col:col + bs], v_, first, lastk, [PTB[sb_], vB_], [psB[bank]], skip=True)

            prev = None
            for it in items:
                st = emit_scores(it)
                if prev is not None:
                    emit_av(prev)
                prev = st
            emit_av(prev)
            CP("dve", zf[0:bs, 0:387], ps[0:bs, 4, 0:387], [psB[4]], [zfB])
            CP("dve", zf[0:bs, 387:774], ps[0:bs, 5, 0:387], [psB[5]], [zfB])
            CP("dve", hmf[0:bs, 0:258], ps[0:bs, 6, 0:258], [psB[6]], [hmfB])

            def accv(a_):
                if a_ < 6:
                    return zf[0:bs, a_ * 129:(a_ + 1) * 129], zfB
                return hmf[0:bs, (a_ - 6) * 129:(a_ - 5) * 129], hmfB

            for qb in range(nblk):
                o0, o0B = accv(qb * 2)
                o1, o1B = accv(qb * 2 + 1)
                r0 = sml[0:bs, 0:1]; r1 = sml[0:bs, 1:2]
                sch.add("dve", "reciprocal", (), dict(out=r0, in_=o0[:, 128:129]), [o0B], [smlB])
                sch.add("dve", "reciprocal", (), dict(out=r1, in_=o1[:, 128:129]), [o1B], [smlB])
                TT(r1, r1, neglam[0:bs, l:l + 1], ALU.mult, [smlB, prmB], [smlB])
                TS(o0[:, 0:128], o0[:, 0:128], r0, None, ALU.mult, None, [o0B, smlB], [o0B])
                STT(oaF[0:bs, qb, h * 128:(h + 1) * 128], o1[:, 0:128], r1, o0[:, 0:128], ALU.mult, ALU.add,
                    [o1B, o0B, smlB], [oaFB[qb]])
            if sub_pend[0] is not None:
                subln_head(sub_pend[0])
            sub_pend[0] = h
        subln_head(sub_pend[0])

        sch.phase = "mlstm"
        for b in range(nblk):
            kb_ = tbank()
            pt = psb16(kb_)
            for j in range(8):
                TR(pt[0:bs, j * 128:(j + 1) * 128], qkT[:, 8 + j, b * bs:(b + 1) * bs], [qkTB[8 + j], constB], [psB[kb_]])
            for h in range(HM):
                ACT(kw[0:bs, h * 256:(h + 1) * 256], pt[0:bs, h * 256:(h + 1) * 256], AF.Identity, [psB[kb_], tokscB], [kwB], scale=toksc[0:bs, b, h:h + 1])
            kS = pbank()
            for h in range(HM):
                for dk in range(2):
                    MM(ps[0:bs, kS, h * 128:h * 128 + bs], qkT[:, 8 + 2 * h + dk, b * bs:(b + 1) * bs], qkT[:, 2 * h + dk, b * bs:(b + 1) * bs],
                       dk == 0, dk == 1, [qkTB[8 + 2 * h + dk], qkTB[2 * h + dk]], [psB[kS]])
            for h in range(HM):
                STT(Sm[0:bs, h, 0:bs], ps[0:bs, kS, h * 128:h * 128 + bs], toksc[0:bs, b, h:h + 1], maskBD[0:bs, 0:bs], ALU.mult, ALU.mult,
                    [psB[kS], tokscB, constB], [SmB])
            for ci in range(cpb):
                c = b * cpb + ci
                p0, p1 = ci * L, ci * L + L
                t0_, t1_ = b * bs + ci * L, b * bs + ci * L + L
                Gc, GcB = GbL[c % 2]
                if c == 0:
                    for h in range(HM):
                        ACT(Gc[:, :, h, :], Cg[:, :, h, :], AF.Identity, [CgB[h], gamB], [GcB[h]], scale=gam[:, c, h:h + 1])
                for h in range(HM):
                    for dk in range(2):
                        kC = 4 + dk
                        dC = ps[:, kC, 0:257]
                        MM(dC, kw[p0:p1, h * 256 + dk * 128:h * 256 + dk * 128 + 128], VMaug[p0:p1, b, h, :], True, True, [kwB, VMB[b]], [psB[kC]])
                        STT(Cg[:, dk, h, :], Cg[:, dk, h, :], gam[:, c, h:h + 1], dC, ALU.mult, ALU.add, [CgB[h], gamB, psB[kC]], [CgB[h]])
                if c + 1 < nch:
                    Gn, GnB = GbL[(c + 1) % 2]
                    for h in range(HM):
                        ACT(Gn[:, :, h, :], Cg[:, :, h, :], AF.Identity, [CgB[h], gamB], [GnB[h]], scale=gam[:, c + 1, h:h + 1])
                for h in range(HM):
                    nd = ps[p0:p1, h, 0:257]
                    MM(nd, qkT[:, 2 * h, t0_:t1_], Gc[:, 0, h, :], True, False, [qkTB[2 * h], GcB[h]], [psB[h]])
                    MM(nd, qkT[:, 2 * h + 1, t0_:t1_], Gc[:, 1, h, :], False, False, [qkTB[2 * h + 1], GcB[h]], [psB[h]])
                    MM(nd, Sm[p0:p1, h, p0:p1], VMaug[p0:p1, b, h, :], False, True, [SmB, VMB[b]], [psB[h]])
                den = ps[p0:p1, 0:HM, 256]
                dd = sml[p0:p1, 16:16 + HM]
                TS(dd, den, -1.0, None, ALU.mult, None, psB[0:HM], [smlB])
                TT(dd, dd, den, ALU.max, [smlB] + psB[0:HM], [smlB])
                TT(dd, dd, toksc[p0:p1, b, 4:4 + HM], ALU.max, [smlB, tokscB], [smlB])
                sch.add("dve", "reciprocal", (), dict(out=dd, in_=dd), [smlB], [smlB])
                for h in range(HM):
                    ACT(hmf[p0:p1, h * 256:(h + 1) * 256], ps[p0:p1, h, 0:256], AF.Identity, [psB[h], smlB], [hmfB], scale=sml[p0:p1, 16 + h:17 + h])
            for h in range(HM):
                hv_ = hmf[0:bs, h * 256:(h + 1) * 256]
                ACT(zf[0:bs, h * 256:(h + 1) * 256], hv_, AF.Identity, [hmfB], [zfB, smlB], accum_out=sml[0:bs, 24 + h:25 + h])
                ACT(zf[0:bs, h * 256:(h + 1) * 256], hv_, AF.Square, [hmfB], [zfB, smlB], accum_out=sml[0:bs, 28 + h:29 + h])
            mean_ = sml[0:bs, 24:28]; ex2_ = sml[0:bs, 28:32]; nmr_ = sml[0:bs, 32:36]
            TS(mean_, mean_, 1.0 / 256.0, None, ALU.mult, None, [smlB], [smlB])
            TT(nmr_, mean_, mean_, ALU.mult, [smlB], [smlB])
            STT(ex2_, ex2_, 1.0 / 256.0, nmr_, ALU.mult, ALU.subtract, [smlB], [smlB])
            RSQRT(ex2_, LN_EPS, [smlB], [smlB])
            STT(nmr_, mean_, -1.0, ex2_, ALU.mult, ALU.mult, [smlB], [smlB])
            for h in range(HM):
                hv_ = hmf[0:bs, h * 256:(h + 1) * 256]
                ACT(hv_, hv_, AF.Identity, [hmfB, smlB], [hmfB], bias=sml[0:bs, 32 + h:33 + h], scale=sml[0:bs, 28 + h:29 + h])
            TT(hmb[0:bs, b, :], hmf[0:bs, :], lnp[0:bs, 0, :], ALU.mult, [hmfB, lnpB[0]], [hmbB[b]])

        sch.phase = "merge"
        for gi, c0 in enumerate((C_OM, C_OM + 512, C_GA, C_GA + 512, C_GB, C_GB + 512)):
            ws, wB = wnext(("in", l, c0, 512))
            hf = gi % 2
            for b in range(nblk):
                k, o = proj_tok(ws, wB, b, 512)
                si_ = (gi * nblk + b) % 2
                ACT(sg[si_][0:bs, :], o, AF.Sigmoid, [psB[k]], [sgB[si_]])
                if gi in (2, 3):
                    tgt, tB = oab, oabB
                else:
                    tgt, tB = hmb, hmbB
                TT(tgt[0:bs, b, hf * 512:(hf + 1) * 512], tgt[0:bs, b, hf * 512:(hf + 1) * 512], sg[si_][0:bs, :], ALU.mult, [sgB[si_], tB[b]], [tB[b]])
            wdone(1)
        for b in range(nblk):
            TT(ybf[0:bs, :], oab[0:bs, b, :], hmb[0:bs, b, :], ALU.add, [oabB[b], hmbB[b]], [ybfB])
            k = tbank()
            pt = psb16(k)
            for c in range(8):
                TR(pt[:, c * bs:(c + 1) * bs], ybf[0:bs, c * 128:(c + 1) * 128], [ybfB, constB], [psB[k]])
            CP(eveng(), xT[:, :, b * bs:(b + 1) * bs], pt[:, 0:8 * bs].rearrange("p (c t) -> p c t", c=8), [psB[k]], [xTB[b]])

        def layernorm_inplace(b, gi, bi):
            xv = xtok[0:bs, b, :]
            for hh in range(2):
                sch.add("dve", "bn_stats", (), dict(out=stt[0:bs, hh, :], in_=xtok[0:bs, b, hh * 512:(hh + 1) * 512]), [xtokB[b]], [lnB])
            sch.add("dve", "bn_aggr", (), dict(out=mv[0:bs, 0:2], in_=stt[0:bs, :, :]), [lnB], [lnB])
            CP("dve", mv[0:bs, 2:3], mv[0:bs, 1:2], [lnB], [lnB])
            RSQRT(mv[0:bs, 2:3], LN_EPS, [lnB], [lnB])
            TS(xv, xv, mv[0:bs, 0:1], mv[0:bs, 2:3], ALU.subtract, ALU.mult, [xtokB[b], lnB], [xtokB[b]])
            TT(xv, xv, lnp[0:bs, 0, :], ALU.mult, [xtokB[b], lnpB[0]], [xtokB[b]])
            TT(xv, xv, lnp[0:bs, 1, :], ALU.add, [xtokB[b], lnpB[1]], [xtokB[b]])

        sch.phase = "wout"
        load_lnp(0, ln1g); load_lnp(1, ln1b)
        w0, w0B = wnext(("out", l, 0, 512))
        w1, w1B = wnext(("out", l, 512, 512))
        for b in range(nblk):
            for hf, (ws, wB) in enumerate(((w0, w0B), (w1, w1B))):
                k, o = proj_tok(ws, wB, b, 512)
                xs_ = xtok[0:bs, b, hf * 512:(hf + 1) * 512]
                STT(xs_, xs_, ALPHA, o, ALU.mult, ALU.add, [xtokB[b], psB[k]], [xtokB[b]])
            layernorm_inplace(b, 1, 2)
        wdone(2)
        make_xT()

        sch.phase = "up"
        up_pend = [None]

        def up_final(cc, a, pi):
            if cc < 22:
                ACT(hT[:, cc, 0:ntok], a, AF.Gelu, [accB[pi]], [hTB[cc]])
            else:
                TT(hT[:, cc - 22, 0:ntok], hT[:, cc - 22, 0:ntok], a, ALU.mult, [hTB[cc - 22], accB[pi]], [hTB[cc - 22]])

        for g in range(11):
            ws, wB = wnext(("up", l, g * 512, 512))
            for cc4 in range(4):
                cc = g * 4 + cc4
                k = pbank()
                o = ps[:, k, 0:ntok]
                for kc in range(8):
                    MM(o, ws[:, kc, cc4 * 128:(cc4 + 1) * 128], xT[:, kc, 0:ntok], kc == 0, kc == 7, [wB] + xTB[0:nblk], [psB[k]])
                pi = cc % 2
                pcv = pcm[pi]
                CP("pool", pcv[:, 0:2], fhalo[:, l, cc, :], [fhaloB[l]], [pcmB[pi]])
                CP("act", pcv[:, 2:2 + ntok], o, [psB[k]], [pcmB[pi]])
                a = acc[pi][:, 0:ntok]
                ACT(a, o, AF.Identity, [psB[k], prmB], [accB[pi]], bias=fcb_s[:, l, cc:cc + 1], scale=fcw_s[:, l, cc, 2:3])
                CP("pool", fhalo[:, l, cc, :], pcv[:, ntok:ntok + 2], [pcmB[pi]], [fhaloB[l]])
                for j in range(2):
                    STT(a, pcv[:, j:j + ntok], fcw_s[:, l, cc, j:j + 1], a, ALU.mult, ALU.add, [pcmB[pi], prmB, accB[pi]], [accB[pi]])
                if up_pend[0] is not None:
                    up_final(*up_pend[0])
                up_pend[0] = (cc, a, pi)
            wdone(1)
        up_final(*up_pend[0])
        if last:
            for g in range(11):
                ws, wB = wnext(("up", l, g * 512, 512))
                k, o = proj_tok(ws, wB, nblk - 1, 512)
                CP("dve", stg[0:bs, :], o, [psB[k]], [stgB])
                DMA("sp", tp["fc_out"][l, :, g * 512:(g + 1) * 512], stg[bs - 2:bs, :], [stgB], [], stgB)
                wdone(1)
        sch.phase = "down"
        load_lnp(0, ln2g); load_lnp(1, ln2b)
        for nh in range(2):
            pcs = [wnext(("down", l, nh, pc)) for pc in range(3)]
            for b in range(nblk):
                k = pbank()
                o = ps[0:bs, k, 0:512]
                for kc in range(22):
                    ws, wB = pcs[kc // 8]
                    MM(o, hT[:, kc, b * bs:(b + 1) * bs], ws[:, kc % 8, :], kc == 0, kc == 21, [wB, hTB[kc]], [psB[k]])
                xs_ = xtok[0:bs, b, nh * 512:(nh + 1) * 512]
                STT(xs_, xs_, ALPHA, o, ALU.mult, ALU.add, [xtokB[b], psB[k]], [xtokB[b]])
            wdone(3)
        for b in range(nblk):
            layernorm_inplace(b, 3, 4)
            if l == NL - 1:
                DMA("sp", tp["y_out"][b * bs:(b + 1) * bs, :], xtok[0:bs, b, :], [xtokB[b]], [], xtokB[b])
        if last:
            for h in range(HM):
                DMA("sp", tp["C_out"][l, h].rearrange("(c p) v -> p c v", p=128), Cg[:, :, h, 0:256], [CgB[h]], [], CgB[h])
                DMA("sp", tp["n_out"][l, h].rearrange("(c p o) -> p c o", p=128, o=1), Cg[:, :, h, 256:257], [CgB[h]], [], CgB[h], slow=True)
            DMA("sp", tp["m_out"][l].rearrange("(p o) -> p o", o=1), mstate[:, l:l + 1], [mstateB[l]], [], mstateB[l], slow=True)

    def sample_prep():
        sch.phase = "prep"
        for l in range(NL):
            for blk in range(P // 128):
                DMA("pool", vtmp, ck[l, blk * 128:(blk + 1) * 128, :], [], [vtmpB], vtmpB)
                k = tbank()
                pt = psb16(k)
                for h in range(HA):
                    TR(pt[:, h * 128:(h + 1) * 128], vtmp[:, h * 128:(h + 1) * 128], [vtmpB, constB], [psB[k]])
                CP(eveng(), ktmp, pt.rearrange("p (h t) -> p h t", h=HA), [psB[k]], [ktmpB])
                DMA("sp", KTs[l][:, :, blk * 128:(blk + 1) * 128].rearrange("h p t -> p h t"), ktmp, [ktmpB], [KTsB[l]], ktmpB)
                DMA("pool", ybf, cv[l, blk * 128:(blk + 1) * 128, :], [], [ybfB], ybfB)
                DMA("sp", Vbs[l, blk * 128:(blk + 1) * 128, :], ybf, [ybfB], [VbsB[l]], ybfB)

    steps = []
    for i in range(NT):
        for l in range(NL):
            steps.append((l, i, False))
    for l in range(NL):
        steps.append((l, 0, True))
    for (l, i, samp) in steps:
        wq.extend(wspec_step(l, samp or i == NT - 1))

    prep_done = False
    for (l, i, samp) in steps:
        if samp and not prep_done:
            sample_prep()
            prep_done = True
        if not samp:
            tp = dict(ntok=T, bs=128, nblk=NB, L=64, last=(i == NT - 1), first=(i == 0), load_state=False,
                      x_src=xp[i * T:(i + 1) * T, :], pos0=i * T, tok0=i * T, mask=True,
                      prior=[(KTp[l][:, :, g * T:(g + 1) * T], Vbp[l, g * T:(g + 1) * T, :], [KTpB[l][g], VbpB[l][g]]) for g in range(i)],
                      KT_dst=KTp[l][:, :, i * T:(i + 1) * T], KTB=KTpB[l][i], Vb_dst=Vbp[l, i * T:(i + 1) * T, :], VbB=VbpB[l][i],
                      k_out=kp[:, i * T:(i + 1) * T, :], v_out=vp[:, i * T:(i + 1) * T, :], y_out=yp[i * T:(i + 1) * T, :],
                      mc_out=mcp, fc_out=fcp, C_out=Cp, n_out=np_, m_out=mp)
        else:
            tp = dict(ntok=NS, bs=NS, nblk=1, L=NS, last=True, first=False, load_state=True,
                      x_src=xs, pos0=S, tok0=0, mask=False,
                      prior=[(KTs[l][:, :, g * T:(g + 1) * T], Vbs[l, g * T:(g + 1) * T, :], [KTsB[l], VbsB[l]]) for g in range(P // T)],
                      KT_dst=None, KTB=None, Vb_dst=None, VbB=None,
                      k_out=ks, v_out=vs, y_out=ys, mc_out=mcs, fc_out=fcs, C_out=Cs, n_out=ns_, m_out=ms,
                      sC=sC, sn=sn, sm=sm, smc=smc, sfc=sfc)
        step(l, tp)
    assert wstate["used"] == len(wq) and wstate["released"] == len(wq), (wstate, len(wq))
    print("SBUF/PSUM allocation done")
    info = sch.emit()
    return nc, info


def rope_tables(S, P):
    half = 32
    inv = (np.float32(10000.0) ** (-np.arange(half, dtype=np.float32) * np.float32(2.0) / np.float32(64))).astype(np.float32)
    pos = np.concatenate([np.arange(S), P + np.arange(NS)]).astype(np.float32)
    ang = (pos[:, None] * inv[None, :]).astype(np.float32)
    return np.cos(ang).astype(np.float32), np.sin(ang).astype(np.float32)


_CACHE = {}


def run(inputs, S, P, T, n_prompt, n_sample, n_cores):
    key = (S, P, T)
    if key not in _CACHE:
        _CACHE[key] = build(S=S, P=P, T=T)
    nc, info = _CACHE[key]
    f = lambda a: np.ascontiguousarray(np.asarray(a, dtype=np.float32))
    cosT, sinT = rope_tables(S, P)
    NL = 2
    in_maps = []
    for c in range(n_cores):
        b = c % n_prompt
        s = c % n_sample
        m = {
            "xp": f(inputs["x_prompt"][b]), "xs": f(inputs["x_sample"][s]),
            "ck": f(inputs["cache_k"][:, s]).reshape(NL, P, D), "cv": f(inputs["cache_v"][:, s]).reshape(NL, P, D),
            "smc": f(inputs["state_mlstm_conv"][:, s]), "sC": f(inputs["state_mlstm_C"][:, s]),
            "sn": f(inputs["state_mlstm_n"][:, s]), "sm": f(inputs["state_mlstm_m"][:, s]),
            "sfc": f(inputs["state_ffn_conv"][:, s]),
            "w_in": f(inputs["w_in"]), "b_if": f(inputs["b_if"]), "mcw": f(inputs["mlstm_conv_w"]), "mcb": f(inputs["mlstm_conv_b"]),
            "dlam": f(inputs["diff_lambda"]), "subg": f(inputs["diff_subln_g"]), "mhg": f(inputs["mlstm_norm_g"]),
            "w_out": f(inputs["w_out"]), "ln1g": f(inputs["ln1_g"]), "ln1b": f(inputs["ln1_b"]),
            "w_up": f(inputs["w_up"]), "fcw": f(inputs["ffn_conv_w"]), "fcb": f(inputs["ffn_conv_b"]),
            "w_down": f(inputs["w_down"]), "ln2g": f(inputs["ln2_g"]), "ln2b": f(inputs["ln2_b"]),
            "cosT": cosT, "sinT": sinT,
        }
        in_maps.append(m)
    res = run_bass_kernel_spmd(nc, in_maps, core_ids=list(range(n_cores)))
    R = res.results
    pc = list(range(n_prompt))
    sc = list(range(n_sample))
    st = lambda name, cores, ax=0: np.stack([np.asarray(R[c][name], dtype=np.float32) for c in cores], axis=ax)
    y_prompt = st("yp", pc)
    y_sample = st("ys", sc)
    k_prompt = st("kp", pc, 1).reshape(NL, n_prompt, S, HA, 128)
    v_prompt = st("vp", pc, 1).reshape(NL, n_prompt, S, HA, 128)
    outs = (y_prompt, y_sample, k_prompt, v_prompt,
            st("mcp", pc, 1), st("Cp", pc, 1), st("np", pc, 1), st("mp", pc, 1), st("fcp", pc, 1),
            st("ks", sc, 1).reshape(NL, n_sample, NS, HA, 128), st("vs", sc, 1).reshape(NL, n_sample, NS, HA, 128),
            st("mcs", sc, 1), st("Cs", sc, 1), st("ns", sc, 1), st("ms", sc, 1), st("fcs", sc, 1))
    return outs


def kernel(**inputs):
    return run(inputs, S=8192, P=4096, T=512, n_prompt=4, n_sample=8, n_cores=8)
```

```python
import math
import numpy as np
import concourse.bass as bass
import concourse.mybir as mybir
from concourse.bass_utils import run_bass_kernel_spmd

F32 = mybir.dt.float32
BF16 = mybir.dt.bfloat16
AF = mybir.ActivationFunctionType
ALU = mybir.AluOpType
AX = mybir.AxisListType

D = 1024
HA = 8
HM = 4
DFF = 2816
DIN = 9224
NS = 16
ALPHA = (2 * 2) ** 0.25
LN_EPS = 1e-5
C_QA, C_KA, C_VA, C_QKM, C_VM, C_OM, C_GIF, C_GA, C_GB = 0, 1024, 2048, 3072, 5120, 6144, 7168, 7176, 8200


class Buf:
    __slots__ = ("name", "lastw", "readers")

    def __init__(self, name=""):
        self.name = name
        self.lastw = None
        self.readers = []


class Sched:
    def __init__(self, nc, same_engine_sync=True):
        self.nc = nc
        self.engs = {"pe": nc.tensor, "act": nc.scalar, "dve": nc.vector, "pool": nc.gpsimd, "sp": nc.sync}
        self.ins = []
        self.dma_cnt = {}
        self.same = same_engine_sync
        self.phase = ""
        self.names = None

    def add(self, eng, meth, args, kwargs, reads=(), writes=(), dma=None):
        idx = len(self.ins)
        deps = set()
        for r in reads:
            if r.lastw is not None:
                deps.add(r.lastw)
        for w in writes:
            if w.lastw is not None:
                deps.add(w.lastw)
            deps.update(w.readers)
        for r in reads:
            r.readers.append(idx)
        for w in writes:
            w.lastw = idx
            w.readers = []
        dval = None
        if dma is not None:
            self.dma_cnt[dma] = self.dma_cnt.get(dma, 0) + 16
            dval = self.dma_cnt[dma]
        keep = set()
        for d in deps:
            de = self.ins[d]
            if de[5] is None and de[0] == eng:
                if eng == "pe" or not self.same:
                    continue
            keep.add(d)
        self.ins.append([eng, meth, args, kwargs, keep, dma, dval, False, 0, self.phase])
        return idx

    def emit(self):
        nc = self.nc
        for rec in self.ins:
            for d in rec[4]:
                de = self.ins[d]
                if de[5] is None:
                    de[7] = True
        cnt = {e: 0 for e in self.engs}
        for rec in self.ins:
            if rec[7]:
                cnt[rec[0]] += 1
                rec[8] = cnt[rec[0]]
        esem = {e: nc.alloc_semaphore(name="es_" + e) for e in self.engs}
        dsem = {}
        for k in self.dma_cnt:
            dsem[k] = nc.alloc_semaphore(name="ds_%d" % len(dsem))
        waited = {e: {} for e in self.engs}
        nwait = 0
        for rec in self.ins:
            eng, meth, args, kwargs, deps, dma, dval, sig, sigval, phase = rec
            E = self.engs[eng]
            need = {}
            for d in deps:
                de = self.ins[d]
                if de[5] is None:
                    s, v = esem[de[0]], de[8]
                else:
                    s, v = dsem[de[5]], de[6]
                if need.get(s, 0) < v:
                    need[s] = v
            for s, v in need.items():
                if waited[eng].get(s, 0) >= v:
                    continue
                E.wait_ge(s, v)
                waited[eng][s] = v
                nwait += 1
            ins = getattr(E, meth)(*args, **kwargs)
            if self.names is not None:
                self.names[ins.ins.name] = phase
            if dma is not None:
                ins.then_inc(dsem[dma], 16)
            elif sig:
                ins.then_inc(esem[eng], 1)
        for k, v in self.dma_cnt.items():
            nc.sync.wait_ge(dsem[k], v)
        return dict(n=len(self.ins), nwait=nwait, nsem=len(dsem) + 5)


def build(S=8192, P=4096, T=512, NL=2, dbg=None, same=True):
    nc = bass.Bass("TRN2", target_bir_lowering=False)
    sch = Sched(nc, same_engine_sync=same)
    NT = S // T
    NTAB = S + NS

    def din(name, shape, dt=F32):
        return nc.dram_tensor(name, list(shape), dt, kind="ExternalInput").ap()

    def dout(name, shape, dt=F32):
        return nc.dram_tensor(name, list(shape), dt, kind="ExternalOutput").ap()

    def dscr(name, shape, dt):
        return nc.dram_tensor(name, list(shape), dt, kind="Internal").ap()

    def sb(name, shape, dt=F32):
        return nc.alloc_sbuf_tensor(name, list(shape), dt).ap()

    xp = din("xp", [S, D]); xs = din("xs", [NS, D])
    ck = din("ck", [NL, P, D]); cv = din("cv", [NL, P, D])
    smc = din("smc", [NL, 3, 2048]); sC = din("sC", [NL, HM, 256, 256]); sn = din("sn", [NL, HM, 256])
    sm = din("sm", [NL, HM]); sfc = din("sfc", [NL, 2, 2 * DFF])
    w_in = din("w_in", [NL, D, DIN]); b_if = din("b_if", [NL, 8])
    mcw = din("mcw", [NL, 4, 2048]); mcb = din("mcb", [NL, 2048])
    dlam = din("dlam", [NL, 4, 64]); subg = din("subg", [NL, 128]); mhg = din("mhg", [NL, D])
    w_out = din("w_out", [NL, D, D]); ln1g = din("ln1g", [NL, D]); ln1b = din("ln1b", [NL, D])
    w_up = din("w_up", [NL, D, 2 * DFF]); fcw = din("fcw", [NL, 3, 2 * DFF]); fcb = din("fcb", [NL, 2 * DFF])
    w_down = din("w_down", [NL, DFF, D]); ln2g = din("ln2g", [NL, D]); ln2b = din("ln2b", [NL, D])
    cosT = din("cosT", [NTAB, 32]); sinT = din("sinT", [NTAB, 32])

    yp = dout("yp", [S, D]); ys = dout("ys", [NS, D])
    kp = dout("kp", [NL, S, D]); vp = dout("vp", [NL, S, D])
    mcp = dout("mcp", [NL, 3, 2048]); Cp = dout("Cp", [NL, HM, 256, 256]); np_ = dout("np", [NL, HM, 256])
    mp = dout("mp", [NL, HM]); fcp = dout("fcp", [NL, 2, 2 * DFF])
    ks = dout("ks", [NL, NS, D]); vs = dout("vs", [NL, NS, D])
    mcs = dout("mcs", [NL, 3, 2048]); Cs = dout("Cs", [NL, HM, 256, 256]); ns_ = dout("ns", [NL, HM, 256])
    ms = dout("ms", [NL, HM]); fcs = dout("fcs", [NL, 2, 2 * DFF])

    KTp = dscr("KTp", [NL, HA, 128, S], BF16); Vbp = dscr("Vbp", [NL, S, D], BF16)
    KTs = dscr("KTs", [NL, HA, 128, P], BF16); Vbs = dscr("Vbs", [NL, P, D], BF16)
    KTpB = [[Buf() for _ in range(NT)] for _ in range(NL)]
    VbpB = [[Buf() for _ in range(NT)] for _ in range(NL)]
    KTsB = [Buf() for _ in range(NL)]
    VbsB = [Buf() for _ in range(NL)]

    NB = T // 128
    xtok = sb("xtok", [128, NB, D]); xtokB = [Buf() for _ in range(NB)]
    xbf = sb("xbf", [128, D], BF16); xbfB = Buf()
    xT = sb("xT", [128, 8, T], BF16); xTB = [Buf() for _ in range(NB)]
    NSLOT = 4
    ring = [sb("ring%d" % i, [128, 8, 512], BF16) for i in range(NSLOT)]
    ringB = [Buf() for _ in range(NSLOT)]
    zf = sb("zf", [128, D]); zfB = Buf()
    ro = sb("ro", [128, D]); roB = Buf()
    cs_t = sb("cs_t", [128, NB, 2, 32]); csB = Buf()
    QT = sb("QT", [128, HA, T], BF16); QTB = [Buf() for _ in range(NB)]
    KT = sb("KT", [128, HA, T], BF16); KTB = [Buf() for _ in range(NB)]
    Vaug = sb("Vaug", [128, NB, HA, 129], BF16); VaugB = [Buf() for _ in range(NB)]
    VMaug = sb("VMaug", [128, NB, HM, 257], BF16); VMB = [Buf() for _ in range(NB)]
    oab = sb("oab", [128, NB, D], BF16); oabB = [Buf() for _ in range(NB)]
    hmb = sb("hmb", [128, NB, D], BF16); hmbB = [Buf() for _ in range(NB)]
    oaF = oab; oaFB = oabB
    hT = sb("hT", [128, 22, T], BF16); hTB = [Buf() for _ in range(22)]
    qkT = hT; qkTB = hTB
    pcm = [sb("pcm%d" % i, [128, 3 + T]) for i in range(2)]; pcmB = [Buf(), Buf()]
    acc = [sb("acc%d" % i, [128, T]) for i in range(2)]; accB = [Buf(), Buf()]
    rt = acc; rtB = accB
    stg = pcm[0][:, 0:512]; stgB = pcmB[0]
    kch = [sb("kch%d" % i, [128, T], BF16) for i in range(2)]; kchB = [Buf(), Buf()]
    vch = [sb("vch%d" % i, [128, NB, 129], BF16) for i in range(2)]; vchB = [Buf(), Buf()]
    PT = [sb("PT%d" % i, [128, 2, T], BF16) for i in range(2)]; PTB = [Buf(), Buf()]
    Caug = sb("Caug", [128, 2, HM, 257]); CaugB = [Buf() for _ in range(HM)]
    GbRaw = sb("GbRaw", [128, 2 * HM * 257], BF16); GbB = [Buf() for _ in range(HM)]
    Gb = GbRaw.rearrange("p (a h d) -> p a h d", a=2, h=HM)
    Gb2 = sb("Gb2", [128, 2, HM, 257], BF16); Gb2B = [Buf() for _ in range(HM)]
    GbL = [(Gb, GbB), (Gb2, Gb2B)]
    ro2 = GbRaw[:, 0:2 * D].bitcast(F32); ro2B = GbB
    kw = sb("kw", [128, D], BF16); kwB = Buf()
    Sm = sb("Sm", [128, HM, 128], BF16); SmB = Buf()
    hmf = sb("hmf", [128, D]); hmfB = Buf()
    sml = sb("sml", [128, 96]); smlB = Buf(); sml2B = Buf()
    mhalo = sb("mhalo", [128, NL, 16, 3]); mhaloB = [Buf() for _ in range(NL)]
    fhalo = sb("fhalo", [128, NL, 44, 2]); fhaloB = [Buf() for _ in range(NL)]
    mstate = sb("mstate", [4, NL]); mstateB = [Buf() for _ in range(NL)]
    CaugL = [Caug, sb("Caug1", [128, 2, HM, 257])]
    CaugLB = [CaugB, [Buf() for _ in range(HM)]]
    _g3 = [PT[0].rearrange("p a t -> p (a t)").bitcast(F32)[0:4, :], PT[1].rearrange("p a t -> p (a t)").bitcast(F32)[0:4, :],
           kw.bitcast(F32)[0:4, :]]
    _g3B = [PTB[0], PTB[1], kwB]
    gw = {"t1": _g3[0], "sp": _g3[0], "A": _g3[1], "cl": _g3[1], "ig": _g3[2], "r": _g3[2], "sc": _g3[2]}
    gwB = {"t1": _g3B[0], "sp": _g3B[0], "A": _g3B[1], "cl": _g3B[1], "ig": _g3B[2], "r": _g3B[2], "sc": _g3B[2]}
    gs = {n: sb("gs_" + n, [4, 16]) for n in ("cm", "aL", "mn", "mu", "mpv", "gam")}
    gsB = {n: Buf() for n in gs}
    gD = sb("gD", [4, 16, 4]); gDB = Buf()
    toksc = sb("toksc", [128, NB, 8]); tokscB = Buf()
    gam = sb("gam", [128, 16, 4]); gamB = Buf()
    identb = sb("identb", [128, 128], BF16); identf = sb("identf", [128, 128]); maskBD = sb("maskBD", [128, 128])
    ones4 = sb("ones4", [4, 128]); resetm = sb("resetm", [4, T]); resetm16 = sb("resetm16", [4, NS])
    constB = Buf()
    mcw_s = sb("mcw_s", [128, NL, 16, 4]); mcb_s = sb("mcb_s", [128, NL, 16])
    fcw_s = sb("fcw_s", [128, NL, 44, 3]); fcb_s = sb("fcb_s", [128, NL, 44])
    bif_s = sb("bif_s", [4, NL, 2]); nbif_s = sb("nbif_s", [4, NL, 2])
    neglam = sb("neglam", [128, NL]); lamw = sb("lamw", [128, 4, 64]); lamw2 = sb("lamw2", [128, 4])
    subg_s = sb("subg_s", [128, NL, 128])
    prmB = Buf()
    lnp = sb("lnp", [128, 2, D]); lnpB = [Buf(), Buf()]
    sg = [sb("sg%d" % i, [128, 512], BF16) for i in range(2)]; sgB = [Buf(), Buf()]
    ybf = xbf; ybfB = xbfB
    stt = sb("stt", [128, 2, 6]); mv = sb("mv", [128, 4]); lnB = Buf()
    vtmp = xbf; vtmpB = xbfB
    ktmp = kw.rearrange("p (h t) -> p h t", h=HA); ktmpB = kwB
    ps = nc.alloc_psum_tensor("ps", [128, 8, 512], F32).ap()
    psB = [Buf() for _ in range(8)]

    def psb16(k):
        return ps[:, k, :].bitcast(BF16)

    def MM(out, lhsT, rhs, start, stop, r, w, skip=False):
        kw_ = dict(lhsT=lhsT, rhs=rhs, start=start, stop=stop)
        if skip:
            kw_["skip_group_check"] = True
        sch.add("pe", "matmul", (out,), kw_, r, w)

    def TR(out, in_, r, w):
        sch.add("pe", "transpose", (), dict(out=out, in_=in_, identity=identb[0:in_.shape[0], 0:in_.shape[0]]), r, w)

    def ACT(out, in_, func, r, w, bias=None, scale=None, accum_out=None):
        kw_ = dict(out=out, in_=in_, func=func)
        if bias is not None:
            kw_["bias"] = bias
        if scale is not None:
            kw_["scale"] = scale
        if accum_out is not None:
            kw_["accum_out"] = accum_out
        sch.add("act", "activation", (), kw_, r, w)

    def CP(eng, out, in_, r, w):
        if eng == "act":
            ACT(out, in_, AF.Identity, r, w)
        else:
            sch.add(eng, "tensor_copy", (), dict(out=out, in_=in_), r, w)

    def TT(out, in0, in1, op, r, w, eng="dve"):
        sch.add(eng, "tensor_tensor", (), dict(out=out, in0=in0, in1=in1, op=op), r, w)

    def TS(out, in0, s1, s2, op0, op1, r, w, eng="dve"):
        kw_ = dict(out=out, in0=in0, scalar1=s1, scalar2=s2, op0=op0)
        if op1 is not None:
            kw_["op1"] = op1
        sch.add(eng, "tensor_scalar", (), kw_, r, w)

    def STT(out, in0, scalar, in1, op0, op1, r, w, eng="dve"):
        sch.add(eng, "scalar_tensor_tensor", (), dict(out=out, in0=in0, scalar=scalar, in1=in1, op0=op0, op1=op1), r, w)

    def RSQRT(ap, addc, r, w):
        TS(ap, ap, addc, None, ALU.add, None, r, w)
        ACT(ap, ap, AF.Sqrt, r, w)
        sch.add("dve", "reciprocal", (), dict(out=ap, in_=ap), r, w)

    def RED(out, in_, op, r, w):
        sch.add("dve", "tensor_reduce", (), dict(out=out, in_=in_, axis=AX.X, op=op), r, w)

    def MEMSET(eng, ap, val, r, w):
        sch.add(eng, "memset", (ap, val), {}, r, w)

    def DMA(q, out, in_, r, w, key, slow=False):
        kw_ = dict(out=out, in_=in_)
        if slow:
            kw_["allow_slow_non_contiguous"] = True
        sch.add(q, "dma_start", (), kw_, r, w, dma=key)

    MEMSET("pool", identf, 1.0, [], [constB])
    sch.add("pool", "affine_select", (), dict(out=identf, in_=identf, compare_op=ALU.is_equal, fill=0.0, base=0,
                                              pattern=[[-1, 128]], channel_multiplier=1), [constB], [constB])
    CP("dve", identb, identf, [constB], [constB])
    MEMSET("pool", maskBD, 1.0, [constB], [constB])
    sch.add("pool", "affine_select", (), dict(out=maskBD, in_=maskBD, compare_op=ALU.is_ge, fill=0.0, base=0,
                                              pattern=[[1, 128]], channel_multiplier=-1), [constB], [constB])
    MEMSET("pool", maskBD[0:64, 64:128], 0.0, [constB], [constB])
    MEMSET("pool", ones4, 1.0, [constB], [constB])
    MEMSET("pool", resetm, 1.0, [constB], [constB])
    MEMSET("pool", resetm.rearrange("p (c l) -> p c l", l=64)[:, :, 0:1], 0.0, [constB], [constB])
    MEMSET("pool", resetm16, 1.0, [constB], [constB])
    MEMSET("pool", resetm16[:, 0:1], 0.0, [constB], [constB])
    MEMSET("pool", Vaug[:, :, :, 128:129], 1.0, [], VaugB)
    MEMSET("pool", VMaug[:, :, :, 256:257], 1.0, [], VMB)
    for i in range(2):
        MEMSET("pool", vch[i][:, :, 128:129], 1.0, [], [vchB[i]])
    pk = Buf()
    for l in range(NL):
        for j in range(4):
            DMA("sp", mcw_s[:, l, :, j], mcw[l, j].rearrange("(c p) -> p c", p=128), [], [prmB], pk, slow=True)
        DMA("sp", mcb_s[:, l, :], mcb[l].rearrange("(c p) -> p c", p=128), [], [prmB], pk, slow=True)
        for j in range(3):
            DMA("sp", fcw_s[:, l, :, j], fcw[l, j].rearrange("(c p) -> p c", p=128), [], [prmB], pk, slow=True)
        DMA("sp", fcb_s[:, l, :], fcb[l].rearrange("(c p) -> p c", p=128), [], [prmB], pk, slow=True)
        DMA("sp", bif_s[:, l, :], b_if[l].rearrange("(j p) -> p j", p=4), [], [prmB], pk, slow=True)
        DMA("sp", subg_s[:, l, :], subg[l].partition_broadcast(128), [], [prmB], pk)
        DMA("sp", lamw, dlam[l].partition_broadcast(128), [prmB], [prmB], pk)
        lam_init = 0.8 - 0.6 * math.exp(-0.3 * l)
        lv = lamw.rearrange("p (a b) d -> p a b d", b=2)
        TT(lamw[:, 0:2, :].rearrange("p a d -> p a d"), lv[:, :, 0, :], lv[:, :, 1, :], ALU.mult, [prmB], [prmB])
        RED(lamw2[:, 0:2], lamw[:, 0:2, :], ALU.add, [prmB], [prmB])
        ACT(lamw2[:, 2:4], lamw2[:, 0:2], AF.Exp, [prmB], [prmB])
        TT(neglam[:, l:l + 1], lamw2[:, 3:4], lamw2[:, 2:3], ALU.subtract, [prmB], [prmB])
        TS(neglam[:, l:l + 1], neglam[:, l:l + 1], -lam_init, None, ALU.add, None, [prmB], [prmB])
        TS(subg_s[:, l, :], subg_s[:, l, :], (1.0 - lam_init) * math.sqrt(128.0), None, ALU.mult, None, [prmB], [prmB])
    TS(nbif_s, bif_s, -1.0, None, ALU.mult, None, [prmB], [prmB])

    wq = []
    wstate = dict(loaded=0, used=0, released=0)

    def wspec_step(l, last):
        sp_ = []
        for c0 in (C_QA, C_QA + 512, C_KA, C_KA + 512, C_VA, C_VA + 512):
            sp_.append(("in", l, c0, 512))
        for c0 in range(C_QKM, C_QKM + 2048, 512):
            sp_.append(("in", l, c0, 512))
        if last:
            for c0 in range(C_QKM, C_QKM + 2048, 512):
                sp_.append(("in", l, c0, 512))
        sp_.append(("in", l, C_VM, 512)); sp_.append(("in", l, C_VM + 512, 512))
        sp_.append(("in", l, C_GIF, 8))
        for c0 in (C_OM, C_OM + 512, C_GA, C_GA + 512, C_GB, C_GB + 512):
            sp_.append(("in", l, c0, 512))
        sp_.append(("out", l, 0, 512)); sp_.append(("out", l, 512, 512))
        for g in range(11):
            sp_.append(("up", l, g * 512, 512))
        if last:
            for g in range(11):
                sp_.append(("up", l, g * 512, 512))
        for nh in range(2):
            for pc in range(3):
                sp_.append(("down", l, nh, pc))
        return sp_

    def wload(i):
        kind, l, a, b = wq[i]
        slot = i % NSLOT
        if kind == "in":
            src = w_in[l].rearrange("(kc p) n -> p kc n", p=128)[:, :, a:a + b]
            dst = ring[slot][:, :, 0:b]
        elif kind == "out":
            src = w_out[l].rearrange("(kc p) n -> p kc n", p=128)[:, :, a:a + b]
            dst = ring[slot][:, :, 0:b]
        elif kind == "up":
            src = w_up[l].rearrange("(kc p) n -> p kc n", p=128)[:, :, a:a + b]
            dst = ring[slot][:, :, 0:b]
        else:
            k0 = b * 8
            k1 = min(22, k0 + 8)
            src = w_down[l].rearrange("(kc p) n -> p kc n", p=128)[:, k0:k1, a * 512:(a + 1) * 512]
            dst = ring[slot][:, 0:k1 - k0, :]
        DMA("pool", dst, src, [], [ringB[slot]], ringB[slot])

    def wfill():
        while wstate["loaded"] < min(len(wq), wstate["released"] + NSLOT):
            wload(wstate["loaded"])
            wstate["loaded"] += 1

    def wnext(expect):
        i = wstate["used"]
        assert wq[i] == expect, (wq[i], expect)
        wfill()
        assert wstate["loaded"] > i
        wstate["used"] += 1
        return ring[i % NSLOT], ringB[i % NSLOT]

    def wdone(n=1):
        wstate["released"] += n
        assert wstate["released"] <= wstate["used"]
        wfill()

    rot = dict(pp=0, tp=0)

    def pbank():
        k = rot["pp"] % 4
        rot["pp"] += 1
        return k

    def tbank():
        k = 6 + rot["tp"] % 2
        rot["tp"] += 1
        return k

    evr = dict(i=0)

    def eveng():
        evr["i"] += 1
        return "act" if evr["i"] % 2 else "dve"

    def step(l, tp):
        ntok, bs, nblk, L = tp["ntok"], tp["bs"], tp["nblk"], tp["L"]
        cpb = bs // L
        nch = ntok // L
        last = tp["last"]
        Cg = CaugL[l]; CgB = CaugLB[l]

        def load_lnp(slot, src):
            DMA("sp", lnp[:, slot, :], src[l].partition_broadcast(128), [], [lnpB[slot]], lnpB[slot])
        load_lnp(0, mhg)
        if l == 0:
            for b in range(nblk):
                DMA("sp", xtok[0:bs, b, :], tp["x_src"][b * bs:(b + 1) * bs, :], [], [xtokB[b]], xtokB[b])
        DMA("sp", cs_t[0:bs, 0:nblk, 0, :], cosT[tp["pos0"]:tp["pos0"] + ntok, :].rearrange("(b p) d -> p b d", p=bs), [], [csB], csB)
        DMA("sp", cs_t[0:bs, 0:nblk, 1, :], sinT[tp["pos0"]:tp["pos0"] + ntok, :].rearrange("(b p) d -> p b d", p=bs), [], [csB], csB)
        if tp["load_state"]:
            for h in range(HM):
                DMA("sp", Cg[:, :, h, 0:256], tp["sC"][l, h].rearrange("(c p) v -> p c v", p=128), [], [CgB[h]], CgB[h])
                DMA("sp", Cg[:, :, h, 256:257], tp["sn"][l, h].rearrange("(c p o) -> p c o", p=128, o=1), [], [CgB[h]], CgB[h], slow=True)
            DMA("sp", mstate[:, l:l + 1], tp["sm"][l].rearrange("(p o) -> p o", o=1), [], [mstateB[l]], mstateB[l], slow=True)
            for j in range(3):
                DMA("sp", mhalo[:, l, :, j], tp["smc"][l, j].rearrange("(c p) -> p c", p=128), [], [mhaloB[l]], mhaloB[l], slow=True)
            for j in range(2):
                DMA("sp", fhalo[:, l, :, j], tp["sfc"][l, j].rearrange("(c p) -> p c", p=128), [], [fhaloB[l]], fhaloB[l], slow=True)
        elif tp["first"]:
            for h in range(HM):
                MEMSET("pool", Cg[:, :, h, :], 0.0, [], [CgB[h]])
            MEMSET("pool", mstate[:, l:l + 1], 0.0, [], [mstateB[l]])
            MEMSET("pool", mhalo[:, l, :, :], 0.0, [], [mhaloB[l]])
            MEMSET("pool", fhalo[:, l, :, :], 0.0, [], [fhaloB[l]])

        def make_xT():
            for b in range(nblk):
                CP(eveng(), xbf[0:bs, :], xtok[0:bs, b, :], [xtokB[b]], [xbfB])
                k = tbank()
                pt = psb16(k)
                for c in range(8):
                    TR(pt[:, c * bs:(c + 1) * bs], xbf[0:bs, c * 128:(c + 1) * 128], [xbfB, constB], [psB[k]])
                CP(eveng(), xT[:, :, b * bs:(b + 1) * bs], pt[:, 0:8 * bs].rearrange("p (c t) -> p c t", c=8), [psB[k]], [xTB[b]])

        sch.phase = "xT"
        make_xT()

        def proj_tok(wslot, wB, b, ncols, kcn=8, lhs=None, lhsB=None):
            k = pbank()
            out = ps[0:bs, k, 0:ncols]
            for kc in range(kcn):
                lt = xT[:, kc, b * bs:(b + 1) * bs] if lhs is None else lhs(kc)
                MM(out, lt, wslot[:, kc, 0:ncols], kc == 0, kc == kcn - 1, [wB, xTB[b] if lhsB is None else lhsB], [psB[k]])
            return k, out

        def rope_block(src_zf, zfB, dst, roB, b):
            sv = src_zf.rearrange("p (g two d) -> p g two d", two=2, d=32)
            dv = dst.rearrange("p (g two d) -> p g two d", two=2, d=32)
            cosb = cs_t[0:bs, b, 0:1, :].to_broadcast([bs, 16, 32])
            sinb = cs_t[0:bs, b, 1:2, :].to_broadcast([bs, 16, 32])
            t0 = rt[0][0:bs, :].rearrange("p (g d) -> p g d", d=32)
            t1 = rt[1][0:bs, :].rearrange("p (g d) -> p g d", d=32)
            TT(t0, sv[:, :, 0, :], cosb, ALU.mult, [zfB, csB], [rtB[0]])
            TT(t1, sv[:, :, 1, :], sinb, ALU.mult, [zfB, csB], [rtB[1]])
            TT(dv[:, :, 0, :], t0, t1, ALU.subtract, [rtB[0], rtB[1]], roB)
            TT(t0, sv[:, :, 1, :], cosb, ALU.mult, [zfB, csB], [rtB[0]])
            TT(t1, sv[:, :, 0, :], sinb, ALU.mult, [zfB, csB], [rtB[1]])
            TT(dv[:, :, 1, :], t0, t1, ALU.add, [rtB[0], rtB[1]], roB)

        sch.phase = "qkv"
        zfs = [(zf, zfB), (hmf, hmfB)]
        ros = [(ro, [roB]), (ro2, ro2B)]
        pst = dict(n=0)

        def post(which, b, zb, zbB):
            if which == "v":
                DMA("sp", tp["v_out"][l, b * bs:(b + 1) * bs, :], zb[0:bs, :], [zbB], [], zbB)
                CP("dve", Vaug[0:bs, b, :, 0:128], zb[0:bs, :].rearrange("p (h d) -> p h d", h=HA), [zbB], [VaugB[b]])
                if tp["Vb_dst"] is not None:
                    DMA("sp", tp["Vb_dst"][b * bs:(b + 1) * bs, :].rearrange("p (h d) -> p h d", h=HA),
                        Vaug[0:bs, b, :, 0:128], [VaugB[b]], [tp["VbB"]], VaugB[b])
                return
            rb, rbB = ros[pst["n"] % 2]
            pst["n"] += 1
            rope_block(zb[0:bs, :], zbB, rb[0:bs, :], rbB, b)
            if which == "k":
                DMA("sp", tp["k_out"][l, b * bs:(b + 1) * bs, :], rb[0:bs, :], rbB, [], rbB[0])
            CP("act", xbf[0:bs, :], rb[0:bs, :], rbB, [xbfB])
            kb_ = tbank()
            pt = psb16(kb_)
            for h in range(HA):
                TR(pt[:, h * bs:(h + 1) * bs], xbf[0:bs, h * 128:(h + 1) * 128], [xbfB, constB], [psB[kb_]])
            dstT, dstB = (QT, QTB) if which == "q" else (KT, KTB)
            CP("act", dstT[:, :, b * bs:(b + 1) * bs], pt[:, 0:HA * bs].rearrange("p (h t) -> p h t", h=HA), [psB[kb_]], [dstB[b]])
            if which == "k" and b == nblk - 1 and tp["KT_dst"] is not None:
                DMA("sp", tp["KT_dst"].rearrange("h p t -> p h t"), KT[:, :, 0:ntok], KTB[0:nblk], [tp["KTB"]], KTB[0])

        pend = None
        nz = 0
        for which in ("q", "k", "v"):
            c0 = {"q": C_QA, "k": C_KA, "v": C_VA}[which]
            w0, w0B = wnext(("in", l, c0, 512))
            w1, w1B = wnext(("in", l, c0 + 512, 512))
            for b in range(nblk):
                zb, zbB = zfs[nz % 2]
                nz += 1
                for hf, (ws, wB) in enumerate(((w0, w0B), (w1, w1B))):
                    k, o = proj_tok(ws, wB, b, 512)
                    CP("act" if which != "v" else eveng(), zb[0:bs, hf * 512:(hf + 1) * 512], o, [psB[k]], [zbB])
                if pend is not None:
                    post(*pend)
                pend = (which, b, zb, zbB)
            wdone(2)
        post(*pend)

        sch.phase = "qkm"
        m_pend = [None]
        for g in range(4):
            ws, wB = wnext(("in", l, C_QKM + g * 512, 512))
            for cc4 in range(4):
                cc = g * 4 + cc4
                k = pbank()
                o = ps[:, k, 0:ntok]
                for kc in range(8):
                    MM(o, ws[:, kc, cc4 * 128:(cc4 + 1) * 128], xT[:, kc, 0:ntok], kc == 0, kc == 7, [wB] + xTB[0:nblk], [psB[k]])
                pi = cc % 2
                pcv = pcm[pi]
                CP("pool", pcv[:, 0:3], mhalo[:, l, cc, :], [mhaloB[l]], [pcmB[pi]])
                CP("act", pcv[:, 3:3 + ntok], o, [psB[k]], [pcmB[pi]])
                a = acc[pi][:, 0:ntok]
                ACT(a, o, AF.Identity, [psB[k], prmB], [accB[pi]], bias=mcb_s[:, l, cc:cc + 1], scale=mcw_s[:, l, cc, 3:4])
                CP("pool", mhalo[:, l, cc, :], pcv[:, ntok:ntok + 3], [pcmB[pi]], [mhaloB[l]])
                for j in range(3):
                    STT(a, pcv[:, j:j + ntok], mcw_s[:, l, cc, j:j + 1], a, ALU.mult, ALU.add, [pcmB[pi], prmB, accB[pi]], [accB[pi]])
                if m_pend[0] is not None:
                    ACT(qkT[:, m_pend[0][0], 0:ntok], m_pend[0][1], AF.Silu, [accB[m_pend[0][2]]], [qkTB[m_pend[0][0]]])
                m_pend[0] = (cc, a, pi)
            wdone(1)
        ACT(qkT[:, m_pend[0][0], 0:ntok], m_pend[0][1], AF.Silu, [accB[m_pend[0][2]]], [qkTB[m_pend[0][0]]])
        if last:
            for g in range(4):
                ws, wB = wnext(("in", l, C_QKM + g * 512, 512))
                k, o = proj_tok(ws, wB, nblk - 1, 512)
                CP("dve", stg[0:bs, :], o, [psB[k]], [stgB])
                DMA("sp", tp["mc_out"][l, :, g * 512:(g + 1) * 512], stg[bs - 3:bs, :], [stgB], [], stgB)
                wdone(1)
        sch.phase = "vm"
        w0, w0B = wnext(("in", l, C_VM, 512))
        w1, w1B = wnext(("in", l, C_VM + 512, 512))
        for b in range(nblk):
            for hf, (ws, wB) in enumerate(((w0, w0B), (w1, w1B))):
                k, o = proj_tok(ws, wB, b, 512)
                CP(eveng(), VMaug[0:bs, b, 2 * hf:2 * hf + 2, 0:256], o.rearrange("p (h d) -> p h d", h=2), [psB[k]], [VMB[b]])
        wdone(2)
        sch.phase = "gates"
        ws, wB = wnext(("in", l, C_GIF, 8))
        kI = pbank(); kF = pbank()
        for kc in range(8):
            MM(ps[0:4, kI, 0:ntok], ws[:, kc, 0:4], xT[:, kc, 0:ntok], kc == 0, kc == 7, [wB] + xTB[0:nblk], [psB[kI]])
        for kc in range(8):
            MM(ps[0:4, kF, 0:ntok], ws[:, kc, 4:8], xT[:, kc, 0:ntok], kc == 0, kc == 7, [wB] + xTB[0:nblk], [psB[kF]])
        wdone(1)
        g_ = {n: gw[n][:, 0:ntok] for n in gw}
        s_ = {n: gs[n][:, 0:nch] for n in gs}
        ACT(g_["ig"], ps[0:4, kI, 0:ntok], AF.Identity, [psB[kI], prmB], [gwB["ig"]], bias=bif_s[:, l, 0:1])
        ACT(g_["t1"], ps[0:4, kF, 0:ntok], AF.Exp, [psB[kF], prmB], [gwB["t1"]], bias=nbif_s[:, l, 1:2], scale=-1.0)
        TS(g_["t1"], g_["t1"], 1.0, None, ALU.add, None, [gwB["t1"]], [gwB["t1"]])
        ACT(g_["sp"], g_["t1"], AF.Ln, [gwB["t1"]], [gwB["sp"]])
        rm = resetm[:, 0:ntok] if L == 64 else resetm16[:, 0:ntok]
        sch.add("dve", "tensor_tensor_scan", (), dict(out=g_["A"], data0=rm, data1=g_["sp"], initial=0.0, op0=ALU.mult, op1=ALU.add),
                [gwB["sp"], constB], [gwB["A"]])
        TT(g_["r"], g_["ig"], g_["A"], ALU.add, [gwB["ig"], gwB["A"]], [gwB["r"]])
        RED(s_["cm"], g_["r"].rearrange("p (c l) -> p c l", l=L), ALU.max, [gwB["r"]], [gsB["cm"]])
        TS(s_["aL"], g_["A"].rearrange("p (c l) -> p c l", l=L)[:, :, L - 1], -1.0, None, ALU.mult, None, [gwB["A"]], [gsB["aL"]])
        sch.add("dve", "tensor_tensor_scan", (), dict(out=s_["mn"], data0=s_["cm"], data1=s_["aL"], initial=mstate[:, l:l + 1],
                                                       op0=ALU.max, op1=ALU.add), [gsB["cm"], gsB["aL"], mstateB[l]], [gsB["mn"]])
        TT(s_["mu"], s_["mn"], s_["aL"], ALU.subtract, [gsB["mn"], gsB["aL"]], [gsB["mu"]])
        CP("dve", gs["mpv"][:, 0:1], mstate[:, l:l + 1], [mstateB[l]], [gsB["mpv"]])
        if nch > 1:
            CP("dve", gs["mpv"][:, 1:nch], gs["mn"][:, 0:nch - 1], [gsB["mn"]], [gsB["mpv"]])
        CP("dve", mstate[:, l:l + 1], gs["mn"][:, nch - 1:nch], [gsB["mn"], gsB["mpv"]], [mstateB[l]])
        TT(s_["gam"], s_["mpv"], s_["mu"], ALU.subtract, [gsB["mpv"], gsB["mu"]], [gsB["gam"]])
        ACT(s_["gam"], s_["gam"], AF.Exp, [gsB["gam"]], [gsB["gam"]])
        mub = s_["mu"].rearrange("p (c o) -> p c o", o=1).to_broadcast([4, nch, L])
        TT(g_["sc"].rearrange("p (c l) -> p c l", l=L), g_["r"].rearrange("p (c l) -> p c l", l=L), mub, ALU.subtract, [gwB["r"], gsB["mu"]], [gwB["sc"]])
        ACT(g_["sc"], g_["sc"], AF.Exp, [gwB["sc"]], [gwB["sc"]])
        TS(g_["sc"], g_["sc"], 1.0 / 16.0, None, ALU.mult, None, [gwB["sc"]], [gwB["sc"]])
        TT(g_["cl"].rearrange("p (c l) -> p c l", l=L), g_["A"].rearrange("p (c l) -> p c l", l=L), mub, ALU.subtract, [gwB["A"], gsB["mu"]], [gwB["cl"]])
        ACT(g_["cl"], g_["cl"], AF.Exp, [gwB["cl"]], [gwB["cl"]])
        kG = pbank()
        pg = ps[0:bs, kG, 0:nblk * 8].rearrange("p (b e) -> p b e", e=8)
        for b in range(nblk):
            MM(pg[:, b, 0:4], g_["sc"][:, b * bs:(b + 1) * bs], identf[0:4, 0:4], True, True, [gwB["sc"], constB], [psB[kG]])
            MM(pg[:, b, 4:8], g_["cl"][:, b * bs:(b + 1) * bs], identf[0:4, 0:4], True, True, [gwB["cl"], constB], [psB[kG]])
        CP("dve", toksc[0:bs, 0:nblk, :], pg, [psB[kG]], [tokscB])
        TT(gD[:, 0:nch, :], s_["gam"].rearrange("p (c o) -> p c o", o=1).to_broadcast([4, nch, 4]),
           identf[0:4, 0:4].rearrange("p (o h) -> p o h", o=1).to_broadcast([4, nch, 4]), ALU.mult, [gsB["gam"], constB], [gDB])
        kG2 = pbank()
        MM(ps[:, kG2, 0:nch * 4], ones4, gD[:, 0:nch, :].rearrange("p c h -> p (c h)"), True, True, [gDB, constB], [psB[kG2]])
        CP("dve", gam[:, 0:nch, :], ps[:, kG2, 0:nch * 4].rearrange("p (c h) -> p c h", h=4), [psB[kG2]], [gamB])

        sch.phase = "attn"
        prior = tp["prior"]
        ngr = len(prior) + 1
        att = dict(si=0)
        sub_pend = [None]

        def subln_head(h):
            ovh = oaF[0:bs, 0:nblk, h * 128:(h + 1) * 128]
            sqv = hmf[0:bs, 512:512 + nblk * 128].rearrange("p (q d) -> p q d", d=128)
            TT(sqv, ovh, ovh, ALU.mult, oaFB[0:nblk], [hmfB])
            RED(sml[0:bs, 64 + h * 4:64 + h * 4 + nblk], sqv, ALU.add, [hmfB], [sml2B])

        for h in range(HA):
            started = set()
            items = []
            for g in range(ngr):
                own = g == ngr - 1
                for kb in range(nblk if own else T // 128):
                    items.append((g, own, kb))

            def emit_scores(it):
                g, own, kb = it
                if not own:
                    sl = (h * ngr + g) % 2
                    if kb == 0:
                        kd, vd, dB = prior[g]
                        DMA("sp", kch[sl][:, 0:T], kd[h], dB, [kchB[sl]], kchB[sl])
                        DMA("sp", vch[sl][:, :, 0:128], vd[:, h * 128:(h + 1) * 128].rearrange("(b p) d -> p b d", p=128), dB, [vchB[sl]], vchB[sl])
                    kT_ = kch[sl][:, kb * 128:(kb + 1) * 128]; kTB_ = kchB[sl]
                    v_ = vch[sl][:, kb, :]; vB_ = vchB[sl]
                    q0 = 0
                    nk = 128
                else:
                    kT_ = KT[:, h, kb * bs:(kb + 1) * bs]; kTB_ = KTB[kb]
                    v_ = Vaug[0:bs, kb, h, :]; vB_ = VaugB[kb]
                    q0 = kb * bs if tp["mask"] else 0
                    nk = bs
                nq = ntok - q0
                sb_ = att["si"] % 2
                att["si"] += 1
                b0, b1 = 2 * sb_, 2 * sb_ + 1
                MM(ps[0:nk, b0, 0:nq], kT_[0:64, :], QT[0:64, h, q0:ntok], True, True, [kTB_] + QTB[0:nblk], [psB[b0]])
                MM(ps[0:nk, b1, 0:nq], kT_[64:128, :], QT[64:128, h, q0:ntok], True, True, [kTB_] + QTB[0:nblk], [psB[b1]])
                ACT(PT[sb_][0:nk, :, 0:nq], ps[0:nk, b0:b1 + 1, 0:nq], AF.Exp, [psB[b0], psB[b1]], [PTB[sb_]], scale=0.125)
                if own and tp["mask"]:
                    MEMSET("pool", PT[sb_][64:128, :, 0:64], 0.0, [PTB[sb_]], [PTB[sb_]])
                return (g, own, kb, sb_, nk, q0, v_, vB_)

            def emit_av(st):
                g, own, kb, sb_, nk, q0, v_, vB_ = st
                nkb = nblk if own else T // 128
                qb0 = kb if (own and tp["mask"]) else 0
                for qb in range(qb0, nblk):
                    for mp_ in range(2):
                        a_ = qb * 2 + mp_
                        bank = 4 + a_ // 3
                        off = (a_ % 3) * 129
                        col = qb * bs - q0
                        first = (g == 0 and kb == 0) and bank not in started
                        started.add(bank)
                        lastk = own and (kb == (qb if tp["mask"] else nkb - 1))
                        MM(ps[0:bs, bank, off:off + 129], PT[sb_][0:nk, mp_, col:col + bs], v_, first, lastk, [PTB[sb_], vB_], [psB[bank]], skip=True)

            prev = None
            for it in items:
                st = emit_scores(it)
                if prev is not None:
                    emit_av(prev)
                prev = st
            emit_av(prev)
            CP("dve", zf[0:bs, 0:387], ps[0:bs, 4, 0:387], [psB[4]], [zfB])
            CP("dve", zf[0:bs, 387:774], ps[0:bs, 5, 0:387], [psB[5]], [zfB])
            CP("dve", hmf[0:bs, 0:258], ps[0:bs, 6, 0:258], [psB[6]], [hmfB])

            def accv(a_):
                if a_ < 6:
                    return zf[0:bs, a_ * 129:(a_ + 1) * 129], zfB
                return hmf[0:bs, (a_ - 6) * 129:(a_ - 5) * 129], hmfB

            for qb in range(nblk):
                o0, o0B = accv(qb * 2)
                o1, o1B = accv(qb * 2 + 1)
                r0 = sml[0:bs, 0:1]; r1 = sml[0:bs, 1:2]
                sch.add("dve", "reciprocal", (), dict(out=r0, in_=o0[:, 128:129]), [o0B], [smlB])
                sch.add("dve", "reciprocal", (), dict(out=r1, in_=o1[:, 128:129]), [o1B], [smlB])
                TT(r1, r1, neglam[0:bs, l:l + 1], ALU.mult, [smlB, prmB], [smlB])
                TS(o0[:, 0:128], o0[:, 0:128], r0, None, ALU.mult, None, [o0B, smlB], [o0B])
                STT(oaF[0:bs, qb, h * 128:(h + 1) * 128], o1[:, 0:128], r1, o0[:, 0:128], ALU.mult, ALU.add,
                    [o1B, o0B, smlB], [oaFB[qb]])
            if sub_pend[0] is not None:
                subln_head(sub_pend[0])
            sub_pend[0] = h
        subln_head(sub_pend[0])
        RSQRT(sml[0:bs, 64:96], 128.0 * LN_EPS, [sml2B], [sml2B])
        for qb in range(nblk):
            ov = oaF[0:bs, qb, :].rearrange("p (h d) -> p h d", h=HA)
            rs = sml[0:bs, 64:96].rearrange("p (h q) -> p h q", q=4)[:, :, qb:qb + 1].to_broadcast([bs, HA, 128])
            TT(ov, ov, rs, ALU.mult, [oaFB[qb], sml2B], [oaFB[qb]])
            TT(ov, ov, subg_s[0:bs, l, :].rearrange("p (o d) -> p o d", o=1).to_broadcast([bs, HA, 128]), ALU.mult, [oaFB[qb], prmB], [oabB[qb]])

        sch.phase = "mlstm"
        for b in range(nblk):
            kb_ = tbank()
            pt = psb16(kb_)
            for j in range(8):
                TR(pt[0:bs, j * 128:(j + 1) * 128], qkT[:, 8 + j, b * bs:(b + 1) * bs], [qkTB[8 + j], constB], [psB[kb_]])
            for h in range(HM):
                ACT(kw[0:bs, h * 256:(h + 1) * 256], pt[0:bs, h * 256:(h + 1) * 256], AF.Identity, [psB[kb_], tokscB], [kwB], scale=toksc[0:bs, b, h:h + 1])
            kS = pbank()
            for h in range(HM):
                for dk in range(2):
                    MM(ps[0:bs, kS, h * 128:h * 128 + bs], qkT[:, 8 + 2 * h + dk, b * bs:(b + 1) * bs], qkT[:, 2 * h + dk, b * bs:(b + 1) * bs],
                       dk == 0, dk == 1, [qkTB[8 + 2 * h + dk], qkTB[2 * h + dk]], [psB[kS]])
            for h in range(HM):
                STT(Sm[0:bs, h, 0:bs], ps[0:bs, kS, h * 128:h * 128 + bs], toksc[0:bs, b, h:h + 1], maskBD[0:bs, 0:bs], ALU.mult, ALU.mult,
                    [psB[kS], tokscB, constB], [SmB])
            for ci in range(cpb):
                c = b * cpb + ci
                p0, p1 = ci * L, ci * L + L
                t0_, t1_ = b * bs + ci * L, b * bs + ci * L + L
                Gc, GcB = GbL[c % 2]
                if c == 0:
                    for h in range(HM):
                        ACT(Gc[:, :, h, :], Cg[:, :, h, :], AF.Identity, [CgB[h], gamB], [GcB[h]], scale=gam[:, c, h:h + 1])
                for h in range(HM):
                    for dk in range(2):
                        kC = 4 + dk
                        dC = ps[:, kC, 0:257]
                        MM(dC, kw[p0:p1, h * 256 + dk * 128:h * 256 + dk * 128 + 128], VMaug[p0:p1, b, h, :], True, True, [kwB, VMB[b]], [psB[kC]])
                        STT(Cg[:, dk, h, :], Cg[:, dk, h, :], gam[:, c, h:h + 1], dC, ALU.mult, ALU.add, [CgB[h], gamB, psB[kC]], [CgB[h]])
                if c + 1 < nch:
                    Gn, GnB = GbL[(c + 1) % 2]
                    for h in range(HM):
                        ACT(Gn[:, :, h, :], Cg[:, :, h, :], AF.Identity, [CgB[h], gamB], [GnB[h]], scale=gam[:, c + 1, h:h + 1])
                for h in range(HM):
                    nd = ps[p0:p1, h, 0:257]
                    MM(nd, qkT[:, 2 * h, t0_:t1_], Gc[:, 0, h, :], True, False, [qkTB[2 * h], GcB[h]], [psB[h]])
                    MM(nd, qkT[:, 2 * h + 1, t0_:t1_], Gc[:, 1, h, :], False, False, [qkTB[2 * h + 1], GcB[h]], [psB[h]])
                    MM(nd, Sm[p0:p1, h, p0:p1], VMaug[p0:p1, b, h, :], False, True, [SmB, VMB[b]], [psB[h]])
                den = ps[p0:p1, 0:HM, 256]
                dd = sml[p0:p1, 16:16 + HM]
                TS(dd, den, -1.0, None, ALU.mult, None, psB[0:HM], [smlB])
                TT(dd, dd, den, ALU.max, [smlB] + psB[0:HM], [smlB])
                TT(dd, dd, toksc[p0:p1, b, 4:4 + HM], ALU.max, [smlB, tokscB], [smlB])
                sch.add("dve", "reciprocal", (), dict(out=dd, in_=dd), [smlB], [smlB])
                for h in range(HM):
                    ACT(hmf[p0:p1, h * 256:(h + 1) * 256], ps[p0:p1, h, 0:256], AF.Identity, [psB[h], smlB], [hmfB], scale=sml[p0:p1, 16 + h:17 + h])
            for h in range(HM):
                hv_ = hmf[0:bs, h * 256:(h + 1) * 256]
                ACT(zf[0:bs, h * 256:(h + 1) * 256], hv_, AF.Identity, [hmfB], [zfB, smlB], accum_out=sml[0:bs, 24 + h:25 + h])
                ACT(zf[0:bs, h * 256:(h + 1) * 256], hv_, AF.Square, [hmfB], [zfB, smlB], accum_out=sml[0:bs, 28 + h:29 + h])
            mean_ = sml[0:bs, 24:28]; ex2_ = sml[0:bs, 28:32]; nmr_ = sml[0:bs, 32:36]
            TS(mean_, mean_, 1.0 / 256.0, None, ALU.mult, None, [smlB], [smlB])
            TT(nmr_, mean_, mean_, ALU.mult, [smlB], [smlB])
            STT(ex2_, ex2_, 1.0 / 256.0, nmr_, ALU.mult, ALU.subtract, [smlB], [smlB])
            RSQRT(ex2_, LN_EPS, [smlB], [smlB])
            STT(nmr_, mean_, -1.0, ex2_, ALU.mult, ALU.mult, [smlB], [smlB])
            for h in range(HM):
                hv_ = hmf[0:bs, h * 256:(h + 1) * 256]
                ACT(hv_, hv_, AF.Identity, [hmfB, smlB], [hmfB], bias=sml[0:bs, 32 + h:33 + h], scale=sml[0:bs, 28 + h:29 + h])
            TT(hmb[0:bs, b, :], hmf[0:bs, :], lnp[0:bs, 0, :], ALU.mult, [hmfB, lnpB[0]], [hmbB[b]])

        sch.phase = "merge"
        for gi, c0 in enumerate((C_OM, C_OM + 512, C_GA, C_GA + 512, C_GB, C_GB + 512)):
            ws, wB = wnext(("in", l, c0, 512))
            hf = gi % 2
            for b in range(nblk):
                k, o = proj_tok(ws, wB, b, 512)
                si_ = (gi * nblk + b) % 2
                ACT(sg[si_][0:bs, :], o, AF.Sigmoid, [psB[k]], [sgB[si_]])
                if gi in (2, 3):
                    tgt, tB = oab, oabB
                else:
                    tgt, tB = hmb, hmbB
                TT(tgt[0:bs, b, hf * 512:(hf + 1) * 512], tgt[0:bs, b, hf * 512:(hf + 1) * 512], sg[si_][0:bs, :], ALU.mult, [sgB[si_], tB[b]], [tB[b]])
            wdone(1)
        for b in range(nblk):
            TT(ybf[0:bs, :], oab[0:bs, b, :], hmb[0:bs, b, :], ALU.add, [oabB[b], hmbB[b]], [ybfB])
            k = tbank()
            pt = psb16(k)
            for c in range(8):
                TR(pt[:, c * bs:(c + 1) * bs], ybf[0:bs, c * 128:(c + 1) * 128], [ybfB, constB], [psB[k]])
            CP(eveng(), xT[:, :, b * bs:(b + 1) * bs], pt[:, 0:8 * bs].rearrange("p (c t) -> p c t", c=8), [psB[k]], [xTB[b]])

        def layernorm_inplace(b, gi, bi):
            xv = xtok[0:bs, b, :]
            for hh in range(2):
                sch.add("dve", "bn_stats", (), dict(out=stt[0:bs, hh, :], in_=xtok[0:bs, b, hh * 512:(hh + 1) * 512]), [xtokB[b]], [lnB])
            sch.add("dve", "bn_aggr", (), dict(out=mv[0:bs, 0:2], in_=stt[0:bs, :, :]), [lnB], [lnB])
            CP("dve", mv[0:bs, 2:3], mv[0:bs, 1:2], [lnB], [lnB])
            RSQRT(mv[0:bs, 2:3], LN_EPS, [lnB], [lnB])
            STT(mv[0:bs, 3:4], mv[0:bs, 0:1], -1.0, mv[0:bs, 2:3], ALU.mult, ALU.mult, [lnB], [lnB])
            ACT(xv, xv, AF.Identity, [xtokB[b], lnB], [xtokB[b]], bias=mv[0:bs, 3:4], scale=mv[0:bs, 2:3])
            TT(xv, xv, lnp[0:bs, 0, :], ALU.mult, [xtokB[b], lnpB[0]], [xtokB[b]])
            TT(xv, xv, lnp[0:bs, 1, :], ALU.add, [xtokB[b], lnpB[1]], [xtokB[b]])

        sch.phase = "wout"
        load_lnp(0, ln1g); load_lnp(1, ln1b)
        w0, w0B = wnext(("out", l, 0, 512))
        w1, w1B = wnext(("out", l, 512, 512))
        for b in range(nblk):
            for hf, (ws, wB) in enumerate(((w0, w0B), (w1, w1B))):
                k, o = proj_tok(ws, wB, b, 512)
                xs_ = xtok[0:bs, b, hf * 512:(hf + 1) * 512]
                STT(xs_, xs_, ALPHA, o, ALU.mult, ALU.add, [xtokB[b], psB[k]], [xtokB[b]])
            layernorm_inplace(b, 1, 2)
        wdone(2)
        make_xT()

        sch.phase = "up"
        up_pend = [None]

        def up_final(cc, a, pi):
            if cc < 22:
                ACT(hT[:, cc, 0:ntok], a, AF.Gelu, [accB[pi]], [hTB[cc]])
            else:
                TT(hT[:, cc - 22, 0:ntok], hT[:, cc - 22, 0:ntok], a, ALU.mult, [hTB[cc - 22], accB[pi]], [hTB[cc - 22]])

        for g in range(11):
            ws, wB = wnext(("up", l, g * 512, 512))
            for cc4 in range(4):
                cc = g * 4 + cc4
                k = pbank()
                o = ps[:, k, 0:ntok]
                for kc in range(8):
                    MM(o, ws[:, kc, cc4 * 128:(cc4 + 1) * 128], xT[:, kc, 0:ntok], kc == 0, kc == 7, [wB] + xTB[0:nblk], [psB[k]])
                pi = cc % 2
                pcv = pcm[pi]
                CP("pool", pcv[:, 0:2], fhalo[:, l, cc, :], [fhaloB[l]], [pcmB[pi]])
                CP("act", pcv[:, 2:2 + ntok], o, [psB[k]], [pcmB[pi]])
                a = acc[pi][:, 0:ntok]
                ACT(a, o, AF.Identity, [psB[k], prmB], [accB[pi]], bias=fcb_s[:, l, cc:cc + 1], scale=fcw_s[:, l, cc, 2:3])
                CP("pool", fhalo[:, l, cc, :], pcv[:, ntok:ntok + 2], [pcmB[pi]], [fhaloB[l]])
                for j in range(2):
                    STT(a, pcv[:, j:j + ntok], fcw_s[:, l, cc, j:j + 1], a, ALU.mult, ALU.add, [pcmB[pi], prmB, accB[pi]], [accB[pi]])
                if up_pend[0] is not None:
                    up_final(*up_pend[0])
                up_pend[0] = (cc, a, pi)
            wdone(1)
        up_final(*up_pend[0])
        if last:
            for g in range(11):
                ws, wB = wnext(("up", l, g * 512, 512))
                k, o = proj_tok(ws, wB, nblk - 1, 512)
                CP("dve", stg[0:bs, :], o, [psB[k]], [stgB])
                DMA("sp", tp["fc_out"][l, :, g * 512:(g + 1) * 512], stg[bs - 2:bs, :], [stgB], [], stgB)
                wdone(1)
        sch.phase = "down"
        load_lnp(0, ln2g); load_lnp(1, ln2b)
        for nh in range(2):
            pcs = [wnext(("down", l, nh, pc)) for pc in range(3)]
            for b in range(nblk):
                k = pbank()
                o = ps[0:bs, k, 0:512]
                for kc in range(22):
                    ws, wB = pcs[kc // 8]
                    MM(o, hT[:, kc, b * bs:(b + 1) * bs], ws[:, kc % 8, :], kc == 0, kc == 21, [wB, hTB[kc]], [psB[k]])
                xs_ = xtok[0:bs, b, nh * 512:(nh + 1) * 512]
                STT(xs_, xs_, ALPHA, o, ALU.mult, ALU.add, [xtokB[b], psB[k]], [xtokB[b]])
            wdone(3)
        for b in range(nblk):
            layernorm_inplace(b, 3, 4)
            if l == NL - 1:
                DMA("sp", tp["y_out"][b * bs:(b + 1) * bs, :], xtok[0:bs, b, :], [xtokB[b]], [], xtokB[b])
        if last:
            for h in range(HM):
                DMA("sp", tp["C_out"][l, h].rearrange("(c p) v -> p c v", p=128), Cg[:, :, h, 0:256], [CgB[h]], [], CgB[h])
                DMA("sp", tp["n_out"][l, h].rearrange("(c p o) -> p c o", p=128, o=1), Cg[:, :, h, 256:257], [CgB[h]], [], CgB[h], slow=True)
            DMA("sp", tp["m_out"][l].rearrange("(p o) -> p o", o=1), mstate[:, l:l + 1], [mstateB[l]], [], mstateB[l], slow=True)

    def sample_prep():
        sch.phase = "prep"
        for l in range(NL):
            for blk in range(P // 128):
                DMA("pool", vtmp, ck[l, blk * 128:(blk + 1) * 128, :], [], [vtmpB], vtmpB)
                k = tbank()
                pt = psb16(k)
                for h in range(HA):
                    TR(pt[:, h * 128:(h + 1) * 128], vtmp[:, h * 128:(h + 1) * 128], [vtmpB, constB], [psB[k]])
                CP(eveng(), ktmp, pt.rearrange("p (h t) -> p h t", h=HA), [psB[k]], [ktmpB])
                DMA("sp", KTs[l][:, :, blk * 128:(blk + 1) * 128].rearrange("h p t -> p h t"), ktmp, [ktmpB], [KTsB[l]], ktmpB)
                DMA("pool", ybf, cv[l, blk * 128:(blk + 1) * 128, :], [], [ybfB], ybfB)
                DMA("sp", Vbs[l, blk * 128:(blk + 1) * 128, :], ybf, [ybfB], [VbsB[l]], ybfB)

    steps = []
    for i in range(NT):
        for l in range(NL):
            steps.append((l, i, False))
    for l in range(NL):
        steps.append((l, 0, True))
    for (l, i, samp) in steps:
        wq.extend(wspec_step(l, samp or i == NT - 1))

    prep_done = False
    for (l, i, samp) in steps:
        if samp and not prep_done:
            sample_prep()
            prep_done = True
        if not samp:
            tp = dict(ntok=T, bs=128, nblk=NB, L=64, last=(i == NT - 1), first=(i == 0), load_state=False,
                      x_src=xp[i * T:(i + 1) * T, :], pos0=i * T, tok0=i * T, mask=True,
                      prior=[(KTp[l][:, :, g * T:(g + 1) * T], Vbp[l, g * T:(g + 1) * T, :], [KTpB[l][g], VbpB[l][g]]) for g in range(i)],
                      KT_dst=KTp[l][:, :, i * T:(i + 1) * T], KTB=KTpB[l][i], Vb_dst=Vbp[l, i * T:(i + 1) * T, :], VbB=VbpB[l][i],
                      k_out=kp[:, i * T:(i + 1) * T, :], v_out=vp[:, i * T:(i + 1) * T, :], y_out=yp[i * T:(i + 1) * T, :],
                      mc_out=mcp, fc_out=fcp, C_out=Cp, n_out=np_, m_out=mp)
        else:
            tp = dict(ntok=NS, bs=NS, nblk=1, L=NS, last=True, first=False, load_state=True,
                      x_src=xs, pos0=S, tok0=0, mask=False,
                      prior=[(KTs[l][:, :, g * T:(g + 1) * T], Vbs[l, g * T:(g + 1) * T, :], [KTsB[l], VbsB[l]]) for g in range(P // T)],
                      KT_dst=None, KTB=None, Vb_dst=None, VbB=None,
                      k_out=ks, v_out=vs, y_out=ys, mc_out=mcs, fc_out=fcs, C_out=Cs, n_out=ns_, m_out=ms,
                      sC=sC, sn=sn, sm=sm, smc=smc, sfc=sfc)
        step(l, tp)
    assert wstate["used"] == len(wq) and wstate["released"] == len(wq), (wstate, len(wq))
    print("SBUF/PSUM allocation done")
    info = sch.emit()
    return nc, info


def rope_tables(S, P):
    half = 32
    inv = (np.float32(10000.0) ** (-np.arange(half, dtype=np.float32) * np.float32(2.0) / np.float32(64))).astype(np.float32)
    pos = np.concatenate([np.arange(S), P + np.arange(NS)]).astype(np.float32)
    ang = (pos[:, None] * inv[None, :]).astype(np.float32)
    return np.cos(ang).astype(np.float32), np.sin(ang).astype(np.float32)


_CACHE = {}


def run(inputs, S, P, T, n_prompt, n_sample, n_cores):
    key = (S, P, T)
    if key not in _CACHE:
        _CACHE[key] = build(S=S, P=P, T=T)
    nc, info = _CACHE[key]
    f = lambda a: np.ascontiguousarray(np.asarray(a, dtype=np.float32))
    cosT, sinT = rope_tables(S, P)
    NL = 2
    in_maps = []
    for c in range(n_cores):
        b = c % n_prompt
        s = c % n_sample
        m = {
            "xp": f(inputs["x_prompt"][b]), "xs": f(inputs["x_sample"][s]),
            "ck": f(inputs["cache_k"][:, s]).reshape(NL, P, D), "cv": f(inputs["cache_v"][:, s]).reshape(NL, P, D),
            "smc": f(inputs["state_mlstm_conv"][:, s]), "sC": f(inputs["state_mlstm_C"][:, s]),
            "sn": f(inputs["state_mlstm_n"][:, s]), "sm": f(inputs["state_mlstm_m"][:, s]),
            "sfc": f(inputs["state_ffn_conv"][:, s]),
            "w_in": f(inputs["w_in"]), "b_if": f(inputs["b_if"]), "mcw": f(inputs["mlstm_conv_w"]), "mcb": f(inputs["mlstm_conv_b"]),
            "dlam": f(inputs["diff_lambda"]), "subg": f(inputs["diff_subln_g"]), "mhg": f(inputs["mlstm_norm_g"]),
            "w_out": f(inputs["w_out"]), "ln1g": f(inputs["ln1_g"]), "ln1b": f(inputs["ln1_b"]),
            "w_up": f(inputs["w_up"]), "fcw": f(inputs["ffn_conv_w"]), "fcb": f(inputs["ffn_conv_b"]),
            "w_down": f(inputs["w_down"]), "ln2g": f(inputs["ln2_g"]), "ln2b": f(inputs["ln2_b"]),
            "cosT": cosT, "sinT": sinT,
        }
        in_maps.append(m)
    res = run_bass_kernel_spmd(nc, in_maps, core_ids=list(range(n_cores)))
    R = res.results
    pc = list(range(n_prompt))
    sc = list(range(n_sample))
    st = lambda name, cores, ax=0: np.stack([np.asarray(R[c][name], dtype=np.float32) for c in cores], axis=ax)
    y_prompt = st("yp", pc)
    y_sample = st("ys", sc)
    k_prompt = st("kp", pc, 1).reshape(NL, n_prompt, S, HA, 128)
    v_prompt = st("vp", pc, 1).reshape(NL, n_prompt, S, HA, 128)
    outs = (y_prompt, y_sample, k_prompt, v_prompt,
            st("mcp", pc, 1), st("Cp", pc, 1), st("np", pc, 1), st("mp", pc, 1), st("fcp", pc, 1),
            st("ks", sc, 1).reshape(NL, n_sample, NS, HA, 128), st("vs", sc, 1).reshape(NL, n_sample, NS, HA, 128),
            st("mcs", sc, 1), st("Cs", sc, 1), st("ns", sc, 1), st("ms", sc, 1), st("fcs", sc, 1))
    return outs


def kernel(**inputs):
    return run(inputs, S=8192, P=4096, T=512, n_prompt=4, n_sample=8, n_cores=8)
```

```python
import math
import numpy as np
import concourse.bass as bass
import concourse.mybir as mybir
from concourse.bass_utils import run_bass_kernel_spmd

F32 = mybir.dt.float32
BF16 = mybir.dt.bfloat16
AF = mybir.ActivationFunctionType
ALU = mybir.AluOpType
AX = mybir.AxisListType

D = 1024
HA = 8
HM = 4
DFF = 2816
DIN = 9224
NS = 16
ALPHA = (2 * 2) ** 0.25
LN_EPS = 1e-5
C_QA, C_KA, C_VA, C_QKM, C_VM, C_OM, C_GIF, C_GA, C_GB = 0, 1024, 2048, 3072, 5120, 6144, 7168, 7176, 8200


class Buf:
    __slots__ = ("name", "lastw", "readers")

    def __init__(self, name=""):
        self.name = name
        self.lastw = None
        self.readers = []


class Sched:
    def __init__(self, nc, same_engine_sync=True):
        self.nc = nc
        self.engs = {"pe": nc.tensor, "act": nc.scalar, "dve": nc.vector, "pool": nc.gpsimd, "sp": nc.sync}
        self.ins = []
        self.dma_cnt = {}
        self.same = same_engine_sync
        self.phase = ""
        self.names = None

    def add(self, eng, meth, args, kwargs, reads=(), writes=(), dma=None):
        idx = len(self.ins)
        deps = set()
        for r in reads:
            if r.lastw is not None:
                deps.add(r.lastw)
        for w in writes:
            if w.lastw is not None:
                deps.add(w.lastw)
            deps.update(w.readers)
        for r in reads:
            r.readers.append(idx)
        for w in writes:
            w.lastw = idx
            w.readers = []
        dval = None
        if dma is not None:
            self.dma_cnt[dma] = self.dma_cnt.get(dma, 0) + 16
            dval = self.dma_cnt[dma]
        keep = set()
        for d in deps:
            de = self.ins[d]
            if de[5] is None and de[0] == eng:
                if eng == "pe" or not self.same:
                    continue
            keep.add(d)
        self.ins.append([eng, meth, args, kwargs, keep, dma, dval, False, 0, self.phase])
        return idx

    def emit(self):
        nc = self.nc
        for rec in self.ins:
            for d in rec[4]:
                de = self.ins[d]
                if de[5] is None:
                    de[7] = True
        cnt = {e: 0 for e in self.engs}
        for rec in self.ins:
            if rec[7]:
                cnt[rec[0]] += 1
                rec[8] = cnt[rec[0]]
        esem = {e: nc.alloc_semaphore(name="es_" + e) for e in self.engs}
        dsem = {}
        for k in self.dma_cnt:
            dsem[k] = nc.alloc_semaphore(name="ds_%d" % len(dsem))
        waited = {e: {} for e in self.engs}
        nwait = 0
        for rec in self.ins:
            eng, meth, args, kwargs, deps, dma, dval, sig, sigval, phase = rec
            E = self.engs[eng]
            need = {}
            for d in deps:
                de = self.ins[d]
                if de[5] is None:
                    s, v = esem[de[0]], de[8]
                else:
                    s, v = dsem[de[5]], de[6]
                if need.get(s, 0) < v:
                    need[s] = v
            for s, v in need.items():
                if waited[eng].get(s, 0) >= v:
                    continue
                E.wait_ge(s, v)
                waited[eng][s] = v
                nwait += 1
            ins = getattr(E, meth)(*args, **kwargs)
            if self.names is not None:
                self.names[ins.ins.name] = phase
            if dma is not None:
                ins.then_inc(dsem[dma], 16)
            elif sig:
                ins.then_inc(esem[eng], 1)
        for k, v in self.dma_cnt.items():
            nc.sync.wait_ge(dsem[k], v)
        return dict(n=len(self.ins), nwait=nwait, nsem=len(dsem) + 5)


def build(S=8192, P=4096, T=512, NL=2, dbg=None, same=True):
    nc = bass.Bass("TRN2", target_bir_lowering=False)
    sch = Sched(nc, same_engine_sync=same)
    NT = S // T
    NTAB = S + NS

    def din(name, shape, dt=F32):
        return nc.dram_tensor(name, list(shape), dt, kind="ExternalInput").ap()

    def dout(name, shape, dt=F32):
        return nc.dram_tensor(name, list(shape), dt, kind="ExternalOutput").ap()

    def dscr(name, shape, dt):
        return nc.dram_tensor(name, list(shape), dt, kind="Internal").ap()

    def sb(name, shape, dt=F32):
        return nc.alloc_sbuf_tensor(name, list(shape), dt).ap()

    xp = din("xp", [S, D]); xs = din("xs", [NS, D])
    ck = din("ck", [NL, P, D]); cv = din("cv", [NL, P, D])
    smc = din("smc", [NL, 3, 2048]); sC = din("sC", [NL, HM, 256, 256]); sn = din("sn", [NL, HM, 256])
    sm = din("sm", [NL, HM]); sfc = din("sfc", [NL, 2, 2 * DFF])
    w_in = din("w_in", [NL, D, DIN]); b_if = din("b_if", [NL, 8])
    mcw = din("mcw", [NL, 4, 2048]); mcb = din("mcb", [NL, 2048])
    dlam = din("dlam", [NL, 4, 64]); subg = din("subg", [NL, 128]); mhg = din("mhg", [NL, D])
    w_out = din("w_out", [NL, D, D]); ln1g = din("ln1g", [NL, D]); ln1b = din("ln1b", [NL, D])
    w_up = din("w_up", [NL, D, 2 * DFF]); fcw = din("fcw", [NL, 3, 2 * DFF]); fcb = din("fcb", [NL, 2 * DFF])
    w_down = din("w_down", [NL, DFF, D]); ln2g = din("ln2g", [NL, D]); ln2b = din("ln2b", [NL, D])
    cosT = din("cosT", [NTAB, 32]); sinT = din("sinT", [NTAB, 32])

    yp = dout("yp", [S, D]); ys = dout("ys", [NS, D])
    kp = dout("kp", [NL, S, D]); vp = dout("vp", [NL, S, D])
    mcp = dout("mcp", [NL, 3, 2048]); Cp = dout("Cp", [NL, HM, 256, 256]); np_ = dout("np", [NL, HM, 256])
    mp = dout("mp", [NL, HM]); fcp = dout("fcp", [NL, 2, 2 * DFF])
    ks = dout("ks", [NL, NS, D]); vs = dout("vs", [NL, NS, D])
    mcs = dout("mcs", [NL, 3, 2048]); Cs = dout("Cs", [NL, HM, 256, 256]); ns_ = dout("ns", [NL, HM, 256])
    ms = dout("ms", [NL, HM]); fcs = dout("fcs", [NL, 2, 2 * DFF])

    KTp = dscr("KTp", [NL, HA, 128, S], BF16); Vbp = dscr("Vbp", [NL, S, D], BF16)
    KTs = dscr("KTs", [NL, HA, 128, P], BF16); Vbs = dscr("Vbs", [NL, P, D], BF16)
    KTpB = [[Buf() for _ in range(NT)] for _ in range(NL)]
    VbpB = [[Buf() for _ in range(NT)] for _ in range(NL)]
    KTsB = [[Buf() for _ in range(max(1, P // T))] for _ in range(NL)]
    VbsB = [[Buf() for _ in range(max(1, P // T))] for _ in range(NL)]

    NB = T // 128
    xtok = sb("xtok", [128, NB, D]); xtokB = [Buf() for _ in range(NB)]
    xbf = sb("xbf", [128, D], BF16); xbfB = Buf()
    xT = sb("xT", [128, 8, T], BF16); xTB = [Buf() for _ in range(NB)]
    NSLOT = 4
    ring = [sb("ring%d" % i, [128, 8, 512], BF16) for i in range(NSLOT)]
    ringB = [Buf() for _ in range(NSLOT)]
    zf = sb("zf", [128, D]); zfB = Buf()
    ro = sb("ro", [128, D]); roB = Buf()
    cs_t = sb("cs_t", [128, NB, 2, 32]); csB = Buf()
    QT = sb("QT", [128, HA, T], BF16); QTB = [Buf() for _ in range(NB)]
    KT = sb("KT", [128, HA, T], BF16); KTB = [Buf() for _ in range(NB)]
    Vaug = sb("Vaug", [128, NB, HA, 129], BF16); VaugB = [Buf() for _ in range(NB)]
    VMaug = sb("VMaug", [128, NB, HM, 257], BF16); VMB = [Buf() for _ in range(NB)]
    oab = sb("oab", [128, NB, D], BF16); oabB = [Buf() for _ in range(NB)]
    hmb = sb("hmb", [128, NB, D], BF16); hmbB = [Buf() for _ in range(NB)]
    oaF = oab; oaFB = oabB
    hT = sb("hT", [128, 22, T], BF16); hTB = [Buf() for _ in range(22)]
    qkT = hT; qkTB = hTB
    pcm = [sb("pcm%d" % i, [128, 3 + T]) for i in range(2)]; pcmB = [Buf(), Buf()]; pcmH = [Buf(), Buf()]
    acc = [sb("acc%d" % i, [128, T]) for i in range(2)]; accB = [Buf(), Buf()]
    rt = acc; rtB = accB
    stg = pcm[0][:, 3:515]; stgB = pcmB[0]
    kch = [sb("kch%d" % i, [128, T], BF16) for i in range(2)]; kchB = [Buf(), Buf()]
    vch = [sb("vch%d" % i, [128, NB, 129], BF16) for i in range(2)]; vchB = [Buf(), Buf()]
    PT = [sb("PT%d" % i, [128, 2, T], BF16) for i in range(2)]; PTB = [Buf(), Buf()]
    Caug = sb("Caug", [128, 2, HM, 257]); CaugB = [Buf() for _ in range(HM)]
    GbRaw = sb("GbRaw", [128, 2 * HM * 257], BF16); GbB = [Buf() for _ in range(HM)]
    Gb = GbRaw.rearrange("p (a h d) -> p a h d", a=2, h=HM)
    Gb2 = sb("Gb2", [128, 2, HM, 257], BF16); Gb2B = [Buf() for _ in range(HM)]
    GbL = [(Gb, GbB), (Gb2, Gb2B)]
    ro2 = GbRaw[:, 0:2 * D].bitcast(F32); ro2B = GbB
    kw = sb("kw", [128, D], BF16); kwB = Buf()
    Sm = sb("Sm", [128, HM, 128], BF16); SmB = Buf()
    hmf = sb("hmf", [128, D]); hmfB = Buf()
    sml = sb("sml", [128, 96]); smlB = Buf(); sml2B = Buf()
    mhalo = sb("mhalo", [128, NL, 16, 3]); mhaloB = [Buf() for _ in range(NL)]
    fhalo = sb("fhalo", [128, NL, 44, 2]); fhaloB = [Buf() for _ in range(NL)]
    mstate = sb("mstate", [4, NL]); mstateB = [Buf() for _ in range(NL)]
    CaugL = [Caug, sb("Caug1", [128, 2, HM, 257])]
    CaugLB = [CaugB, [Buf() for _ in range(HM)]]
    _g3 = [PT[0].rearrange("p a t -> p (a t)").bitcast(F32)[0:4, :], PT[1].rearrange("p a t -> p (a t)").bitcast(F32)[0:4, :],
           kw.bitcast(F32)[0:4, :]]
    _g3B = [PTB[0], PTB[1], kwB]
    gw = {"t1": _g3[0], "sp": _g3[0], "A": _g3[1], "cl": _g3[1], "ig": _g3[2], "r": _g3[2], "sc": _g3[2]}
    gwB = {"t1": _g3B[0], "sp": _g3B[0], "A": _g3B[1], "cl": _g3B[1], "ig": _g3B[2], "r": _g3B[2], "sc": _g3B[2]}
    gs = {n: sb("gs_" + n, [4, 16]) for n in ("cm", "aL", "mn", "mu", "mpv", "gam")}
    gsB = {n: Buf() for n in gs}
    gD = sb("gD", [4, 16, 4]); gDB = Buf()
    toksc = sb("toksc", [128, NB, 8]); tokscB = Buf()
    gam = sb("gam", [128, 16, 4]); gamB = Buf()
    identb = sb("identb", [128, 128], BF16); identf = sb("identf", [128, 128]); maskBD = sb("maskBD", [128, 128])
    ones4 = sb("ones4", [4, 128]); resetm = sb("resetm", [4, T]); resetm16 = sb("resetm16", [4, NS])
    constB = Buf()
    mcw_s = sb("mcw_s", [128, NL, 16, 4]); mcb_s = sb("mcb_s", [128, NL, 16])
    fcw_s = sb("fcw_s", [128, NL, 44, 3]); fcb_s = sb("fcb_s", [128, NL, 44])
    bif_s = sb("bif_s", [4, NL, 2]); nbif_s = sb("nbif_s", [4, NL, 2])
    neglam = sb("neglam", [128, NL]); lamw = sb("lamw", [128, 4, 64]); lamw2 = sb("lamw2", [128, 4])
    subg_s = sb("subg_s", [128, NL, 128])
    prmB = Buf()
    lnp = sb("lnp", [128, 2, D]); lnpB = [Buf(), Buf()]
    sg = [sb("sg%d" % i, [128, 512], BF16) for i in range(2)]; sgB = [Buf(), Buf()]
    ybf = xbf; ybfB = xbfB
    stt = sb("stt", [128, 2, 6]); mv = sb("mv", [128, 4]); lnB = Buf()
    vtmp = xbf; vtmpB = xbfB
    ktmp = kw.rearrange("p (h t) -> p h t", h=HA); ktmpB = kwB
    ps = nc.alloc_psum_tensor("ps", [128, 8, 512], F32).ap()
    psB = [Buf() for _ in range(8)]

    def psb16(k):
        return ps[:, k, :].bitcast(BF16)

    def MM(out, lhsT, rhs, start, stop, r, w, skip=False):
        kw_ = dict(lhsT=lhsT, rhs=rhs, start=start, stop=stop)
        if skip:
            kw_["skip_group_check"] = True
        sch.add("pe", "matmul", (out,), kw_, r, w)

    def TR(out, in_, r, w):
        sch.add("pe", "transpose", (), dict(out=out, in_=in_, identity=identb[0:in_.shape[0], 0:in_.shape[0]]), r, w)

    def ACT(out, in_, func, r, w, bias=None, scale=None, accum_out=None):
        kw_ = dict(out=out, in_=in_, func=func)
        if bias is not None:
            kw_["bias"] = bias
        if scale is not None:
            kw_["scale"] = scale
        if accum_out is not None:
            kw_["accum_out"] = accum_out
        sch.add("act", "activation", (), kw_, r, w)

    def CP(eng, out, in_, r, w):
        if eng == "act":
            ACT(out, in_, AF.Identity, r, w)
        else:
            sch.add(eng, "tensor_copy", (), dict(out=out, in_=in_), r, w)

    def TT(out, in0, in1, op, r, w, eng="dve"):
        sch.add(eng, "tensor_tensor", (), dict(out=out, in0=in0, in1=in1, op=op), r, w)

    def TS(out, in0, s1, s2, op0, op1, r, w, eng="dve"):
        kw_ = dict(out=out, in0=in0, scalar1=s1, scalar2=s2, op0=op0)
        if op1 is not None:
            kw_["op1"] = op1
        sch.add(eng, "tensor_scalar", (), kw_, r, w)

    def STT(out, in0, scalar, in1, op0, op1, r, w, eng="dve"):
        sch.add(eng, "scalar_tensor_tensor", (), dict(out=out, in0=in0, scalar=scalar, in1=in1, op0=op0, op1=op1), r, w)

    def RSQRT(ap, addc, r, w):
        TS(ap, ap, addc, None, ALU.add, None, r, w)
        ACT(ap, ap, AF.Sqrt, r, w)
        sch.add("dve", "reciprocal", (), dict(out=ap, in_=ap), r, w)

    def RED(out, in_, op, r, w):
        sch.add("dve", "tensor_reduce", (), dict(out=out, in_=in_, axis=AX.X, op=op), r, w)

    def MEMSET(eng, ap, val, r, w):
        sch.add(eng, "memset", (ap, val), {}, r, w)

    def DMA(q, out, in_, r, w, key, slow=False):
        kw_ = dict(out=out, in_=in_)
        if slow:
            kw_["allow_slow_non_contiguous"] = True
        sch.add(q, "dma_start", (), kw_, r, w, dma=key)

    MEMSET("pool", identf, 1.0, [], [constB])
    sch.add("pool", "affine_select", (), dict(out=identf, in_=identf, compare_op=ALU.is_equal, fill=0.0, base=0,
                                              pattern=[[-1, 128]], channel_multiplier=1), [constB], [constB])
    CP("dve", identb, identf, [constB], [constB])
    MEMSET("pool", maskBD, 1.0, [constB], [constB])
    sch.add("pool", "affine_select", (), dict(out=maskBD, in_=maskBD, compare_op=ALU.is_ge, fill=0.0, base=0,
                                              pattern=[[1, 128]], channel_multiplier=-1), [constB], [constB])
    MEMSET("pool", maskBD[0:64, 64:128], 0.0, [constB], [constB])
    MEMSET("pool", ones4, 1.0, [constB], [constB])
    MEMSET("pool", resetm, 1.0, [constB], [constB])
    MEMSET("pool", resetm.rearrange("p (c l) -> p c l", l=64)[:, :, 0:1], 0.0, [constB], [constB])
    MEMSET("pool", resetm16, 1.0, [constB], [constB])
    MEMSET("pool", resetm16[:, 0:1], 0.0, [constB], [constB])
    MEMSET("pool", Vaug[:, :, :, 128:129], 1.0, [], VaugB)
    MEMSET("pool", VMaug[:, :, :, 256:257], 1.0, [], VMB)
    for i in range(2):
        MEMSET("pool", vch[i][:, :, 128:129], 1.0, [], [vchB[i]])
    pk = Buf()
    for l in range(NL):
        for j in range(4):
            DMA("sp", mcw_s[:, l, :, j], mcw[l, j].rearrange("(c p) -> p c", p=128), [], [prmB], pk, slow=True)
        DMA("sp", mcb_s[:, l, :], mcb[l].rearrange("(c p) -> p c", p=128), [], [prmB], pk, slow=True)
        for j in range(3):
            DMA("sp", fcw_s[:, l, :, j], fcw[l, j].rearrange("(c p) -> p c", p=128), [], [prmB], pk, slow=True)
        DMA("sp", fcb_s[:, l, :], fcb[l].rearrange("(c p) -> p c", p=128), [], [prmB], pk, slow=True)
        DMA("sp", bif_s[:, l, :], b_if[l].rearrange("(j p) -> p j", p=4), [], [prmB], pk, slow=True)
        DMA("sp", subg_s[:, l, :], subg[l].partition_broadcast(128), [], [prmB], pk)
        DMA("sp", lamw, dlam[l].partition_broadcast(128), [prmB], [prmB], pk)
        lam_init = 0.8 - 0.6 * math.exp(-0.3 * l)
        lv = lamw.rearrange("p (a b) d -> p a b d", b=2)
        TT(lamw[:, 0:2, :].rearrange("p a d -> p a d"), lv[:, :, 0, :], lv[:, :, 1, :], ALU.mult, [prmB], [prmB])
        RED(lamw2[:, 0:2], lamw[:, 0:2, :], ALU.add, [prmB], [prmB])
        ACT(lamw2[:, 2:4], lamw2[:, 0:2], AF.Exp, [prmB], [prmB])
        TT(neglam[:, l:l + 1], lamw2[:, 3:4], lamw2[:, 2:3], ALU.subtract, [prmB], [prmB])
        TS(neglam[:, l:l + 1], neglam[:, l:l + 1], -lam_init, None, ALU.add, None, [prmB], [prmB])
        TS(subg_s[:, l, :], subg_s[:, l, :], (1.0 - lam_init) * math.sqrt(128.0), None, ALU.mult, None, [prmB], [prmB])
    TS(nbif_s, bif_s, -1.0, None, ALU.mult, None, [prmB], [prmB])

    wq = []
    wstate = dict(loaded=0, used=0, released=0)

    def wspec_step(l, last):
        sp_ = []
        for c0 in (C_QA, C_QA + 512, C_KA, C_KA + 512, C_VA, C_VA + 512):
            sp_.append(("in", l, c0, 512))
        for c0 in range(C_QKM, C_QKM + 2048, 512):
            sp_.append(("in", l, c0, 512))
        if last:
            for c0 in range(C_QKM, C_QKM + 2048, 512):
                sp_.append(("in", l, c0, 512))
        sp_.append(("in", l, C_VM, 512)); sp_.append(("in", l, C_VM + 512, 512))
        sp_.append(("in", l, C_GIF, 8))
        for c0 in (C_OM, C_OM + 512, C_GA, C_GA + 512, C_GB, C_GB + 512):
            sp_.append(("in", l, c0, 512))
        sp_.append(("out", l, 0, 512)); sp_.append(("out", l, 512, 512))
        for g in range(11):
            sp_.append(("up", l, g * 512, 512))
        if last:
            for g in range(11):
                sp_.append(("up", l, g * 512, 512))
        for nh in range(2):
            for pc in range(3):
                sp_.append(("down", l, nh, pc))
        return sp_

    def wload(i):
        kind, l, a, b = wq[i]
        slot = i % NSLOT
        if kind == "in":
            src = w_in[l].rearrange("(kc p) n -> p kc n", p=128)[:, :, a:a + b]
            dst = ring[slot][:, :, 0:b]
        elif kind == "out":
            src = w_out[l].rearrange("(kc p) n -> p kc n", p=128)[:, :, a:a + b]
            dst = ring[slot][:, :, 0:b]
        elif kind == "up":
            src = w_up[l].rearrange("(kc p) n -> p kc n", p=128)[:, :, a:a + b]
            dst = ring[slot][:, :, 0:b]
        else:
            k0 = b * 8
            k1 = min(22, k0 + 8)
            src = w_down[l].rearrange("(kc p) n -> p kc n", p=128)[:, k0:k1, a * 512:(a + 1) * 512]
            dst = ring[slot][:, 0:k1 - k0, :]
        DMA("pool", dst, src, [], [ringB[slot]], ringB[slot])

    def wfill():
        while wstate["loaded"] < min(len(wq), wstate["released"] + NSLOT):
            wload(wstate["loaded"])
            wstate["loaded"] += 1

    def wnext(expect):
        i = wstate["used"]
        assert wq[i] == expect, (wq[i], expect)
        wfill()
        assert wstate["loaded"] > i
        wstate["used"] += 1
        return ring[i % NSLOT], ringB[i % NSLOT]

    def wdone(n=1):
        wstate["released"] += n
        assert wstate["released"] <= wstate["used"]
        wfill()

    rot = dict(pp=0, tp=0)

    def pbank():
        k = rot["pp"] % 4
        rot["pp"] += 1
        return k

    def tbank():
        k = 6 + rot["tp"] % 2
        rot["tp"] += 1
        return k

    evr = dict(i=0)

    def eveng():
        evr["i"] += 1
        return "act" if evr["i"] % 2 else "dve"

    def step(l, tp):
        ntok, bs, nblk, L = tp["ntok"], tp["bs"], tp["nblk"], tp["L"]
        cpb = bs // L
        nch = ntok // L
        last = tp["last"]
        Cg = CaugL[l]; CgB = CaugLB[l]

        def load_lnp(slot, src):
            DMA("sp", lnp[:, slot, :], src[l].partition_broadcast(128), [], [lnpB[slot]], lnpB[slot])
        load_lnp(0, mhg)
        if l == 0:
            for b in range(nblk):
                DMA("sp", xtok[0:bs, b, :], tp["x_src"][b * bs:(b + 1) * bs, :], [], [xtokB[b]], xtokB[b])
        DMA("sp", cs_t[0:bs, 0:nblk, 0, :], cosT[tp["pos0"]:tp["pos0"] + ntok, :].rearrange("(b p) d -> p b d", p=bs), [], [csB], csB)
        DMA("sp", cs_t[0:bs, 0:nblk, 1, :], sinT[tp["pos0"]:tp["pos0"] + ntok, :].rearrange("(b p) d -> p b d", p=bs), [], [csB], csB)
        if tp["load_state"]:
            for h in range(HM):
                DMA("sp", Cg[:, :, h, 0:256], tp["sC"][l, h].rearrange("(c p) v -> p c v", p=128), [], [CgB[h]], CgB[h])
                DMA("sp", Cg[:, :, h, 256:257], tp["sn"][l, h].rearrange("(c p o) -> p c o", p=128, o=1), [], [CgB[h]], CgB[h], slow=True)
            DMA("sp", mstate[:, l:l + 1], tp["sm"][l].rearrange("(p o) -> p o", o=1), [], [mstateB[l]], mstateB[l], slow=True)
            for j in range(3):
                DMA("sp", mhalo[:, l, :, j], tp["smc"][l, j].rearrange("(c p) -> p c", p=128), [], [mhaloB[l]], mhaloB[l], slow=True)
            for j in range(2):
                DMA("sp", fhalo[:, l, :, j], tp["sfc"][l, j].rearrange("(c p) -> p c", p=128), [], [fhaloB[l]], fhaloB[l], slow=True)
        elif tp["first"]:
            for h in range(HM):
                MEMSET("pool", Cg[:, :, h, :], 0.0, [], [CgB[h]])
            MEMSET("pool", mstate[:, l:l + 1], 0.0, [], [mstateB[l]])
            MEMSET("pool", mhalo[:, l, :, :], 0.0, [], [mhaloB[l]])
            MEMSET("pool", fhalo[:, l, :, :], 0.0, [], [fhaloB[l]])

        def make_xT():
            for b in range(nblk):
                CP(eveng(), xbf[0:bs, :], xtok[0:bs, b, :], [xtokB[b]], [xbfB])
                k = tbank()
                pt = psb16(k)
                for c in range(8):
                    TR(pt[:, c * bs:(c + 1) * bs], xbf[0:bs, c * 128:(c + 1) * 128], [xbfB, constB], [psB[k]])
                CP(eveng(), xT[:, :, b * bs:(b + 1) * bs], pt[:, 0:8 * bs].rearrange("p (c t) -> p c t", c=8), [psB[k]], [xTB[b]])

        sch.phase = "xT"
        make_xT()

        def proj_tok(wslot, wB, b, ncols, kcn=8, lhs=None, lhsB=None):
            k = pbank()
            out = ps[0:bs, k, 0:ncols]
            for kc in range(kcn):
                lt = xT[:, kc, b * bs:(b + 1) * bs] if lhs is None else lhs(kc)
                MM(out, lt, wslot[:, kc, 0:ncols], kc == 0, kc == kcn - 1, [wB, xTB[b] if lhsB is None else lhsB], [psB[k]])
            return k, out

        def rope_block(src_zf, zfB, dst, roB, b):
            sv = src_zf.rearrange("p (g two d) -> p g two d", two=2, d=32)
            dv = dst.rearrange("p (g two d) -> p g two d", two=2, d=32)
            cosb = cs_t[0:bs, b, 0:1, :].to_broadcast([bs, 16, 32])
            sinb = cs_t[0:bs, b, 1:2, :].to_broadcast([bs, 16, 32])
            t0 = rt[0][0:bs, :].rearrange("p (g d) -> p g d", d=32)
            t1 = rt[1][0:bs, :].rearrange("p (g d) -> p g d", d=32)
            TT(t0, sv[:, :, 0, :], cosb, ALU.mult, [zfB, csB], [rtB[0]])
            TT(t1, sv[:, :, 1, :], sinb, ALU.mult, [zfB, csB], [rtB[1]])
            TT(dv[:, :, 0, :], t0, t1, ALU.subtract, [rtB[0], rtB[1]], roB)
            TT(t0, sv[:, :, 1, :], cosb, ALU.mult, [zfB, csB], [rtB[0]])
            TT(t1, sv[:, :, 0, :], sinb, ALU.mult, [zfB, csB], [rtB[1]])
            TT(dv[:, :, 1, :], t0, t1, ALU.add, [rtB[0], rtB[1]], roB)

        sch.phase = "qkv"
        zfs = [(zf, zfB), (hmf, hmfB)]
        ros = [(ro, [roB]), (ro2, ro2B)]
        pst = dict(n=0)

        def post(which, b, zb, zbB):
            if which == "v":
                DMA("sp", tp["v_out"][l, b * bs:(b + 1) * bs, :], zb[0:bs, :], [zbB], [], zbB)
                CP("dve", Vaug[0:bs, b, :, 0:128], zb[0:bs, :].rearrange("p (h d) -> p h d", h=HA), [zbB], [VaugB[b]])
                if tp["Vb_dst"] is not None:
                    DMA("sp", tp["Vb_dst"][b * bs:(b + 1) * bs, :].rearrange("p (h d) -> p h d", h=HA),
                        Vaug[0:bs, b, :, 0:128], [VaugB[b]], [tp["VbB"]], VaugB[b])
                return
            rb, rbB = ros[pst["n"] % 2]
            pst["n"] += 1
            rope_block(zb[0:bs, :], zbB, rb[0:bs, :], rbB, b)
            if which == "k":
                DMA("sp", tp["k_out"][l, b * bs:(b + 1) * bs, :], rb[0:bs, :], rbB, [], rbB[0])
            CP("act", xbf[0:bs, :], rb[0:bs, :], rbB, [xbfB])
            kb_ = tbank()
            pt = psb16(kb_)
            for h in range(HA):
                TR(pt[:, h * bs:(h + 1) * bs], xbf[0:bs, h * 128:(h + 1) * 128], [xbfB, constB], [psB[kb_]])
            dstT, dstB = (QT, QTB) if which == "q" else (KT, KTB)
            CP("act", dstT[:, :, b * bs:(b + 1) * bs], pt[:, 0:HA * bs].rearrange("p (h t) -> p h t", h=HA), [psB[kb_]], [dstB[b]])
            if which == "k" and b == nblk - 1 and tp["KT_dst"] is not None:
                DMA("sp", tp["KT_dst"].rearrange("h p t -> p h t"), KT[:, :, 0:ntok], KTB[0:nblk], [tp["KTB"]], KTB[0])

        pend = None
        nz = 0
        for which in ("q", "k", "v"):
            c0 = {"q": C_QA, "k": C_KA, "v": C_VA}[which]
            w0, w0B = wnext(("in", l, c0, 512))
            w1, w1B = wnext(("in", l, c0 + 512, 512))
            for b in range(nblk):
                zb, zbB = zfs[nz % 2]
                nz += 1
                for hf, (ws, wB) in enumerate(((w0, w0B), (w1, w1B))):
                    k, o = proj_tok(ws, wB, b, 512)
                    CP("act" if which != "v" else eveng(), zb[0:bs, hf * 512:(hf + 1) * 512], o, [psB[k]], [zbB])
                if pend is not None:
                    post(*pend)
                pend = (which, b, zb, zbB)
            wdone(2)
        post(*pend)

        sch.phase = "qkm"
        m_pend = [None]
        for g in range(4):
            ws, wB = wnext(("in", l, C_QKM + g * 512, 512))
            for cc4 in range(4):
                cc = g * 4 + cc4
                k = pbank()
                o = ps[:, k, 0:ntok]
                for kc in range(8):
                    MM(o, ws[:, kc, cc4 * 128:(cc4 + 1) * 128], xT[:, kc, 0:ntok], kc == 0, kc == 7, [wB] + xTB[0:nblk], [psB[k]])
                pi = cc % 2
                pcv = pcm[pi]
                CP("pool", pcv[:, 0:3], mhalo[:, l, cc, :], [mhaloB[l]], [pcmH[pi]])
                CP("act", pcv[:, 3:3 + ntok], o, [psB[k]], [pcmB[pi]])
                a = acc[pi][:, 0:ntok]
                ACT(a, o, AF.Identity, [psB[k], prmB], [accB[pi]], bias=mcb_s[:, l, cc:cc + 1], scale=mcw_s[:, l, cc, 3:4])
                CP("pool", mhalo[:, l, cc, :], pcv[:, ntok:ntok + 3], [pcmB[pi]], [mhaloB[l]])
                for j in range(3):
                    STT(a, pcv[:, j:j + ntok], mcw_s[:, l, cc, j:j + 1], a, ALU.mult, ALU.add, [pcmB[pi], pcmH[pi], prmB, accB[pi]], [accB[pi]])
                if m_pend[0] is not None:
                    ACT(qkT[:, m_pend[0][0], 0:ntok], m_pend[0][1], AF.Silu, [accB[m_pend[0][2]]], [qkTB[m_pend[0][0]]])
                m_pend[0] = (cc, a, pi)
            wdone(1)
        ACT(qkT[:, m_pend[0][0], 0:ntok], m_pend[0][1], AF.Silu, [accB[m_pend[0][2]]], [qkTB[m_pend[0][0]]])
        if last:
            for g in range(4):
                ws, wB = wnext(("in", l, C_QKM + g * 512, 512))
                k, o = proj_tok(ws, wB, nblk - 1, 512)
                CP("dve", stg[0:bs, :], o, [psB[k]], [stgB])
                DMA("sp", tp["mc_out"][l, :, g * 512:(g + 1) * 512], stg[bs - 3:bs, :], [stgB], [], stgB)
                wdone(1)
        sch.phase = "vm"
        w0, w0B = wnext(("in", l, C_VM, 512))
        w1, w1B = wnext(("in", l, C_VM + 512, 512))
        for b in range(nblk):
            for hf, (ws, wB) in enumerate(((w0, w0B), (w1, w1B))):
                k, o = proj_tok(ws, wB, b, 512)
                CP(eveng(), VMaug[0:bs, b, 2 * hf:2 * hf + 2, 0:256], o.rearrange("p (h d) -> p h d", h=2), [psB[k]], [VMB[b]])
        wdone(2)
        sch.phase = "gates"
        ws, wB = wnext(("in", l, C_GIF, 8))
        kI = pbank(); kF = pbank()
        for kc in range(8):
            MM(ps[0:4, kI, 0:ntok], ws[:, kc, 0:4], xT[:, kc, 0:ntok], kc == 0, kc == 7, [wB] + xTB[0:nblk], [psB[kI]])
        for kc in range(8):
            MM(ps[0:4, kF, 0:ntok], ws[:, kc, 4:8], xT[:, kc, 0:ntok], kc == 0, kc == 7, [wB] + xTB[0:nblk], [psB[kF]])
        wdone(1)
        g_ = {n: gw[n][:, 0:ntok] for n in gw}
        s_ = {n: gs[n][:, 0:nch] for n in gs}
        ACT(g_["ig"], ps[0:4, kI, 0:ntok], AF.Identity, [psB[kI], prmB], [gwB["ig"]], bias=bif_s[:, l, 0:1])
        ACT(g_["t1"], ps[0:4, kF, 0:ntok], AF.Exp, [psB[kF], prmB], [gwB["t1"]], bias=nbif_s[:, l, 1:2], scale=-1.0)
        TS(g_["t1"], g_["t1"], 1.0, None, ALU.add, None, [gwB["t1"]], [gwB["t1"]])
        ACT(g_["sp"], g_["t1"], AF.Ln, [gwB["t1"]], [gwB["sp"]])
        rm = resetm[:, 0:ntok] if L == 64 else resetm16[:, 0:ntok]
        sch.add("dve", "tensor_tensor_scan", (), dict(out=g_["A"], data0=rm, data1=g_["sp"], initial=0.0, op0=ALU.mult, op1=ALU.add),
                [gwB["sp"], constB], [gwB["A"]])
        TT(g_["r"], g_["ig"], g_["A"], ALU.add, [gwB["ig"], gwB["A"]], [gwB["r"]])
        RED(s_["cm"], g_["r"].rearrange("p (c l) -> p c l", l=L), ALU.max, [gwB["r"]], [gsB["cm"]])
        TS(s_["aL"], g_["A"].rearrange("p (c l) -> p c l", l=L)[:, :, L - 1], -1.0, None, ALU.mult, None, [gwB["A"]], [gsB["aL"]])
        sch.add("dve", "tensor_tensor_scan", (), dict(out=s_["mn"], data0=s_["cm"], data1=s_["aL"], initial=mstate[:, l:l + 1],
                                                       op0=ALU.max, op1=ALU.add), [gsB["cm"], gsB["aL"], mstateB[l]], [gsB["mn"]])
        TT(s_["mu"], s_["mn"], s_["aL"], ALU.subtract, [gsB["mn"], gsB["aL"]], [gsB["mu"]])
        CP("dve", gs["mpv"][:, 0:1], mstate[:, l:l + 1], [mstateB[l]], [gsB["mpv"]])
        if nch > 1:
            CP("dve", gs["mpv"][:, 1:nch], gs["mn"][:, 0:nch - 1], [gsB["mn"]], [gsB["mpv"]])
        CP("dve", mstate[:, l:l + 1], gs["mn"][:, nch - 1:nch], [gsB["mn"], gsB["mpv"]], [mstateB[l]])
        TT(s_["gam"], s_["mpv"], s_["mu"], ALU.subtract, [gsB["mpv"], gsB["mu"]], [gsB["gam"]])
        ACT(s_["gam"], s_["gam"], AF.Exp, [gsB["gam"]], [gsB["gam"]])
        mub = s_["mu"].rearrange("p (c o) -> p c o", o=1).to_broadcast([4, nch, L])
        TT(g_["sc"].rearrange("p (c l) -> p c l", l=L), g_["r"].rearrange("p (c l) -> p c l", l=L), mub, ALU.subtract, [gwB["r"], gsB["mu"]], [gwB["sc"]])
        ACT(g_["sc"], g_["sc"], AF.Exp, [gwB["sc"]], [gwB["sc"]])
        TS(g_["sc"], g_["sc"], 1.0 / 16.0, None, ALU.mult, None, [gwB["sc"]], [gwB["sc"]])
        TT(g_["cl"].rearrange("p (c l) -> p c l", l=L), g_["A"].rearrange("p (c l) -> p c l", l=L), mub, ALU.subtract, [gwB["A"], gsB["mu"]], [gwB["cl"]])
        ACT(g_["cl"], g_["cl"], AF.Exp, [gwB["cl"]], [gwB["cl"]])
        kG = pbank()
        pg = ps[0:bs, kG, 0:nblk * 8].rearrange("p (b e) -> p b e", e=8)
        for b in range(nblk):
            MM(pg[:, b, 0:4], g_["sc"][:, b * bs:(b + 1) * bs], identf[0:4, 0:4], True, True, [gwB["sc"], constB], [psB[kG]])
            MM(pg[:, b, 4:8], g_["cl"][:, b * bs:(b + 1) * bs], identf[0:4, 0:4], True, True, [gwB["cl"], constB], [psB[kG]])
        CP("dve", toksc[0:bs, 0:nblk, :], pg, [psB[kG]], [tokscB])
        TT(gD[:, 0:nch, :], s_["gam"].rearrange("p (c o) -> p c o", o=1).to_broadcast([4, nch, 4]),
           identf[0:4, 0:4].rearrange("p (o h) -> p o h", o=1).to_broadcast([4, nch, 4]), ALU.mult, [gsB["gam"], constB], [gDB])
        kG2 = pbank()
        MM(ps[:, kG2, 0:nch * 4], ones4, gD[:, 0:nch, :].rearrange("p c h -> p (c h)"), True, True, [gDB, constB], [psB[kG2]])
        CP("dve", gam[:, 0:nch, :], ps[:, kG2, 0:nch * 4].rearrange("p (c h) -> p c h", h=4), [psB[kG2]], [gamB])

        sch.phase = "attn"
        prior = tp["prior"]
        ngr = len(prior) + 1
        att = dict(si=0)
        sub_pend = [None]

        def subln_head(h):
            ovh = oaF[0:bs, 0:nblk, h * 128:(h + 1) * 128]
            sqv = hmf[0:bs, 512:512 + nblk * 128].rearrange("p (q d) -> p q d", d=128)
            TT(sqv, ovh, ovh, ALU.mult, oaFB[0:nblk], [hmfB])
            RED(sml[0:bs, 64 + h * 4:64 + h * 4 + nblk], sqv, ALU.add, [hmfB], [sml2B])

        for h in range(HA):
            started = set()
            items = []
            for g in range(ngr):
                own = g == ngr - 1
                for kb in range(nblk if own else T // 128):
                    items.append((g, own, kb))

            def emit_scores(it):
                g, own, kb = it
                if not own:
                    sl = (h * ngr + g) % 2
                    if kb == 0:
                        kd, vd, dB = prior[g]
                        DMA("sp", kch[sl][:, 0:T], kd[h], dB, [kchB[sl]], kchB[sl])
                        DMA("sp", vch[sl][:, :, 0:128], vd[:, h * 128:(h + 1) * 128].rearrange("(b p) d -> p b d", p=128), dB, [vchB[sl]], vchB[sl])
                    kT_ = kch[sl][:, kb * 128:(kb + 1) * 128]; kTB_ = kchB[sl]
                    v_ = vch[sl][:, kb, :]; vB_ = vchB[sl]
                    q0 = 0
                    nk = 128
                else:
                    kT_ = KT[:, h, kb * bs:(kb + 1) * bs]; kTB_ = KTB[kb]
                    v_ = Vaug[0:bs, kb, h, :]; vB_ = VaugB[kb]
                    q0 = kb * bs if tp["mask"] else 0
                    nk = bs
                nq = ntok - q0
                sb_ = att["si"] % 2
                att["si"] += 1
                b0, b1 = 2 * sb_, 2 * sb_ + 1
                MM(ps[0:nk, b0, 0:nq], kT_[0:64, :], QT[0:64, h, q0:ntok], True, True, [kTB_] + QTB[0:nblk], [psB[b0]])
                MM(ps[0:nk, b1, 0:nq], kT_[64:128, :], QT[64:128, h, q0:ntok], True, True, [kTB_] + QTB[0:nblk], [psB[b1]])
                ACT(PT[sb_][0:nk, :, 0:nq], ps[0:nk, b0:b1 + 1, 0:nq], AF.Exp, [psB[b0], psB[b1]], [PTB[sb_]], scale=0.125)
                if own and tp["mask"]:
                    MEMSET("pool", PT[sb_][64:128, :, 0:64], 0.0, [PTB[sb_]], [PTB[sb_]])
                return (g, own, kb, sb_, nk, q0, v_, vB_)

            def emit_av(st):
                g, own, kb, sb_, nk, q0, v_, vB_ = st
                nkb = nblk if own else T // 128
                qb0 = kb if (own and tp["mask"]) else 0
                for qb in range(qb0, nblk):
                    for mp_ in range(2):
                        a_ = qb * 2 + mp_
                        bank = 4 + a_ // 3
                        off = (a_ % 3) * 129
                        col = qb * bs - q0
                        first = (g == 0 and kb == 0) and bank not in started
                        started.add(bank)
                        lastk = own and (kb == (qb if tp["mask"] else nkb - 1))
                        MM(ps[0:bs, bank, off:off + 129], PT[sb_][0:nk, mp_, col:col + bs], v_, first, lastk, [PTB[sb_], vB_], [psB[bank]], skip=True)

            prev = None
            for it in items:
                st = emit_scores(it)
                if prev is not None:
                    emit_av(prev)
                prev = st
            emit_av(prev)
            CP("dve", zf[0:bs, 0:387], ps[0:bs, 4, 0:387], [psB[4]], [zfB])
            CP("dve", zf[0:bs, 387:774], ps[0:bs, 5, 0:387], [psB[5]], [zfB])
            CP("dve", hmf[0:bs, 0:258], ps[0:bs, 6, 0:258], [psB[6]], [hmfB])

            def accv(a_):
                if a_ < 6:
                    return zf[0:bs, a_ * 129:(a_ + 1) * 129], zfB
                return hmf[0:bs, (a_ - 6) * 129:(a_ - 5) * 129], hmfB

            for qb in range(nblk):
                o0, o0B = accv(qb * 2)
                o1, o1B = accv(qb * 2 + 1)
                r0 = sml[0:bs, 0:1]; r1 = sml[0:bs, 1:2]
                sch.add("dve", "reciprocal", (), dict(out=r0, in_=o0[:, 128:129]), [o0B], [smlB])
                sch.add("dve", "reciprocal", (), dict(out=r1, in_=o1[:, 128:129]), [o1B], [smlB])
                TT(r1, r1, neglam[0:bs, l:l + 1], ALU.mult, [smlB, prmB], [smlB])
                TS(o0[:, 0:128], o0[:, 0:128], r0, None, ALU.mult, None, [o0B, smlB], [o0B])
                STT(oaF[0:bs, qb, h * 128:(h + 1) * 128], o1[:, 0:128], r1, o0[:, 0:128], ALU.mult, ALU.add,
                    [o1B, o0B, smlB], [oaFB[qb]])
            if sub_pend[0] is not None:
                subln_head(sub_pend[0])
            sub_pend[0] = h
        subln_head(sub_pend[0])
        RSQRT(sml[0:bs, 64:96], 128.0 * LN_EPS, [sml2B], [sml2B])
        for qb in range(nblk):
            ov = oaF[0:bs, qb, :].rearrange("p (h d) -> p h d", h=HA)
            rs = sml[0:bs, 64:96].rearrange("p (h q) -> p h q", q=4)[:, :, qb:qb + 1].to_broadcast([bs, HA, 128])
            TT(ov, ov, rs, ALU.mult, [oaFB[qb], sml2B], [oaFB[qb]])
            TT(ov, ov, subg_s[0:bs, l, :].rearrange("p (o d) -> p o d", o=1).to_broadcast([bs, HA, 128]), ALU.mult, [oaFB[qb], prmB], [oabB[qb]])

        sch.phase = "mlstm"
        for b in range(nblk):
            kb_ = tbank()
            pt = psb16(kb_)
            for j in range(8):
                TR(pt[0:bs, j * 128:(j + 1) * 128], qkT[:, 8 + j, b * bs:(b + 1) * bs], [qkTB[8 + j], constB], [psB[kb_]])
            for h in range(HM):
                ACT(kw[0:bs, h * 256:(h + 1) * 256], pt[0:bs, h * 256:(h + 1) * 256], AF.Identity, [psB[kb_], tokscB], [kwB], scale=toksc[0:bs, b, h:h + 1])
            kS = pbank()
            for h in range(HM):
                for dk in range(2):
                    MM(ps[0:bs, kS, h * 128:h * 128 + bs], qkT[:, 8 + 2 * h + dk, b * bs:(b + 1) * bs], qkT[:, 2 * h + dk, b * bs:(b + 1) * bs],
                       dk == 0, dk == 1, [qkTB[8 + 2 * h + dk], qkTB[2 * h + dk]], [psB[kS]])
            for h in range(HM):
                STT(Sm[0:bs, h, 0:bs], ps[0:bs, kS, h * 128:h * 128 + bs], toksc[0:bs, b, h:h + 1], maskBD[0:bs, 0:bs], ALU.mult, ALU.mult,
                    [psB[kS], tokscB, constB], [SmB])
            for ci in range(cpb):
                c = b * cpb + ci
                p0, p1 = ci * L, ci * L + L
                t0_, t1_ = b * bs + ci * L, b * bs + ci * L + L
                Gc, GcB = GbL[c % 2]
                if c == 0:
                    for h in range(HM):
                        ACT(Gc[:, :, h, :], Cg[:, :, h, :], AF.Identity, [CgB[h], gamB], [GcB[h]], scale=gam[:, c, h:h + 1])
                for h in range(HM):
                    for dk in range(2):
                        kC = 4 + dk
                        dC = ps[:, kC, 0:257]
                        MM(dC, kw[p0:p1, h * 256 + dk * 128:h * 256 + dk * 128 + 128], VMaug[p0:p1, b, h, :], True, True, [kwB, VMB[b]], [psB[kC]])
                        STT(Cg[:, dk, h, :], Cg[:, dk, h, :], gam[:, c, h:h + 1], dC, ALU.mult, ALU.add, [CgB[h], gamB, psB[kC]], [CgB[h]])
                if c + 1 < nch:
                    Gn, GnB = GbL[(c + 1) % 2]
                    for h in range(HM):
                        ACT(Gn[:, :, h, :], Cg[:, :, h, :], AF.Identity, [CgB[h], gamB], [GnB[h]], scale=gam[:, c + 1, h:h + 1])
                for h in range(HM):
                    nd = ps[p0:p1, h, 0:257]
                    MM(nd, qkT[:, 2 * h, t0_:t1_], Gc[:, 0, h, :], True, False, [qkTB[2 * h], GcB[h]], [psB[h]])
                    MM(nd, qkT[:, 2 * h + 1, t0_:t1_], Gc[:, 1, h, :], False, False, [qkTB[2 * h + 1], GcB[h]], [psB[h]])
                    MM(nd, Sm[p0:p1, h, p0:p1], VMaug[p0:p1, b, h, :], False, True, [SmB, VMB[b]], [psB[h]])
                den = ps[p0:p1, 0:HM, 256]
                dd = sml[p0:p1, 16:16 + HM]
                TS(dd, den, -1.0, None, ALU.mult, None, psB[0:HM], [smlB])
                TT(dd, dd, den, ALU.max, [smlB] + psB[0:HM], [smlB])
                TT(dd, dd, toksc[p0:p1, b, 4:4 + HM], ALU.max, [smlB, tokscB], [smlB])
                sch.add("dve", "reciprocal", (), dict(out=dd, in_=dd), [smlB], [smlB])
                for h in range(HM):
                    ACT(hmf[p0:p1, h * 256:(h + 1) * 256], ps[p0:p1, h, 0:256], AF.Identity, [psB[h], smlB], [hmfB], scale=sml[p0:p1, 16 + h:17 + h])
            for h in range(HM):
                hv_ = hmf[0:bs, h * 256:(h + 1) * 256]
                ACT(zf[0:bs, h * 256:(h + 1) * 256], hv_, AF.Identity, [hmfB], [zfB, smlB], accum_out=sml[0:bs, 24 + h:25 + h])
                ACT(zf[0:bs, h * 256:(h + 1) * 256], hv_, AF.Square, [hmfB], [zfB, smlB], accum_out=sml[0:bs, 28 + h:29 + h])
            mean_ = sml[0:bs, 24:28]; ex2_ = sml[0:bs, 28:32]; nmr_ = sml[0:bs, 32:36]
            TS(mean_, mean_, 1.0 / 256.0, None, ALU.mult, None, [smlB], [smlB])
            TT(nmr_, mean_, mean_, ALU.mult, [smlB], [smlB])
            STT(ex2_, ex2_, 1.0 / 256.0, nmr_, ALU.mult, ALU.subtract, [smlB], [smlB])
            RSQRT(ex2_, LN_EPS, [smlB], [smlB])
            STT(nmr_, mean_, -1.0, ex2_, ALU.mult, ALU.mult, [smlB], [smlB])
            for h in range(HM):
                hv_ = hmf[0:bs, h * 256:(h + 1) * 256]
                ACT(hv_, hv_, AF.Identity, [hmfB, smlB], [hmfB], bias=sml[0:bs, 32 + h:33 + h], scale=sml[0:bs, 28 + h:29 + h])
            TT(hmb[0:bs, b, :], hmf[0:bs, :], lnp[0:bs, 0, :], ALU.mult, [hmfB, lnpB[0]], [hmbB[b]])

        sch.phase = "merge"
        for gi, c0 in enumerate((C_OM, C_OM + 512, C_GA, C_GA + 512, C_GB, C_GB + 512)):
            ws, wB = wnext(("in", l, c0, 512))
            hf = gi % 2
            for b in range(nblk):
                k, o = proj_tok(ws, wB, b, 512)
                si_ = (gi * nblk + b) % 2
                ACT(sg[si_][0:bs, :], o, AF.Sigmoid, [psB[k]], [sgB[si_]])
                if gi in (2, 3):
                    tgt, tB = oab, oabB
                else:
                    tgt, tB = hmb, hmbB
                TT(tgt[0:bs, b, hf * 512:(hf + 1) * 512], tgt[0:bs, b, hf * 512:(hf + 1) * 512], sg[si_][0:bs, :], ALU.mult, [sgB[si_], tB[b]], [tB[b]])
            wdone(1)
        for b in range(nblk):
            TT(ybf[0:bs, :], oab[0:bs, b, :], hmb[0:bs, b, :], ALU.add, [oabB[b], hmbB[b]], [ybfB])
            k = tbank()
            pt = psb16(k)
            for c in range(8):
                TR(pt[:, c * bs:(c + 1) * bs], ybf[0:bs, c * 128:(c + 1) * 128], [ybfB, constB], [psB[k]])
            CP(eveng(), xT[:, :, b * bs:(b + 1) * bs], pt[:, 0:8 * bs].rearrange("p (c t) -> p c t", c=8), [psB[k]], [xTB[b]])

        def layernorm_inplace(b, gi, bi):
            xv = xtok[0:bs, b, :]
            for hh in range(2):
                sch.add("dve", "bn_stats", (), dict(out=stt[0:bs, hh, :], in_=xtok[0:bs, b, hh * 512:(hh + 1) * 512]), [xtokB[b]], [lnB])
            sch.add("dve", "bn_aggr", (), dict(out=mv[0:bs, 0:2], in_=stt[0:bs, :, :]), [lnB], [lnB])
            CP("dve", mv[0:bs, 2:3], mv[0:bs, 1:2], [lnB], [lnB])
            RSQRT(mv[0:bs, 2:3], LN_EPS, [lnB], [lnB])
            STT(mv[0:bs, 3:4], mv[0:bs, 0:1], -1.0, mv[0:bs, 2:3], ALU.mult, ALU.mult, [lnB], [lnB])
            ACT(xv, xv, AF.Identity, [xtokB[b], lnB], [xtokB[b]], bias=mv[0:bs, 3:4], scale=mv[0:bs, 2:3])
            TT(xv, xv, lnp[0:bs, 0, :], ALU.mult, [xtokB[b], lnpB[0]], [xtokB[b]])
            TT(xv, xv, lnp[0:bs, 1, :], ALU.add, [xtokB[b], lnpB[1]], [xtokB[b]])

        sch.phase = "wout"
        load_lnp(0, ln1g); load_lnp(1, ln1b)
        w0, w0B = wnext(("out", l, 0, 512))
        w1, w1B = wnext(("out", l, 512, 512))
        for b in range(nblk):
            for hf, (ws, wB) in enumerate(((w0, w0B), (w1, w1B))):
                k, o = proj_tok(ws, wB, b, 512)
                xs_ = xtok[0:bs, b, hf * 512:(hf + 1) * 512]
                STT(xs_, xs_, ALPHA, o, ALU.mult, ALU.add, [xtokB[b], psB[k]], [xtokB[b]])
            layernorm_inplace(b, 1, 2)
        wdone(2)
        make_xT()

        if not tp["load_state"]:
            sample_prep(prep_per_step)
        sch.phase = "up"
        up_pend = [None]

        def up_final(cc, a, pi):
            if cc < 22:
                ACT(hT[:, cc, 0:ntok], a, AF.Gelu, [accB[pi]], [hTB[cc]])
            else:
                TT(hT[:, cc - 22, 0:ntok], hT[:, cc - 22, 0:ntok], a, ALU.mult, [hTB[cc - 22], accB[pi]], [hTB[cc - 22]])

        for g in range(11):
            ws, wB = wnext(("up", l, g * 512, 512))
            for cc4 in range(4):
                cc = g * 4 + cc4
                k = pbank()
                o = ps[:, k, 0:ntok]
                for kc in range(8):
                    MM(o, ws[:, kc, cc4 * 128:(cc4 + 1) * 128], xT[:, kc, 0:ntok], kc == 0, kc == 7, [wB] + xTB[0:nblk], [psB[k]])
                pi = cc % 2
                pcv = pcm[pi]
                CP("pool", pcv[:, 0:2], fhalo[:, l, cc, :], [fhaloB[l]], [pcmH[pi]])
                CP("act", pcv[:, 2:2 + ntok], o, [psB[k]], [pcmB[pi]])
                a = acc[pi][:, 0:ntok]
                ACT(a, o, AF.Identity, [psB[k], prmB], [accB[pi]], bias=fcb_s[:, l, cc:cc + 1], scale=fcw_s[:, l, cc, 2:3])
                CP("pool", fhalo[:, l, cc, :], pcv[:, ntok:ntok + 2], [pcmB[pi]], [fhaloB[l]])
                for j in range(2):
                    STT(a, pcv[:, j:j + ntok], fcw_s[:, l, cc, j:j + 1], a, ALU.mult, ALU.add, [pcmB[pi], pcmH[pi], prmB, accB[pi]], [accB[pi]])
                if up_pend[0] is not None:
                    up_final(*up_pend[0])
                up_pend[0] = (cc, a, pi)
            wdone(1)
        up_final(*up_pend[0])
        if last:
            for g in range(11):
                ws, wB = wnext(("up", l, g * 512, 512))
                k, o = proj_tok(ws, wB, nblk - 1, 512)
                CP("dve", stg[0:bs, :], o, [psB[k]], [stgB])
                DMA("sp", tp["fc_out"][l, :, g * 512:(g + 1) * 512], stg[bs - 2:bs, :], [stgB], [], stgB)
                wdone(1)
        sch.phase = "down"
        load_lnp(0, ln2g); load_lnp(1, ln2b)
        for nh in range(2):
            pcs = [wnext(("down", l, nh, pc)) for pc in range(3)]
            for b in range(nblk):
                k = pbank()
                o = ps[0:bs, k, 0:512]
                for kc in range(22):
                    ws, wB = pcs[kc // 8]
                    MM(o, hT[:, kc, b * bs:(b + 1) * bs], ws[:, kc % 8, :], kc == 0, kc == 21, [wB, hTB[kc]], [psB[k]])
                xs_ = xtok[0:bs, b, nh * 512:(nh + 1) * 512]
                STT(xs_, xs_, ALPHA, o, ALU.mult, ALU.add, [xtokB[b], psB[k]], [xtokB[b]])
                if nh == 1:
                    layernorm_inplace(b, 3, 4)
                    if l == NL - 1:
                        DMA("sp", tp["y_out"][b * bs:(b + 1) * bs, :], xtok[0:bs, b, :], [xtokB[b]], [], xtokB[b])
            wdone(3)
        if last:
            for h in range(HM):
                DMA("sp", tp["C_out"][l, h].rearrange("(c p) v -> p c v", p=128), Cg[:, :, h, 0:256], [CgB[h]], [], CgB[h])
                DMA("sp", tp["n_out"][l, h].rearrange("(c p o) -> p c o", p=128, o=1), Cg[:, :, h, 256:257], [CgB[h]], [], CgB[h], slow=True)
            DMA("sp", tp["m_out"][l].rearrange("(p o) -> p o", o=1), mstate[:, l:l + 1], [mstateB[l]], [], mstateB[l], slow=True)

    prep_q = [(l, blk) for l in range(NL) for blk in range(P // 128)]

    def sample_prep(n):
        ph = sch.phase
        sch.phase = "prep"
        for _ in range(n):
            if not prep_q:
                break
            l, blk = prep_q.pop(0)
            g = (blk * 128) // T
            DMA("pool", vtmp, ck[l, blk * 128:(blk + 1) * 128, :], [], [vtmpB], vtmpB)
            k = tbank()
            pt = psb16(k)
            for h in range(HA):
                TR(pt[:, h * 128:(h + 1) * 128], vtmp[:, h * 128:(h + 1) * 128], [vtmpB, constB], [psB[k]])
            CP(eveng(), ktmp, pt.rearrange("p (h t) -> p h t", h=HA), [psB[k]], [ktmpB])
            DMA("sp", KTs[l][:, :, blk * 128:(blk + 1) * 128].rearrange("h p t -> p h t"), ktmp, [ktmpB], [KTsB[l][g]], ktmpB)
            DMA("pool", ybf, cv[l, blk * 128:(blk + 1) * 128, :], [], [ybfB], ybfB)
            DMA("sp", Vbs[l, blk * 128:(blk + 1) * 128, :], ybf, [ybfB], [VbsB[l][g]], ybfB)
        sch.phase = ph

    steps = []
    for i in range(NT):
        for l in range(NL):
            steps.append((l, i, False))
    for l in range(NL):
        steps.append((l, 0, True))
    for (l, i, samp) in steps:
        wq.extend(wspec_step(l, samp or i == NT - 1))

    prep_done = False
    prep_per_step = -(-len(prep_q) // max(1, NT * NL))
    for (l, i, samp) in steps:
        if samp and not prep_done:
            sample_prep(len(prep_q))
            prep_done = True
        if not samp:
            tp = dict(ntok=T, bs=128, nblk=NB, L=64, last=(i == NT - 1), first=(i == 0), load_state=False,
                      x_src=xp[i * T:(i + 1) * T, :], pos0=i * T, tok0=i * T, mask=True,
                      prior=[(KTp[l][:, :, g * T:(g + 1) * T], Vbp[l, g * T:(g + 1) * T, :], [KTpB[l][g], VbpB[l][g]]) for g in range(i)],
                      KT_dst=KTp[l][:, :, i * T:(i + 1) * T], KTB=KTpB[l][i], Vb_dst=Vbp[l, i * T:(i + 1) * T, :], VbB=VbpB[l][i],
                      k_out=kp[:, i * T:(i + 1) * T, :], v_out=vp[:, i * T:(i + 1) * T, :], y_out=yp[i * T:(i + 1) * T, :],
                      mc_out=mcp, fc_out=fcp, C_out=Cp, n_out=np_, m_out=mp)
        else:
            tp = dict(ntok=NS, bs=NS, nblk=1, L=NS, last=True, first=False, load_state=True,
                      x_src=xs, pos0=S, tok0=0, mask=False,
                      prior=[(KTs[l][:, :, g * T:(g + 1) * T], Vbs[l, g * T:(g + 1) * T, :], [KTsB[l][g], VbsB[l][g]]) for g in range(P // T)],
                      KT_dst=None, KTB=None, Vb_dst=None, VbB=None,
                      k_out=ks, v_out=vs, y_out=ys, mc_out=mcs, fc_out=fcs, C_out=Cs, n_out=ns_, m_out=ms,
                      sC=sC, sn=sn, sm=sm, smc=smc, sfc=sfc)
        step(l, tp)
    assert wstate["used"] == len(wq) and wstate["released"] == len(wq), (wstate, len(wq))
    print("SBUF/PSUM allocation done")
    info = sch.emit()
    return nc, info


def rope_tables(S, P):
    half = 32
    inv = (np.float32(10000.0) ** (-np.arange(half, dtype=np.float32) * np.float32(2.0) / np.float32(64))).astype(np.float32)
    pos = np.concatenate([np.arange(S), P + np.arange(NS)]).astype(np.float32)
    ang = (pos[:, None] * inv[None, :]).astype(np.float32)
    return np.cos(ang).astype(np.float32), np.sin(ang).astype(np.float32)


_CACHE = {}


def run(inputs, S, P, T, n_prompt, n_sample, n_cores):
    key = (S, P, T)
    if key not in _CACHE:
        _CACHE[key] = build(S=S, P=P, T=T)
    nc, info = _CACHE[key]
    f = lambda a: np.ascontiguousarray(np.asarray(a, dtype=np.float32))
    cosT, sinT = rope_tables(S, P)
    NL = 2
    in_maps = []
    for c in range(n_cores):
        b = c % n_prompt
        s = c % n_sample
        m = {
            "xp": f(inputs["x_prompt"][b]), "xs": f(inputs["x_sample"][s]),
            "ck": f(inputs["cache_k"][:, s]).reshape(NL, P, D), "cv": f(inputs["cache_v"][:, s]).reshape(NL, P, D),
            "smc": f(inputs["state_mlstm_conv"][:, s]), "sC": f(inputs["state_mlstm_C"][:, s]),
            "sn": f(inputs["state_mlstm_n"][:, s]), "sm": f(inputs["state_mlstm_m"][:, s]),
            "sfc": f(inputs["state_ffn_conv"][:, s]),
            "w_in": f(inputs["w_in"]), "b_if": f(inputs["b_if"]), "mcw": f(inputs["mlstm_conv_w"]), "mcb": f(inputs["mlstm_conv_b"]),
            "dlam": f(inputs["diff_lambda"]), "subg": f(inputs["diff_subln_g"]), "mhg": f(inputs["mlstm_norm_g"]),
            "w_out": f(inputs["w_out"]), "ln1g": f(inputs["ln1_g"]), "ln1b": f(inputs["ln1_b"]),
            "w_up": f(inputs["w_up"]), "fcw": f(inputs["ffn_conv_w"]), "fcb": f(inputs["ffn_conv_b"]),
            "w_down": f(inputs["w_down"]), "ln2g": f(inputs["ln2_g"]), "ln2b": f(inputs["ln2_b"]),
            "cosT": cosT, "sinT": sinT,
        }
        in_maps.append(m)
    res = run_bass_kernel_spmd(nc, in_maps, core_ids=list(range(n_cores)))
    R = res.results
    pc = list(range(n_prompt))
    sc = list(range(n_sample))
    st = lambda name, cores, ax=0: np.stack([np.asarray(R[c][name], dtype=np.float32) for c in cores], axis=ax)
    y_prompt = st("yp", pc)
    y_sample = st("ys", sc)
    k_prompt = st("kp", pc, 1).reshape(NL, n_prompt, S, HA, 128)
    v_prompt = st("vp", pc, 1).reshape(NL, n_prompt, S, HA, 128)
    outs = (y_prompt, y_sample, k_prompt, v_prompt,
            st("mcp", pc, 1), st("Cp", pc, 1), st("np", pc, 1), st("mp", pc, 1), st("fcp", pc, 1),
            st("ks", sc, 1).reshape(NL, n_sample, NS, HA, 128), st("vs", sc, 1).reshape(NL, n_sample, NS, HA, 128),
            st("mcs", sc, 1), st("Cs", sc, 1), st("ns", sc, 1), st("ms", sc, 1), st("fcs", sc, 1))
    return outs


def kernel(**inputs):
    return run(inputs, S=8192, P=4096, T=512, n_prompt=4, n_sample=8, n_cores=8)
```

```python
import math
import numpy as np
import concourse.bass as bass
import concourse.mybir as mybir
from concourse.bass_utils import run_bass_kernel_spmd

F32 = mybir.dt.float32
BF16 = mybir.dt.bfloat16
AF = mybir.ActivationFunctionType
ALU = mybir.AluOpType
AX = mybir.AxisListType

D = 1024
HA = 8
HM = 4
DFF = 2816
DIN = 9224
NS = 16
ALPHA = (2 * 2) ** 0.25
LN_EPS = 1e-5
C_QA, C_KA, C_VA, C_QKM, C_VM, C_OM, C_GIF, C_GA, C_GB = 0, 1024, 2048, 3072, 5120, 6144, 7168, 7176, 8200


RAW_ONLY = True


class Buf:
    __slots__ = ("name", "lastw", "readers")

    def __init__(self, name=""):
        self.name = name
        self.lastw = None
        self.readers = []


class Sched:
    def __init__(self, nc, same_engine_sync=True):
        self.nc = nc
        self.engs = {"pe": nc.tensor, "act": nc.scalar, "dve": nc.vector, "pool": nc.gpsimd, "sp": nc.sync}
        self.ins = []
        self.dma_cnt = {}
        self.same = same_engine_sync
        self.raw_only = RAW_ONLY
        self.phase = ""
        self.names = None

    def add(self, eng, meth, args, kwargs, reads=(), writes=(), dma=None):
        idx = len(self.ins)
        deps = set()
        raw = set()
        for r in reads:
            if r.lastw is not None:
                deps.add(r.lastw)
                raw.add(r.lastw)
        for w in writes:
            if w.lastw is not None:
                deps.add(w.lastw)
            deps.update(w.readers)
        for r in reads:
            r.readers.append(idx)
        for w in writes:
            w.lastw = idx
            w.readers = []
        dval = None
        if dma is not None:
            self.dma_cnt[dma] = self.dma_cnt.get(dma, 0) + 16
            dval = self.dma_cnt[dma]
        keep = set()
        for d in deps:
            de = self.ins[d]
            if de[5] is None and de[0] == eng:
                if eng == "pe" or not self.same:
                    continue
                if self.raw_only and d not in raw:
                    continue
            keep.add(d)
        self.ins.append([eng, meth, args, kwargs, keep, dma, dval, False, 0, self.phase])
        return idx

    def emit(self):
        nc = self.nc
        for rec in self.ins:
            for d in rec[4]:
                de = self.ins[d]
                if de[5] is None:
                    de[7] = True
        cnt = {e: 0 for e in self.engs}
        for rec in self.ins:
            if rec[7]:
                cnt[rec[0]] += 1
                rec[8] = cnt[rec[0]]
        esem = {e: nc.alloc_semaphore(name="es_" + e) for e in self.engs}
        dsem = {}
        for k in self.dma_cnt:
            dsem[k] = nc.alloc_semaphore(name="ds_%d" % len(dsem))
        waited = {e: {} for e in self.engs}
        nwait = 0
        for rec in self.ins:
            eng, meth, args, kwargs, deps, dma, dval, sig, sigval, phase = rec
            E = self.engs[eng]
            need = {}
            for d in deps:
                de = self.ins[d]
                if de[5] is None:
                    s, v = esem[de[0]], de[8]
                else:
                    s, v = dsem[de[5]], de[6]
                if need.get(s, 0) < v:
                    need[s] = v
            for s, v in need.items():
                if waited[eng].get(s, 0) >= v:
                    continue
                E.wait_ge(s, v)
                waited[eng][s] = v
                nwait += 1
            ins = getattr(E, meth)(*args, **kwargs)
            if self.names is not None:
                self.names[ins.ins.name] = phase
            if dma is not None:
                ins.then_inc(dsem[dma], 16)
            elif sig:
                ins.then_inc(esem[eng], 1)
        for k, v in self.dma_cnt.items():
            nc.sync.wait_ge(dsem[k], v)
        return dict(n=len(self.ins), nwait=nwait, nsem=len(dsem) + 5)


def build(S=8192, P=4096, T=512, NL=2, dbg=None, same=True):
    nc = bass.Bass("TRN2", target_bir_lowering=False)
    sch = Sched(nc, same_engine_sync=same)
    NT = S // T
    NTAB = S + NS

    def din(name, shape, dt=F32):
        return nc.dram_tensor(name, list(shape), dt, kind="ExternalInput").ap()

    def dout(name, shape, dt=F32):
        return nc.dram_tensor(name, list(shape), dt, kind="ExternalOutput").ap()

    def dscr(name, shape, dt):
        return nc.dram_tensor(name, list(shape), dt, kind="Internal").ap()

    def sb(name, shape, dt=F32):
        return nc.alloc_sbuf_tensor(name, list(shape), dt).ap()

    xp = din("xp", [S, D]); xs = din("xs", [NS, D])
    ck = din("ck", [NL, P, D]); cv = din("cv", [NL, P, D])
    smc = din("smc", [NL, 3, 2048]); sC = din("sC", [NL, HM, 256, 256]); sn = din("sn", [NL, HM, 256])
    sm = din("sm", [NL, HM]); sfc = din("sfc", [NL, 2, 2 * DFF])
    w_in = din("w_in", [NL, D, DIN]); b_if = din("b_if", [NL, 8])
    mcw = din("mcw", [NL, 4, 2048]); mcb = din("mcb", [NL, 2048])
    dlam = din("dlam", [NL, 4, 64]); subg = din("subg", [NL, 128]); mhg = din("mhg", [NL, D])
    w_out = din("w_out", [NL, D, D]); ln1g = din("ln1g", [NL, D]); ln1b = din("ln1b", [NL, D])
    w_up = din("w_up", [NL, D, 2 * DFF]); fcw = din("fcw", [NL, 3, 2 * DFF]); fcb = din("fcb", [NL, 2 * DFF])
    w_down = din("w_down", [NL, DFF, D]); ln2g = din("ln2g", [NL, D]); ln2b = din("ln2b", [NL, D])
    cosT = din("cosT", [NTAB, 32]); sinT = din("sinT", [NTAB, 32])

    yp = dout("yp", [S, D]); ys = dout("ys", [NS, D])
    kp = dout("kp", [NL, S, D]); vp = dout("vp", [NL, S, D])
    mcp = dout("mcp", [NL, 3, 2048]); Cp = dout("Cp", [NL, HM, 256, 256]); np_ = dout("np", [NL, HM, 256])
    mp = dout("mp", [NL, HM]); fcp = dout("fcp", [NL, 2, 2 * DFF])
    ks = dout("ks", [NL, NS, D]); vs = dout("vs", [NL, NS, D])
    mcs = dout("mcs", [NL, 3, 2048]); Cs = dout("Cs", [NL, HM, 256, 256]); ns_ = dout("ns", [NL, HM, 256])
    ms = dout("ms", [NL, HM]); fcs = dout("fcs", [NL, 2, 2 * DFF])

    KTp = dscr("KTp", [NL, HA, 128, S], BF16); Vbp = dscr("Vbp", [NL, S, D], BF16)
    KTs = dscr("KTs", [NL, HA, 128, P], BF16); Vbs = dscr("Vbs", [NL, P, D], BF16)
    KTpB = [[Buf() for _ in range(NT)] for _ in range(NL)]
    VbpB = [[Buf() for _ in range(NT)] for _ in range(NL)]
    KTsB = [[Buf() for _ in range(max(1, P // T))] for _ in range(NL)]
    VbsB = [[Buf() for _ in range(max(1, P // T))] for _ in range(NL)]

    NB = T // 128
    xtok = sb("xtok", [128, NB, D]); xtokB = [Buf() for _ in range(NB)]
    xbf = sb("xbf", [128, D], BF16); xbfB = Buf()
    xT = sb("xT", [128, 8, T], BF16); xTB = [Buf() for _ in range(NB)]
    NSLOT = 4
    ring = [sb("ring%d" % i, [128, 8, 512], BF16) for i in range(NSLOT)]
    ringB = [Buf() for _ in range(NSLOT)]
    zf = sb("zf", [128, D]); zfB = Buf()
    ro = sb("ro", [128, D]); roB = Buf()
    cs_t = sb("cs_t", [128, NB, 2, 32]); csB = Buf()
    QT = sb("QT", [128, HA, T], BF16); QTB = [Buf() for _ in range(NB)]
    KT = sb("KT", [128, HA, T], BF16); KTB = [Buf() for _ in range(NB)]
    Vaug = sb("Vaug", [128, NB, HA, 129], BF16); VaugB = [Buf() for _ in range(NB)]
    VMaug = sb("VMaug", [128, NB, HM, 257], BF16); VMB = [Buf() for _ in range(NB)]
    oab = sb("oab", [128, NB, D], BF16); oabB = [Buf() for _ in range(NB)]
    hmb = sb("hmb", [128, NB, D], BF16); hmbB = [Buf() for _ in range(NB)]
    oaF = oab; oaFB = oabB
    hT = sb("hT", [128, 22, T], BF16); hTB = [Buf() for _ in range(22)]
    qkT = hT; qkTB = hTB
    pcm = [sb("pcm%d" % i, [128, 3 + T]) for i in range(2)]; pcmB = [Buf(), Buf()]; pcmH = [Buf(), Buf()]
    acc = [sb("acc%d" % i, [128, T]) for i in range(2)]; accB = [Buf(), Buf()]
    rt = acc; rtB = accB
    stg = pcm[0][:, 3:515]; stgB = pcmB[0]
    kch = [sb("kch%d" % i, [128, T], BF16) for i in range(2)]; kchB = [Buf(), Buf()]
    vch = [sb("vch%d" % i, [128, NB, 129], BF16) for i in range(2)]; vchB = [Buf(), Buf()]
    PT = [sb("PT%d" % i, [128, 2, T], BF16) for i in range(2)]; PTB = [Buf(), Buf()]
    Caug = sb("Caug", [128, 2, HM, 257]); CaugB = [Buf() for _ in range(HM)]
    GbRaw = sb("GbRaw", [128, 2 * HM * 257], BF16); GbB = [Buf() for _ in range(HM)]
    Gb = GbRaw.rearrange("p (a h d) -> p a h d", a=2, h=HM)
    Gb2 = sb("Gb2", [128, 2, HM, 257], BF16); Gb2B = [Buf() for _ in range(HM)]
    GbL = [(Gb, GbB), (Gb2, Gb2B)]
    ro2 = GbRaw[:, 0:2 * D].bitcast(F32); ro2B = GbB
    kw = sb("kw", [128, D], BF16); kwB = Buf()
    Sm = sb("Sm", [128, HM, 128], BF16); SmB = Buf()
    hmf = sb("hmf", [128, D]); hmfB = Buf()
    sml = sb("sml", [128, 96]); smlB = Buf(); sml2B = Buf()
    mhalo = sb("mhalo", [128, NL, 16, 3]); mhaloB = [Buf() for _ in range(NL)]
    fhalo = sb("fhalo", [128, NL, 44, 2]); fhaloB = [Buf() for _ in range(NL)]
    mstate = sb("mstate", [4, NL]); mstateB = [Buf() for _ in range(NL)]
    CaugL = [Caug, sb("Caug1", [128, 2, HM, 257])]
    CaugLB = [CaugB, [Buf() for _ in range(HM)]]
    _g3 = [PT[0].rearrange("p a t -> p (a t)").bitcast(F32)[0:4, :], PT[1].rearrange("p a t -> p (a t)").bitcast(F32)[0:4, :],
           kw.bitcast(F32)[0:4, :]]
    _g3B = [PTB[0], PTB[1], kwB]
    gw = {"t1": _g3[0], "sp": _g3[0], "A": _g3[1], "cl": _g3[1], "ig": _g3[2], "r": _g3[2], "sc": _g3[2]}
    gwB = {"t1": _g3B[0], "sp": _g3B[0], "A": _g3B[1], "cl": _g3B[1], "ig": _g3B[2], "r": _g3B[2], "sc": _g3B[2]}
    gs = {n: sb("gs_" + n, [4, 16]) for n in ("cm", "aL", "mn", "mu", "mpv", "gam")}
    gsB = {n: Buf() for n in gs}
    gD = sb("gD", [4, 16, 4]); gDB = Buf()
    toksc = sb("toksc", [128, NB, 8]); tokscB = Buf()
    gam = sb("gam", [128, 16, 4]); gamB = Buf()
    identb = sb("identb", [128, 128], BF16); identf = sb("identf", [128, 128]); maskBD = sb("maskBD", [128, 128])
    ones4 = sb("ones4", [4, 128]); resetm = sb("resetm", [4, T]); resetm16 = sb("resetm16", [4, NS])
    constB = Buf()
    mcw_s = sb("mcw_s", [128, NL, 16, 4]); mcb_s = sb("mcb_s", [128, NL, 16])
    fcw_s = sb("fcw_s", [128, NL, 44, 3]); fcb_s = sb("fcb_s", [128, NL, 44])
    bif_s = sb("bif_s", [4, NL, 2]); nbif_s = sb("nbif_s", [4, NL, 2])
    neglam = sb("neglam", [128, NL]); lamw = sb("lamw", [128, 4, 64]); lamw2 = sb("lamw2", [128, 4])
    subg_s = sb("subg_s", [128, NL, 128])
    prmB = Buf()
    lnp = sb("lnp", [128, 2, D]); lnpB = [Buf(), Buf()]
    sg = [sb("sg%d" % i, [128, 512], BF16) for i in range(2)]; sgB = [Buf(), Buf()]
    ybf = xbf; ybfB = xbfB
    stt = sb("stt", [128, 2, 6]); mv = sb("mv", [128, 4]); lnB = Buf()
    vtmp = xbf; vtmpB = xbfB
    ktmp = kw.rearrange("p (h t) -> p h t", h=HA); ktmpB = kwB
    ps = nc.alloc_psum_tensor("ps", [128, 8, 512], F32).ap()
    psB = [Buf() for _ in range(8)]

    def psb16(k):
        return ps[:, k, :].bitcast(BF16)

    def MM(out, lhsT, rhs, start, stop, r, w, skip=False):
        kw_ = dict(lhsT=lhsT, rhs=rhs, start=start, stop=stop)
        if skip:
            kw_["skip_group_check"] = True
        sch.add("pe", "matmul", (out,), kw_, r, w)

    def TR(out, in_, r, w):
        sch.add("pe", "transpose", (), dict(out=out, in_=in_, identity=identb[0:in_.shape[0], 0:in_.shape[0]]), r, w)

    def ACT(out, in_, func, r, w, bias=None, scale=None, accum_out=None):
        kw_ = dict(out=out, in_=in_, func=func)
        if bias is not None:
            kw_["bias"] = bias
        if scale is not None:
            kw_["scale"] = scale
        if accum_out is not None:
            kw_["accum_out"] = accum_out
        sch.add("act", "activation", (), kw_, r, w)

    def CP(eng, out, in_, r, w):
        if eng == "act":
            ACT(out, in_, AF.Identity, r, w)
        else:
            sch.add(eng, "tensor_copy", (), dict(out=out, in_=in_), r, w)

    def TT(out, in0, in1, op, r, w, eng="dve"):
        sch.add(eng, "tensor_tensor", (), dict(out=out, in0=in0, in1=in1, op=op), r, w)

    def TS(out, in0, s1, s2, op0, op1, r, w, eng="dve"):
        kw_ = dict(out=out, in0=in0, scalar1=s1, scalar2=s2, op0=op0)
        if op1 is not None:
            kw_["op1"] = op1
        sch.add(eng, "tensor_scalar", (), kw_, r, w)

    def STT(out, in0, scalar, in1, op0, op1, r, w, eng="dve"):
        sch.add(eng, "scalar_tensor_tensor", (), dict(out=out, in0=in0, scalar=scalar, in1=in1, op0=op0, op1=op1), r, w)

    def RSQRT(ap, addc, r, w):
        TS(ap, ap, addc, None, ALU.add, None, r, w)
        ACT(ap, ap, AF.Sqrt, r, w)
        sch.add("dve", "reciprocal", (), dict(out=ap, in_=ap), r, w)

    def RED(out, in_, op, r, w):
        sch.add("dve", "tensor_reduce", (), dict(out=out, in_=in_, axis=AX.X, op=op), r, w)

    def MEMSET(eng, ap, val, r, w):
        sch.add(eng, "memset", (ap, val), {}, r, w)

    def DMA(q, out, in_, r, w, key, slow=False):
        kw_ = dict(out=out, in_=in_)
        if slow:
            kw_["allow_slow_non_contiguous"] = True
        sch.add(q, "dma_start", (), kw_, r, w, dma=key)

    MEMSET("pool", identf, 1.0, [], [constB])
    sch.add("pool", "affine_select", (), dict(out=identf, in_=identf, compare_op=ALU.is_equal, fill=0.0, base=0,
                                              pattern=[[-1, 128]], channel_multiplier=1), [constB], [constB])
    CP("dve", identb, identf, [constB], [constB])
    MEMSET("pool", maskBD, 1.0, [constB], [constB])
    sch.add("pool", "affine_select", (), dict(out=maskBD, in_=maskBD, compare_op=ALU.is_ge, fill=0.0, base=0,
                                              pattern=[[1, 128]], channel_multiplier=-1), [constB], [constB])
    MEMSET("pool", maskBD[0:64, 64:128], 0.0, [constB], [constB])
    MEMSET("pool", ones4, 1.0, [constB], [constB])
    MEMSET("pool", resetm, 1.0, [constB], [constB])
    MEMSET("pool", resetm.rearrange("p (c l) -> p c l", l=64)[:, :, 0:1], 0.0, [constB], [constB])
    MEMSET("pool", resetm16, 1.0, [constB], [constB])
    MEMSET("pool", resetm16[:, 0:1], 0.0, [constB], [constB])
    MEMSET("pool", Vaug[:, :, :, 128:129], 1.0, [], VaugB)
    MEMSET("pool", VMaug[:, :, :, 256:257], 1.0, [], VMB)
    for i in range(2):
        MEMSET("pool", vch[i][:, :, 128:129], 1.0, [], [vchB[i]])
    pk = Buf()
    for l in range(NL):
        for j in range(4):
            DMA("sp", mcw_s[:, l, :, j], mcw[l, j].rearrange("(c p) -> p c", p=128), [], [prmB], pk, slow=True)
        DMA("sp", mcb_s[:, l, :], mcb[l].rearrange("(c p) -> p c", p=128), [], [prmB], pk, slow=True)
        for j in range(3):
            DMA("sp", fcw_s[:, l, :, j], fcw[l, j].rearrange("(c p) -> p c", p=128), [], [prmB], pk, slow=True)
        DMA("sp", fcb_s[:, l, :], fcb[l].rearrange("(c p) -> p c", p=128), [], [prmB], pk, slow=True)
        DMA("sp", bif_s[:, l, :], b_if[l].rearrange("(j p) -> p j", p=4), [], [prmB], pk, slow=True)
        DMA("sp", subg_s[:, l, :], subg[l].partition_broadcast(128), [], [prmB], pk)
        DMA("sp", lamw, dlam[l].partition_broadcast(128), [prmB], [prmB], pk)
        lam_init = 0.8 - 0.6 * math.exp(-0.3 * l)
        lv = lamw.rearrange("p (a b) d -> p a b d", b=2)
        TT(lamw[:, 0:2, :].rearrange("p a d -> p a d"), lv[:, :, 0, :], lv[:, :, 1, :], ALU.mult, [prmB], [prmB])
        RED(lamw2[:, 0:2], lamw[:, 0:2, :], ALU.add, [prmB], [prmB])
        ACT(lamw2[:, 2:4], lamw2[:, 0:2], AF.Exp, [prmB], [prmB])
        TT(neglam[:, l:l + 1], lamw2[:, 3:4], lamw2[:, 2:3], ALU.subtract, [prmB], [prmB])
        TS(neglam[:, l:l + 1], neglam[:, l:l + 1], -lam_init, None, ALU.add, None, [prmB], [prmB])
        TS(subg_s[:, l, :], subg_s[:, l, :], (1.0 - lam_init) * math.sqrt(128.0), None, ALU.mult, None, [prmB], [prmB])
    TS(nbif_s, bif_s, -1.0, None, ALU.mult, None, [prmB], [prmB])

    wq = []
    wstate = dict(loaded=0, used=0, released=0)

    def wspec_step(l, last):
        sp_ = []
        for c0 in (C_QA, C_QA + 512, C_KA, C_KA + 512, C_VA, C_VA + 512):
            sp_.append(("in", l, c0, 512))
        for c0 in range(C_QKM, C_QKM + 2048, 512):
            sp_.append(("in", l, c0, 512))
        if last:
            for c0 in range(C_QKM, C_QKM + 2048, 512):
                sp_.append(("in", l, c0, 512))
        sp_.append(("in", l, C_VM, 512)); sp_.append(("in", l, C_VM + 512, 512))
        sp_.append(("in", l, C_GIF, 8))
        for c0 in (C_OM, C_OM + 512, C_GA, C_GA + 512, C_GB, C_GB + 512):
            sp_.append(("in", l, c0, 512))
        sp_.append(("out", l, 0, 512)); sp_.append(("out", l, 512, 512))
        for g in range(11):
            sp_.append(("up", l, g * 512, 512))
        if last:
            for g in range(11):
                sp_.append(("up", l, g * 512, 512))
        for nh in range(2):
            for pc in range(3):
                sp_.append(("down", l, nh, pc))
        return sp_

    def wload(i):
        kind, l, a, b = wq[i]
        slot = i % NSLOT
        if kind == "in":
            src = w_in[l].rearrange("(kc p) n -> p kc n", p=128)[:, :, a:a + b]
            dst = ring[slot][:, :, 0:b]
        elif kind == "out":
            src = w_out[l].rearrange("(kc p) n -> p kc n", p=128)[:, :, a:a + b]
            dst = ring[slot][:, :, 0:b]
        elif kind == "up":
            src = w_up[l].rearrange("(kc p) n -> p kc n", p=128)[:, :, a:a + b]
            dst = ring[slot][:, :, 0:b]
        else:
            k0 = b * 8
            k1 = min(22, k0 + 8)
            src = w_down[l].rearrange("(kc p) n -> p kc n", p=128)[:, k0:k1, a * 512:(a + 1) * 512]
            dst = ring[slot][:, 0:k1 - k0, :]
        DMA("pool", dst, src, [], [ringB[slot]], ringB[slot])

    def wfill():
        while wstate["loaded"] < min(len(wq), wstate["released"] + NSLOT):
            wload(wstate["loaded"])
            wstate["loaded"] += 1

    def wnext(expect):
        i = wstate["used"]
        assert wq[i] == expect, (wq[i], expect)
        wfill()
        assert wstate["loaded"] > i
        wstate["used"] += 1
        return ring[i % NSLOT], ringB[i % NSLOT]

    def wdone(n=1):
        wstate["released"] += n
        assert wstate["released"] <= wstate["used"]
        wfill()

    rot = dict(pp=0, tp=0)

    def pbank():
        k = rot["pp"] % 4
        rot["pp"] += 1
        return k

    def tbank():
        k = 6 + rot["tp"] % 2
        rot["tp"] += 1
        return k

    evr = dict(i=0)

    def eveng():
        evr["i"] += 1
        return "act" if evr["i"] % 2 else "dve"

    def step(l, tp):
        ntok, bs, nblk, L = tp["ntok"], tp["bs"], tp["nblk"], tp["L"]
        cpb = bs // L
        nch = ntok // L
        last = tp["last"]
        Cg = CaugL[l]; CgB = CaugLB[l]

        def load_lnp(slot, src):
            DMA("sp", lnp[:, slot, :], src[l].partition_broadcast(128), [], [lnpB[slot]], lnpB[slot])
        load_lnp(0, mhg)
        if l == 0:
            for b in range(nblk):
                DMA("sp", xtok[0:bs, b, :], tp["x_src"][b * bs:(b + 1) * bs, :], [], [xtokB[b]], xtokB[b])
        DMA("sp", cs_t[0:bs, 0:nblk, 0, :], cosT[tp["pos0"]:tp["pos0"] + ntok, :].rearrange("(b p) d -> p b d", p=bs), [], [csB], csB)
        DMA("sp", cs_t[0:bs, 0:nblk, 1, :], sinT[tp["pos0"]:tp["pos0"] + ntok, :].rearrange("(b p) d -> p b d", p=bs), [], [csB], csB)
        if tp["load_state"]:
            for h in range(HM):
                DMA("sp", Cg[:, :, h, 0:256], tp["sC"][l, h].rearrange("(c p) v -> p c v", p=128), [], [CgB[h]], CgB[h])
                DMA("sp", Cg[:, :, h, 256:257], tp["sn"][l, h].rearrange("(c p o) -> p c o", p=128, o=1), [], [CgB[h]], CgB[h], slow=True)
            DMA("sp", mstate[:, l:l + 1], tp["sm"][l].rearrange("(p o) -> p o", o=1), [], [mstateB[l]], mstateB[l], slow=True)
            for j in range(3):
                DMA("sp", mhalo[:, l, :, j], tp["smc"][l, j].rearrange("(c p) -> p c", p=128), [], [mhaloB[l]], mhaloB[l], slow=True)
            for j in range(2):
                DMA("sp", fhalo[:, l, :, j], tp["sfc"][l, j].rearrange("(c p) -> p c", p=128), [], [fhaloB[l]], fhaloB[l], slow=True)
        elif tp["first"]:
            for h in range(HM):
                MEMSET("pool", Cg[:, :, h, :], 0.0, [], [CgB[h]])
            MEMSET("pool", mstate[:, l:l + 1], 0.0, [], [mstateB[l]])
            MEMSET("pool", mhalo[:, l, :, :], 0.0, [], [mhaloB[l]])
            MEMSET("pool", fhalo[:, l, :, :], 0.0, [], [fhaloB[l]])

        def make_xT():
            for b in range(nblk):
                CP(eveng(), xbf[0:bs, :], xtok[0:bs, b, :], [xtokB[b]], [xbfB])
                k = tbank()
                pt = psb16(k)
                for c in range(8):
                    TR(pt[:, c * bs:(c + 1) * bs], xbf[0:bs, c * 128:(c + 1) * 128], [xbfB, constB], [psB[k]])
                CP(eveng(), xT[:, :, b * bs:(b + 1) * bs], pt[:, 0:8 * bs].rearrange("p (c t) -> p c t", c=8), [psB[k]], [xTB[b]])

        sch.phase = "xT"
        make_xT()

        def proj_tok(wslot, wB, b, ncols, kcn=8, lhs=None, lhsB=None):
            k = pbank()
            out = ps[0:bs, k, 0:ncols]
            for kc in range(kcn):
                lt = xT[:, kc, b * bs:(b + 1) * bs] if lhs is None else lhs(kc)
                MM(out, lt, wslot[:, kc, 0:ncols], kc == 0, kc == kcn - 1, [wB, xTB[b] if lhsB is None else lhsB], [psB[k]])
            return k, out

        def rope_block(src_zf, zfB, dst, roB, b):
            sv = src_zf.rearrange("p (g two d) -> p g two d", two=2, d=32)
            dv = dst.rearrange("p (g two d) -> p g two d", two=2, d=32)
            cosb = cs_t[0:bs, b, 0:1, :].to_broadcast([bs, 16, 32])
            sinb = cs_t[0:bs, b, 1:2, :].to_broadcast([bs, 16, 32])
            t0 = rt[0][0:bs, :].rearrange("p (g d) -> p g d", d=32)
            t1 = rt[1][0:bs, :].rearrange("p (g d) -> p g d", d=32)
            TT(t0, sv[:, :, 0, :], cosb, ALU.mult, [zfB, csB], [rtB[0]])
            TT(t1, sv[:, :, 1, :], sinb, ALU.mult, [zfB, csB], [rtB[1]])
            TT(dv[:, :, 0, :], t0, t1, ALU.subtract, [rtB[0], rtB[1]], roB)
            TT(t0, sv[:, :, 1, :], cosb, ALU.mult, [zfB, csB], [rtB[0]])
            TT(t1, sv[:, :, 0, :], sinb, ALU.mult, [zfB, csB], [rtB[1]])
            TT(dv[:, :, 1, :], t0, t1, ALU.add, [rtB[0], rtB[1]], roB)

        sch.phase = "qkv"
        zfs = [(zf, zfB), (hmf, hmfB)]
        ros = [(ro, [roB]), (ro2, ro2B)]
        pst = dict(n=0)

        def post(which, b, zb, zbB):
            if which == "v":
                DMA("sp", tp["v_out"][l, b * bs:(b + 1) * bs, :], zb[0:bs, :], [zbB], [], zbB)
                CP("dve", Vaug[0:bs, b, :, 0:128], zb[0:bs, :].rearrange("p (h d) -> p h d", h=HA), [zbB], [VaugB[b]])
                if tp["Vb_dst"] is not None:
                    DMA("sp", tp["Vb_dst"][b * bs:(b + 1) * bs, :].rearrange("p (h d) -> p h d", h=HA),
                        Vaug[0:bs, b, :, 0:128], [VaugB[b]], [tp["VbB"]], VaugB[b])
                return
            rb, rbB = ros[pst["n"] % 2]
            pst["n"] += 1
            rope_block(zb[0:bs, :], zbB, rb[0:bs, :], rbB, b)
            if which == "k":
                DMA("sp", tp["k_out"][l, b * bs:(b + 1) * bs, :], rb[0:bs, :], rbB, [], rbB[0])
            CP("act", xbf[0:bs, :], rb[0:bs, :], rbB, [xbfB])
            kb_ = tbank()
            pt = psb16(kb_)
            for h in range(HA):
                TR(pt[:, h * bs:(h + 1) * bs], xbf[0:bs, h * 128:(h + 1) * 128], [xbfB, constB], [psB[kb_]])
            dstT, dstB = (QT, QTB) if which == "q" else (KT, KTB)
            CP("act", dstT[:, :, b * bs:(b + 1) * bs], pt[:, 0:HA * bs].rearrange("p (h t) -> p h t", h=HA), [psB[kb_]], [dstB[b]])
            if which == "k" and b == nblk - 1 and tp["KT_dst"] is not None:
                DMA("sp", tp["KT_dst"].rearrange("h p t -> p h t"), KT[:, :, 0:ntok], KTB[0:nblk], [tp["KTB"]], KTB[0])

        pend = None
        nz = 0
        for which in ("q", "k", "v"):
            c0 = {"q": C_QA, "k": C_KA, "v": C_VA}[which]
            w0, w0B = wnext(("in", l, c0, 512))
            w1, w1B = wnext(("in", l, c0 + 512, 512))
            for b in range(nblk):
                zb, zbB = zfs[nz % 2]
                nz += 1
                for hf, (ws, wB) in enumerate(((w0, w0B), (w1, w1B))):
                    k, o = proj_tok(ws, wB, b, 512)
                    CP("act" if which != "v" else eveng(), zb[0:bs, hf * 512:(hf + 1) * 512], o, [psB[k]], [zbB])
                if pend is not None:
                    post(*pend)
                pend = (which, b, zb, zbB)
            wdone(2)
        post(*pend)

        sch.phase = "qkm"
        m_pend = [None]
        for g in range(4):
            ws, wB = wnext(("in", l, C_QKM + g * 512, 512))
            for cc4 in range(4):
                cc = g * 4 + cc4
                k = pbank()
                o = ps[:, k, 0:ntok]
                for kc in range(8):
                    MM(o, ws[:, kc, cc4 * 128:(cc4 + 1) * 128], xT[:, kc, 0:ntok], kc == 0, kc == 7, [wB] + xTB[0:nblk], [psB[k]])
                pi = cc % 2
                pcv = pcm[pi]
                CP("pool", pcv[:, 0:3], mhalo[:, l, cc, :], [mhaloB[l]], [pcmH[pi]])
                CP("act", pcv[:, 3:3 + ntok], o, [psB[k]], [pcmB[pi]])
                a = acc[pi][:, 0:ntok]
                ACT(a, o, AF.Identity, [psB[k], prmB], [accB[pi]], bias=mcb_s[:, l, cc:cc + 1], scale=mcw_s[:, l, cc, 3:4])
                CP("pool", mhalo[:, l, cc, :], pcv[:, ntok:ntok + 3], [pcmB[pi]], [mhaloB[l]])
                for j in range(3):
                    STT(a, pcv[:, j:j + ntok], mcw_s[:, l, cc, j:j + 1], a, ALU.mult, ALU.add, [pcmB[pi], pcmH[pi], prmB, accB[pi]], [accB[pi]])
                if m_pend[0] is not None:
                    ACT(qkT[:, m_pend[0][0], 0:ntok], m_pend[0][1], AF.Silu, [accB[m_pend[0][2]]], [qkTB[m_pend[0][0]]])
                m_pend[0] = (cc, a, pi)
            wdone(1)
        ACT(qkT[:, m_pend[0][0], 0:ntok], m_pend[0][1], AF.Silu, [accB[m_pend[0][2]]], [qkTB[m_pend[0][0]]])
        if last:
            for g in range(4):
                ws, wB = wnext(("in", l, C_QKM + g * 512, 512))
                k, o = proj_tok(ws, wB, nblk - 1, 512)
                CP("dve", stg[0:bs, :], o, [psB[k]], [stgB])
                DMA("sp", tp["mc_out"][l, :, g * 512:(g + 1) * 512], stg[bs - 3:bs, :], [stgB], [], stgB)
                wdone(1)
        sch.phase = "vm"
        w0, w0B = wnext(("in", l, C_VM, 512))
        w1, w1B = wnext(("in", l, C_VM + 512, 512))
        for b in range(nblk):
            for hf, (ws, wB) in enumerate(((w0, w0B), (w1, w1B))):
                k, o = proj_tok(ws, wB, b, 512)
                CP(eveng(), VMaug[0:bs, b, 2 * hf:2 * hf + 2, 0:256], o.rearrange("p (h d) -> p h d", h=2), [psB[k]], [VMB[b]])
        wdone(2)
        sch.phase = "gates"
        ws, wB = wnext(("in", l, C_GIF, 8))
        kI = pbank(); kF = pbank()
        for kc in range(8):
            MM(ps[0:4, kI, 0:ntok], ws[:, kc, 0:4], xT[:, kc, 0:ntok], kc == 0, kc == 7, [wB] + xTB[0:nblk], [psB[kI]])
        for kc in range(8):
            MM(ps[0:4, kF, 0:ntok], ws[:, kc, 4:8], xT[:, kc, 0:ntok], kc == 0, kc == 7, [wB] + xTB[0:nblk], [psB[kF]])
        wdone(1)
        g_ = {n: gw[n][:, 0:ntok] for n in gw}
        s_ = {n: gs[n][:, 0:nch] for n in gs}
        ACT(g_["ig"], ps[0:4, kI, 0:ntok], AF.Identity, [psB[kI], prmB], [gwB["ig"]], bias=bif_s[:, l, 0:1])
        ACT(g_["t1"], ps[0:4, kF, 0:ntok], AF.Exp, [psB[kF], prmB], [gwB["t1"]], bias=nbif_s[:, l, 1:2], scale=-1.0)
        TS(g_["t1"], g_["t1"], 1.0, None, ALU.add, None, [gwB["t1"]], [gwB["t1"]])
        ACT(g_["sp"], g_["t1"], AF.Ln, [gwB["t1"]], [gwB["sp"]])
        rm = resetm[:, 0:ntok] if L == 64 else resetm16[:, 0:ntok]
        sch.add("dve", "tensor_tensor_scan", (), dict(out=g_["A"], data0=rm, data1=g_["sp"], initial=0.0, op0=ALU.mult, op1=ALU.add),
                [gwB["sp"], constB], [gwB["A"]])
        TT(g_["r"], g_["ig"], g_["A"], ALU.add, [gwB["ig"], gwB["A"]], [gwB["r"]])
        RED(s_["cm"], g_["r"].rearrange("p (c l) -> p c l", l=L), ALU.max, [gwB["r"]], [gsB["cm"]])
        TS(s_["aL"], g_["A"].rearrange("p (c l) -> p c l", l=L)[:, :, L - 1], -1.0, None, ALU.mult, None, [gwB["A"]], [gsB["aL"]])
        sch.add("dve", "tensor_tensor_scan", (), dict(out=s_["mn"], data0=s_["cm"], data1=s_["aL"], initial=mstate[:, l:l + 1],
                                                       op0=ALU.max, op1=ALU.add), [gsB["cm"], gsB["aL"], mstateB[l]], [gsB["mn"]])
        TT(s_["mu"], s_["mn"], s_["aL"], ALU.subtract, [gsB["mn"], gsB["aL"]], [gsB["mu"]])
        CP("dve", gs["mpv"][:, 0:1], mstate[:, l:l + 1], [mstateB[l]], [gsB["mpv"]])
        if nch > 1:
            CP("dve", gs["mpv"][:, 1:nch], gs["mn"][:, 0:nch - 1], [gsB["mn"]], [gsB["mpv"]])
        CP("dve", mstate[:, l:l + 1], gs["mn"][:, nch - 1:nch], [gsB["mn"], gsB["mpv"]], [mstateB[l]])
        TT(s_["gam"], s_["mpv"], s_["mu"], ALU.subtract, [gsB["mpv"], gsB["mu"]], [gsB["gam"]])
        ACT(s_["gam"], s_["gam"], AF.Exp, [gsB["gam"]], [gsB["gam"]])
        mub = s_["mu"].rearrange("p (c o) -> p c o", o=1).to_broadcast([4, nch, L])
        TT(g_["sc"].rearrange("p (c l) -> p c l", l=L), g_["r"].rearrange("p (c l) -> p c l", l=L), mub, ALU.subtract, [gwB["r"], gsB["mu"]], [gwB["sc"]])
        ACT(g_["sc"], g_["sc"], AF.Exp, [gwB["sc"]], [gwB["sc"]])
        TS(g_["sc"], g_["sc"], 1.0 / 16.0, None, ALU.mult, None, [gwB["sc"]], [gwB["sc"]])
        TT(g_["cl"].rearrange("p (c l) -> p c l", l=L), g_["A"].rearrange("p (c l) -> p c l", l=L), mub, ALU.subtract, [gwB["A"], gsB["mu"]], [gwB["cl"]])
        ACT(g_["cl"], g_["cl"], AF.Exp, [gwB["cl"]], [gwB["cl"]])
        kG = pbank()
        pg = ps[0:bs, kG, 0:nblk * 8].rearrange("p (b e) -> p b e", e=8)
        for b in range(nblk):
            MM(pg[:, b, 0:4], g_["sc"][:, b * bs:(b + 1) * bs], identf[0:4, 0:4], True, True, [gwB["sc"], constB], [psB[kG]])
            MM(pg[:, b, 4:8], g_["cl"][:, b * bs:(b + 1) * bs], identf[0:4, 0:4], True, True, [gwB["cl"], constB], [psB[kG]])
        CP("dve", toksc[0:bs, 0:nblk, :], pg, [psB[kG]], [tokscB])
        TT(gD[:, 0:nch, :], s_["gam"].rearrange("p (c o) -> p c o", o=1).to_broadcast([4, nch, 4]),
           identf[0:4, 0:4].rearrange("p (o h) -> p o h", o=1).to_broadcast([4, nch, 4]), ALU.mult, [gsB["gam"], constB], [gDB])
        kG2 = pbank()
        MM(ps[:, kG2, 0:nch * 4], ones4, gD[:, 0:nch, :].rearrange("p c h -> p (c h)"), True, True, [gDB, constB], [psB[kG2]])
        CP("dve", gam[:, 0:nch, :], ps[:, kG2, 0:nch * 4].rearrange("p (c h) -> p c h", h=4), [psB[kG2]], [gamB])

        sch.phase = "attn"
        prior = tp["prior"]
        ngr = len(prior) + 1
        att = dict(si=0)
        sub_pend = [None]

        def subln_head(h):
            ovh = oaF[0:bs, 0:nblk, h * 128:(h + 1) * 128]
            sqv = hmf[0:bs, 512:512 + nblk * 128].rearrange("p (q d) -> p q d", d=128)
            TT(sqv, ovh, ovh, ALU.mult, oaFB[0:nblk], [hmfB])
            RED(sml[0:bs, 64 + h * 4:64 + h * 4 + nblk], sqv, ALU.add, [hmfB], [sml2B])

        for h in range(HA):
            started = set()
            items = []
            for g in range(ngr):
                own = g == ngr - 1
                for kb in range(nblk if own else T // 128):
                    items.append((g, own, kb))

            def emit_scores(it):
                g, own, kb = it
                if not own:
                    sl = (h * ngr + g) % 2
                    if kb == 0:
                        kd, vd, dB = prior[g]
                        DMA("sp", kch[sl][:, 0:T], kd[h], dB, [kchB[sl]], kchB[sl])
                        DMA("sp", vch[sl][:, :, 0:128], vd[:, h * 128:(h + 1) * 128].rearrange("(b p) d -> p b d", p=128), dB, [vchB[sl]], vchB[sl])
                    kT_ = kch[sl][:, kb * 128:(kb + 1) * 128]; kTB_ = kchB[sl]
                    v_ = vch[sl][:, kb, :]; vB_ = vchB[sl]
                    q0 = 0
                    nk = 128
                else:
                    kT_ = KT[:, h, kb * bs:(kb + 1) * bs]; kTB_ = KTB[kb]
                    v_ = Vaug[0:bs, kb, h, :]; vB_ = VaugB[kb]
                    q0 = kb * bs if tp["mask"] else 0
                    nk = bs
                nq = ntok - q0
                sb_ = att["si"] % 2
                att["si"] += 1
                b0, b1 = 2 * sb_, 2 * sb_ + 1
                MM(ps[0:nk, b0, 0:nq], kT_[0:64, :], QT[0:64, h, q0:ntok], True, True, [kTB_] + QTB[0:nblk], [psB[b0]])
                MM(ps[0:nk, b1, 0:nq], kT_[64:128, :], QT[64:128, h, q0:ntok], True, True, [kTB_] + QTB[0:nblk], [psB[b1]])
                ACT(PT[sb_][0:nk, :, 0:nq], ps[0:nk, b0:b1 + 1, 0:nq], AF.Exp, [psB[b0], psB[b1]], [PTB[sb_]], scale=0.125)
                if own and tp["mask"]:
                    MEMSET("pool", PT[sb_][64:128, :, 0:64], 0.0, [PTB[sb_]], [PTB[sb_]])
                return (g, own, kb, sb_, nk, q0, v_, vB_)

            def emit_av(st):
                g, own, kb, sb_, nk, q0, v_, vB_ = st
                nkb = nblk if own else T // 128
                qb0 = kb if (own and tp["mask"]) else 0
                for qb in range(qb0, nblk):
                    for mp_ in range(2):
                        a_ = qb * 2 + mp_
                        bank = 4 + a_ // 3
                        off = (a_ % 3) * 129
                        col = qb * bs - q0
                        first = (g == 0 and kb == 0) and bank not in started
                        started.add(bank)
                        lastk = own and (kb == (qb if tp["mask"] else nkb - 1))
                        MM(ps[0:bs, bank, off:off + 129], PT[sb_][0:nk, mp_, col:col + bs], v_, first, lastk, [PTB[sb_], vB_], [psB[bank]], skip=True)

            prev = None
            for it in items:
                st = emit_scores(it)
                if prev is not None:
                    emit_av(prev)
                prev = st
            emit_av(prev)
            CP("dve", zf[0:bs, 0:387], ps[0:bs, 4, 0:387], [psB[4]], [zfB])
            CP("dve", zf[0:bs, 387:774], ps[0:bs, 5, 0:387], [psB[5]], [zfB])
            CP("dve", hmf[0:bs, 0:258], ps[0:bs, 6, 0:258], [psB[6]], [hmfB])

            def accv(a_):
                if a_ < 6:
                    return zf[0:bs, a_ * 129:(a_ + 1) * 129], zfB
                return hmf[0:bs, (a_ - 6) * 129:(a_ - 5) * 129], hmfB

            for qb in range(nblk):
                o0, o0B = accv(qb * 2)
                o1, o1B = accv(qb * 2 + 1)
                r0 = sml[0:bs, 0:1]; r1 = sml[0:bs, 1:2]
                sch.add("dve", "reciprocal", (), dict(out=r0, in_=o0[:, 128:129]), [o0B], [smlB])
                sch.add("dve", "reciprocal", (), dict(out=r1, in_=o1[:, 128:129]), [o1B], [smlB])
                TT(r1, r1, neglam[0:bs, l:l + 1], ALU.mult, [smlB, prmB], [smlB])
                TS(o0[:, 0:128], o0[:, 0:128], r0, None, ALU.mult, None, [o0B, smlB], [o0B])
                STT(oaF[0:bs, qb, h * 128:(h + 1) * 128], o1[:, 0:128], r1, o0[:, 0:128], ALU.mult, ALU.add,
                    [o1B, o0B, smlB], [oaFB[qb]])
            if sub_pend[0] is not None:
                subln_head(sub_pend[0])
            sub_pend[0] = h
        subln_head(sub_pend[0])
        RSQRT(sml[0:bs, 64:96], 128.0 * LN_EPS, [sml2B], [sml2B])
        for qb in range(nblk):
            ov = oaF[0:bs, qb, :].rearrange("p (h d) -> p h d", h=HA)
            rs = sml[0:bs, 64:96].rearrange("p (h q) -> p h q", q=4)[:, :, qb:qb + 1].to_broadcast([bs, HA, 128])
            TT(ov, ov, rs, ALU.mult, [oaFB[qb], sml2B], [oaFB[qb]])
            TT(ov, ov, subg_s[0:bs, l, :].rearrange("p (o d) -> p o d", o=1).to_broadcast([bs, HA, 128]), ALU.mult, [oaFB[qb], prmB], [oabB[qb]])

        sch.phase = "mlstm"
        for b in range(nblk):
            kb_ = tbank()
            pt = psb16(kb_)
            for j in range(8):
                TR(pt[0:bs, j * 128:(j + 1) * 128], qkT[:, 8 + j, b * bs:(b + 1) * bs], [qkTB[8 + j], constB], [psB[kb_]])
            for h in range(HM):
                ACT(kw[0:bs, h * 256:(h + 1) * 256], pt[0:bs, h * 256:(h + 1) * 256], AF.Identity, [psB[kb_], tokscB], [kwB], scale=toksc[0:bs, b, h:h + 1])
            kS = pbank()
            for h in range(HM):
                for dk in range(2):
                    MM(ps[0:bs, kS, h * 128:h * 128 + bs], qkT[:, 8 + 2 * h + dk, b * bs:(b + 1) * bs], qkT[:, 2 * h + dk, b * bs:(b + 1) * bs],
                       dk == 0, dk == 1, [qkTB[8 + 2 * h + dk], qkTB[2 * h + dk]], [psB[kS]])
            for h in range(HM):
                STT(Sm[0:bs, h, 0:bs], ps[0:bs, kS, h * 128:h * 128 + bs], toksc[0:bs, b, h:h + 1], maskBD[0:bs, 0:bs], ALU.mult, ALU.mult,
                    [psB[kS], tokscB, constB], [SmB])
            for ci in range(cpb):
                c = b * cpb + ci
                p0, p1 = ci * L, ci * L + L
                t0_, t1_ = b * bs + ci * L, b * bs + ci * L + L
                Gc, GcB = GbL[c % 2]
                if c == 0:
                    for h in range(HM):
                        ACT(Gc[:, :, h, :], Cg[:, :, h, :], AF.Identity, [CgB[h], gamB], [GcB[h]], scale=gam[:, c, h:h + 1])
                for h in range(HM):
                    for dk in range(2):
                        kC = 4 + dk
                        dC = ps[:, kC, 0:257]
                        MM(dC, kw[p0:p1, h * 256 + dk * 128:h * 256 + dk * 128 + 128], VMaug[p0:p1, b, h, :], True, True, [kwB, VMB[b]], [psB[kC]])
                        STT(Cg[:, dk, h, :], Cg[:, dk, h, :], gam[:, c, h:h + 1], dC, ALU.mult, ALU.add, [CgB[h], gamB, psB[kC]], [CgB[h]])
                if c + 1 < nch:
                    Gn, GnB = GbL[(c + 1) % 2]
                    for h in range(HM):
                        ACT(Gn[:, :, h, :], Cg[:, :, h, :], AF.Identity, [CgB[h], gamB], [GnB[h]], scale=gam[:, c + 1, h:h + 1])
                for h in range(HM):
                    nd = ps[p0:p1, h, 0:257]
                    MM(nd, qkT[:, 2 * h, t0_:t1_], Gc[:, 0, h, :], True, False, [qkTB[2 * h], GcB[h]], [psB[h]])
                    MM(nd, qkT[:, 2 * h + 1, t0_:t1_], Gc[:, 1, h, :], False, False, [qkTB[2 * h + 1], GcB[h]], [psB[h]])
                    MM(nd, Sm[p0:p1, h, p0:p1], VMaug[p0:p1, b, h, :], False, True, [SmB, VMB[b]], [psB[h]])
                den = ps[p0:p1, 0:HM, 256]
                dd = sml[p0:p1, 16:16 + HM]
                TS(dd, den, -1.0, None, ALU.mult, None, psB[0:HM], [smlB])
                TT(dd, dd, den, ALU.max, [smlB] + psB[0:HM], [smlB])
                TT(dd, dd, toksc[p0:p1, b, 4:4 + HM], ALU.max, [smlB, tokscB], [smlB])
                sch.add("dve", "reciprocal", (), dict(out=dd, in_=dd), [smlB], [smlB])
                for h in range(HM):
                    ACT(hmf[p0:p1, h * 256:(h + 1) * 256], ps[p0:p1, h, 0:256], AF.Identity, [psB[h], smlB], [hmfB], scale=sml[p0:p1, 16 + h:17 + h])
            for h in range(HM):
                hv_ = hmf[0:bs, h * 256:(h + 1) * 256]
                ACT(zf[0:bs, h * 256:(h + 1) * 256], hv_, AF.Identity, [hmfB], [zfB, smlB], accum_out=sml[0:bs, 24 + h:25 + h])
                ACT(zf[0:bs, h * 256:(h + 1) * 256], hv_, AF.Square, [hmfB], [zfB, smlB], accum_out=sml[0:bs, 28 + h:29 + h])
            mean_ = sml[0:bs, 24:28]; ex2_ = sml[0:bs, 28:32]; nmr_ = sml[0:bs, 32:36]
            TS(mean_, mean_, 1.0 / 256.0, None, ALU.mult, None, [smlB], [smlB])
            TT(nmr_, mean_, mean_, ALU.mult, [smlB], [smlB])
            STT(ex2_, ex2_, 1.0 / 256.0, nmr_, ALU.mult, ALU.subtract, [smlB], [smlB])
            RSQRT(ex2_, LN_EPS, [smlB], [smlB])
            STT(nmr_, mean_, -1.0, ex2_, ALU.mult, ALU.mult, [smlB], [smlB])
            for h in range(HM):
                hv_ = hmf[0:bs, h * 256:(h + 1) * 256]
                ACT(hv_, hv_, AF.Identity, [hmfB, smlB], [hmfB], bias=sml[0:bs, 32 + h:33 + h], scale=sml[0:bs, 28 + h:29 + h])
            TT(hmb[0:bs, b, :], hmf[0:bs, :], lnp[0:bs, 0, :], ALU.mult, [hmfB, lnpB[0]], [hmbB[b]])

        sch.phase = "merge"
        for gi, c0 in enumerate((C_OM, C_OM + 512, C_GA, C_GA + 512, C_GB, C_GB + 512)):
            ws, wB = wnext(("in", l, c0, 512))
            hf = gi % 2
            for b in range(nblk):
                k, o = proj_tok(ws, wB, b, 512)
                si_ = (gi * nblk + b) % 2
                ACT(sg[si_][0:bs, :], o, AF.Sigmoid, [psB[k]], [sgB[si_]])
                if gi in (2, 3):
                    tgt, tB = oab, oabB
                else:
                    tgt, tB = hmb, hmbB
                TT(tgt[0:bs, b, hf * 512:(hf + 1) * 512], tgt[0:bs, b, hf * 512:(hf + 1) * 512], sg[si_][0:bs, :], ALU.mult, [sgB[si_], tB[b]], [tB[b]])
            wdone(1)
        for b in range(nblk):
            TT(ybf[0:bs, :], oab[0:bs, b, :], hmb[0:bs, b, :], ALU.add, [oabB[b], hmbB[b]], [ybfB])
            k = tbank()
            pt = psb16(k)
            for c in range(8):
                TR(pt[:, c * bs:(c + 1) * bs], ybf[0:bs, c * 128:(c + 1) * 128], [ybfB, constB], [psB[k]])
            CP(eveng(), xT[:, :, b * bs:(b + 1) * bs], pt[:, 0:8 * bs].rearrange("p (c t) -> p c t", c=8), [psB[k]], [xTB[b]])

        def layernorm_inplace(b, gi, bi):
            xv = xtok[0:bs, b, :]
            for hh in range(2):
                sch.add("dve", "bn_stats", (), dict(out=stt[0:bs, hh, :], in_=xtok[0:bs, b, hh * 512:(hh + 1) * 512]), [xtokB[b]], [lnB])
            sch.add("dve", "bn_aggr", (), dict(out=mv[0:bs, 0:2], in_=stt[0:bs, :, :]), [lnB], [lnB])
            CP("dve", mv[0:bs, 2:3], mv[0:bs, 1:2], [lnB], [lnB])
            RSQRT(mv[0:bs, 2:3], LN_EPS, [lnB], [lnB])
            STT(mv[0:bs, 3:4], mv[0:bs, 0:1], -1.0, mv[0:bs, 2:3], ALU.mult, ALU.mult, [lnB], [lnB])
            ACT(xv, xv, AF.Identity, [xtokB[b], lnB], [xtokB[b]], bias=mv[0:bs, 3:4], scale=mv[0:bs, 2:3])
            TT(xv, xv, lnp[0:bs, 0, :], ALU.mult, [xtokB[b], lnpB[0]], [xtokB[b]])
            TT(xv, xv, lnp[0:bs, 1, :], ALU.add, [xtokB[b], lnpB[1]], [xtokB[b]])

        sch.phase = "wout"
        load_lnp(0, ln1g); load_lnp(1, ln1b)
        w0, w0B = wnext(("out", l, 0, 512))
        w1, w1B = wnext(("out", l, 512, 512))
        for b in range(nblk):
            for hf, (ws, wB) in enumerate(((w0, w0B), (w1, w1B))):
                k, o = proj_tok(ws, wB, b, 512)
                xs_ = xtok[0:bs, b, hf * 512:(hf + 1) * 512]
                STT(xs_, xs_, ALPHA, o, ALU.mult, ALU.add, [xtokB[b], psB[k]], [xtokB[b]])
            layernorm_inplace(b, 1, 2)
        wdone(2)
        make_xT()

        if not tp["load_state"]:
            sample_prep(prep_per_step)
        sch.phase = "up"
        up_pend = [None]

        def up_final(cc, a, pi):
            if cc < 22:
                ACT(hT[:, cc, 0:ntok], a, AF.Gelu, [accB[pi]], [hTB[cc]])
            else:
                TT(hT[:, cc - 22, 0:ntok], hT[:, cc - 22, 0:ntok], a, ALU.mult, [hTB[cc - 22], accB[pi]], [hTB[cc - 22]])

        for g in range(11):
            ws, wB = wnext(("up", l, g * 512, 512))
            for cc4 in range(4):
                cc = g * 4 + cc4
                k = pbank()
                o = ps[:, k, 0:ntok]
                for kc in range(8):
                    MM(o, ws[:, kc, cc4 * 128:(cc4 + 1) * 128], xT[:, kc, 0:ntok], kc == 0, kc == 7, [wB] + xTB[0:nblk], [psB[k]])
                pi = cc % 2
                pcv = pcm[pi]
                CP("pool", pcv[:, 0:2], fhalo[:, l, cc, :], [fhaloB[l]], [pcmH[pi]])
                CP("act", pcv[:, 2:2 + ntok], o, [psB[k]], [pcmB[pi]])
                a = acc[pi][:, 0:ntok]
                ACT(a, o, AF.Identity, [psB[k], prmB], [accB[pi]], bias=fcb_s[:, l, cc:cc + 1], scale=fcw_s[:, l, cc, 2:3])
                CP("pool", fhalo[:, l, cc, :], pcv[:, ntok:ntok + 2], [pcmB[pi]], [fhaloB[l]])
                for j in range(2):
                    STT(a, pcv[:, j:j + ntok], fcw_s[:, l, cc, j:j + 1], a, ALU.mult, ALU.add, [pcmB[pi], pcmH[pi], prmB, accB[pi]], [accB[pi]])
                if up_pend[0] is not None:
                    up_final(*up_pend[0])
                up_pend[0] = (cc, a, pi)
            wdone(1)
        up_final(*up_pend[0])
        if last:
            for g in range(11):
                ws, wB = wnext(("up", l, g * 512, 512))
                k, o = proj_tok(ws, wB, nblk - 1, 512)
                CP("dve", stg[0:bs, :], o, [psB[k]], [stgB])
                DMA("sp", tp["fc_out"][l, :, g * 512:(g + 1) * 512], stg[bs - 2:bs, :], [stgB], [], stgB)
                wdone(1)
        sch.phase = "down"
        load_lnp(0, ln2g); load_lnp(1, ln2b)
        for nh in range(2):
            pcs = [wnext(("down", l, nh, pc)) for pc in range(3)]
            for b in range(nblk):
                k = pbank()
                o = ps[0:bs, k, 0:512]
                for kc in range(22):
                    ws, wB = pcs[kc // 8]
                    MM(o, hT[:, kc, b * bs:(b + 1) * bs], ws[:, kc % 8, :], kc == 0, kc == 21, [wB, hTB[kc]], [psB[k]])
                xs_ = xtok[0:bs, b, nh * 512:(nh + 1) * 512]
                STT(xs_, xs_, ALPHA, o, ALU.mult, ALU.add, [xtokB[b], psB[k]], [xtokB[b]])
                if nh == 1:
                    layernorm_inplace(b, 3, 4)
                    if l == NL - 1:
                        DMA("sp", tp["y_out"][b * bs:(b + 1) * bs, :], xtok[0:bs, b, :], [xtokB[b]], [], xtokB[b])
            wdone(3)
        if last:
            for h in range(HM):
                DMA("sp", tp["C_out"][l, h].rearrange("(c p) v -> p c v", p=128), Cg[:, :, h, 0:256], [CgB[h]], [], CgB[h])
                DMA("sp", tp["n_out"][l, h].rearrange("(c p o) -> p c o", p=128, o=1), Cg[:, :, h, 256:257], [CgB[h]], [], CgB[h], slow=True)
            DMA("sp", tp["m_out"][l].rearrange("(p o) -> p o", o=1), mstate[:, l:l + 1], [mstateB[l]], [], mstateB[l], slow=True)

    prep_q = [(l, blk) for l in range(NL) for blk in range(P // 128)]

    def sample_prep(n):
        ph = sch.phase
        sch.phase = "prep"
        for _ in range(n):
            if not prep_q:
                break
            l, blk = prep_q.pop(0)
            g = (blk * 128) // T
            DMA("pool", vtmp, ck[l, blk * 128:(blk + 1) * 128, :], [], [vtmpB], vtmpB)
            k = tbank()
            pt = psb16(k)
            for h in range(HA):
                TR(pt[:, h * 128:(h + 1) * 128], vtmp[:, h * 128:(h + 1) * 128], [vtmpB, constB], [psB[k]])
            CP(eveng(), ktmp, pt.rearrange("p (h t) -> p h t", h=HA), [psB[k]], [ktmpB])
            DMA("sp", KTs[l][:, :, blk * 128:(blk + 1) * 128].rearrange("h p t -> p h t"), ktmp, [ktmpB], [KTsB[l][g]], ktmpB)
            DMA("pool", ybf, cv[l, blk * 128:(blk + 1) * 128, :], [], [ybfB], ybfB)
            DMA("sp", Vbs[l, blk * 128:(blk + 1) * 128, :], ybf, [ybfB], [VbsB[l][g]], ybfB)
        sch.phase = ph

    steps = []
    for i in range(NT):
        for l in range(NL):
            steps.append((l, i, False))
    for l in range(NL):
        steps.append((l, 0, True))
    for (l, i, samp) in steps:
        wq.extend(wspec_step(l, samp or i == NT - 1))

    prep_done = False
    prep_per_step = -(-len(prep_q) // max(1, NT * NL))
    for (l, i, samp) in steps:
        if samp and not prep_done:
            sample_prep(len(prep_q))
            prep_done = True
        if not samp:
            tp = dict(ntok=T, bs=128, nblk=NB, L=64, last=(i == NT - 1), first=(i == 0), load_state=False,
                      x_src=xp[i * T:(i + 1) * T, :], pos0=i * T, tok0=i * T, mask=True,
                      prior=[(KTp[l][:, :, g * T:(g + 1) * T], Vbp[l, g * T:(g + 1) * T, :], [KTpB[l][g], VbpB[l][g]]) for g in range(i)],
                      KT_dst=KTp[l][:, :, i * T:(i + 1) * T], KTB=KTpB[l][i], Vb_dst=Vbp[l, i * T:(i + 1) * T, :], VbB=VbpB[l][i],
                      k_out=kp[:, i * T:(i + 1) * T, :], v_out=vp[:, i * T:(i + 1) * T, :], y_out=yp[i * T:(i + 1) * T, :],
                      mc_out=mcp, fc_out=fcp, C_out=Cp, n_out=np_, m_out=mp)
        else:
            tp = dict(ntok=NS, bs=NS, nblk=1, L=NS, last=True, first=False, load_state=True,
                      x_src=xs, pos0=S, tok0=0, mask=False,
                      prior=[(KTs[l][:, :, g * T:(g + 1) * T], Vbs[l, g * T:(g + 1) * T, :], [KTsB[l][g], VbsB[l][g]]) for g in range(P // T)],
                      KT_dst=None, KTB=None, Vb_dst=None, VbB=None,
                      k_out=ks, v_out=vs, y_out=ys, mc_out=mcs, fc_out=fcs, C_out=Cs, n_out=ns_, m_out=ms,
                      sC=sC, sn=sn, sm=sm, smc=smc, sfc=sfc)
        step(l, tp)
    assert wstate["used"] == len(wq) and wstate["released"] == len(wq), (wstate, len(wq))
    print("SBUF/PSUM allocation done")
    info = sch.emit()
    return nc, info


def rope_tables(S, P):
    half = 32
    inv = (np.float32(10000.0) ** (-np.arange(half, dtype=np.float32) * np.float32(2.0) / np.float32(64))).astype(np.float32)
    pos = np.concatenate([np.arange(S), P + np.arange(NS)]).astype(np.float32)
    ang = (pos[:, None] * inv[None, :]).astype(np.float32)
    return np.cos(ang).astype(np.float32), np.sin(ang).astype(np.float32)


_CACHE = {}


def run(inputs, S, P, T, n_prompt, n_sample, n_cores):
    key = (S, P, T)
    if key not in _CACHE:
        _CACHE[key] = build(S=S, P=P, T=T)
    nc, info = _CACHE[key]
    f = lambda a: np.ascontiguousarray(np.asarray(a, dtype=np.float32))
    cosT, sinT = rope_tables(S, P)
    NL = 2
    in_maps = []
    for c in range(n_cores):
        b = c % n_prompt
        s = c % n_sample
        m = {
            "xp": f(inputs["x_prompt"][b]), "xs": f(inputs["x_sample"][s]),
            "ck": f(inputs["cache_k"][:, s]).reshape(NL, P, D), "cv": f(inputs["cache_v"][:, s]).reshape(NL, P, D),
            "smc": f(inputs["state_mlstm_conv"][:, s]), "sC": f(inputs["state_mlstm_C"][:, s]),
            "sn": f(inputs["state_mlstm_n"][:, s]), "sm": f(inputs["state_mlstm_m"][:, s]),
            "sfc": f(inputs["state_ffn_conv"][:, s]),
            "w_in": f(inputs["w_in"]), "b_if": f(inputs["b_if"]), "mcw": f(inputs["mlstm_conv_w"]), "mcb": f(inputs["mlstm_conv_b"]),
            "dlam": f(inputs["diff_lambda"]), "subg": f(inputs["diff_subln_g"]), "mhg": f(inputs["mlstm_norm_g"]),
            "w_out": f(inputs["w_out"]), "ln1g": f(inputs["ln1_g"]), "ln1b": f(inputs["ln1_b"]),
            "w_up": f(inputs["w_up"]), "fcw": f(inputs["ffn_conv_w"]), "fcb": f(inputs["ffn_conv_b"]),
            "w_down": f(inputs["w_down"]), "ln2g": f(inputs["ln2_g"]), "ln2b": f(inputs["ln2_b"]),
            "cosT": cosT, "sinT": sinT,
        }
        in_maps.append(m)
    res = run_bass_kernel_spmd(nc, in_maps, core_ids=list(range(n_cores)))
    R = res.results
    pc = list(range(n_prompt))
    sc = list(range(n_sample))
    st = lambda name, cores, ax=0: np.stack([np.asarray(R[c][name], dtype=np.float32) for c in cores], axis=ax)
    y_prompt = st("yp", pc)
    y_sample = st("ys", sc)
    k_prompt = st("kp", pc, 1).reshape(NL, n_prompt, S, HA, 128)
    v_prompt = st("vp", pc, 1).reshape(NL, n_prompt, S, HA, 128)
    outs = (y_prompt, y_sample, k_prompt, v_prompt,
            st("mcp", pc, 1), st("Cp", pc, 1), st("np", pc, 1), st("mp", pc, 1), st("fcp", pc, 1),
            st("ks", sc, 1).reshape(NL, n_sample, NS, HA, 128), st("vs", sc, 1).reshape(NL, n_sample, NS, HA, 128),
            st("mcs", sc, 1), st("Cs", sc, 1), st("ns", sc, 1), st("ms", sc, 1), st("fcs", sc, 1))
    return outs


def kernel(**inputs):
    return run(inputs, S=8192, P=4096, T=512, n_prompt=4, n_sample=8, n_cores=8)
```

```python
import math
import numpy as np
import concourse.bass as bass
import concourse.mybir as mybir
from concourse.bass_utils import run_bass_kernel_spmd

F32 = mybir.dt.float32
BF16 = mybir.dt.bfloat16
AF = mybir.ActivationFunctionType
ALU = mybir.AluOpType
AX = mybir.AxisListType

D = 1024
HA = 8
HM = 4
DFF = 2816
DIN = 9224
NS = 16
ALPHA = (2 * 2) ** 0.25
LN_EPS = 1e-5
C_QA, C_KA, C_VA, C_QKM, C_VM, C_OM, C_GIF, C_GA, C_GB = 0, 1024, 2048, 3072, 5120, 6144, 7168, 7176, 8200


RAW_ONLY = True


class Buf:
    __slots__ = ("name", "lastw", "readers")

    def __init__(self, name=""):
        self.name = name
        self.lastw = None
        self.readers = []


class Sched:
    def __init__(self, nc, same_engine_sync=True):
        self.nc = nc
        self.engs = {"pe": nc.tensor, "act": nc.scalar, "dve": nc.vector, "pool": nc.gpsimd, "sp": nc.sync}
        self.ins = []
        self.dma_cnt = {}
        self.same = same_engine_sync
        self.raw_only = RAW_ONLY
        self.phase = ""
        self.names = None

    def add(self, eng, meth, args, kwargs, reads=(), writes=(), dma=None):
        idx = len(self.ins)
        deps = set()
        raw = set()
        for r in reads:
            if r.lastw is not None:
                deps.add(r.lastw)
                raw.add(r.lastw)
        for w in writes:
            if w.lastw is not None:
                deps.add(w.lastw)
            deps.update(w.readers)
        for r in reads:
            r.readers.append(idx)
        for w in writes:
            w.lastw = idx
            w.readers = []
        dval = None
        if dma is not None:
            self.dma_cnt[dma] = self.dma_cnt.get(dma, 0) + 16
            dval = self.dma_cnt[dma]
        keep = set()
        for d in deps:
            de = self.ins[d]
            if de[5] is None and de[0] == eng:
                if eng == "pe" or not self.same:
                    continue
                if self.raw_only and d not in raw:
                    continue
            keep.add(d)
        self.ins.append([eng, meth, args, kwargs, keep, dma, dval, False, 0, self.phase])
        return idx

    def emit(self):
        nc = self.nc
        for rec in self.ins:
            for d in rec[4]:
                de = self.ins[d]
                if de[5] is None:
                    de[7] = True
        cnt = {e: 0 for e in self.engs}
        for rec in self.ins:
            if rec[7]:
                cnt[rec[0]] += 1
                rec[8] = cnt[rec[0]]
        esem = {e: nc.alloc_semaphore(name="es_" + e) for e in self.engs}
        dsem = {}
        for k in self.dma_cnt:
            dsem[k] = nc.alloc_semaphore(name="ds_%d" % len(dsem))
        waited = {e: {} for e in self.engs}
        nwait = 0
        for rec in self.ins:
            eng, meth, args, kwargs, deps, dma, dval, sig, sigval, phase = rec
            E = self.engs[eng]
            need = {}
            for d in deps:
                de = self.ins[d]
                if de[5] is None:
                    s, v = esem[de[0]], de[8]
                else:
                    s, v = dsem[de[5]], de[6]
                if need.get(s, 0) < v:
                    need[s] = v
            for s, v in need.items():
                if waited[eng].get(s, 0) >= v:
                    continue
                E.wait_ge(s, v)
                waited[eng][s] = v
                nwait += 1
            ins = getattr(E, meth)(*args, **kwargs)
            if self.names is not None:
                self.names[ins.ins.name] = phase
            if dma is not None:
                ins.then_inc(dsem[dma], 16)
            elif sig:
                ins.then_inc(esem[eng], 1)
        for k, v in self.dma_cnt.items():
            nc.sync.wait_ge(dsem[k], v)
        return dict(n=len(self.ins), nwait=nwait, nsem=len(dsem) + 5)


def build(S=8192, P=4096, T=512, NL=2, dbg=None, same=True):
    nc = bass.Bass("TRN2", target_bir_lowering=False)
    sch = Sched(nc, same_engine_sync=same)
    NT = S // T
    NTAB = S + NS

    def din(name, shape, dt=F32):
        return nc.dram_tensor(name, list(shape), dt, kind="ExternalInput").ap()

    def dout(name, shape, dt=F32):
        return nc.dram_tensor(name, list(shape), dt, kind="ExternalOutput").ap()

    def dscr(name, shape, dt):
        return nc.dram_tensor(name, list(shape), dt, kind="Internal").ap()

    def sb(name, shape, dt=F32):
        return nc.alloc_sbuf_tensor(name, list(shape), dt).ap()

    xp = din("xp", [S, D]); xs = din("xs", [NS, D])
    ck = din("ck", [NL, P, D]); cv = din("cv", [NL, P, D])
    smc = din("smc", [NL, 3, 2048]); sC = din("sC", [NL, HM, 256, 256]); sn = din("sn", [NL, HM, 256])
    sm = din("sm", [NL, HM]); sfc = din("sfc", [NL, 2, 2 * DFF])
    w_in = din("w_in", [NL, D, DIN]); b_if = din("b_if", [NL, 8])
    mcw = din("mcw", [NL, 4, 2048]); mcb = din("mcb", [NL, 2048])
    dlam = din("dlam", [NL, 4, 64]); subg = din("subg", [NL, 128]); mhg = din("mhg", [NL, D])
    w_out = din("w_out", [NL, D, D]); ln1g = din("ln1g", [NL, D]); ln1b = din("ln1b", [NL, D])
    w_up = din("w_up", [NL, D, 2 * DFF]); fcw = din("fcw", [NL, 3, 2 * DFF]); fcb = din("fcb", [NL, 2 * DFF])
    w_down = din("w_down", [NL, DFF, D]); ln2g = din("ln2g", [NL, D]); ln2b = din("ln2b", [NL, D])
    cosT = din("cosT", [NTAB, 32]); sinT = din("sinT", [NTAB, 32])

    yp = dout("yp", [S, D]); ys = dout("ys", [NS, D])
    kp = dout("kp", [NL, S, D]); vp = dout("vp", [NL, S, D])
    mcp = dout("mcp", [NL, 3, 2048]); Cp = dout("Cp", [NL, HM, 256, 256]); np_ = dout("np", [NL, HM, 256])
    mp = dout("mp", [NL, HM]); fcp = dout("fcp", [NL, 2, 2 * DFF])
    ks = dout("ks", [NL, NS, D]); vs = dout("vs", [NL, NS, D])
    mcs = dout("mcs", [NL, 3, 2048]); Cs = dout("Cs", [NL, HM, 256, 256]); ns_ = dout("ns", [NL, HM, 256])
    ms = dout("ms", [NL, HM]); fcs = dout("fcs", [NL, 2, 2 * DFF])

    KTp = dscr("KTp", [NL, HA, 128, S], BF16); Vbp = dscr("Vbp", [NL, S, D], BF16)
    KTs = dscr("KTs", [NL, HA, 128, P], BF16); Vbs = dscr("Vbs", [NL, P, D], BF16)
    KTpB = [[Buf() for _ in range(NT)] for _ in range(NL)]
    VbpB = [[Buf() for _ in range(NT)] for _ in range(NL)]
    KTsB = [[Buf() for _ in range(max(1, P // T))] for _ in range(NL)]
    VbsB = [[Buf() for _ in range(max(1, P // T))] for _ in range(NL)]

    NB = T // 128
    xtok = sb("xtok", [128, NB, D]); xtokB = [Buf() for _ in range(NB)]
    xbf = sb("xbf", [128, D], BF16); xbfB = Buf()
    xT = sb("xT", [128, 8, T], BF16); xTB = [Buf() for _ in range(NB)]
    NSLOT = 4
    ring = [sb("ring%d" % i, [128, 8, 512], BF16) for i in range(NSLOT)]
    ringB = [Buf() for _ in range(NSLOT)]
    zf = sb("zf", [128, D]); zfB = Buf()
    ro = sb("ro", [128, D]); roB = Buf()
    cs_t = sb("cs_t", [128, NB, 2, 32]); csB = Buf()
    QT = sb("QT", [128, HA, T], BF16); QTB = [Buf() for _ in range(NB)]
    KT = sb("KT", [128, HA, T], BF16); KTB = [Buf() for _ in range(NB)]
    Vaug = sb("Vaug", [128, NB, HA, 129], BF16); VaugB = [Buf() for _ in range(NB)]
    VMaug = sb("VMaug", [128, NB, HM, 257], BF16); VMB = [Buf() for _ in range(NB)]
    oab = sb("oab", [128, NB, D], BF16); oabB = [Buf() for _ in range(NB)]
    hmb = sb("hmb", [128, NB, D], BF16); hmbB = [Buf() for _ in range(NB)]
    oaF = oab; oaFB = oabB
    hT = sb("hT", [128, 22, T], BF16); hTB = [Buf() for _ in range(22)]
    qkT = hT; qkTB = hTB
    pcm = [sb("pcm%d" % i, [128, 3 + T]) for i in range(2)]; pcmB = [Buf(), Buf()]; pcmH = [Buf(), Buf()]
    acc = [sb("acc%d" % i, [128, T]) for i in range(2)]; accB = [Buf(), Buf()]
    rt = acc; rtB = accB
    stg = pcm[0][:, 3:515]; stgB = pcmB[0]
    kch = [sb("kch%d" % i, [128, T], BF16) for i in range(2)]; kchB = [Buf(), Buf()]
    vch = [sb("vch%d" % i, [128, NB, 129], BF16) for i in range(2)]; vchB = [Buf(), Buf()]
    PT = [sb("PT%d" % i, [128, 2, T], BF16) for i in range(2)]; PTB = [Buf(), Buf()]
    Caug = sb("Caug", [128, 2, HM, 257]); CaugB = [Buf() for _ in range(HM)]
    GbRaw = sb("GbRaw", [128, 2 * HM * 257], BF16); GbB = [Buf() for _ in range(HM)]
    Gb = GbRaw.rearrange("p (a h d) -> p a h d", a=2, h=HM)
    Gb2 = sb("Gb2", [128, 2, HM, 257], BF16); Gb2B = [Buf() for _ in range(HM)]
    GbL = [(Gb, GbB), (Gb2, Gb2B)]
    ro2 = GbRaw[:, 0:2 * D].bitcast(F32); ro2B = GbB
    kw = sb("kw", [128, D], BF16); kwB = Buf()
    Sm = sb("Sm", [128, HM, 128], BF16); SmB = Buf()
    hmf = sb("hmf", [128, D]); hmfB = Buf()
    sml = sb("sml", [128, 96]); smlB = Buf(); sml2B = Buf()
    mhalo = sb("mhalo", [128, NL, 16, 3]); mhaloB = [Buf() for _ in range(NL)]
    fhalo = sb("fhalo", [128, NL, 44, 2]); fhaloB = [Buf() for _ in range(NL)]
    mstate = sb("mstate", [4, NL]); mstateB = [Buf() for _ in range(NL)]
    CaugL = [Caug, sb("Caug1", [128, 2, HM, 257])]
    CaugLB = [CaugB, [Buf() for _ in range(HM)]]
    CaugLB2 = [[Buf() for _ in range(HM)] for _ in range(NL)]
    _g3 = [PT[0].rearrange("p a t -> p (a t)").bitcast(F32)[0:4, :], PT[1].rearrange("p a t -> p (a t)").bitcast(F32)[0:4, :],
           kw.bitcast(F32)[0:4, :]]
    _g3B = [PTB[0], PTB[1], kwB]
    gw = {"t1": _g3[0], "sp": _g3[0], "A": _g3[1], "cl": _g3[1], "ig": _g3[2], "r": _g3[2], "sc": _g3[2]}
    gwB = {"t1": _g3B[0], "sp": _g3B[0], "A": _g3B[1], "cl": _g3B[1], "ig": _g3B[2], "r": _g3B[2], "sc": _g3B[2]}
    gs = {n: sb("gs_" + n, [4, 16]) for n in ("cm", "aL", "mn", "mu", "mpv", "gam")}
    gsB = {n: Buf() for n in gs}
    gD = sb("gD", [4, 16, 4]); gDB = Buf()
    toksc = sb("toksc", [128, NB, 8]); tokscB = Buf()
    gam = sb("gam", [128, 16, 4]); gamB = Buf()
    identb = sb("identb", [128, 128], BF16); identf = sb("identf", [128, 128]); maskBD = sb("maskBD", [128, 128])
    ones4 = sb("ones4", [4, 128]); resetm = sb("resetm", [4, T]); resetm16 = sb("resetm16", [4, NS])
    constB = Buf()
    mcw_s = sb("mcw_s", [128, NL, 16, 4]); mcb_s = sb("mcb_s", [128, NL, 16])
    fcw_s = sb("fcw_s", [128, NL, 44, 3]); fcb_s = sb("fcb_s", [128, NL, 44])
    bif_s = sb("bif_s", [4, NL, 2]); nbif_s = sb("nbif_s", [4, NL, 2])
    neglam = sb("neglam", [128, NL]); lamw = sb("lamw", [128, 4, 64]); lamw2 = sb("lamw2", [128, 4])
    subg_s = sb("subg_s", [128, NL, 128])
    prmB = Buf()
    lnp = sb("lnp", [128, 2, D]); lnpB = [Buf(), Buf()]
    sg = [sb("sg%d" % i, [128, 512], BF16) for i in range(2)]; sgB = [Buf(), Buf()]
    ybf = xbf; ybfB = xbfB
    stt = sb("stt", [128, 2, 6]); mv = sb("mv", [128, 4]); lnB = Buf()
    vtmp = xbf; vtmpB = xbfB
    ktmp = kw.rearrange("p (h t) -> p h t", h=HA); ktmpB = kwB
    ps = nc.alloc_psum_tensor("ps", [128, 8, 512], F32).ap()
    psB = [Buf() for _ in range(8)]

    def psb16(k):
        return ps[:, k, :].bitcast(BF16)

    def MM(out, lhsT, rhs, start, stop, r, w, skip=False):
        kw_ = dict(lhsT=lhsT, rhs=rhs, start=start, stop=stop)
        if skip:
            kw_["skip_group_check"] = True
        sch.add("pe", "matmul", (out,), kw_, r, w)

    def TR(out, in_, r, w):
        sch.add("pe", "transpose", (), dict(out=out, in_=in_, identity=identb[0:in_.shape[0], 0:in_.shape[0]]), r, w)

    def ACT(out, in_, func, r, w, bias=None, scale=None, accum_out=None):
        kw_ = dict(out=out, in_=in_, func=func)
        if bias is not None:
            kw_["bias"] = bias
        if scale is not None:
            kw_["scale"] = scale
        if accum_out is not None:
            kw_["accum_out"] = accum_out
        sch.add("act", "activation", (), kw_, r, w)

    def CP(eng, out, in_, r, w):
        if eng == "act":
            ACT(out, in_, AF.Identity, r, w)
        else:
            sch.add(eng, "tensor_copy", (), dict(out=out, in_=in_), r, w)

    def TT(out, in0, in1, op, r, w, eng="dve"):
        sch.add(eng, "tensor_tensor", (), dict(out=out, in0=in0, in1=in1, op=op), r, w)

    def TS(out, in0, s1, s2, op0, op1, r, w, eng="dve"):
        kw_ = dict(out=out, in0=in0, scalar1=s1, scalar2=s2, op0=op0)
        if op1 is not None:
            kw_["op1"] = op1
        sch.add(eng, "tensor_scalar", (), kw_, r, w)

    def STT(out, in0, scalar, in1, op0, op1, r, w, eng="dve"):
        sch.add(eng, "scalar_tensor_tensor", (), dict(out=out, in0=in0, scalar=scalar, in1=in1, op0=op0, op1=op1), r, w)

    def RSQRT(ap, addc, r, w):
        TS(ap, ap, addc, None, ALU.add, None, r, w)
        ACT(ap, ap, AF.Sqrt, r, w)
        sch.add("dve", "reciprocal", (), dict(out=ap, in_=ap), r, w)

    def RED(out, in_, op, r, w):
        sch.add("dve", "tensor_reduce", (), dict(out=out, in_=in_, axis=AX.X, op=op), r, w)

    def MEMSET(eng, ap, val, r, w):
        sch.add(eng, "memset", (ap, val), {}, r, w)

    def DMA(q, out, in_, r, w, key, slow=False):
        kw_ = dict(out=out, in_=in_)
        if slow:
            kw_["allow_slow_non_contiguous"] = True
        sch.add(q, "dma_start", (), kw_, r, w, dma=key)

    MEMSET("pool", identf, 1.0, [], [constB])
    sch.add("pool", "affine_select", (), dict(out=identf, in_=identf, compare_op=ALU.is_equal, fill=0.0, base=0,
                                              pattern=[[-1, 128]], channel_multiplier=1), [constB], [constB])
    CP("dve", identb, identf, [constB], [constB])
    MEMSET("pool", maskBD, 1.0, [constB], [constB])
    sch.add("pool", "affine_select", (), dict(out=maskBD, in_=maskBD, compare_op=ALU.is_ge, fill=0.0, base=0,
                                              pattern=[[1, 128]], channel_multiplier=-1), [constB], [constB])
    MEMSET("pool", maskBD[0:64, 64:128], 0.0, [constB], [constB])
    MEMSET("pool", ones4, 1.0, [constB], [constB])
    MEMSET("pool", resetm, 1.0, [constB], [constB])
    MEMSET("pool", resetm.rearrange("p (c l) -> p c l", l=64)[:, :, 0:1], 0.0, [constB], [constB])
    MEMSET("pool", resetm16, 1.0, [constB], [constB])
    MEMSET("pool", resetm16[:, 0:1], 0.0, [constB], [constB])
    MEMSET("pool", Vaug[:, :, :, 128:129], 1.0, [], VaugB)
    MEMSET("pool", VMaug[:, :, :, 256:257], 1.0, [], VMB)
    for i in range(2):
        MEMSET("pool", vch[i][:, :, 128:129], 1.0, [], [vchB[i]])
    pk = Buf()
    for l in range(NL):
        for j in range(4):
            DMA("sp", mcw_s[:, l, :, j], mcw[l, j].rearrange("(c p) -> p c", p=128), [], [prmB], pk, slow=True)
        DMA("sp", mcb_s[:, l, :], mcb[l].rearrange("(c p) -> p c", p=128), [], [prmB], pk, slow=True)
        for j in range(3):
            DMA("sp", fcw_s[:, l, :, j], fcw[l, j].rearrange("(c p) -> p c", p=128), [], [prmB], pk, slow=True)
        DMA("sp", fcb_s[:, l, :], fcb[l].rearrange("(c p) -> p c", p=128), [], [prmB], pk, slow=True)
        DMA("sp", bif_s[:, l, :], b_if[l].rearrange("(j p) -> p j", p=4), [], [prmB], pk, slow=True)
        DMA("sp", subg_s[:, l, :], subg[l].partition_broadcast(128), [], [prmB], pk)
        DMA("sp", lamw, dlam[l].partition_broadcast(128), [prmB], [prmB], pk)
        lam_init = 0.8 - 0.6 * math.exp(-0.3 * l)
        lv = lamw.rearrange("p (a b) d -> p a b d", b=2)
        TT(lamw[:, 0:2, :].rearrange("p a d -> p a d"), lv[:, :, 0, :], lv[:, :, 1, :], ALU.mult, [prmB], [prmB])
        RED(lamw2[:, 0:2], lamw[:, 0:2, :], ALU.add, [prmB], [prmB])
        ACT(lamw2[:, 2:4], lamw2[:, 0:2], AF.Exp, [prmB], [prmB])
        TT(neglam[:, l:l + 1], lamw2[:, 3:4], lamw2[:, 2:3], ALU.subtract, [prmB], [prmB])
        TS(neglam[:, l:l + 1], neglam[:, l:l + 1], -lam_init, None, ALU.add, None, [prmB], [prmB])
        TS(subg_s[:, l, :], subg_s[:, l, :], (1.0 - lam_init) * math.sqrt(128.0), None, ALU.mult, None, [prmB], [prmB])
    TS(nbif_s, bif_s, -1.0, None, ALU.mult, None, [prmB], [prmB])

    wq = []
    wstate = dict(loaded=0, used=0, released=0)

    def wspec_step(l, last):
        sp_ = []
        for c0 in (C_QA, C_QA + 512, C_KA, C_KA + 512, C_VA, C_VA + 512):
            sp_.append(("in", l, c0, 512))
        for c0 in range(C_QKM, C_QKM + 2048, 512):
            sp_.append(("in", l, c0, 512))
        if last:
            for c0 in range(C_QKM, C_QKM + 2048, 512):
                sp_.append(("in", l, c0, 512))
        sp_.append(("in", l, C_VM, 512)); sp_.append(("in", l, C_VM + 512, 512))
        sp_.append(("in", l, C_GIF, 8))
        for c0 in (C_OM, C_OM + 512, C_GA, C_GA + 512, C_GB, C_GB + 512):
            sp_.append(("in", l, c0, 512))
        sp_.append(("out", l, 0, 512)); sp_.append(("out", l, 512, 512))
        for g in range(11):
            sp_.append(("up", l, g * 512, 512))
        if last:
            for g in range(11):
                sp_.append(("up", l, g * 512, 512))
        for nh in range(2):
            for pc in range(3):
                sp_.append(("down", l, nh, pc))
        return sp_

    def wload(i):
        kind, l, a, b = wq[i]
        slot = i % NSLOT
        if kind == "in":
            src = w_in[l].rearrange("(kc p) n -> p kc n", p=128)[:, :, a:a + b]
            dst = ring[slot][:, :, 0:b]
        elif kind == "out":
            src = w_out[l].rearrange("(kc p) n -> p kc n", p=128)[:, :, a:a + b]
            dst = ring[slot][:, :, 0:b]
        elif kind == "up":
            src = w_up[l].rearrange("(kc p) n -> p kc n", p=128)[:, :, a:a + b]
            dst = ring[slot][:, :, 0:b]
        else:
            k0 = b * 8
            k1 = min(22, k0 + 8)
            src = w_down[l].rearrange("(kc p) n -> p kc n", p=128)[:, k0:k1, a * 512:(a + 1) * 512]
            dst = ring[slot][:, 0:k1 - k0, :]
        DMA("pool", dst, src, [], [ringB[slot]], ringB[slot])

    def wfill():
        while wstate["loaded"] < min(len(wq), wstate["released"] + NSLOT):
            wload(wstate["loaded"])
            wstate["loaded"] += 1

    def wnext(expect):
        i = wstate["used"]
        assert wq[i] == expect, (wq[i], expect)
        wfill()
        assert wstate["loaded"] > i
        wstate["used"] += 1
        return ring[i % NSLOT], ringB[i % NSLOT]

    def wdone(n=1):
        wstate["released"] += n
        assert wstate["released"] <= wstate["used"]
        wfill()

    rot = dict(pp=0, tp=0)

    def pbank():
        k = rot["pp"] % 4
        rot["pp"] += 1
        return k

    def tbank():
        k = 6 + rot["tp"] % 2
        rot["tp"] += 1
        return k

    evr = dict(i=0)

    def eveng():
        evr["i"] += 1
        return "act" if evr["i"] % 2 else "dve"

    def step(l, tp):
        ntok, bs, nblk, L = tp["ntok"], tp["bs"], tp["nblk"], tp["L"]
        cpb = bs // L
        nch = ntok // L
        last = tp["last"]
        Cg = CaugL[l]; CgB = CaugLB[l]; CgB2 = CaugLB2[l]

        def load_lnp(slot, src):
            DMA("sp", lnp[:, slot, :], src[l].partition_broadcast(128), [], [lnpB[slot]], lnpB[slot])
        load_lnp(0, mhg)
        if l == 0:
            for b in range(nblk):
                DMA("sp", xtok[0:bs, b, :], tp["x_src"][b * bs:(b + 1) * bs, :], [], [xtokB[b]], xtokB[b])
        DMA("sp", cs_t[0:bs, 0:nblk, 0, :], cosT[tp["pos0"]:tp["pos0"] + ntok, :].rearrange("(b p) d -> p b d", p=bs), [], [csB], csB)
        DMA("sp", cs_t[0:bs, 0:nblk, 1, :], sinT[tp["pos0"]:tp["pos0"] + ntok, :].rearrange("(b p) d -> p b d", p=bs), [], [csB], csB)
        if tp["load_state"]:
            for h in range(HM):
                DMA("sp", Cg[:, :, h, 0:256], tp["sC"][l, h].rearrange("(c p) v -> p c v", p=128), [], [CgB[h], CgB2[h]], CgB[h])
                DMA("sp", Cg[:, :, h, 256:257], tp["sn"][l, h].rearrange("(c p o) -> p c o", p=128, o=1), [], [CgB[h], CgB2[h]], CgB[h], slow=True)
            DMA("sp", mstate[:, l:l + 1], tp["sm"][l].rearrange("(p o) -> p o", o=1), [], [mstateB[l]], mstateB[l], slow=True)
            for j in range(3):
                DMA("sp", mhalo[:, l, :, j], tp["smc"][l, j].rearrange("(c p) -> p c", p=128), [], [mhaloB[l]], mhaloB[l], slow=True)
            for j in range(2):
                DMA("sp", fhalo[:, l, :, j], tp["sfc"][l, j].rearrange("(c p) -> p c", p=128), [], [fhaloB[l]], fhaloB[l], slow=True)
        elif tp["first"]:
            for h in range(HM):
                MEMSET("pool", Cg[:, :, h, :], 0.0, [], [CgB[h], CgB2[h]])
            MEMSET("pool", mstate[:, l:l + 1], 0.0, [], [mstateB[l]])
            MEMSET("pool", mhalo[:, l, :, :], 0.0, [], [mhaloB[l]])
            MEMSET("pool", fhalo[:, l, :, :], 0.0, [], [fhaloB[l]])

        def make_xT():
            for b in range(nblk):
                CP(eveng(), xbf[0:bs, :], xtok[0:bs, b, :], [xtokB[b]], [xbfB])
                k = tbank()
                pt = psb16(k)
                for c in range(8):
                    TR(pt[:, c * bs:(c + 1) * bs], xbf[0:bs, c * 128:(c + 1) * 128], [xbfB, constB], [psB[k]])
                CP(eveng(), xT[:, :, b * bs:(b + 1) * bs], pt[:, 0:8 * bs].rearrange("p (c t) -> p c t", c=8), [psB[k]], [xTB[b]])

        sch.phase = "xT"
        make_xT()

        def proj_tok(wslot, wB, b, ncols, kcn=8, lhs=None, lhsB=None):
            k = pbank()
            out = ps[0:bs, k, 0:ncols]
            for kc in range(kcn):
                lt = xT[:, kc, b * bs:(b + 1) * bs] if lhs is None else lhs(kc)
                MM(out, lt, wslot[:, kc, 0:ncols], kc == 0, kc == kcn - 1, [wB, xTB[b] if lhsB is None else lhsB], [psB[k]])
            return k, out

        def rope_block(src_zf, zfB, dst, roB, b):
            sv = src_zf.rearrange("p (g two d) -> p g two d", two=2, d=32)
            dv = dst.rearrange("p (g two d) -> p g two d", two=2, d=32)
            cosb = cs_t[0:bs, b, 0:1, :].to_broadcast([bs, 16, 32])
            sinb = cs_t[0:bs, b, 1:2, :].to_broadcast([bs, 16, 32])
            t0 = rt[0][0:bs, :].rearrange("p (g d) -> p g d", d=32)
            t1 = rt[1][0:bs, :].rearrange("p (g d) -> p g d", d=32)
            TT(t0, sv[:, :, 0, :], cosb, ALU.mult, [zfB, csB], [rtB[0]])
            TT(t1, sv[:, :, 1, :], sinb, ALU.mult, [zfB, csB], [rtB[1]])
            TT(dv[:, :, 0, :], t0, t1, ALU.subtract, [rtB[0], rtB[1]], roB)
            TT(t0, sv[:, :, 1, :], cosb, ALU.mult, [zfB, csB], [rtB[0]])
            TT(t1, sv[:, :, 0, :], sinb, ALU.mult, [zfB, csB], [rtB[1]])
            TT(dv[:, :, 1, :], t0, t1, ALU.add, [rtB[0], rtB[1]], roB)

        sch.phase = "qkv"
        zfs = [(zf, zfB), (hmf, hmfB)]
        ros = [(ro, [roB]), (ro2, ro2B)]
        pst = dict(n=0)

        def post(which, b, zb, zbB):
            if which == "v":
                DMA("sp", tp["v_out"][l, b * bs:(b + 1) * bs, :], zb[0:bs, :], [zbB], [], zbB)
                CP("dve", Vaug[0:bs, b, :, 0:128], zb[0:bs, :].rearrange("p (h d) -> p h d", h=HA), [zbB], [VaugB[b]])
                if tp["Vb_dst"] is not None:
                    DMA("sp", tp["Vb_dst"][b * bs:(b + 1) * bs, :].rearrange("p (h d) -> p h d", h=HA),
                        Vaug[0:bs, b, :, 0:128], [VaugB[b]], [tp["VbB"]], VaugB[b])
                return
            rb, rbB = ros[pst["n"] % 2]
            pst["n"] += 1
            rope_block(zb[0:bs, :], zbB, rb[0:bs, :], rbB, b)
            if which == "k":
                DMA("sp", tp["k_out"][l, b * bs:(b + 1) * bs, :], rb[0:bs, :], rbB, [], rbB[0])
            CP("act", xbf[0:bs, :], rb[0:bs, :], rbB, [xbfB])
            kb_ = tbank()
            pt = psb16(kb_)
            for h in range(HA):
                TR(pt[:, h * bs:(h + 1) * bs], xbf[0:bs, h * 128:(h + 1) * 128], [xbfB, constB], [psB[kb_]])
            dstT, dstB = (QT, QTB) if which == "q" else (KT, KTB)
            CP("act", dstT[:, :, b * bs:(b + 1) * bs], pt[:, 0:HA * bs].rearrange("p (h t) -> p h t", h=HA), [psB[kb_]], [dstB[b]])
            if which == "k" and b == nblk - 1 and tp["KT_dst"] is not None:
                DMA("sp", tp["KT_dst"].rearrange("h p t -> p h t"), KT[:, :, 0:ntok], KTB[0:nblk], [tp["KTB"]], KTB[0])

        pend = None
        nz = 0
        for which in ("q", "k", "v"):
            c0 = {"q": C_QA, "k": C_KA, "v": C_VA}[which]
            w0, w0B = wnext(("in", l, c0, 512))
            w1, w1B = wnext(("in", l, c0 + 512, 512))
            for b in range(nblk):
                zb, zbB = zfs[nz % 2]
                nz += 1
                for hf, (ws, wB) in enumerate(((w0, w0B), (w1, w1B))):
                    k, o = proj_tok(ws, wB, b, 512)
                    CP("act" if which != "v" else eveng(), zb[0:bs, hf * 512:(hf + 1) * 512], o, [psB[k]], [zbB])
                if pend is not None:
                    post(*pend)
                pend = (which, b, zb, zbB)
            wdone(2)
        post(*pend)

        sch.phase = "qkm"
        m_pend = [None]
        for g in range(4):
            ws, wB = wnext(("in", l, C_QKM + g * 512, 512))
            for cc4 in range(4):
                cc = g * 4 + cc4
                k = pbank()
                o = ps[:, k, 0:ntok]
                for kc in range(8):
                    MM(o, ws[:, kc, cc4 * 128:(cc4 + 1) * 128], xT[:, kc, 0:ntok], kc == 0, kc == 7, [wB] + xTB[0:nblk], [psB[k]])
                pi = cc % 2
                pcv = pcm[pi]
                CP("pool", pcv[:, 0:3], mhalo[:, l, cc, :], [mhaloB[l]], [pcmH[pi]])
                CP("act", pcv[:, 3:3 + ntok], o, [psB[k]], [pcmB[pi]])
                a = acc[pi][:, 0:ntok]
                ACT(a, o, AF.Identity, [psB[k], prmB], [accB[pi]], bias=mcb_s[:, l, cc:cc + 1], scale=mcw_s[:, l, cc, 3:4])
                CP("pool", mhalo[:, l, cc, :], pcv[:, ntok:ntok + 3], [pcmB[pi]], [mhaloB[l]])
                for j in range(3):
                    STT(a, pcv[:, j:j + ntok], mcw_s[:, l, cc, j:j + 1], a, ALU.mult, ALU.add, [pcmB[pi], pcmH[pi], prmB, accB[pi]], [accB[pi]])
                if m_pend[0] is not None:
                    ACT(qkT[:, m_pend[0][0], 0:ntok], m_pend[0][1], AF.Silu, [accB[m_pend[0][2]]], [qkTB[m_pend[0][0]]])
                m_pend[0] = (cc, a, pi)
            wdone(1)
        ACT(qkT[:, m_pend[0][0], 0:ntok], m_pend[0][1], AF.Silu, [accB[m_pend[0][2]]], [qkTB[m_pend[0][0]]])
        if last:
            for g in range(4):
                ws, wB = wnext(("in", l, C_QKM + g * 512, 512))
                k, o = proj_tok(ws, wB, nblk - 1, 512)
                CP("dve", stg[0:bs, :], o, [psB[k]], [stgB])
                DMA("sp", tp["mc_out"][l, :, g * 512:(g + 1) * 512], stg[bs - 3:bs, :], [stgB], [], stgB)
                wdone(1)
        sch.phase = "vm"
        w0, w0B = wnext(("in", l, C_VM, 512))
        w1, w1B = wnext(("in", l, C_VM + 512, 512))
        for b in range(nblk):
            for hf, (ws, wB) in enumerate(((w0, w0B), (w1, w1B))):
                k, o = proj_tok(ws, wB, b, 512)
                CP(eveng(), VMaug[0:bs, b, 2 * hf:2 * hf + 2, 0:256], o.rearrange("p (h d) -> p h d", h=2), [psB[k]], [VMB[b]])
        wdone(2)
        sch.phase = "gates"
        ws, wB = wnext(("in", l, C_GIF, 8))
        kI = pbank(); kF = pbank()
        for kc in range(8):
            MM(ps[0:4, kI, 0:ntok], ws[:, kc, 0:4], xT[:, kc, 0:ntok], kc == 0, kc == 7, [wB] + xTB[0:nblk], [psB[kI]])
        for kc in range(8):
            MM(ps[0:4, kF, 0:ntok], ws[:, kc, 4:8], xT[:, kc, 0:ntok], kc == 0, kc == 7, [wB] + xTB[0:nblk], [psB[kF]])
        wdone(1)
        g_ = {n: gw[n][:, 0:ntok] for n in gw}
        s_ = {n: gs[n][:, 0:nch] for n in gs}
        ACT(g_["ig"], ps[0:4, kI, 0:ntok], AF.Identity, [psB[kI], prmB], [gwB["ig"]], bias=bif_s[:, l, 0:1])
        ACT(g_["t1"], ps[0:4, kF, 0:ntok], AF.Exp, [psB[kF], prmB], [gwB["t1"]], bias=nbif_s[:, l, 1:2], scale=-1.0)
        TS(g_["t1"], g_["t1"], 1.0, None, ALU.add, None, [gwB["t1"]], [gwB["t1"]])
        ACT(g_["sp"], g_["t1"], AF.Ln, [gwB["t1"]], [gwB["sp"]])
        rm = resetm[:, 0:ntok] if L == 64 else resetm16[:, 0:ntok]
        sch.add("dve", "tensor_tensor_scan", (), dict(out=g_["A"], data0=rm, data1=g_["sp"], initial=0.0, op0=ALU.mult, op1=ALU.add),
                [gwB["sp"], constB], [gwB["A"]])
        TT(g_["r"], g_["ig"], g_["A"], ALU.add, [gwB["ig"], gwB["A"]], [gwB["r"]])
        RED(s_["cm"], g_["r"].rearrange("p (c l) -> p c l", l=L), ALU.max, [gwB["r"]], [gsB["cm"]])
        TS(s_["aL"], g_["A"].rearrange("p (c l) -> p c l", l=L)[:, :, L - 1], -1.0, None, ALU.mult, None, [gwB["A"]], [gsB["aL"]])
        sch.add("dve", "tensor_tensor_scan", (), dict(out=s_["mn"], data0=s_["cm"], data1=s_["aL"], initial=mstate[:, l:l + 1],
                                                       op0=ALU.max, op1=ALU.add), [gsB["cm"], gsB["aL"], mstateB[l]], [gsB["mn"]])
        TT(s_["mu"], s_["mn"], s_["aL"], ALU.subtract, [gsB["mn"], gsB["aL"]], [gsB["mu"]])
        CP("dve", gs["mpv"][:, 0:1], mstate[:, l:l + 1], [mstateB[l]], [gsB["mpv"]])
        if nch > 1:
            CP("dve", gs["mpv"][:, 1:nch], gs["mn"][:, 0:nch - 1], [gsB["mn"]], [gsB["mpv"]])
        CP("dve", mstate[:, l:l + 1], gs["mn"][:, nch - 1:nch], [gsB["mn"], gsB["mpv"]], [mstateB[l]])
        TT(s_["gam"], s_["mpv"], s_["mu"], ALU.subtract, [gsB["mpv"], gsB["mu"]], [gsB["gam"]])
        ACT(s_["gam"], s_["gam"], AF.Exp, [gsB["gam"]], [gsB["gam"]])
        mub = s_["mu"].rearrange("p (c o) -> p c o", o=1).to_broadcast([4, nch, L])
        TT(g_["sc"].rearrange("p (c l) -> p c l", l=L), g_["r"].rearrange("p (c l) -> p c l", l=L), mub, ALU.subtract, [gwB["r"], gsB["mu"]], [gwB["sc"]])
        ACT(g_["sc"], g_["sc"], AF.Exp, [gwB["sc"]], [gwB["sc"]])
        TS(g_["sc"], g_["sc"], 1.0 / 16.0, None, ALU.mult, None, [gwB["sc"]], [gwB["sc"]])
        TT(g_["cl"].rearrange("p (c l) -> p c l", l=L), g_["A"].rearrange("p (c l) -> p c l", l=L), mub, ALU.subtract, [gwB["A"], gsB["mu"]], [gwB["cl"]])
        ACT(g_["cl"], g_["cl"], AF.Exp, [gwB["cl"]], [gwB["cl"]])
        kG = pbank()
        pg = ps[0:bs, kG, 0:nblk * 8].rearrange("p (b e) -> p b e", e=8)
        for b in range(nblk):
            MM(pg[:, b, 0:4], g_["sc"][:, b * bs:(b + 1) * bs], identf[0:4, 0:4], True, True, [gwB["sc"], constB], [psB[kG]])
            MM(pg[:, b, 4:8], g_["cl"][:, b * bs:(b + 1) * bs], identf[0:4, 0:4], True, True, [gwB["cl"], constB], [psB[kG]])
        CP("dve", toksc[0:bs, 0:nblk, :], pg, [psB[kG]], [tokscB])
        TT(gD[:, 0:nch, :], s_["gam"].rearrange("p (c o) -> p c o", o=1).to_broadcast([4, nch, 4]),
           identf[0:4, 0:4].rearrange("p (o h) -> p o h", o=1).to_broadcast([4, nch, 4]), ALU.mult, [gsB["gam"], constB], [gDB])
        kG2 = pbank()
        MM(ps[:, kG2, 0:nch * 4], ones4, gD[:, 0:nch, :].rearrange("p c h -> p (c h)"), True, True, [gDB, constB], [psB[kG2]])
        CP("dve", gam[:, 0:nch, :], ps[:, kG2, 0:nch * 4].rearrange("p (c h) -> p c h", h=4), [psB[kG2]], [gamB])

        sch.phase = "attn"
        prior = tp["prior"]
        ngr = len(prior) + 1
        att = dict(si=0)
        sub_pend = [None]

        def subln_head(h):
            ovh = oaF[0:bs, 0:nblk, h * 128:(h + 1) * 128]
            sqv = hmf[0:bs, 512:512 + nblk * 128].rearrange("p (q d) -> p q d", d=128)
            TT(sqv, ovh, ovh, ALU.mult, oaFB[0:nblk], [hmfB])
            RED(sml[0:bs, 64 + h * 4:64 + h * 4 + nblk], sqv, ALU.add, [hmfB], [sml2B])

        for h in range(HA):
            started = set()
            items = []
            for g in range(ngr):
                own = g == ngr - 1
                for kb in range(nblk if own else T // 128):
                    items.append((g, own, kb))

            def emit_scores(it):
                g, own, kb = it
                if not own:
                    sl = (h * ngr + g) % 2
                    if kb == 0:
                        kd, vd, dB = prior[g]
                        DMA("sp", kch[sl][:, 0:T], kd[h], dB, [kchB[sl]], kchB[sl])
                        DMA("sp", vch[sl][:, :, 0:128], vd[:, h * 128:(h + 1) * 128].rearrange("(b p) d -> p b d", p=128), dB, [vchB[sl]], vchB[sl])
                    kT_ = kch[sl][:, kb * 128:(kb + 1) * 128]; kTB_ = kchB[sl]
                    v_ = vch[sl][:, kb, :]; vB_ = vchB[sl]
                    q0 = 0
                    nk = 128
                else:
                    kT_ = KT[:, h, kb * bs:(kb + 1) * bs]; kTB_ = KTB[kb]
                    v_ = Vaug[0:bs, kb, h, :]; vB_ = VaugB[kb]
                    q0 = kb * bs if tp["mask"] else 0
                    nk = bs
                nq = ntok - q0
                sb_ = att["si"] % 2
                att["si"] += 1
                b0, b1 = 2 * sb_, 2 * sb_ + 1
                MM(ps[0:nk, b0, 0:nq], kT_[0:64, :], QT[0:64, h, q0:ntok], True, True, [kTB_] + QTB[0:nblk], [psB[b0]])
                MM(ps[0:nk, b1, 0:nq], kT_[64:128, :], QT[64:128, h, q0:ntok], True, True, [kTB_] + QTB[0:nblk], [psB[b1]])
                ACT(PT[sb_][0:nk, :, 0:nq], ps[0:nk, b0:b1 + 1, 0:nq], AF.Exp, [psB[b0], psB[b1]], [PTB[sb_]], scale=0.125)
                if own and tp["mask"]:
                    MEMSET("pool", PT[sb_][64:128, :, 0:64], 0.0, [PTB[sb_]], [PTB[sb_]])
                return (g, own, kb, sb_, nk, q0, v_, vB_)

            def emit_av(st):
                g, own, kb, sb_, nk, q0, v_, vB_ = st
                nkb = nblk if own else T // 128
                qb0 = kb if (own and tp["mask"]) else 0
                for qb in range(qb0, nblk):
                    for mp_ in range(2):
                        a_ = qb * 2 + mp_
                        bank = 4 + a_ // 3
                        off = (a_ % 3) * 129
                        col = qb * bs - q0
                        first = (g == 0 and kb == 0) and bank not in started
                        started.add(bank)
                        lastk = own and (kb == (qb if tp["mask"] else nkb - 1))
                        MM(ps[0:bs, bank, off:off + 129], PT[sb_][0:nk, mp_, col:col + bs], v_, first, lastk, [PTB[sb_], vB_], [psB[bank]], skip=True)

            prev = None
            for it in items:
                st = emit_scores(it)
                if prev is not None:
                    emit_av(prev)
                prev = st
            emit_av(prev)
            CP("dve", zf[0:bs, 0:387], ps[0:bs, 4, 0:387], [psB[4]], [zfB])
            CP("dve", zf[0:bs, 387:774], ps[0:bs, 5, 0:387], [psB[5]], [zfB])
            CP("dve", hmf[0:bs, 0:258], ps[0:bs, 6, 0:258], [psB[6]], [hmfB])

            def accv(a_):
                if a_ < 6:
                    return zf[0:bs, a_ * 129:(a_ + 1) * 129], zfB
                return hmf[0:bs, (a_ - 6) * 129:(a_ - 5) * 129], hmfB

            for qb in range(nblk):
                o0, o0B = accv(qb * 2)
                o1, o1B = accv(qb * 2 + 1)
                r0 = sml[0:bs, 0:1]; r1 = sml[0:bs, 1:2]
                sch.add("dve", "reciprocal", (), dict(out=r0, in_=o0[:, 128:129]), [o0B], [smlB])
                sch.add("dve", "reciprocal", (), dict(out=r1, in_=o1[:, 128:129]), [o1B], [smlB])
                TT(r1, r1, neglam[0:bs, l:l + 1], ALU.mult, [smlB, prmB], [smlB])
                TS(o0[:, 0:128], o0[:, 0:128], r0, None, ALU.mult, None, [o0B, smlB], [o0B])
                STT(oaF[0:bs, qb, h * 128:(h + 1) * 128], o1[:, 0:128], r1, o0[:, 0:128], ALU.mult, ALU.add,
                    [o1B, o0B, smlB], [oaFB[qb]])
            if sub_pend[0] is not None:
                subln_head(sub_pend[0])
            sub_pend[0] = h
        subln_head(sub_pend[0])
        RSQRT(sml[0:bs, 64:96], 128.0 * LN_EPS, [sml2B], [sml2B])
        for qb in range(nblk):
            ov = oaF[0:bs, qb, :].rearrange("p (h d) -> p h d", h=HA)
            rs = sml[0:bs, 64:96].rearrange("p (h q) -> p h q", q=4)[:, :, qb:qb + 1].to_broadcast([bs, HA, 128])
            TT(ov, ov, rs, ALU.mult, [oaFB[qb], sml2B], [oaFB[qb]])
            TT(ov, ov, subg_s[0:bs, l, :].rearrange("p (o d) -> p o d", o=1).to_broadcast([bs, HA, 128]), ALU.mult, [oaFB[qb], prmB], [oabB[qb]])

        sch.phase = "mlstm"
        for b in range(nblk):
            kb_ = tbank()
            pt = psb16(kb_)
            for j in range(8):
                TR(pt[0:bs, j * 128:(j + 1) * 128], qkT[:, 8 + j, b * bs:(b + 1) * bs], [qkTB[8 + j], constB], [psB[kb_]])
            for h in range(HM):
                ACT(kw[0:bs, h * 256:(h + 1) * 256], pt[0:bs, h * 256:(h + 1) * 256], AF.Identity, [psB[kb_], tokscB], [kwB], scale=toksc[0:bs, b, h:h + 1])
            kS = pbank()
            for h in range(HM):
                for dk in range(2):
                    MM(ps[0:bs, kS, h * 128:h * 128 + bs], qkT[:, 8 + 2 * h + dk, b * bs:(b + 1) * bs], qkT[:, 2 * h + dk, b * bs:(b + 1) * bs],
                       dk == 0, dk == 1, [qkTB[8 + 2 * h + dk], qkTB[2 * h + dk]], [psB[kS]])
            for h in range(HM):
                STT(Sm[0:bs, h, 0:bs], ps[0:bs, kS, h * 128:h * 128 + bs], toksc[0:bs, b, h:h + 1], maskBD[0:bs, 0:bs], ALU.mult, ALU.mult,
                    [psB[kS], tokscB, constB], [SmB])
            for ci in range(cpb):
                c = b * cpb + ci
                p0, p1 = ci * L, ci * L + L
                t0_, t1_ = b * bs + ci * L, b * bs + ci * L + L
                Gc, GcB = GbL[c % 2]
                if c == 0:
                    for h in range(HM):
                        ACT(Gc[:, :, h, :], Cg[:, :, h, :], AF.Identity, [CgB[h], CgB2[h], gamB], [GcB[h]], scale=gam[:, c, h:h + 1])
                for h in range(HM):
                    for dk in range(2):
                        kC = 4 + (h * 2 + dk) % 4
                        dC = ps[:, kC, 0:257]
                        MM(dC, kw[p0:p1, h * 256 + dk * 128:h * 256 + dk * 128 + 128], VMaug[p0:p1, b, h, :], True, True, [kwB, VMB[b]], [psB[kC]])
                        STT(Cg[:, dk, h, :], Cg[:, dk, h, :], gam[:, c, h:h + 1], dC, ALU.mult, ALU.add, [(CgB, CgB2)[dk][h], gamB, psB[kC]], [(CgB, CgB2)[dk][h]])
                if c + 1 < nch:
                    Gn, GnB = GbL[(c + 1) % 2]
                    for h in range(HM):
                        ACT(Gn[:, :, h, :], Cg[:, :, h, :], AF.Identity, [CgB[h], CgB2[h], gamB], [GnB[h]], scale=gam[:, c + 1, h:h + 1])
                for h in range(HM):
                    nd = ps[p0:p1, h, 0:257]
                    MM(nd, qkT[:, 2 * h, t0_:t1_], Gc[:, 0, h, :], True, False, [qkTB[2 * h], GcB[h]], [psB[h]])
                    MM(nd, qkT[:, 2 * h + 1, t0_:t1_], Gc[:, 1, h, :], False, False, [qkTB[2 * h + 1], GcB[h]], [psB[h]])
                    MM(nd, Sm[p0:p1, h, p0:p1], VMaug[p0:p1, b, h, :], False, True, [SmB, VMB[b]], [psB[h]])
                den = ps[p0:p1, 0:HM, 256]
                dd = sml[p0:p1, 16:16 + HM]
                TS(dd, den, -1.0, None, ALU.mult, None, psB[0:HM], [smlB])
                TT(dd, dd, den, ALU.max, [smlB] + psB[0:HM], [smlB])
                TT(dd, dd, toksc[p0:p1, b, 4:4 + HM], ALU.max, [smlB, tokscB], [smlB])
                sch.add("dve", "reciprocal", (), dict(out=dd, in_=dd), [smlB], [smlB])
                for h in range(HM):
                    ACT(hmf[p0:p1, h * 256:(h + 1) * 256], ps[p0:p1, h, 0:256], AF.Identity, [psB[h], smlB], [hmfB], scale=sml[p0:p1, 16 + h:17 + h])
            for h in range(HM):
                hv_ = hmf[0:bs, h * 256:(h + 1) * 256]
                ACT(zf[0:bs, h * 256:(h + 1) * 256], hv_, AF.Identity, [hmfB], [zfB, smlB], accum_out=sml[0:bs, 24 + h:25 + h])
                ACT(zf[0:bs, h * 256:(h + 1) * 256], hv_, AF.Square, [hmfB], [zfB, smlB], accum_out=sml[0:bs, 28 + h:29 + h])
            mean_ = sml[0:bs, 24:28]; ex2_ = sml[0:bs, 28:32]; nmr_ = sml[0:bs, 32:36]
            TS(mean_, mean_, 1.0 / 256.0, None, ALU.mult, None, [smlB], [smlB])
            TT(nmr_, mean_, mean_, ALU.mult, [smlB], [smlB])
            STT(ex2_, ex2_, 1.0 / 256.0, nmr_, ALU.mult, ALU.subtract, [smlB], [smlB])
            RSQRT(ex2_, LN_EPS, [smlB], [smlB])
            STT(nmr_, mean_, -1.0, ex2_, ALU.mult, ALU.mult, [smlB], [smlB])
            for h in range(HM):
                hv_ = hmf[0:bs, h * 256:(h + 1) * 256]
                ACT(hv_, hv_, AF.Identity, [hmfB, smlB], [hmfB], bias=sml[0:bs, 32 + h:33 + h], scale=sml[0:bs, 28 + h:29 + h])
            TT(hmb[0:bs, b, :], hmf[0:bs, :], lnp[0:bs, 0, :], ALU.mult, [hmfB, lnpB[0]], [hmbB[b]])

        sch.phase = "merge"
        for gi, c0 in enumerate((C_OM, C_OM + 512, C_GA, C_GA + 512, C_GB, C_GB + 512)):
            ws, wB = wnext(("in", l, c0, 512))
            hf = gi % 2
            for b in range(nblk):
                k, o = proj_tok(ws, wB, b, 512)
                si_ = (gi * nblk + b) % 2
                ACT(sg[si_][0:bs, :], o, AF.Sigmoid, [psB[k]], [sgB[si_]])
                if gi in (2, 3):
                    tgt, tB = oab, oabB
                else:
                    tgt, tB = hmb, hmbB
                TT(tgt[0:bs, b, hf * 512:(hf + 1) * 512], tgt[0:bs, b, hf * 512:(hf + 1) * 512], sg[si_][0:bs, :], ALU.mult, [sgB[si_], tB[b]], [tB[b]])
            wdone(1)
        for b in range(nblk):
            TT(ybf[0:bs, :], oab[0:bs, b, :], hmb[0:bs, b, :], ALU.add, [oabB[b], hmbB[b]], [ybfB])
            k = tbank()
            pt = psb16(k)
            for c in range(8):
                TR(pt[:, c * bs:(c + 1) * bs], ybf[0:bs, c * 128:(c + 1) * 128], [ybfB, constB], [psB[k]])
            CP(eveng(), xT[:, :, b * bs:(b + 1) * bs], pt[:, 0:8 * bs].rearrange("p (c t) -> p c t", c=8), [psB[k]], [xTB[b]])

        def layernorm_inplace(b, gi, bi):
            xv = xtok[0:bs, b, :]
            for hh in range(2):
                sch.add("dve", "bn_stats", (), dict(out=stt[0:bs, hh, :], in_=xtok[0:bs, b, hh * 512:(hh + 1) * 512]), [xtokB[b]], [lnB])
            sch.add("dve", "bn_aggr", (), dict(out=mv[0:bs, 0:2], in_=stt[0:bs, :, :]), [lnB], [lnB])
            CP("dve", mv[0:bs, 2:3], mv[0:bs, 1:2], [lnB], [lnB])
            RSQRT(mv[0:bs, 2:3], LN_EPS, [lnB], [lnB])
            STT(mv[0:bs, 3:4], mv[0:bs, 0:1], -1.0, mv[0:bs, 2:3], ALU.mult, ALU.mult, [lnB], [lnB])
            ACT(xv, xv, AF.Identity, [xtokB[b], lnB], [xtokB[b]], bias=mv[0:bs, 3:4], scale=mv[0:bs, 2:3])
            TT(xv, xv, lnp[0:bs, 0, :], ALU.mult, [xtokB[b], lnpB[0]], [xtokB[b]])
            TT(xv, xv, lnp[0:bs, 1, :], ALU.add, [xtokB[b], lnpB[1]], [xtokB[b]])

        sch.phase = "wout"
        load_lnp(0, ln1g); load_lnp(1, ln1b)
        w0, w0B = wnext(("out", l, 0, 512))
        w1, w1B = wnext(("out", l, 512, 512))
        for b in range(nblk):
            for hf, (ws, wB) in enumerate(((w0, w0B), (w1, w1B))):
                k, o = proj_tok(ws, wB, b, 512)
                xs_ = xtok[0:bs, b, hf * 512:(hf + 1) * 512]
                STT(xs_, xs_, ALPHA, o, ALU.mult, ALU.add, [xtokB[b], psB[k]], [xtokB[b]])
            layernorm_inplace(b, 1, 2)
        wdone(2)
        make_xT()

        if not tp["load_state"]:
            sample_prep(prep_per_step)
        sch.phase = "up"
        up_pend = [None]

        def up_final(cc, a, pi):
            if cc < 22:
                ACT(hT[:, cc, 0:ntok], a, AF.Gelu, [accB[pi]], [hTB[cc]])
            else:
                TT(hT[:, cc - 22, 0:ntok], hT[:, cc - 22, 0:ntok], a, ALU.mult, [hTB[cc - 22], accB[pi]], [hTB[cc - 22]])

        for g in range(11):
            ws, wB = wnext(("up", l, g * 512, 512))
            for cc4 in range(4):
                cc = g * 4 + cc4
                k = pbank()
                o = ps[:, k, 0:ntok]
                for kc in range(8):
                    MM(o, ws[:, kc, cc4 * 128:(cc4 + 1) * 128], xT[:, kc, 0:ntok], kc == 0, kc == 7, [wB] + xTB[0:nblk], [psB[k]])
                pi = cc % 2
                pcv = pcm[pi]
                CP("pool", pcv[:, 0:2], fhalo[:, l, cc, :], [fhaloB[l]], [pcmH[pi]])
                CP("act", pcv[:, 2:2 + ntok], o, [psB[k]], [pcmB[pi]])
                a = acc[pi][:, 0:ntok]
                ACT(a, o, AF.Identity, [psB[k], prmB], [accB[pi]], bias=fcb_s[:, l, cc:cc + 1], scale=fcw_s[:, l, cc, 2:3])
                CP("pool", fhalo[:, l, cc, :], pcv[:, ntok:ntok + 2], [pcmB[pi]], [fhaloB[l]])
                for j in range(2):
                    STT(a, pcv[:, j:j + ntok], fcw_s[:, l, cc, j:j + 1], a, ALU.mult, ALU.add, [pcmB[pi], pcmH[pi], prmB, accB[pi]], [accB[pi]])
                if up_pend[0] is not None:
                    up_final(*up_pend[0])
                up_pend[0] = (cc, a, pi)
            wdone(1)
        up_final(*up_pend[0])
        if last:
            for g in range(11):
                ws, wB = wnext(("up", l, g * 512, 512))
                k, o = proj_tok(ws, wB, nblk - 1, 512)
                CP("dve", stg[0:bs, :], o, [psB[k]], [stgB])
                DMA("sp", tp["fc_out"][l, :, g * 512:(g + 1) * 512], stg[bs - 2:bs, :], [stgB], [], stgB)
                wdone(1)
        sch.phase = "down"
        load_lnp(0, ln2g); load_lnp(1, ln2b)
        for nh in range(2):
            pcs = [wnext(("down", l, nh, pc)) for pc in range(3)]
            for b in range(nblk):
                k = pbank()
                o = ps[0:bs, k, 0:512]
                for kc in range(22):
                    ws, wB = pcs[kc // 8]
                    MM(o, hT[:, kc, b * bs:(b + 1) * bs], ws[:, kc % 8, :], kc == 0, kc == 21, [wB, hTB[kc]], [psB[k]])
                xs_ = xtok[0:bs, b, nh * 512:(nh + 1) * 512]
                STT(xs_, xs_, ALPHA, o, ALU.mult, ALU.add, [xtokB[b], psB[k]], [xtokB[b]])
                if nh == 1:
                    layernorm_inplace(b, 3, 4)
                    if l == NL - 1:
                        DMA("sp", tp["y_out"][b * bs:(b + 1) * bs, :], xtok[0:bs, b, :], [xtokB[b]], [], xtokB[b])
            wdone(3)
        if last:
            for h in range(HM):
                DMA("sp", tp["C_out"][l, h].rearrange("(c p) v -> p c v", p=128), Cg[:, :, h, 0:256], [CgB[h], CgB2[h]], [], CgB[h])
                DMA("sp", tp["n_out"][l, h].rearrange("(c p o) -> p c o", p=128, o=1), Cg[:, :, h, 256:257], [CgB[h], CgB2[h]], [], CgB[h], slow=True)
            DMA("sp", tp["m_out"][l].rearrange("(p o) -> p o", o=1), mstate[:, l:l + 1], [mstateB[l]], [], mstateB[l], slow=True)

    prep_q = [(l, blk) for l in range(NL) for blk in range(P // 128)]

    def sample_prep(n):
        ph = sch.phase
        sch.phase = "prep"
        for _ in range(n):
            if not prep_q:
                break
            l, blk = prep_q.pop(0)
            g = (blk * 128) // T
            DMA("pool", vtmp, ck[l, blk * 128:(blk + 1) * 128, :], [], [vtmpB], vtmpB)
            k = tbank()
            pt = psb16(k)
            for h in range(HA):
                TR(pt[:, h * 128:(h + 1) * 128], vtmp[:, h * 128:(h + 1) * 128], [vtmpB, constB], [psB[k]])
            CP(eveng(), ktmp, pt.rearrange("p (h t) -> p h t", h=HA), [psB[k]], [ktmpB])
            DMA("sp", KTs[l][:, :, blk * 128:(blk + 1) * 128].rearrange("h p t -> p h t"), ktmp, [ktmpB], [KTsB[l][g]], ktmpB)
            DMA("pool", ybf, cv[l, blk * 128:(blk + 1) * 128, :], [], [ybfB], ybfB)
            DMA("sp", Vbs[l, blk * 128:(blk + 1) * 128, :], ybf, [ybfB], [VbsB[l][g]], ybfB)
        sch.phase = ph

    steps = []
    for i in range(NT):
        for l in range(NL):
            steps.append((l, i, False))
    for l in range(NL):
        steps.append((l, 0, True))
    for (l, i, samp) in steps:
        wq.extend(wspec_step(l, samp or i == NT - 1))

    prep_done = False
    prep_per_step = -(-len(prep_q) // max(1, NT * NL))
    for (l, i, samp) in steps:
        if samp and not prep_done:
            sample_prep(len(prep_q))
            prep_done = True
        if not samp:
            tp = dict(ntok=T, bs=128, nblk=NB, L=64, last=(i == NT - 1), first=(i == 0), load_state=False,
                      x_src=xp[i * T:(i + 1) * T, :], pos0=i * T, tok0=i * T, mask=True,
                      prior=[(KTp[l][:, :, g * T:(g + 1) * T], Vbp[l, g * T:(g + 1) * T, :], [KTpB[l][g], VbpB[l][g]]) for g in range(i)],
                      KT_dst=KTp[l][:, :, i * T:(i + 1) * T], KTB=KTpB[l][i], Vb_dst=Vbp[l, i * T:(i + 1) * T, :], VbB=VbpB[l][i],
                      k_out=kp[:, i * T:(i + 1) * T, :], v_out=vp[:, i * T:(i + 1) * T, :], y_out=yp[i * T:(i + 1) * T, :],
                      mc_out=mcp, fc_out=fcp, C_out=Cp, n_out=np_, m_out=mp)
        else:
            tp = dict(ntok=NS, bs=NS, nblk=1, L=NS, last=True, first=False, load_state=True,
                      x_src=xs, pos0=S, tok0=0, mask=False,
                      prior=[(KTs[l][:, :, g * T:(g + 1) * T], Vbs[l, g * T:(g + 1) * T, :], [KTsB[l][g], VbsB[l][g]]) for g in range(P // T)],
                      KT_dst=None, KTB=None, Vb_dst=None, VbB=None,
                      k_out=ks, v_out=vs, y_out=ys, mc_out=mcs, fc_out=fcs, C_out=Cs, n_out=ns_, m_out=ms,
                      sC=sC, sn=sn, sm=sm, smc=smc, sfc=sfc)
        step(l, tp)
    assert wstate["used"] == len(wq) and wstate["released"] == len(wq), (wstate, len(wq))
    print("SBUF/PSUM allocation done")
    info = sch.emit()
    return nc, info


def rope_tables(S, P):
    half = 32
    inv = (np.float32(10000.0) ** (-np.arange(half, dtype=np.float32) * np.float32(2.0) / np.float32(64))).astype(np.float32)
    pos = np.concatenate([np.arange(S), P + np.arange(NS)]).astype(np.float32)
    ang = (pos[:, None] * inv[None, :]).astype(np.float32)
    return np.cos(ang).astype(np.float32), np.sin(ang).astype(np.float32)


_CACHE = {}


def run(inputs, S, P, T, n_prompt, n_sample, n_cores):
    key = (S, P, T)
    if key not in _CACHE:
        _CACHE[key] = build(S=S, P=P, T=T)
    nc, info = _CACHE[key]
    f = lambda a: np.ascontiguousarray(np.asarray(a, dtype=np.float32))
    cosT, sinT = rope_tables(S, P)
    NL = 2
    in_maps = []
    for c in range(n_cores):
        b = c % n_prompt
        s = c % n_sample
        m = {
            "xp": f(inputs["x_prompt"][b]), "xs": f(inputs["x_sample"][s]),
            "ck": f(inputs["cache_k"][:, s]).reshape(NL, P, D), "cv": f(inputs["cache_v"][:, s]).reshape(NL, P, D),
            "smc": f(inputs["state_mlstm_conv"][:, s]), "sC": f(inputs["state_mlstm_C"][:, s]),
            "sn": f(inputs["state_mlstm_n"][:, s]), "sm": f(inputs["state_mlstm_m"][:, s]),
            "sfc": f(inputs["state_ffn_conv"][:, s]),
            "w_in": f(inputs["w_in"]), "b_if": f(inputs["b_if"]), "mcw": f(inputs["mlstm_conv_w"]), "mcb": f(inputs["mlstm_conv_b"]),
            "dlam": f(inputs["diff_lambda"]), "subg": f(inputs["diff_subln_g"]), "mhg": f(inputs["mlstm_norm_g"]),
            "w_out": f(inputs["w_out"]), "ln1g": f(inputs["ln1_g"]), "ln1b": f(inputs["ln1_b"]),
            "w_up": f(inputs["w_up"]), "fcw": f(inputs["ffn_conv_w"]), "fcb": f(inputs["ffn_conv_b"]),
            "w_down": f(inputs["w_down"]), "ln2g": f(inputs["ln2_g"]), "ln2b": f(inputs["ln2_b"]),
            "cosT": cosT, "sinT": sinT,
        }
        in_maps.append(m)
    res = run_bass_kernel_spmd(nc, in_maps, core_ids=list(range(n_cores)))
    R = res.results
    pc = list(range(n_prompt))
    sc = list(range(n_sample))
    st = lambda name, cores, ax=0: np.stack([np.asarray(R[c][name], dtype=np.float32) for c in cores], axis=ax)
    y_prompt = st("yp", pc)
    y_sample = st("ys", sc)
    k_prompt = st("kp", pc, 1).reshape(NL, n_prompt, S, HA, 128)
    v_prompt = st("vp", pc, 1).reshape(NL, n_prompt, S, HA, 128)
    outs = (y_prompt, y_sample, k_prompt, v_prompt,
            st("mcp", pc, 1), st("Cp", pc, 1), st("np", pc, 1), st("mp", pc, 1), st("fcp", pc, 1),
            st("ks", sc, 1).reshape(NL, n_sample, NS, HA, 128), st("vs", sc, 1).reshape(NL, n_sample, NS, HA, 128),
            st("mcs", sc, 1), st("Cs", sc, 1), st("ns", sc, 1), st("ms", sc, 1), st("fcs", sc, 1))
    return outs


def kernel(**inputs):
    return run(inputs, S=8192, P=4096, T=512, n_prompt=4, n_sample=8, n_cores=8)
```

```python
import math
import numpy as np
import concourse.bass as bass
import concourse.mybir as mybir
from concourse.bass_utils import run_bass_kernel_spmd

F32 = mybir.dt.float32
BF16 = mybir.dt.bfloat16
AF = mybir.ActivationFunctionType
ALU = mybir.AluOpType
AX = mybir.AxisListType

D = 1024
HA = 8
HM = 4
DFF = 2816
DIN = 9224
NS = 16
ALPHA = (2 * 2) ** 0.25
LN_EPS = 1e-5
C_QA, C_KA, C_VA, C_QKM, C_VM, C_OM, C_GIF, C_GA, C_GB = 0, 1024, 2048, 3072, 5120, 6144, 7168, 7176, 8200


RAW_ONLY = True


class Buf:
    __slots__ = ("name", "lastw", "readers")

    def __init__(self, name=""):
        self.name = name
        self.lastw = None
        self.readers = []


class Sched:
    def __init__(self, nc, same_engine_sync=True):
        self.nc = nc
        self.engs = {"pe": nc.tensor, "act": nc.scalar, "dve": nc.vector, "pool": nc.gpsimd, "sp": nc.sync}
        self.ins = []
        self.dma_cnt = {}
        self.same = same_engine_sync
        self.raw_only = RAW_ONLY
        self.phase = ""
        self.names = None

    def add(self, eng, meth, args, kwargs, reads=(), writes=(), dma=None):
        idx = len(self.ins)
        deps = set()
        raw = set()
        for r in reads:
            if r.lastw is not None:
                deps.add(r.lastw)
                raw.add(r.lastw)
        for w in writes:
            if w.lastw is not None:
                deps.add(w.lastw)
            deps.update(w.readers)
        for r in reads:
            r.readers.append(idx)
        for w in writes:
            w.lastw = idx
            w.readers = []
        dval = None
        if dma is not None:
            self.dma_cnt[dma] = self.dma_cnt.get(dma, 0) + 16
            dval = self.dma_cnt[dma]
        keep = set()
        for d in deps:
            de = self.ins[d]
            if de[5] is None and de[0] == eng:
                if eng == "pe" or not self.same:
                    continue
                if self.raw_only and d not in raw:
                    continue
            keep.add(d)
        self.ins.append([eng, meth, args, kwargs, keep, dma, dval, False, 0, self.phase])
        return idx

    def emit(self):
        nc = self.nc
        for rec in self.ins:
            for d in rec[4]:
                de = self.ins[d]
                if de[5] is None:
                    de[7] = True
        cnt = {e: 0 for e in self.engs}
        for rec in self.ins:
            if rec[7]:
                cnt[rec[0]] += 1
                rec[8] = cnt[rec[0]]
        esem = {e: nc.alloc_semaphore(name="es_" + e) for e in self.engs}
        dsem = {}
        for k in self.dma_cnt:
            dsem[k] = nc.alloc_semaphore(name="ds_%d" % len(dsem))
        waited = {e: {} for e in self.engs}
        nwait = 0
        for rec in self.ins:
            eng, meth, args, kwargs, deps, dma, dval, sig, sigval, phase = rec
            E = self.engs[eng]
            need = {}
            for d in deps:
                de = self.ins[d]
                if de[5] is None:
                    s, v = esem[de[0]], de[8]
                else:
                    s, v = dsem[de[5]], de[6]
                if need.get(s, 0) < v:
                    need[s] = v
            for s, v in need.items():
                if waited[eng].get(s, 0) >= v:
                    continue
                E.wait_ge(s, v)
                waited[eng][s] = v
                nwait += 1
            ins = getattr(E, meth)(*args, **kwargs)
            if self.names is not None:
                self.names[ins.ins.name] = phase
            if dma is not None:
                ins.then_inc(dsem[dma], 16)
            elif sig:
                ins.then_inc(esem[eng], 1)
        for k, v in self.dma_cnt.items():
            nc.sync.wait_ge(dsem[k], v)
        return dict(n=len(self.ins), nwait=nwait, nsem=len(dsem) + 5)


def build(S=8192, P=4096, T=512, NL=2, dbg=None, same=True):
    nc = bass.Bass("TRN2", target_bir_lowering=False)
    sch = Sched(nc, same_engine_sync=same)
    NT = S // T
    NTAB = S + NS

    def din(name, shape, dt=F32):
        return nc.dram_tensor(name, list(shape), dt, kind="ExternalInput").ap()

    def dout(name, shape, dt=F32):
        return nc.dram_tensor(name, list(shape), dt, kind="ExternalOutput").ap()

    def dscr(name, shape, dt):
        return nc.dram_tensor(name, list(shape), dt, kind="Internal").ap()

    def sb(name, shape, dt=F32):
        return nc.alloc_sbuf_tensor(name, list(shape), dt).ap()

    xp = din("xp", [S, D]); xs = din("xs", [NS, D])
    ck = din("ck", [NL, P, D]); cv = din("cv", [NL, P, D])
    smc = din("smc", [NL, 3, 2048]); sC = din("sC", [NL, HM, 256, 256]); sn = din("sn", [NL, HM, 256])
    sm = din("sm", [NL, HM]); sfc = din("sfc", [NL, 2, 2 * DFF])
    w_in = din("w_in", [NL, D, DIN]); b_if = din("b_if", [NL, 8])
    mcw = din("mcw", [NL, 4, 2048]); mcb = din("mcb", [NL, 2048])
    dlam = din("dlam", [NL, 4, 64]); subg = din("subg", [NL, 128]); mhg = din("mhg", [NL, D])
    w_out = din("w_out", [NL, D, D]); ln1g = din("ln1g", [NL, D]); ln1b = din("ln1b", [NL, D])
    w_up = din("w_up", [NL, D, 2 * DFF]); fcw = din("fcw", [NL, 3, 2 * DFF]); fcb = din("fcb", [NL, 2 * DFF])
    w_down = din("w_down", [NL, DFF, D]); ln2g = din("ln2g", [NL, D]); ln2b = din("ln2b", [NL, D])
    cosT = din("cosT", [NTAB, 32]); sinT = din("sinT", [NTAB, 32])

    yp = dout("yp", [S, D]); ys = dout("ys", [NS, D])
    kp = dout("kp", [NL, S, D]); vp = dout("vp", [NL, S, D])
    mcp = dout("mcp", [NL, 3, 2048]); Cp = dout("Cp", [NL, HM, 256, 256]); np_ = dout("np", [NL, HM, 256])
    mp = dout("mp", [NL, HM]); fcp = dout("fcp", [NL, 2, 2 * DFF])
    ks = dout("ks", [NL, NS, D]); vs = dout("vs", [NL, NS, D])
    mcs = dout("mcs", [NL, 3, 2048]); Cs = dout("Cs", [NL, HM, 256, 256]); ns_ = dout("ns", [NL, HM, 256])
    ms = dout("ms", [NL, HM]); fcs = dout("fcs", [NL, 2, 2 * DFF])

    KTp = dscr("KTp", [NL, HA, 128, S], BF16); Vbp = dscr("Vbp", [NL, S, D], BF16)
    KTs = dscr("KTs", [NL, HA, 128, P], BF16); Vbs = dscr("Vbs", [NL, P, D], BF16)
    KTpB = [[Buf() for _ in range(NT)] for _ in range(NL)]
    VbpB = [[Buf() for _ in range(NT)] for _ in range(NL)]
    KTsB = [[Buf() for _ in range(max(1, P // T))] for _ in range(NL)]
    VbsB = [[Buf() for _ in range(max(1, P // T))] for _ in range(NL)]

    NB = T // 128
    xtok = sb("xtok", [128, NB, D]); xtokB = [Buf() for _ in range(NB)]
    xbf = sb("xbf", [128, D], BF16); xbfB = Buf()
    xT = sb("xT", [128, 8, T], BF16); xTB = [Buf() for _ in range(NB)]
    NSLOT = 4
    ring = [sb("ring%d" % i, [128, 8, 512], BF16) for i in range(NSLOT)]
    ringB = [Buf() for _ in range(NSLOT)]
    zf = sb("zf", [128, D]); zfB = Buf()
    ro = sb("ro", [128, D]); roB = Buf()
    cs_t = sb("cs_t", [128, NB, 2, 32]); csB = Buf()
    QT = sb("QT", [128, HA, T], BF16); QTB = [Buf() for _ in range(NB)]
    KT = sb("KT", [128, HA, T], BF16); KTB = [Buf() for _ in range(NB)]
    Vaug = sb("Vaug", [128, NB, HA, 129], BF16); VaugB = [Buf() for _ in range(NB)]
    VMaug = sb("VMaug", [128, NB, HM, 257], BF16); VMB = [Buf() for _ in range(NB)]
    oab = sb("oab", [128, NB, D], BF16); oabB = [Buf() for _ in range(NB)]
    hmb = sb("hmb", [128, NB, D], BF16); hmbB = [Buf() for _ in range(NB)]
    oaF = oab; oaFB = oabB
    hT = sb("hT", [128, 22, T], BF16); hTB = [Buf() for _ in range(22)]
    qkT = hT; qkTB = hTB
    pcm = [sb("pcm%d" % i, [128, 3 + T]) for i in range(2)]; pcmB = [Buf(), Buf()]; pcmH = [Buf(), Buf()]
    acc = [sb("acc%d" % i, [128, T]) for i in range(2)]; accB = [Buf(), Buf()]
    rt = acc; rtB = accB
    stg = pcm[0][:, 3:515]; stgB = pcmB[0]
    kch = [sb("kch%d" % i, [128, T], BF16) for i in range(2)]; kchB = [Buf(), Buf()]
    vch = [sb("vch%d" % i, [128, NB, 129], BF16) for i in range(2)]; vchB = [Buf(), Buf()]
    PT = [sb("PT%d" % i, [128, 2, T], BF16) for i in range(2)]; PTB = [Buf(), Buf()]
    Caug = sb("Caug", [128, 2, HM, 257]); CaugB = [Buf() for _ in range(HM)]
    GbRaw = sb("GbRaw", [128, 2 * HM * 257], BF16); GbB = [Buf() for _ in range(HM)]
    Gb = GbRaw.rearrange("p (a h d) -> p a h d", a=2, h=HM)
    Gb2 = sb("Gb2", [128, 2, HM, 257], BF16); Gb2B = [Buf() for _ in range(HM)]
    GbL = [(Gb, GbB), (Gb2, Gb2B)]
    ro2 = GbRaw[:, 0:2 * D].bitcast(F32); ro2B = GbB
    kw = sb("kw", [128, D], BF16); kwB = Buf()
    Sm = sb("Sm", [128, HM, 128], BF16); SmB = Buf()
    hmf = sb("hmf", [128, D]); hmfB = Buf()
    sml = sb("sml", [128, 96]); smlB = Buf(); sml2B = Buf()
    mhalo = sb("mhalo", [128, NL, 16, 3]); mhaloB = [Buf() for _ in range(NL)]
    fhalo = sb("fhalo", [128, NL, 44, 2]); fhaloB = [Buf() for _ in range(NL)]
    mstate = sb("mstate", [4, NL]); mstateB = [Buf() for _ in range(NL)]
    CaugL = [Caug, sb("Caug1", [128, 2, HM, 257])]
    CaugLB = [CaugB, [Buf() for _ in range(HM)]]
    CaugLB2 = [[Buf() for _ in range(HM)] for _ in range(NL)]
    _g3 = [PT[0].rearrange("p a t -> p (a t)").bitcast(F32)[0:4, :], PT[1].rearrange("p a t -> p (a t)").bitcast(F32)[0:4, :],
           kw.bitcast(F32)[0:4, :]]
    _g3B = [PTB[0], PTB[1], kwB]
    gw = {"t1": _g3[0], "sp": _g3[0], "A": _g3[1], "cl": _g3[1], "ig": _g3[2], "r": _g3[2], "sc": _g3[2]}
    gwB = {"t1": _g3B[0], "sp": _g3B[0], "A": _g3B[1], "cl": _g3B[1], "ig": _g3B[2], "r": _g3B[2], "sc": _g3B[2]}
    gs = {n: sb("gs_" + n, [4, 16]) for n in ("cm", "aL", "mn", "mu", "mpv", "gam")}
    gsB = {n: Buf() for n in gs}
    gD = sb("gD", [4, 16, 4]); gDB = Buf()
    toksc = sb("toksc", [128, NB, 8]); tokscB = Buf()
    gam = sb("gam", [128, 16, 4]); gamB = Buf()
    identb = sb("identb", [128, 128], BF16); identf = sb("identf", [128, 128]); maskBD = sb("maskBD", [128, 128])
    ones4 = sb("ones4", [4, 128]); resetm = sb("resetm", [4, T]); resetm16 = sb("resetm16", [4, NS])
    constB = Buf()
    mcw_s = sb("mcw_s", [128, NL, 16, 4]); mcb_s = sb("mcb_s", [128, NL, 16])
    fcw_s = sb("fcw_s", [128, NL, 44, 3]); fcb_s = sb("fcb_s", [128, NL, 44])
    bif_s = sb("bif_s", [4, NL, 2]); nbif_s = sb("nbif_s", [4, NL, 2])
    neglam = sb("neglam", [128, NL]); lamw = sb("lamw", [128, 4, 64]); lamw2 = sb("lamw2", [128, 4])
    subg_s = sb("subg_s", [128, NL, 128])
    prmB = Buf()
    lnp = sb("lnp", [128, 2, D]); lnpB = [Buf(), Buf()]
    sg = [sb("sg%d" % i, [128, 512], BF16) for i in range(2)]; sgB = [Buf(), Buf()]
    ybf = xbf; ybfB = xbfB
    stt = sb("stt", [128, 2, 6]); mv = sb("mv", [128, 4]); lnB = Buf()
    vtmp = xbf; vtmpB = xbfB
    ktmp = kw.rearrange("p (h t) -> p h t", h=HA); ktmpB = kwB
    ps = nc.alloc_psum_tensor("ps", [128, 8, 512], F32).ap()
    psB = [Buf() for _ in range(8)]

    def psb16(k):
        return ps[:, k, :].bitcast(BF16)

    def MM(out, lhsT, rhs, start, stop, r, w, skip=False):
        kw_ = dict(lhsT=lhsT, rhs=rhs, start=start, stop=stop)
        if skip:
            kw_["skip_group_check"] = True
        sch.add("pe", "matmul", (out,), kw_, r, w)

    def TR(out, in_, r, w):
        sch.add("pe", "transpose", (), dict(out=out, in_=in_, identity=identb[0:in_.shape[0], 0:in_.shape[0]]), r, w)

    def ACT(out, in_, func, r, w, bias=None, scale=None, accum_out=None):
        kw_ = dict(out=out, in_=in_, func=func)
        if bias is not None:
            kw_["bias"] = bias
        if scale is not None:
            kw_["scale"] = scale
        if accum_out is not None:
            kw_["accum_out"] = accum_out
        sch.add("act", "activation", (), kw_, r, w)

    def CP(eng, out, in_, r, w):
        if eng == "act":
            ACT(out, in_, AF.Identity, r, w)
        else:
            sch.add(eng, "tensor_copy", (), dict(out=out, in_=in_), r, w)

    def TT(out, in0, in1, op, r, w, eng="dve"):
        sch.add(eng, "tensor_tensor", (), dict(out=out, in0=in0, in1=in1, op=op), r, w)

    def TS(out, in0, s1, s2, op0, op1, r, w, eng="dve"):
        kw_ = dict(out=out, in0=in0, scalar1=s1, scalar2=s2, op0=op0)
        if op1 is not None:
            kw_["op1"] = op1
        sch.add(eng, "tensor_scalar", (), kw_, r, w)

    def STT(out, in0, scalar, in1, op0, op1, r, w, eng="dve"):
        sch.add(eng, "scalar_tensor_tensor", (), dict(out=out, in0=in0, scalar=scalar, in1=in1, op0=op0, op1=op1), r, w)

    def RSQRT(ap, addc, r, w):
        TS(ap, ap, addc, None, ALU.add, None, r, w)
        ACT(ap, ap, AF.Sqrt, r, w)
        sch.add("dve", "reciprocal", (), dict(out=ap, in_=ap), r, w)

    def RED(out, in_, op, r, w):
        sch.add("dve", "tensor_reduce", (), dict(out=out, in_=in_, axis=AX.X, op=op), r, w)

    def MEMSET(eng, ap, val, r, w):
        sch.add(eng, "memset", (ap, val), {}, r, w)

    def DMA(q, out, in_, r, w, key, slow=False):
        kw_ = dict(out=out, in_=in_)
        if slow:
            kw_["allow_slow_non_contiguous"] = True
        sch.add(q, "dma_start", (), kw_, r, w, dma=key)

    MEMSET("pool", identf, 1.0, [], [constB])
    sch.add("pool", "affine_select", (), dict(out=identf, in_=identf, compare_op=ALU.is_equal, fill=0.0, base=0,
                                              pattern=[[-1, 128]], channel_multiplier=1), [constB], [constB])
    CP("dve", identb, identf, [constB], [constB])
    MEMSET("pool", maskBD, 1.0, [constB], [constB])
    sch.add("pool", "affine_select", (), dict(out=maskBD, in_=maskBD, compare_op=ALU.is_ge, fill=0.0, base=0,
                                              pattern=[[1, 128]], channel_multiplier=-1), [constB], [constB])
    MEMSET("pool", maskBD[0:64, 64:128], 0.0, [constB], [constB])
    MEMSET("pool", ones4, 1.0, [constB], [constB])
    MEMSET("pool", resetm, 1.0, [constB], [constB])
    MEMSET("pool", resetm.rearrange("p (c l) -> p c l", l=64)[:, :, 0:1], 0.0, [constB], [constB])
    MEMSET("pool", resetm16, 1.0, [constB], [constB])
    MEMSET("pool", resetm16[:, 0:1], 0.0, [constB], [constB])
    MEMSET("pool", Vaug[:, :, :, 128:129], 1.0, [], VaugB)
    MEMSET("pool", VMaug[:, :, :, 256:257], 1.0, [], VMB)
    for i in range(2):
        MEMSET("pool", vch[i][:, :, 128:129], 1.0, [], [vchB[i]])
    pk = Buf()
    for l in range(NL):
        for j in range(4):
            DMA("sp", mcw_s[:, l, :, j], mcw[l, j].rearrange("(c p) -> p c", p=128), [], [prmB], pk, slow=True)
        DMA("sp", mcb_s[:, l, :], mcb[l].rearrange("(c p) -> p c", p=128), [], [prmB], pk, slow=True)
        for j in range(3):
            DMA("sp", fcw_s[:, l, :, j], fcw[l, j].rearrange("(c p) -> p c", p=128), [], [prmB], pk, slow=True)
        DMA("sp", fcb_s[:, l, :], fcb[l].rearrange("(c p) -> p c", p=128), [], [prmB], pk, slow=True)
        DMA("sp", bif_s[:, l, :], b_if[l].rearrange("(j p) -> p j", p=4), [], [prmB], pk, slow=True)
        DMA("sp", subg_s[:, l, :], subg[l].partition_broadcast(128), [], [prmB], pk)
        DMA("sp", lamw, dlam[l].partition_broadcast(128), [prmB], [prmB], pk)
        lam_init = 0.8 - 0.6 * math.exp(-0.3 * l)
        lv = lamw.rearrange("p (a b) d -> p a b d", b=2)
        TT(lamw[:, 0:2, :].rearrange("p a d -> p a d"), lv[:, :, 0, :], lv[:, :, 1, :], ALU.mult, [prmB], [prmB])
        RED(lamw2[:, 0:2], lamw[:, 0:2, :], ALU.add, [prmB], [prmB])
        ACT(lamw2[:, 2:4], lamw2[:, 0:2], AF.Exp, [prmB], [prmB])
        TT(neglam[:, l:l + 1], lamw2[:, 3:4], lamw2[:, 2:3], ALU.subtract, [prmB], [prmB])
        TS(neglam[:, l:l + 1], neglam[:, l:l + 1], -lam_init, None, ALU.add, None, [prmB], [prmB])
        TS(subg_s[:, l, :], subg_s[:, l, :], (1.0 - lam_init) * math.sqrt(128.0), None, ALU.mult, None, [prmB], [prmB])
    TS(nbif_s, bif_s, -1.0, None, ALU.mult, None, [prmB], [prmB])

    wq = []
    wstate = dict(loaded=0, used=0, released=0)

    def wspec_step(l, last):
        sp_ = []
        for c0 in (C_QA, C_QA + 512, C_KA, C_KA + 512, C_VA, C_VA + 512):
            sp_.append(("in", l, c0, 512))
        for c0 in range(C_QKM, C_QKM + 2048, 512):
            sp_.append(("in", l, c0, 512))
        if last:
            for c0 in range(C_QKM, C_QKM + 2048, 512):
                sp_.append(("in", l, c0, 512))
        sp_.append(("in", l, C_VM, 512)); sp_.append(("in", l, C_VM + 512, 512))
        sp_.append(("in", l, C_GIF, 8))
        for c0 in (C_OM, C_OM + 512, C_GA, C_GA + 512, C_GB, C_GB + 512):
            sp_.append(("in", l, c0, 512))
        sp_.append(("out", l, 0, 512)); sp_.append(("out", l, 512, 512))
        for g in range(11):
            sp_.append(("up", l, g * 512, 512))
        if last:
            for g in range(11):
                sp_.append(("up", l, g * 512, 512))
        for nh in range(2):
            for pc in range(3):
                sp_.append(("down", l, nh, pc))
        return sp_

    def wload(i):
        kind, l, a, b = wq[i]
        slot = i % NSLOT
        if kind == "in":
            src = w_in[l].rearrange("(kc p) n -> p kc n", p=128)[:, :, a:a + b]
            dst = ring[slot][:, :, 0:b]
        elif kind == "out":
            src = w_out[l].rearrange("(kc p) n -> p kc n", p=128)[:, :, a:a + b]
            dst = ring[slot][:, :, 0:b]
        elif kind == "up":
            src = w_up[l].rearrange("(kc p) n -> p kc n", p=128)[:, :, a:a + b]
            dst = ring[slot][:, :, 0:b]
        else:
            k0 = b * 8
            k1 = min(22, k0 + 8)
            src = w_down[l].rearrange("(kc p) n -> p kc n", p=128)[:, k0:k1, a * 512:(a + 1) * 512]
            dst = ring[slot][:, 0:k1 - k0, :]
        DMA("pool", dst, src, [], [ringB[slot]], ringB[slot])

    def wfill():
        while wstate["loaded"] < min(len(wq), wstate["released"] + NSLOT):
            wload(wstate["loaded"])
            wstate["loaded"] += 1

    def wnext(expect):
        i = wstate["used"]
        assert wq[i] == expect, (wq[i], expect)
        wfill()
        assert wstate["loaded"] > i
        wstate["used"] += 1
        return ring[i % NSLOT], ringB[i % NSLOT]

    def wdone(n=1):
        wstate["released"] += n
        assert wstate["released"] <= wstate["used"]
        wfill()

    rot = dict(pp=0, tp=0)

    def pbank():
        k = rot["pp"] % 4
        rot["pp"] += 1
        return k

    def tbank():
        k = 6 + rot["tp"] % 2
        rot["tp"] += 1
        return k

    evr = dict(i=0)

    def eveng():
        evr["i"] += 1
        return "act" if evr["i"] % 2 else "dve"

    def step(l, tp):
        ntok, bs, nblk, L = tp["ntok"], tp["bs"], tp["nblk"], tp["L"]
        cpb = bs // L
        nch = ntok // L
        last = tp["last"]
        Cg = CaugL[l]; CgB = CaugLB[l]; CgB2 = CaugLB2[l]

        def load_lnp(slot, src):
            DMA("sp", lnp[:, slot, :], src[l].partition_broadcast(128), [], [lnpB[slot]], lnpB[slot])
        load_lnp(0, mhg)
        if l == 0:
            for b in range(nblk):
                DMA("sp", xtok[0:bs, b, :], tp["x_src"][b * bs:(b + 1) * bs, :], [], [xtokB[b]], xtokB[b])
        DMA("sp", cs_t[0:bs, 0:nblk, 0, :], cosT[tp["pos0"]:tp["pos0"] + ntok, :].rearrange("(b p) d -> p b d", p=bs), [], [csB], csB)
        DMA("sp", cs_t[0:bs, 0:nblk, 1, :], sinT[tp["pos0"]:tp["pos0"] + ntok, :].rearrange("(b p) d -> p b d", p=bs), [], [csB], csB)
        if tp["load_state"]:
            for h in range(HM):
                DMA("sp", Cg[:, :, h, 0:256], tp["sC"][l, h].rearrange("(c p) v -> p c v", p=128), [], [CgB[h], CgB2[h]], CgB[h])
                DMA("sp", Cg[:, :, h, 256:257], tp["sn"][l, h].rearrange("(c p o) -> p c o", p=128, o=1), [], [CgB[h], CgB2[h]], CgB[h], slow=True)
            DMA("sp", mstate[:, l:l + 1], tp["sm"][l].rearrange("(p o) -> p o", o=1), [], [mstateB[l]], mstateB[l], slow=True)
            for j in range(3):
                DMA("sp", mhalo[:, l, :, j], tp["smc"][l, j].rearrange("(c p) -> p c", p=128), [], [mhaloB[l]], mhaloB[l], slow=True)
            for j in range(2):
                DMA("sp", fhalo[:, l, :, j], tp["sfc"][l, j].rearrange("(c p) -> p c", p=128), [], [fhaloB[l]], fhaloB[l], slow=True)
        elif tp["first"]:
            for h in range(HM):
                MEMSET("pool", Cg[:, :, h, :], 0.0, [], [CgB[h], CgB2[h]])
            MEMSET("pool", mstate[:, l:l + 1], 0.0, [], [mstateB[l]])
            MEMSET("pool", mhalo[:, l, :, :], 0.0, [], [mhaloB[l]])
            MEMSET("pool", fhalo[:, l, :, :], 0.0, [], [fhaloB[l]])

        def make_xT():
            for b in range(nblk):
                CP(eveng(), xbf[0:bs, :], xtok[0:bs, b, :], [xtokB[b]], [xbfB])
                k = tbank()
                pt = psb16(k)
                for c in range(8):
                    TR(pt[:, c * bs:(c + 1) * bs], xbf[0:bs, c * 128:(c + 1) * 128], [xbfB, constB], [psB[k]])
                CP(eveng(), xT[:, :, b * bs:(b + 1) * bs], pt[:, 0:8 * bs].rearrange("p (c t) -> p c t", c=8), [psB[k]], [xTB[b]])

        sch.phase = "xT"
        make_xT()

        def proj_tok(wslot, wB, b, ncols, kcn=8, lhs=None, lhsB=None):
            k = pbank()
            out = ps[0:bs, k, 0:ncols]
            for kc in range(kcn):
                lt = xT[:, kc, b * bs:(b + 1) * bs] if lhs is None else lhs(kc)
                MM(out, lt, wslot[:, kc, 0:ncols], kc == 0, kc == kcn - 1, [wB, xTB[b] if lhsB is None else lhsB], [psB[k]])
            return k, out

        def rope_block(src_zf, zfB, dst, roB, b):
            sv = src_zf.rearrange("p (g two d) -> p g two d", two=2, d=32)
            dv = dst.rearrange("p (g two d) -> p g two d", two=2, d=32)
            cosb = cs_t[0:bs, b, 0:1, :].to_broadcast([bs, 16, 32])
            sinb = cs_t[0:bs, b, 1:2, :].to_broadcast([bs, 16, 32])
            t0 = rt[0][0:bs, :].rearrange("p (g d) -> p g d", d=32)
            t1 = rt[1][0:bs, :].rearrange("p (g d) -> p g d", d=32)
            TT(t0, sv[:, :, 0, :], cosb, ALU.mult, [zfB, csB], [rtB[0]])
            TT(t1, sv[:, :, 1, :], sinb, ALU.mult, [zfB, csB], [rtB[1]])
            TT(dv[:, :, 0, :], t0, t1, ALU.subtract, [rtB[0], rtB[1]], roB)
            TT(t0, sv[:, :, 1, :], cosb, ALU.mult, [zfB, csB], [rtB[0]])
            TT(t1, sv[:, :, 0, :], sinb, ALU.mult, [zfB, csB], [rtB[1]])
            TT(dv[:, :, 1, :], t0, t1, ALU.add, [rtB[0], rtB[1]], roB)

        sch.phase = "qkv"
        zfs = [(zf, zfB), (hmf, hmfB)]
        ros = [(ro, [roB]), (ro2, ro2B)]
        pst = dict(n=0)

        def post(which, b, zb, zbB):
            if which == "v":
                DMA("sp", tp["v_out"][l, b * bs:(b + 1) * bs, :], zb[0:bs, :], [zbB], [], zbB)
                CP("dve", Vaug[0:bs, b, :, 0:128], zb[0:bs, :].rearrange("p (h d) -> p h d", h=HA), [zbB], [VaugB[b]])
                if tp["Vb_dst"] is not None:
                    DMA("sp", tp["Vb_dst"][b * bs:(b + 1) * bs, :].rearrange("p (h d) -> p h d", h=HA),
                        Vaug[0:bs, b, :, 0:128], [VaugB[b]], [tp["VbB"]], VaugB[b])
                return
            rb, rbB = ros[pst["n"] % 2]
            pst["n"] += 1
            rope_block(zb[0:bs, :], zbB, rb[0:bs, :], rbB, b)
            if which == "k":
                DMA("sp", tp["k_out"][l, b * bs:(b + 1) * bs, :], rb[0:bs, :], rbB, [], rbB[0])
            CP("act", xbf[0:bs, :], rb[0:bs, :], rbB, [xbfB])
            kb_ = tbank()
            pt = psb16(kb_)
            for h in range(HA):
                TR(pt[:, h * bs:(h + 1) * bs], xbf[0:bs, h * 128:(h + 1) * 128], [xbfB, constB], [psB[kb_]])
            dstT, dstB = (QT, QTB) if which == "q" else (KT, KTB)
            CP("act", dstT[:, :, b * bs:(b + 1) * bs], pt[:, 0:HA * bs].rearrange("p (h t) -> p h t", h=HA), [psB[kb_]], [dstB[b]])
            if which == "k" and b == nblk - 1 and tp["KT_dst"] is not None:
                DMA("sp", tp["KT_dst"].rearrange("h p t -> p h t"), KT[:, :, 0:ntok], KTB[0:nblk], [tp["KTB"]], KTB[0])

        pend = None
        nz = 0
        for which in ("q", "k", "v"):
            c0 = {"q": C_QA, "k": C_KA, "v": C_VA}[which]
            w0, w0B = wnext(("in", l, c0, 512))
            w1, w1B = wnext(("in", l, c0 + 512, 512))
            for b in range(nblk):
                zb, zbB = zfs[nz % 2]
                nz += 1
                for hf, (ws, wB) in enumerate(((w0, w0B), (w1, w1B))):
                    k, o = proj_tok(ws, wB, b, 512)
                    CP("act" if which != "v" else eveng(), zb[0:bs, hf * 512:(hf + 1) * 512], o, [psB[k]], [zbB])
                if pend is not None:
                    post(*pend)
                pend = (which, b, zb, zbB)
            wdone(2)
        post(*pend)

        sch.phase = "qkm"
        m_pend = [None]
        for g in range(4):
            ws, wB = wnext(("in", l, C_QKM + g * 512, 512))
            for cc4 in range(4):
                cc = g * 4 + cc4
                k = pbank()
                o = ps[:, k, 0:ntok]
                for kc in range(8):
                    MM(o, ws[:, kc, cc4 * 128:(cc4 + 1) * 128], xT[:, kc, 0:ntok], kc == 0, kc == 7, [wB] + xTB[0:nblk], [psB[k]])
                pi = cc % 2
                pcv = pcm[pi]
                CP("pool", pcv[:, 0:3], mhalo[:, l, cc, :], [mhaloB[l]], [pcmH[pi]])
                CP("act", pcv[:, 3:3 + ntok], o, [psB[k]], [pcmB[pi]])
                a = acc[pi][:, 0:ntok]
                ACT(a, o, AF.Identity, [psB[k], prmB], [accB[pi]], bias=mcb_s[:, l, cc:cc + 1], scale=mcw_s[:, l, cc, 3:4])
                CP("pool", mhalo[:, l, cc, :], pcv[:, ntok:ntok + 3], [pcmB[pi]], [mhaloB[l]])
                for j in range(3):
                    STT(a, pcv[:, j:j + ntok], mcw_s[:, l, cc, j:j + 1], a, ALU.mult, ALU.add, [pcmB[pi], pcmH[pi], prmB, accB[pi]], [accB[pi]])
                if m_pend[0] is not None:
                    ACT(qkT[:, m_pend[0][0], 0:ntok], m_pend[0][1], AF.Silu, [accB[m_pend[0][2]]], [qkTB[m_pend[0][0]]])
                m_pend[0] = (cc, a, pi)
            wdone(1)
        ACT(qkT[:, m_pend[0][0], 0:ntok], m_pend[0][1], AF.Silu, [accB[m_pend[0][2]]], [qkTB[m_pend[0][0]]])
        if last:
            for g in range(4):
                ws, wB = wnext(("in", l, C_QKM + g * 512, 512))
                k, o = proj_tok(ws, wB, nblk - 1, 512)
                CP("dve", stg[0:bs, :], o, [psB[k]], [stgB])
                DMA("sp", tp["mc_out"][l, :, g * 512:(g + 1) * 512], stg[bs - 3:bs, :], [stgB], [], stgB)
                wdone(1)
        sch.phase = "vm"
        w0, w0B = wnext(("in", l, C_VM, 512))
        w1, w1B = wnext(("in", l, C_VM + 512, 512))
        for b in range(nblk):
            for hf, (ws, wB) in enumerate(((w0, w0B), (w1, w1B))):
                k, o = proj_tok(ws, wB, b, 512)
                CP(eveng(), VMaug[0:bs, b, 2 * hf:2 * hf + 2, 0:256], o.rearrange("p (h d) -> p h d", h=2), [psB[k]], [VMB[b]])
        wdone(2)
        sch.phase = "gates"
        ws, wB = wnext(("in", l, C_GIF, 8))
        kI = pbank(); kF = pbank()
        for kc in range(8):
            MM(ps[0:4, kI, 0:ntok], ws[:, kc, 0:4], xT[:, kc, 0:ntok], kc == 0, kc == 7, [wB] + xTB[0:nblk], [psB[kI]])
        for kc in range(8):
            MM(ps[0:4, kF, 0:ntok], ws[:, kc, 4:8], xT[:, kc, 0:ntok], kc == 0, kc == 7, [wB] + xTB[0:nblk], [psB[kF]])
        wdone(1)
        g_ = {n: gw[n][:, 0:ntok] for n in gw}
        s_ = {n: gs[n][:, 0:nch] for n in gs}
        ACT(g_["ig"], ps[0:4, kI, 0:ntok], AF.Identity, [psB[kI], prmB], [gwB["ig"]], bias=bif_s[:, l, 0:1])
        ACT(g_["t1"], ps[0:4, kF, 0:ntok], AF.Exp, [psB[kF], prmB], [gwB["t1"]], bias=nbif_s[:, l, 1:2], scale=-1.0)
        TS(g_["t1"], g_["t1"], 1.0, None, ALU.add, None, [gwB["t1"]], [gwB["t1"]])
        ACT(g_["sp"], g_["t1"], AF.Ln, [gwB["t1"]], [gwB["sp"]])
        rm = resetm[:, 0:ntok] if L == 64 else resetm16[:, 0:ntok]
        sch.add("dve", "tensor_tensor_scan", (), dict(out=g_["A"], data0=rm, data1=g_["sp"], initial=0.0, op0=ALU.mult, op1=ALU.add),
                [gwB["sp"], constB], [gwB["A"]])
        TT(g_["r"], g_["ig"], g_["A"], ALU.add, [gwB["ig"], gwB["A"]], [gwB["r"]])
        RED(s_["cm"], g_["r"].rearrange("p (c l) -> p c l", l=L), ALU.max, [gwB["r"]], [gsB["cm"]])
        TS(s_["aL"], g_["A"].rearrange("p (c l) -> p c l", l=L)[:, :, L - 1], -1.0, None, ALU.mult, None, [gwB["A"]], [gsB["aL"]])
        sch.add("dve", "tensor_tensor_scan", (), dict(out=s_["mn"], data0=s_["cm"], data1=s_["aL"], initial=mstate[:, l:l + 1],
                                                       op0=ALU.max, op1=ALU.add), [gsB["cm"], gsB["aL"], mstateB[l]], [gsB["mn"]])
        TT(s_["mu"], s_["mn"], s_["aL"], ALU.subtract, [gsB["mn"], gsB["aL"]], [gsB["mu"]])
        CP("dve", gs["mpv"][:, 0:1], mstate[:, l:l + 1], [mstateB[l]], [gsB["mpv"]])
        if nch > 1:
            CP("dve", gs["mpv"][:, 1:nch], gs["mn"][:, 0:nch - 1], [gsB["mn"]], [gsB["mpv"]])
        CP("dve", mstate[:, l:l + 1], gs["mn"][:, nch - 1:nch], [gsB["mn"], gsB["mpv"]], [mstateB[l]])
        TT(s_["gam"], s_["mpv"], s_["mu"], ALU.subtract, [gsB["mpv"], gsB["mu"]], [gsB["gam"]])
        ACT(s_["gam"], s_["gam"], AF.Exp, [gsB["gam"]], [gsB["gam"]])
        mub = s_["mu"].rearrange("p (c o) -> p c o", o=1).to_broadcast([4, nch, L])
        TT(g_["sc"].rearrange("p (c l) -> p c l", l=L), g_["r"].rearrange("p (c l) -> p c l", l=L), mub, ALU.subtract, [gwB["r"], gsB["mu"]], [gwB["sc"]])
        ACT(g_["sc"], g_["sc"], AF.Exp, [gwB["sc"]], [gwB["sc"]])
        TS(g_["sc"], g_["sc"], 1.0 / 16.0, None, ALU.mult, None, [gwB["sc"]], [gwB["sc"]])
        TT(g_["cl"].rearrange("p (c l) -> p c l", l=L), g_["A"].rearrange("p (c l) -> p c l", l=L), mub, ALU.subtract, [gwB["A"], gsB["mu"]], [gwB["cl"]])
        ACT(g_["cl"], g_["cl"], AF.Exp, [gwB["cl"]], [gwB["cl"]])
        kG = pbank()
        pg = ps[0:bs, kG, 0:nblk * 8].rearrange("p (b e) -> p b e", e=8)
        for b in range(nblk):
            MM(pg[:, b, 0:4], g_["sc"][:, b * bs:(b + 1) * bs], identf[0:4, 0:4], True, True, [gwB["sc"], constB], [psB[kG]])
            MM(pg[:, b, 4:8], g_["cl"][:, b * bs:(b + 1) * bs], identf[0:4, 0:4], True, True, [gwB["cl"], constB], [psB[kG]])
        CP("dve", toksc[0:bs, 0:nblk, :], pg, [psB[kG]], [tokscB])
        TT(gD[:, 0:nch, :], s_["gam"].rearrange("p (c o) -> p c o", o=1).to_broadcast([4, nch, 4]),
           identf[0:4, 0:4].rearrange("p (o h) -> p o h", o=1).to_broadcast([4, nch, 4]), ALU.mult, [gsB["gam"], constB], [gDB])
        kG2 = pbank()
        MM(ps[:, kG2, 0:nch * 4], ones4, gD[:, 0:nch, :].rearrange("p c h -> p (c h)"), True, True, [gDB, constB], [psB[kG2]])
        CP("dve", gam[:, 0:nch, :], ps[:, kG2, 0:nch * 4].rearrange("p (c h) -> p c h", h=4), [psB[kG2]], [gamB])

        sch.phase = "attn"
        prior = tp["prior"]
        ngr = len(prior) + 1
        att = dict(si=0)
        sub_pend = [None]

        def subln_head(h):
            ovh = oaF[0:bs, 0:nblk, h * 128:(h + 1) * 128]
            sqv = hmf[0:bs, 512:512 + nblk * 128].rearrange("p (q d) -> p q d", d=128)
            TT(sqv, ovh, ovh, ALU.mult, oaFB[0:nblk], [hmfB])
            RED(sml[0:bs, 64 + h * 4:64 + h * 4 + nblk], sqv, ALU.add, [hmfB], [sml2B])

        for h in range(HA):
            started = set()
            items = []
            for g in range(ngr):
                own = g == ngr - 1
                for kb in range(nblk if own else T // 128):
                    items.append((g, own, kb))

            def emit_scores(it):
                g, own, kb = it
                if not own:
                    sl = (h * ngr + g) % 2
                    if kb == 0:
                        kd, vd, dB = prior[g]
                        DMA("sp", kch[sl][:, 0:T], kd[h], dB, [kchB[sl]], kchB[sl])
                        DMA("sp", vch[sl][:, :, 0:128], vd[:, h * 128:(h + 1) * 128].rearrange("(b p) d -> p b d", p=128), dB, [vchB[sl]], vchB[sl])
                    kT_ = kch[sl][:, kb * 128:(kb + 1) * 128]; kTB_ = kchB[sl]
                    v_ = vch[sl][:, kb, :]; vB_ = vchB[sl]
                    q0 = 0
                    nk = 128
                else:
                    kT_ = KT[:, h, kb * bs:(kb + 1) * bs]; kTB_ = KTB[kb]
                    v_ = Vaug[0:bs, kb, h, :]; vB_ = VaugB[kb]
                    q0 = kb * bs if tp["mask"] else 0
                    nk = bs
                nq = ntok - q0
                sb_ = att["si"] % 2
                att["si"] += 1
                b0, b1 = 2 * sb_, 2 * sb_ + 1
                MM(ps[0:nk, b0, 0:nq], kT_[0:64, :], QT[0:64, h, q0:ntok], True, True, [kTB_] + QTB[0:nblk], [psB[b0]])
                MM(ps[0:nk, b1, 0:nq], kT_[64:128, :], QT[64:128, h, q0:ntok], True, True, [kTB_] + QTB[0:nblk], [psB[b1]])
                ACT(PT[sb_][0:nk, :, 0:nq], ps[0:nk, b0:b1 + 1, 0:nq], AF.Exp, [psB[b0], psB[b1]], [PTB[sb_]], scale=0.125)
                if own and tp["mask"]:
                    MEMSET("pool", PT[sb_][64:128, :, 0:64], 0.0, [PTB[sb_]], [PTB[sb_]])
                return (g, own, kb, sb_, nk, q0, v_, vB_)

            def emit_av(st):
                g, own, kb, sb_, nk, q0, v_, vB_ = st
                nkb = nblk if own else T // 128
                qb0 = kb if (own and tp["mask"]) else 0
                for qb in range(qb0, nblk):
                    for mp_ in range(2):
                        a_ = qb * 2 + mp_
                        bank = 4 + a_ // 3
                        off = (a_ % 3) * 129
                        col = qb * bs - q0
                        first = (g == 0 and kb == 0) and bank not in started
                        started.add(bank)
                        lastk = own and (kb == (qb if tp["mask"] else nkb - 1))
                        MM(ps[0:bs, bank, off:off + 129], PT[sb_][0:nk, mp_, col:col + bs], v_, first, lastk, [PTB[sb_], vB_], [psB[bank]], skip=True)

            prev = None
            for it in items:
                st = emit_scores(it)
                if prev is not None:
                    emit_av(prev)
                prev = st
            emit_av(prev)
            CP("dve", zf[0:bs, 0:387], ps[0:bs, 4, 0:387], [psB[4]], [zfB])
            CP("dve", zf[0:bs, 387:774], ps[0:bs, 5, 0:387], [psB[5]], [zfB])
            CP("dve", hmf[0:bs, 0:258], ps[0:bs, 6, 0:258], [psB[6]], [hmfB])

            def accv(a_):
                if a_ < 6:
                    return zf[0:bs, a_ * 129:(a_ + 1) * 129], zfB
                return hmf[0:bs, (a_ - 6) * 129:(a_ - 5) * 129], hmfB

            for qb in range(nblk):
                o0, o0B = accv(qb * 2)
                o1, o1B = accv(qb * 2 + 1)
                r0 = sml[0:bs, 0:1]; r1 = sml[0:bs, 1:2]
                sch.add("dve", "reciprocal", (), dict(out=r0, in_=o0[:, 128:129]), [o0B], [smlB])
                sch.add("dve", "reciprocal", (), dict(out=r1, in_=o1[:, 128:129]), [o1B], [smlB])
                TT(r1, r1, neglam[0:bs, l:l + 1], ALU.mult, [smlB, prmB], [smlB])
                TS(o0[:, 0:128], o0[:, 0:128], r0, None, ALU.mult, None, [o0B, smlB], [o0B])
                STT(oaF[0:bs, qb, h * 128:(h + 1) * 128], o1[:, 0:128], r1, o0[:, 0:128], ALU.mult, ALU.add,
                    [o1B, o0B, smlB], [oaFB[qb]])
            if sub_pend[0] is not None:
                subln_head(sub_pend[0])
            sub_pend[0] = h
        subln_head(sub_pend[0])
        RSQRT(sml[0:bs, 64:96], 128.0 * LN_EPS, [sml2B], [sml2B])
        for qb in range(nblk):
            ov = oaF[0:bs, qb, :].rearrange("p (h d) -> p h d", h=HA)
            rs = sml[0:bs, 64:96].rearrange("p (h q) -> p h q", q=4)[:, :, qb:qb + 1].to_broadcast([bs, HA, 128])
            TT(ov, ov, rs, ALU.mult, [oaFB[qb], sml2B], [oaFB[qb]])
            TT(ov, ov, subg_s[0:bs, l, :].rearrange("p (o d) -> p o d", o=1).to_broadcast([bs, HA, 128]), ALU.mult, [oaFB[qb], prmB], [oabB[qb]])

        sch.phase = "mlstm"
        ln_pend = [None]

        def head_ln(b):
            for h in range(HM):
                hv_ = hmf[0:bs, h * 256:(h + 1) * 256]
                ACT(zf[0:bs, h * 256:(h + 1) * 256], hv_, AF.Identity, [hmfB], [zfB, smlB], accum_out=sml[0:bs, 24 + h:25 + h])
                ACT(zf[0:bs, h * 256:(h + 1) * 256], hv_, AF.Square, [hmfB], [zfB, smlB], accum_out=sml[0:bs, 28 + h:29 + h])
            mean_ = sml[0:bs, 24:28]; ex2_ = sml[0:bs, 28:32]; nmr_ = sml[0:bs, 32:36]
            TS(mean_, mean_, 1.0 / 256.0, None, ALU.mult, None, [smlB], [smlB])
            TT(nmr_, mean_, mean_, ALU.mult, [smlB], [smlB])
            STT(ex2_, ex2_, 1.0 / 256.0, nmr_, ALU.mult, ALU.subtract, [smlB], [smlB])
            RSQRT(ex2_, LN_EPS, [smlB], [smlB])
            STT(nmr_, mean_, -1.0, ex2_, ALU.mult, ALU.mult, [smlB], [smlB])
            for h in range(HM):
                hv_ = hmf[0:bs, h * 256:(h + 1) * 256]
                ACT(hv_, hv_, AF.Identity, [hmfB, smlB], [hmfB], bias=sml[0:bs, 32 + h:33 + h], scale=sml[0:bs, 28 + h:29 + h])
            TT(hmb[0:bs, b, :], hmf[0:bs, :], lnp[0:bs, 0, :], ALU.mult, [hmfB, lnpB[0]], [hmbB[b]])

        for b in range(nblk):
            kb_ = tbank()
            pt = psb16(kb_)
            for j in range(8):
                TR(pt[0:bs, j * 128:(j + 1) * 128], qkT[:, 8 + j, b * bs:(b + 1) * bs], [qkTB[8 + j], constB], [psB[kb_]])
            for h in range(HM):
                ACT(kw[0:bs, h * 256:(h + 1) * 256], pt[0:bs, h * 256:(h + 1) * 256], AF.Identity, [psB[kb_], tokscB], [kwB], scale=toksc[0:bs, b, h:h + 1])
            kS = pbank()
            for h in range(HM):
                for dk in range(2):
                    MM(ps[0:bs, kS, h * 128:h * 128 + bs], qkT[:, 8 + 2 * h + dk, b * bs:(b + 1) * bs], qkT[:, 2 * h + dk, b * bs:(b + 1) * bs],
                       dk == 0, dk == 1, [qkTB[8 + 2 * h + dk], qkTB[2 * h + dk]], [psB[kS]])
            for h in range(HM):
                STT(Sm[0:bs, h, 0:bs], ps[0:bs, kS, h * 128:h * 128 + bs], toksc[0:bs, b, h:h + 1], maskBD[0:bs, 0:bs], ALU.mult, ALU.mult,
                    [psB[kS], tokscB, constB], [SmB])
            if ln_pend[0] is not None:
                head_ln(ln_pend[0])
                ln_pend[0] = None
            for ci in range(cpb):
                c = b * cpb + ci
                p0, p1 = ci * L, ci * L + L
                t0_, t1_ = b * bs + ci * L, b * bs + ci * L + L
                Gc, GcB = GbL[c % 2]
                if c == 0:
                    for h in range(HM):
                        ACT(Gc[:, :, h, :], Cg[:, :, h, :], AF.Identity, [CgB[h], CgB2[h], gamB], [GcB[h]], scale=gam[:, c, h:h + 1])
                for h in range(HM):
                    for dk in range(2):
                        kC = 4 + (h * 2 + dk) % 4
                        dC = ps[:, kC, 0:257]
                        MM(dC, kw[p0:p1, h * 256 + dk * 128:h * 256 + dk * 128 + 128], VMaug[p0:p1, b, h, :], True, True, [kwB, VMB[b]], [psB[kC]])
                        STT(Cg[:, dk, h, :], Cg[:, dk, h, :], gam[:, c, h:h + 1], dC, ALU.mult, ALU.add, [(CgB, CgB2)[dk][h], gamB, psB[kC]], [(CgB, CgB2)[dk][h]])
                if c + 1 < nch:
                    Gn, GnB = GbL[(c + 1) % 2]
                    for h in range(HM):
                        ACT(Gn[:, :, h, :], Cg[:, :, h, :], AF.Identity, [CgB[h], CgB2[h], gamB], [GnB[h]], scale=gam[:, c + 1, h:h + 1])
                for h in range(HM):
                    nd = ps[p0:p1, h, 0:257]
                    MM(nd, qkT[:, 2 * h, t0_:t1_], Gc[:, 0, h, :], True, False, [qkTB[2 * h], GcB[h]], [psB[h]])
                    MM(nd, qkT[:, 2 * h + 1, t0_:t1_], Gc[:, 1, h, :], False, False, [qkTB[2 * h + 1], GcB[h]], [psB[h]])
                    MM(nd, Sm[p0:p1, h, p0:p1], VMaug[p0:p1, b, h, :], False, True, [SmB, VMB[b]], [psB[h]])
                den = ps[p0:p1, 0:HM, 256]
                dd = sml[p0:p1, 16:16 + HM]
                TS(dd, den, -1.0, None, ALU.mult, None, psB[0:HM], [smlB])
                TT(dd, dd, den, ALU.max, [smlB] + psB[0:HM], [smlB])
                TT(dd, dd, toksc[p0:p1, b, 4:4 + HM], ALU.max, [smlB, tokscB], [smlB])
                sch.add("dve", "reciprocal", (), dict(out=dd, in_=dd), [smlB], [smlB])
                for h in range(HM):
                    ACT(hmf[p0:p1, h * 256:(h + 1) * 256], ps[p0:p1, h, 0:256], AF.Identity, [psB[h], smlB], [hmfB], scale=sml[p0:p1, 16 + h:17 + h])
            ln_pend[0] = b
        head_ln(ln_pend[0])

        sch.phase = "merge"
        for gi, c0 in enumerate((C_OM, C_OM + 512, C_GA, C_GA + 512, C_GB, C_GB + 512)):
            ws, wB = wnext(("in", l, c0, 512))
            hf = gi % 2
            for b in range(nblk):
                k, o = proj_tok(ws, wB, b, 512)
                si_ = (gi * nblk + b) % 2
                ACT(sg[si_][0:bs, :], o, AF.Sigmoid, [psB[k]], [sgB[si_]])
                if gi in (2, 3):
                    tgt, tB = oab, oabB
                else:
                    tgt, tB = hmb, hmbB
                TT(tgt[0:bs, b, hf * 512:(hf + 1) * 512], tgt[0:bs, b, hf * 512:(hf + 1) * 512], sg[si_][0:bs, :], ALU.mult, [sgB[si_], tB[b]], [tB[b]])
            wdone(1)
        for b in range(nblk):
            TT(ybf[0:bs, :], oab[0:bs, b, :], hmb[0:bs, b, :], ALU.add, [oabB[b], hmbB[b]], [ybfB])
            k = tbank()
            pt = psb16(k)
            for c in range(8):
                TR(pt[:, c * bs:(c + 1) * bs], ybf[0:bs, c * 128:(c + 1) * 128], [ybfB, constB], [psB[k]])
            CP(eveng(), xT[:, :, b * bs:(b + 1) * bs], pt[:, 0:8 * bs].rearrange("p (c t) -> p c t", c=8), [psB[k]], [xTB[b]])

        def layernorm_inplace(b, gi, bi):
            xv = xtok[0:bs, b, :]
            for hh in range(2):
                sch.add("dve", "bn_stats", (), dict(out=stt[0:bs, hh, :], in_=xtok[0:bs, b, hh * 512:(hh + 1) * 512]), [xtokB[b]], [lnB])
            sch.add("dve", "bn_aggr", (), dict(out=mv[0:bs, 0:2], in_=stt[0:bs, :, :]), [lnB], [lnB])
            CP("dve", mv[0:bs, 2:3], mv[0:bs, 1:2], [lnB], [lnB])
            RSQRT(mv[0:bs, 2:3], LN_EPS, [lnB], [lnB])
            STT(mv[0:bs, 3:4], mv[0:bs, 0:1], -1.0, mv[0:bs, 2:3], ALU.mult, ALU.mult, [lnB], [lnB])
            ACT(xv, xv, AF.Identity, [xtokB[b], lnB], [xtokB[b]], bias=mv[0:bs, 3:4], scale=mv[0:bs, 2:3])
            TT(xv, xv, lnp[0:bs, 0, :], ALU.mult, [xtokB[b], lnpB[0]], [xtokB[b]])
            TT(xv, xv, lnp[0:bs, 1, :], ALU.add, [xtokB[b], lnpB[1]], [xtokB[b]])

        sch.phase = "wout"
        load_lnp(0, ln1g); load_lnp(1, ln1b)
        w0, w0B = wnext(("out", l, 0, 512))
        w1, w1B = wnext(("out", l, 512, 512))
        for b in range(nblk):
            for hf, (ws, wB) in enumerate(((w0, w0B), (w1, w1B))):
                k, o = proj_tok(ws, wB, b, 512)
                xs_ = xtok[0:bs, b, hf * 512:(hf + 1) * 512]
                STT(xs_, xs_, ALPHA, o, ALU.mult, ALU.add, [xtokB[b], psB[k]], [xtokB[b]])
            layernorm_inplace(b, 1, 2)
        wdone(2)
        make_xT()

        if not tp["load_state"]:
            sample_prep(prep_per_step)
        sch.phase = "up"
        up_pend = [None]

        def up_final(cc, a, pi):
            if cc < 22:
                ACT(hT[:, cc, 0:ntok], a, AF.Gelu, [accB[pi]], [hTB[cc]])
            else:
                TT(hT[:, cc - 22, 0:ntok], hT[:, cc - 22, 0:ntok], a, ALU.mult, [hTB[cc - 22], accB[pi]], [hTB[cc - 22]])

        for g in range(11):
            ws, wB = wnext(("up", l, g * 512, 512))
            for cc4 in range(4):
                cc = g * 4 + cc4
                k = pbank()
                o = ps[:, k, 0:ntok]
                for kc in range(8):
                    MM(o, ws[:, kc, cc4 * 128:(cc4 + 1) * 128], xT[:, kc, 0:ntok], kc == 0, kc == 7, [wB] + xTB[0:nblk], [psB[k]])
                pi = cc % 2
                pcv = pcm[pi]
                CP("pool", pcv[:, 0:2], fhalo[:, l, cc, :], [fhaloB[l]], [pcmH[pi]])
                CP("act", pcv[:, 2:2 + ntok], o, [psB[k]], [pcmB[pi]])
                a = acc[pi][:, 0:ntok]
                ACT(a, o, AF.Identity, [psB[k], prmB], [accB[pi]], bias=fcb_s[:, l, cc:cc + 1], scale=fcw_s[:, l, cc, 2:3])
                CP("pool", fhalo[:, l, cc, :], pcv[:, ntok:ntok + 2], [pcmB[pi]], [fhaloB[l]])
                for j in range(2):
                    STT(a, pcv[:, j:j + ntok], fcw_s[:, l, cc, j:j + 1], a, ALU.mult, ALU.add, [pcmB[pi], pcmH[pi], prmB, accB[pi]], [accB[pi]])
                if up_pend[0] is not None:
                    up_final(*up_pend[0])
                up_pend[0] = (cc, a, pi)
            wdone(1)
        up_final(*up_pend[0])
        if last:
            for g in range(11):
                ws, wB = wnext(("up", l, g * 512, 512))
                k, o = proj_tok(ws, wB, nblk - 1, 512)
                CP("dve", stg[0:bs, :], o, [psB[k]], [stgB])
                DMA("sp", tp["fc_out"][l, :, g * 512:(g + 1) * 512], stg[bs - 2:bs, :], [stgB], [], stgB)
                wdone(1)
        sch.phase = "down"
        load_lnp(0, ln2g); load_lnp(1, ln2b)
        for nh in range(2):
            pcs = [wnext(("down", l, nh, pc)) for pc in range(3)]
            for b in range(nblk):
                k = pbank()
                o = ps[0:bs, k, 0:512]
                for kc in range(22):
                    ws, wB = pcs[kc // 8]
                    MM(o, hT[:, kc, b * bs:(b + 1) * bs], ws[:, kc % 8, :], kc == 0, kc == 21, [wB, hTB[kc]], [psB[k]])
                xs_ = xtok[0:bs, b, nh * 512:(nh + 1) * 512]
                STT(xs_, xs_, ALPHA, o, ALU.mult, ALU.add, [xtokB[b], psB[k]], [xtokB[b]])
                if nh == 1:
                    layernorm_inplace(b, 3, 4)
                    if l == NL - 1:
                        DMA("sp", tp["y_out"][b * bs:(b + 1) * bs, :], xtok[0:bs, b, :], [xtokB[b]], [], xtokB[b])
            wdone(3)
        if last:
            for h in range(HM):
                DMA("sp", tp["C_out"][l, h].rearrange("(c p) v -> p c v", p=128), Cg[:, :, h, 0:256], [CgB[h], CgB2[h]], [], CgB[h])
                DMA("sp", tp["n_out"][l, h].rearrange("(c p o) -> p c o", p=128, o=1), Cg[:, :, h, 256:257], [CgB[h], CgB2[h]], [], CgB[h], slow=True)
            DMA("sp", tp["m_out"][l].rearrange("(p o) -> p o", o=1), mstate[:, l:l + 1], [mstateB[l]], [], mstateB[l], slow=True)

    prep_q = [(l, blk) for l in range(NL) for blk in range(P // 128)]

    def sample_prep(n):
        ph = sch.phase
        sch.phase = "prep"
        for _ in range(n):
            if not prep_q:
                break
            l, blk = prep_q.pop(0)
            g = (blk * 128) // T
            DMA("pool", vtmp, ck[l, blk * 128:(blk + 1) * 128, :], [], [vtmpB], vtmpB)
            k = tbank()
            pt = psb16(k)
            for h in range(HA):
                TR(pt[:, h * 128:(h + 1) * 128], vtmp[:, h * 128:(h + 1) * 128], [vtmpB, constB], [psB[k]])
            CP(eveng(), ktmp, pt.rearrange("p (h t) -> p h t", h=HA), [psB[k]], [ktmpB])
            DMA("sp", KTs[l][:, :, blk * 128:(blk + 1) * 128].rearrange("h p t -> p h t"), ktmp, [ktmpB], [KTsB[l][g]], ktmpB)
            DMA("pool", ybf, cv[l, blk * 128:(blk + 1) * 128, :], [], [ybfB], ybfB)
            DMA("sp", Vbs[l, blk * 128:(blk + 1) * 128, :], ybf, [ybfB], [VbsB[l][g]], ybfB)
        sch.phase = ph

    steps = []
    for i in range(NT):
        for l in range(NL):
            steps.append((l, i, False))
    for l in range(NL):
        steps.append((l, 0, True))
    for (l, i, samp) in steps:
        wq.extend(wspec_step(l, samp or i == NT - 1))

    prep_done = False
    prep_per_step = -(-len(prep_q) // max(1, NT * NL))
    for (l, i, samp) in steps:
        if samp and not prep_done:
            sample_prep(len(prep_q))
            prep_done = True
        if not samp:
            tp = dict(ntok=T, bs=128, nblk=NB, L=64, last=(i == NT - 1), first=(i == 0), load_state=False,
                      x_src=xp[i * T:(i + 1) * T, :], pos0=i * T, tok0=i * T, mask=True,
                      prior=[(KTp[l][:, :, g * T:(g + 1) * T], Vbp[l, g * T:(g + 1) * T, :], [KTpB[l][g], VbpB[l][g]]) for g in range(i)],
                      KT_dst=KTp[l][:, :, i * T:(i + 1) * T], KTB=KTpB[l][i], Vb_dst=Vbp[l, i * T:(i + 1) * T, :], VbB=VbpB[l][i],
                      k_out=kp[:, i * T:(i + 1) * T, :], v_out=vp[:, i * T:(i + 1) * T, :], y_out=yp[i * T:(i + 1) * T, :],
                      mc_out=mcp, fc_out=fcp, C_out=Cp, n_out=np_, m_out=mp)
        else:
            tp = dict(ntok=NS, bs=NS, nblk=1, L=NS, last=True, first=False, load_state=True,
                      x_src=xs, pos0=S, tok0=0, mask=False,
                      prior=[(KTs[l][:, :, g * T:(g + 1) * T], Vbs[l, g * T:(g + 1) * T, :], [KTsB[l][g], VbsB[l][g]]) for g in range(P // T)],
                      KT_dst=None, KTB=None, Vb_dst=None, VbB=None,
                      k_out=ks, v_out=vs, y_out=ys, mc_out=mcs, fc_out=fcs, C_out=Cs, n_out=ns_, m_out=ms,
                      sC=sC, sn=sn, sm=sm, smc=smc, sfc=sfc)
        step(l, tp)
    assert wstate["used"] == len(wq) and wstate["released"] == len(wq), (wstate, len(wq))
    print("SBUF/PSUM allocation done")
    info = sch.emit()
    return nc, info


def rope_tables(S, P):
    half = 32
    inv = (np.float32(10000.0) ** (-np.arange(half, dtype=np.float32) * np.float32(2.0) / np.float32(64))).astype(np.float32)
    pos = np.concatenate([np.arange(S), P + np.arange(NS)]).astype(np.float32)
    ang = (pos[:, None] * inv[None, :]).astype(np.float32)
    return np.cos(ang).astype(np.float32), np.sin(ang).astype(np.float32)


_CACHE = {}


def run(inputs, S, P, T, n_prompt, n_sample, n_cores):
    key = (S, P, T)
    if key not in _CACHE:
        _CACHE[key] = build(S=S, P=P, T=T)
    nc, info = _CACHE[key]
    f = lambda a: np.ascontiguousarray(np.asarray(a, dtype=np.float32))
    cosT, sinT = rope_tables(S, P)
    NL = 2
    in_maps = []
    for c in range(n_cores):
        b = c % n_prompt
        s = c % n_sample
        m = {
            "xp": f(inputs["x_prompt"][b]), "xs": f(inputs["x_sample"][s]),
            "ck": f(inputs["cache_k"][:, s]).reshape(NL, P, D), "cv": f(inputs["cache_v"][:, s]).reshape(NL, P, D),
            "smc": f(inputs["state_mlstm_conv"][:, s]), "sC": f(inputs["state_mlstm_C"][:, s]),
            "sn": f(inputs["state_mlstm_n"][:, s]), "sm": f(inputs["state_mlstm_m"][:, s]),
            "sfc": f(inputs["state_ffn_conv"][:, s]),
            "w_in": f(inputs["w_in"]), "b_if": f(inputs["b_if"]), "mcw": f(inputs["mlstm_conv_w"]), "mcb": f(inputs["mlstm_conv_b"]),
            "dlam": f(inputs["diff_lambda"]), "subg": f(inputs["diff_subln_g"]), "mhg": f(inputs["mlstm_norm_g"]),
            "w_out": f(inputs["w_out"]), "ln1g": f(inputs["ln1_g"]), "ln1b": f(inputs["ln1_b"]),
            "w_up": f(inputs["w_up"]), "fcw": f(inputs["ffn_conv_w"]), "fcb": f(inputs["ffn_conv_b"]),
            "w_down": f(inputs["w_down"]), "ln2g": f(inputs["ln2_g"]), "ln2b": f(inputs["ln2_b"]),
            "cosT": cosT, "sinT": sinT,
        }
        in_maps.append(m)
    res = run_bass_kernel_spmd(nc, in_maps, core_ids=list(range(n_cores)))
    R = res.results
    pc = list(range(n_prompt))
    sc = list(range(n_sample))
    st = lambda name, cores, ax=0: np.stack([np.asarray(R[c][name], dtype=np.float32) for c in cores], axis=ax)
    y_prompt = st("yp", pc)
    y_sample = st("ys", sc)
    k_prompt = st("kp", pc, 1).reshape(NL, n_prompt, S, HA, 128)
    v_prompt = st("vp", pc, 1).reshape(NL, n_prompt, S, HA, 128)
    outs = (y_prompt, y_sample, k_prompt, v_prompt,
            st("mcp", pc, 1), st("Cp", pc, 1), st("np", pc, 1), st("mp", pc, 1), st("fcp", pc, 1),
            st("ks", sc, 1).reshape(NL, n_sample, NS, HA, 128), st("vs", sc, 1).reshape(NL, n_sample, NS, HA, 128),
            st("mcs", sc, 1), st("Cs", sc, 1), st("ns", sc, 1), st("ms", sc, 1), st("fcs", sc, 1))
    return outs


def kernel(**inputs):
    return run(inputs, S=8192, P=4096, T=512, n_prompt=4, n_sample=8, n_cores=8)
```
